# Optimizing a Trainium2 kernel written in Bass

```python
import math
import jax
import jax.numpy as jnp
from jax import lax
import numpy as np

D_MODEL = 1024
BATCH = 16
SEQ = 256
DEPTH = 4
DEC_BATCH = 8
DEC_SEQ = 1024
PAST_LEN = 512

GRID_W = 64
N_MIXERS = 2
N_GDN = (DEPTH + 1) // 2
N_HYENA = DEPTH // 2
GDN_HEADS = 8
GDN_DK = 128
GDN_DV = 128
GDN_QK_W = GDN_HEADS * GDN_DK
GDN_V_W = GDN_HEADS * GDN_DV
GDN_CONV_W = 2 * GDN_QK_W + GDN_V_W
GDN_IN_W = GDN_CONV_W + GDN_V_W + 4 * GDN_HEADS
GDN_CONV_K = 5
CHUNK = 64
HY_ORDER = 2
HY_CONV_K = 3
HY_EMB = 33
HY_BANDS = (HY_EMB - 1) // 2
HY_FILTER_HID = 64
HY_TARGET = 1e-2
HY_SHORT_PCT = 0.3
HY_LONG_PCT = 1.5
HY_MAX_DECAY = math.log(HY_TARGET) / HY_SHORT_PCT
HY_MIN_DECAY = math.log(HY_TARGET) / HY_LONG_PCT
D_FF = (8 * D_MODEL + 3 * 256 - 1) // (3 * 256) * 256
POS_BASE = 10000.0
EPS = 1e-6

kernel_name = 'hybrid_gdn_hyena_diffusion_step'


def _rms_norm(x, g):
    xf = x.astype(jnp.float32)
    y = xf * lax.rsqrt(jnp.mean(xf * xf, axis=-1, keepdims=True) + EPS)
    return (y * g.astype(jnp.float32)).astype(x.dtype)


def _ada(cvec, w, b):
    m = jax.nn.silu(cvec) @ w + b
    return [t[:, None, :] for t in jnp.split(m, 6, axis=-1)]


def _modulate(h, shift, scale):
    return h * (1.0 + scale) + shift


def _dwconv(x, w):
    k, ch = w.shape
    return lax.conv_general_dilated(
        x, w[:, None, :].astype(x.dtype), window_strides=(1,),
        padding=[(k // 2, k // 2)], dimension_numbers=('NWC', 'WIO', 'NWC'),
        feature_group_count=ch)


def _l2norm(x):
    return x * lax.rsqrt(jnp.sum(x * x, axis=-1, keepdims=True) + EPS)


def _grid_pos_emb(n_tokens, dtype):
    rows = n_tokens // GRID_W
    r, col = jnp.meshgrid(jnp.arange(rows), jnp.arange(GRID_W), indexing='ij')
    quarter = D_MODEL // 4
    omega = 1.0 / (POS_BASE ** (jnp.arange(quarter, dtype=jnp.float32) / quarter))

    def emb1d(p):
        a = p.reshape(-1, 1).astype(jnp.float32) * omega[None, :]
        return jnp.concatenate([jnp.sin(a), jnp.cos(a)], axis=-1)

    return jnp.concatenate([emb1d(r), emb1d(col)], axis=-1).astype(dtype)


def _chunk_gated_delta(q, k, v, g, beta, s0):
    b_sz, seq, heads, _ = q.shape
    dv = v.shape[-1]
    n = seq // CHUNK

    def blocks(t):
        t = t.reshape((b_sz, n, CHUNK) + t.shape[2:])
        return jnp.moveaxis(jnp.moveaxis(t, 1, 0), 3, 2)

    q, k, v, g, beta = (blocks(t) for t in (q, k, v, g, beta))
    gc = jnp.cumsum(g, axis=-1)
    idx = jnp.arange(CHUNK)
    incl = idx[:, None] >= idx[None, :]
    strict = idx[:, None] > idx[None, :]
    diff = gc[..., :, None] - gc[..., None, :]
    decay = jnp.where(incl, jnp.exp(jnp.where(incl, diff, 0.0)), 0.0)
    kb = k * beta[..., None]
    a_kk = jnp.where(strict, jnp.einsum('nbhid,nbhjd->nbhij', kb, k) * decay, 0.0)
    m = a_kk + jnp.eye(CHUNK, dtype=a_kk.dtype)
    w = lax.linalg.triangular_solve(m, kb * jnp.exp(gc)[..., None], left_side=True,
                                    lower=True, unit_diagonal=True)
    u = lax.linalg.triangular_solve(m, v * beta[..., None], left_side=True,
                                    lower=True, unit_diagonal=True)
    a_qk = jnp.einsum('nbhid,nbhjd->nbhij', q, k) * decay
    q_dec = q * jnp.exp(gc)[..., None]
    k_tail = k * jnp.exp(gc[..., -1:] - gc)[..., None]
    g_last = jnp.exp(gc[..., -1])

    def step(s, xs):
        w_c, u_c, qd_c, aqk_c, kt_c, gl_c = xs
        v_new = u_c - jnp.einsum('bhcd,bhde->bhce', w_c, s)
        o_c = (jnp.einsum('bhcd,bhde->bhce', qd_c, s)
               + jnp.einsum('bhij,bhje->bhie', aqk_c, v_new))
        s = s * gl_c[..., None, None] + jnp.einsum('bhcd,bhce->bhde', kt_c, v_new)
        return s, o_c

    s_fin, o = lax.scan(step, s0, (w, u, q_dec, a_qk, k_tail, g_last))
    o = jnp.moveaxis(jnp.moveaxis(o, 2, 3), 0, 1).reshape(b_sz, seq, heads, dv)
    return o, s_fin


def _gdn_mixer(h, w_in, conv_w, a_log, dt_bias, onorm, w_out, s0):
    b_sz, seq, _ = h.shape
    f32 = jnp.float32
    proj = h @ w_in
    qkv = jax.nn.silu(_dwconv(proj[..., :GDN_CONV_W], conv_w)).astype(f32)
    q = _l2norm(qkv[..., :GDN_QK_W].reshape(b_sz, seq, GDN_HEADS, GDN_DK)) * GDN_DK ** -0.5
    k = _l2norm(qkv[..., GDN_QK_W:2 * GDN_QK_W].reshape(b_sz, seq, GDN_HEADS, GDN_DK))
    v = qkv[..., 2 * GDN_QK_W:].reshape(b_sz, seq, GDN_HEADS, GDN_DV)
    gate = proj[..., GDN_CONV_W:GDN_CONV_W + GDN_V_W].astype(f32).reshape(b_sz, seq, GDN_HEADS, GDN_DV)
    ab = proj[..., GDN_CONV_W + GDN_V_W:].astype(f32).reshape(b_sz, seq, 2, 2, GDN_HEADS)
    g = -jnp.exp(a_log.astype(f32)) * jax.nn.softplus(ab[:, :, 0] + dt_bias.astype(f32))
    beta = jax.nn.sigmoid(ab[:, :, 1])
    s0 = s0.astype(f32)
    o_f, s_f = _chunk_gated_delta(q, k, v, g[:, :, 0], beta[:, :, 0], s0[:, 0])

    def rev(t):
        return jnp.flip(t, axis=1)

    o_b, s_b = _chunk_gated_delta(rev(q), rev(k), rev(v), rev(g[:, :, 1]), rev(beta[:, :, 1]), s0[:, 1])
    o = o_f + rev(o_b)
    o = (o * lax.rsqrt(jnp.mean(o * o, axis=-1, keepdims=True) + EPS)
         * onorm.astype(f32) * jax.nn.silu(gate))
    out = o.reshape(b_sz, seq, GDN_V_W).astype(h.dtype) @ w_out
    return out, jnp.stack([s_f, s_b], axis=1)


def _hyena_filters(seq, w1, b1, w2, b2, w3, freq):
    f32 = jnp.float32
    t = jnp.linspace(0.0, 1.0, seq, dtype=f32)[:, None]
    wpos = (2.0 * math.pi / seq) * jnp.arange(seq, dtype=f32)[:, None]
    fb = jnp.linspace(1e-4, HY_BANDS - 1, HY_BANDS, dtype=f32)[None, :]
    z = jnp.concatenate([t, jnp.cos(fb * wpos), -jnp.sin(fb * wpos)], axis=-1)
    fr = freq.astype(f32)
    hid = jnp.sin(fr * (z @ w1.astype(f32) + b1.astype(f32)))
    hid = jnp.sin(fr * (hid @ w2.astype(f32) + b2.astype(f32)))
    hk = (hid @ w3.astype(f32)).reshape(seq, HY_ORDER, 2, D_MODEL)
    deltas = jnp.abs(jnp.linspace(HY_MIN_DECAY, HY_MAX_DECAY, D_MODEL, dtype=f32))
    window = jnp.exp(-t * deltas[None, :])
    return hk * window[:, None, None, :]


def _bidir_fftconv(z, hf, hb):
    seq = z.shape[1]
    h_full = jnp.concatenate([hf, jnp.zeros((1, hf.shape[1]), hf.dtype), hb[:0:-1]], axis=0)
    zf = jnp.fft.rfft(z, n=2 * seq, axis=1)
    kf = jnp.fft.rfft(h_full, axis=0)
    return jnp.fft.irfft(zf * kf[None], n=2 * seq, axis=1)[:, :seq]


def _hyena_mixer(h, w_in, b_in, conv_w, f_w1, f_b1, f_w2, f_b2, f_w3, freq, skip, w_out, b_out):
    seq = h.shape[1]
    u = _dwconv(h @ w_in + b_in, conv_w).astype(jnp.float32)
    parts = jnp.split(u, HY_ORDER + 1, axis=-1)
    filt = _hyena_filters(seq, f_w1, f_b1, f_w2, f_b2, f_w3, freq)
    z = parts[0]
    for n in range(HY_ORDER):
        z = parts[n + 1] * (_bidir_fftconv(z, filt[:, n, 0], filt[:, n, 1])
                            + skip[n].astype(jnp.float32) * z)
    return z.astype(h.dtype) @ w_out + b_out


def _swiglu(h, w_gu, w_down):
    gu = h @ w_gu
    return (jax.nn.silu(gu[..., :D_FF]) * gu[..., D_FF:]) @ w_down


def setup_inputs(seed: int = 0) -> dict:
    key = jax.random.key(seed)
    ks = iter(jax.random.split(key, 40))
    f32 = jnp.float32

    def nrm(shape, scale):
        return jax.random.normal(next(ks), shape, f32) * scale

    dt = jnp.exp(jax.random.uniform(next(ks), (N_GDN, 2, GDN_HEADS), f32,
                                    math.log(1e-3), math.log(1e-1)))
    a_val = jax.random.uniform(next(ks), (N_GDN, 2, GDN_HEADS), f32, 1.0, 16.0)
    hy_w = (HY_ORDER + 1) * D_MODEL
    return {
        'x_prompt': nrm((BATCH, SEQ, D_MODEL), 1.0),
        'x_sample': nrm((DEC_BATCH, DEC_SEQ, D_MODEL), 1.0),
        'state_delta': nrm((DEC_BATCH, N_GDN, 2, GDN_HEADS, GDN_DK, GDN_DV), 0.1),
        'c': nrm((DEC_BATCH, D_MODEL), 1.0),
        'c_ctx': nrm((D_MODEL,), 1.0),
        'ada_w': nrm((DEPTH, D_MODEL, 6 * D_MODEL), 0.5 * D_MODEL ** -0.5),
        'ada_b': nrm((DEPTH, 6 * D_MODEL), 0.02),
        'norm1_g': 1.0 + nrm((DEPTH, D_MODEL), 0.02),
        'norm2_g': 1.0 + nrm((DEPTH, D_MODEL), 0.02),
        'gdn_w_in': nrm((N_GDN, D_MODEL, GDN_IN_W), D_MODEL ** -0.5),
        'gdn_conv': nrm((N_GDN, GDN_CONV_K, GDN_CONV_W), GDN_CONV_K ** -0.5),
        'gdn_a_log': jnp.log(a_val),
        'gdn_dt_bias': dt + jnp.log(-jnp.expm1(-dt)),
        'gdn_onorm': 1.0 + nrm((N_GDN, GDN_DV), 0.02),
        'gdn_w_out': nrm((N_GDN, GDN_V_W, D_MODEL), GDN_V_W ** -0.5),
        'hy_w_in': nrm((N_HYENA, D_MODEL, hy_w), D_MODEL ** -0.5),
        'hy_b_in': nrm((N_HYENA, hy_w), 0.02),
        'hy_conv': nrm((N_HYENA, HY_CONV_K, hy_w), HY_CONV_K ** -0.5),
        'hy_f_w1': nrm((N_HYENA, HY_EMB, HY_FILTER_HID), HY_EMB ** -0.5),
        'hy_f_b1': nrm((N_HYENA, HY_FILTER_HID), 0.1),
        'hy_f_w2': nrm((N_HYENA, HY_FILTER_HID, HY_FILTER_HID), HY_FILTER_HID ** -0.5),
        'hy_f_b2': nrm((N_HYENA, HY_FILTER_HID), 0.1),
        'hy_f_w3': nrm((N_HYENA, HY_FILTER_HID, HY_ORDER * 2 * D_MODEL), 0.05 * HY_FILTER_HID ** -0.5),
        'hy_freq': 1.0 + nrm((N_HYENA, HY_FILTER_HID), 0.1),
        'hy_skip': nrm((N_HYENA, HY_ORDER, D_MODEL), 0.5),
        'hy_w_out': nrm((N_HYENA, D_MODEL, D_MODEL), D_MODEL ** -0.5),
        'hy_b_out': nrm((N_HYENA, D_MODEL), 0.02),
        'ffn_w_gu': nrm((DEPTH, D_MODEL, 2 * D_FF), D_MODEL ** -0.5),
        'ffn_w_down': nrm((DEPTH, D_FF, D_MODEL), D_FF ** -0.5),
        'final_g': 1.0 + nrm((D_MODEL,), 0.02),
    }


def reference(x_prompt, x_sample, state_delta, c, c_ctx, ada_w, ada_b, norm1_g, norm2_g,
              gdn_w_in, gdn_conv, gdn_a_log, gdn_dt_bias, gdn_onorm, gdn_w_out,
              hy_w_in, hy_b_in, hy_conv, hy_f_w1, hy_f_b1, hy_f_w2, hy_f_b2, hy_f_w3,
              hy_freq, hy_skip, hy_w_out, hy_b_out, ffn_w_gu, ffn_w_down, final_g):
    ctx = x_prompt
    lat = x_sample + _grid_pos_emb(x_sample.shape[1], x_sample.dtype)[None]
    n_ctx = x_prompt.shape[0]
    new_states = []
    for layer in range(DEPTH):
        mc = _ada(c_ctx[None, :], ada_w[layer], ada_b[layer])
        ml = _ada(c, ada_w[layer], ada_b[layer])
        hc = _modulate(_rms_norm(ctx, norm1_g[layer]), mc[0], mc[1])
        hl = _modulate(_rms_norm(lat, norm1_g[layer]), ml[0], ml[1])
        j = layer // N_MIXERS
        if layer % N_MIXERS == 0:
            gdn = (gdn_w_in[j], gdn_conv[j], gdn_a_log[j], gdn_dt_bias[j], gdn_onorm[j], gdn_w_out[j])
            s_zero = jnp.zeros((n_ctx, 2, GDN_HEADS, GDN_DK, GDN_DV), jnp.float32)
            oc, s_ctx = _gdn_mixer(hc, *gdn, s_zero)
            ol, _ = _gdn_mixer(hl, *gdn, state_delta[:, j])
            new_states.append(s_ctx)
        else:
            hy = (hy_w_in[j], hy_b_in[j], hy_conv[j], hy_f_w1[j], hy_f_b1[j], hy_f_w2[j],
                  hy_f_b2[j], hy_f_w3[j], hy_freq[j], hy_skip[j], hy_w_out[j], hy_b_out[j])
            oc = _hyena_mixer(hc, *hy)
            ol = _hyena_mixer(hl, *hy)
        ctx = ctx + mc[2] * oc
        lat = lat + ml[2] * ol
        ctx = ctx + mc[5] * _swiglu(_modulate(_rms_norm(ctx, norm2_g[layer]), mc[3], mc[4]),
                                    ffn_w_gu[layer], ffn_w_down[layer])
        lat = lat + ml[5] * _swiglu(_modulate(_rms_norm(lat, norm2_g[layer]), ml[3], ml[4]),
                                    ffn_w_gu[layer], ffn_w_down[layer])
    y_prompt = _rms_norm(ctx, final_g)
    y_sample = _rms_norm(lat, final_g)
    new_state_delta = jnp.stack(new_states, axis=1)
    return (y_prompt, y_sample, new_state_delta)
```

```python
import math
import os
import contextlib
import numpy as np
import concourse.bass as bass
import concourse.mybir as mybir
from concourse.bass_utils import run_bass_kernel_spmd

F32 = mybir.dt.float32
BF16 = mybir.dt.bfloat16
ALU = mybir.AluOpType
AF = mybir.ActivationFunctionType

D = 1024
NCH = 8
LLAT = 1024
LCTX = 256
T = LLAT + 2 * LCTX
NTB = 3
DFF = 2816
NFB = 22
DEPTH = 4
EPS = 1e-6
SAME_ENGINE_SYNC = True


class Res:
    __slots__ = ("w", "rs", "name", "excl")

    def __init__(self, name="", excl=False):
        self.w = None
        self.rs = []
        self.name = name
        self.excl = excl


class Eng:
    def __init__(self, name, h, sem):
        self.name = name
        self.h = h
        self.sem = sem
        self.count = 0
        self.waited = {}


class KB:
    def __init__(self, nc):
        self.nc = nc
        self.es = contextlib.ExitStack()
        self.engs = {}
        for name, h in (("pe", nc.tensor), ("act", nc.scalar), ("dve", nc.vector),
                        ("pool", nc.gpsimd), ("sp", nc.sync)):
            sem = self.es.enter_context(nc.semaphore("sem_" + name))
            self.engs[name] = Eng(name, h, sem)
        self.dsem = {}
        for q in ("sp", "pool"):
            sems = [self.es.enter_context(nc.semaphore(f"dsem_{q}{i}")) for i in range(20)]
            self.dsem[q] = {"sems": sems, "tot": [0] * len(sems), "i": 0}
        self.semkey = {}
        self.ninst = 0

    def sb(self, name, shape, dt):
        return self.es.enter_context(self.nc.sbuf_tensor(name, list(shape), dt))

    def ps(self, name, shape, dt):
        return self.es.enter_context(self.nc.psum_tensor(name, list(shape), dt))

    def _waits(self, eng, reads, writes):
        need = {}
        def add(ev):
            if ev is None:
                return
            sem, val, src = ev
            if src == eng.name:
                if eng.name == "pe" or not SAME_ENGINE_SYNC:
                    return
            k = id(sem)
            if k not in need or need[k][1] < val:
                need[k] = (sem, val, src)
        for r in reads:
            add(r.w)
        for w in writes:
            add(w.w)
            for ev in w.rs:
                add(ev)
        for k, (sem, val, src) in need.items():
            if eng.waited.get(k, 0) >= val:
                continue
            if src in self.engs:
                assert self.engs[src].count >= val, f"pending unsignaled dep {src} {val} > {self.engs[src].count}"
            eng.h.wait_ge(sem, val)
            eng.waited[k] = val
            self.ninst += 1

    def op(self, en, fn, reads=(), writes=(), signal=True):
        eng = self.engs[en]
        xr = [r for r in reads if r.excl]
        if xr:
            writes = list(writes) + [r for r in xr if r not in writes]
        self._waits(eng, reads, writes)
        ins = fn()
        self.ninst += 1
        if signal:
            eng.count += 1
            ins.then_inc(eng.sem, 1)
            ev = (eng.sem, eng.count, en)
        else:
            ev = (eng.sem, eng.count + 1, en)
        for r in reads:
            r.rs.append(ev)
        for w in writes:
            w.w = ev
            w.rs = []
        return ev

    def dma(self, q, out, in_, reads=(), writes=()):
        eng = self.engs[q]
        self._waits(eng, reads, writes)
        d = self.dsem[q]
        i = d["i"] % len(d["sems"])
        d["i"] += 1
        sem = d["sems"][i]
        k = id(sem)
        if d["tot"][i] > 0 and eng.waited.get(k, 0) < d["tot"][i]:
            eng.h.wait_ge(sem, d["tot"][i])
            eng.waited[k] = d["tot"][i]
        eng.h.dma_start(out=out, in_=in_).then_inc(sem, 16)
        self.ninst += 1
        d["tot"][i] += 16
        ev = (sem, d["tot"][i], "dma_" + q)
        for r in reads:
            r.rs.append(ev)
        for w in writes:
            w.w = ev
            w.rs = []
        return ev

    def finish(self, res_list):
        eng = self.engs["sp"]
        self._waits(eng, res_list, res_list)
        self.es.close()


def host_consts():
    c = {}
    c["ident"] = np.eye(128, dtype=np.float32)
    GRID_W = 64
    rows = LLAT // GRID_W
    r, col = np.meshgrid(np.arange(rows), np.arange(GRID_W), indexing="ij")
    quarter = D // 4
    omega = (1.0 / (10000.0 ** (np.arange(quarter, dtype=np.float32) / quarter))).astype(np.float32)

    def emb1d(p):
        a = p.reshape(-1, 1).astype(np.float32) * omega[None, :]
        return np.concatenate([np.sin(a), np.cos(a)], axis=-1)

    c["pos"] = np.concatenate([emb1d(r), emb1d(col)], axis=-1).astype(np.float32)
    B = 512
    k = np.arange(B, dtype=np.float64)
    om = 2.0 * np.pi * (k + 0.5) / (2 * B)
    jj = np.arange(B, dtype=np.float64)
    Cf = np.cos(jj[:, None] * om[None, :])
    Sf = np.sin(jj[:, None] * om[None, :])
    Cr = np.cos((B - jj)[:, None] * om[None, :]); Cr[0, :] = 0.0
    Sr = np.sin((B - jj)[:, None] * om[None, :]); Sr[0, :] = 0.0
    Ci = (2.0 / (2 * B)) * np.cos(om[:, None] * jj[None, :])
    Si = -(2.0 / (2 * B)) * np.sin(om[:, None] * jj[None, :])
    c["hytab"] = np.stack([Cf, Sf, Cr, Sr, Ci, Si], axis=0).astype(np.float32)
    def zemb(seq):
        t = np.linspace(0.0, 1.0, seq, dtype=np.float32)[:, None]
        wpos = (2.0 * math.pi / seq) * np.arange(seq, dtype=np.float32)[:, None]
        fb = np.linspace(1e-4, 15, 16, dtype=np.float32)[None, :]
        z = np.concatenate([t, np.cos(fb * wpos), -np.sin(fb * wpos)], axis=-1).astype(np.float32)
        return np.ascontiguousarray(z.T)
    c["zemb_lat"] = zemb(LLAT)
    c["zemb_ctx"] = zemb(LCTX)
    mx = math.log(1e-2) / 0.3
    mn = math.log(1e-2) / 1.5
    c["hydelta"] = np.abs(np.linspace(mn, mx, D, dtype=np.float32)).reshape(1, D).astype(np.float32)
    tn = np.zeros((128, 10), np.float32)
    for q in range(8):
        tn[:, q] = -(q * 128 + np.arange(128)) / np.float32(LLAT - 1)
    for q in range(2):
        tn[:, 8 + q] = -(q * 128 + np.arange(128)) / np.float32(LCTX - 1)
    tn = -np.stack([np.linspace(0.0, 1.0, LLAT, dtype=np.float32).reshape(8, 128).T] , 0)[0]
    tc = -np.linspace(0.0, 1.0, LCTX, dtype=np.float32).reshape(2, 128).T
    c["hytneg"] = np.ascontiguousarray(np.concatenate([tn, tc], axis=1).astype(np.float32))
    pi_, fi_ = np.meshgrid(np.arange(128), np.arange(128), indexing="ij")
    sb_ = (pi_ // 64) == (fi_ // 64)
    bd16 = (pi_ // 16) == (fi_ // 16)
    s1 = ((pi_ // 32) == (fi_ // 32)) & ~bd16
    s2 = ((pi_ // 64) == (fi_ // 64)) & ((pi_ // 32) != (fi_ // 32))
    c["gmasks"] = np.stack([(pi_ > fi_) & sb_, (fi_ >= pi_) & sb_, (pi_ < fi_) & sb_, (fi_ <= pi_) & sb_,
                            (pi_ <= fi_) & sb_, (pi_ >= fi_) & sb_, bd16, s1, s2], axis=0).astype(np.float32)
    m0 = np.ones((128, 1), np.float32); m0[0, 0] = 0.0
    c["mask0"] = m0
    return c


class Prog:
    def __init__(self, skip=()):
        self.skip = set(skip)
        nc = bass.Bass("TRN2", target_bir_lowering=False)
        self.nc = nc
        self.kb = KB(nc)
        self.build()

    def din(self, name, shape, dt=F32):
        return self.nc.dram_tensor(name, list(shape), dt, kind="ExternalInput").ap()

    def dout(self, name, shape, dt=F32):
        return self.nc.dram_tensor(name, list(shape), dt, kind="ExternalOutput").ap()

    def build(self):
        nc, kb = self.nc, self.kb
        V, A, P, G = nc.vector, nc.scalar, nc.tensor, nc.gpsimd
        x_lat = self.din("x_lat", [LLAT, D])
        x_ctx = self.din("x_ctx", [2 * LCTX, D])
        cvecT = self.din("cvecT", [128, NCH, 2])
        gvecT = self.din("gvecT", [128, 2 * DEPTH + 1, NCH])
        adabT = self.din("adabT", [128, DEPTH, 48])
        pos = self.din("pos", [LLAT, D])
        identd = self.din("ident", [128, 128])
        ada_w = self.din("ada_w", [DEPTH, D, 6 * D])
        ffn_w_gu = self.din("ffn_w_gu", [DEPTH, D, 2 * DFF])
        ffn_w_down = self.din("ffn_w_down", [DEPTH, DFF, D])
        hytab = self.din("hytab", [6, 512, 512])
        zemb_lat = self.din("zemb_lat", [33, LLAT])
        zemb_ctx = self.din("zemb_ctx", [33, LCTX])
        hydelta = self.din("hydelta", [1, D])
        hytneg = self.din("hytneg", [128, 10])
        mask0 = self.din("mask0", [128, 1])
        hy_w_in = self.din("hy_w_in", [2, D, 3 * D])
        hy_w_out = self.din("hy_w_out", [2, D, D])
        hy_f_w1 = self.din("hy_f_w1", [2, 33, 64])
        hy_f_w2 = self.din("hy_f_w2", [2, 64, 64])
        hy_f_w3 = self.din("hy_f_w3", [2, 64, 4 * D])
        hy_vec64 = self.din("hy_vec64", [2, 64, 3])
        hy_vecT = self.din("hy_vecT", [2, 128, 24 + 72 + 16 + 8])
        gmasks = self.din("gmasks", [9, 128, 128])
        gdn_w_in = self.din("gdn_w_in", [2, D, 4128])
        gdn_w_out = self.din("gdn_w_out", [2, D, D])
        gdn_vecT = self.din("gdn_vecT", [2, 128, 5 * 24 + 1])
        gdn_rows = self.din("gdn_rows", [2, 2, 16])
        state_in = self.din("state_in", [2, 2, 8, 128, 128])
        ns_out = self.dout("ns_out", [2, 2, 2, 8, 128, 128])
        import os
        self.debug = bool(os.environ.get('GDN_DEBUG'))
        dbg = self.dout("dbg", [8, 128, T]) if self.debug else None
        xsp = self.nc.dram_tensor("xsp", [128, NCH * T], F32, kind="Internal").ap()
        y_lat = self.dout("y_lat", [LLAT, D])
        y_ctx = self.dout("y_ctx", [2 * LCTX, D])
        self.dr = dict(locals())

        xT = kb.sb("xT", [128, NCH, T], F32)
        hT = kb.sb("hT", [128, NCH, T], BF16)
        self.xT, self.hT = xT, hT
        self.r_x = [[Res(f"x{c}_{b}") for b in range(NTB)] for c in range(NCH)]
        self.r_h = [[Res(f"h{c}_{b}") for b in range(NTB)] for c in range(NCH)]
        big = kb.sb("big", [128, 33792], BF16)
        self.big = big
        self.r_big = Res("big")
        ident = kb.sb("identf", [128, 128], F32)
        identb = kb.sb("identb", [128, 128], BF16)
        onesb = kb.sb("onesb", [128, 128], BF16)
        self.ident, self.identb, self.onesb = ident, identb, onesb
        r_const = Res("const")
        self.r_const = r_const
        kb.dma("sp", ident[:], identd[:, :], writes=[r_const])
        kb.op("dve", lambda: V.tensor_copy(out=identb[:], in_=ident[:]), reads=[r_const], writes=[r_const])
        kb.op("dve", lambda: V.memset(onesb[:], 1.0), writes=[r_const])
        self.NSLOT = 5
        self.wslots = [kb.sb(f"wslot{i}", [128, 4096], BF16) for i in range(self.NSLOT)]
        self.r_wslot = [Res(f"wslot{i}") for i in range(self.NSLOT)]
        self.wi = 0
        self.NPS = 7
        self.psb = [kb.ps(f"psb{i}", [128, 512], F32) for i in range(8)]
        self.r_ps = [Res(f"ps{i}", excl=True) for i in range(8)]
        self.pi = 0
        self.NTMP = 4
        self.tmp = [kb.sb(f"tmp{i}", [128, 512], F32) for i in range(self.NTMP)]
        self.r_tmp = [Res(f"tmp{i}") for i in range(self.NTMP)]
        self.ti = 0
        self.rstdb = [kb.sb(f"rstd{i}", [128, 512], F32) for i in range(2)]
        self.r_rstd = [Res(f"rstd{i}") for i in range(2)]
        self.ri = 0
        self.tmpb = [kb.sb(f"tmpb{i}", [128, 512], BF16) for i in range(self.NTMP)]
        self.r_tmpb = [Res(f"tmpb{i}") for i in range(self.NTMP)]
        self.tbi = 0
        self.mods = kb.sb("mods", [128, DEPTH, 48, 2], F32)
        self.r_mods = [Res(f"mods{l}") for l in range(DEPTH)]
        self.csil = kb.sb("csil", [128, NCH, 2], F32)
        self.r_csil = Res("csil")
        self.small = kb.sb("small", [128, 64, 2], F32)
        self.r_small = Res("small")
        self.gvec = kb.sb("gvec", [128, 2 * DEPTH + 1, NCH], F32)
        self.r_gvec = Res("gvec")
        self.adab = kb.sb("adab", [128, DEPTH, 48], F32)
        self.r_adab = Res("adab")

        self.r_nsout = Res("nsout")
        self.load_inputs()
        self.ada_pending = None
        for l in range(DEPTH):
            if l == 0:
                self.ada(l)
            else:
                self.ada_step(1000)
            if f"mix{l}" not in self.skip:
                if l % 2 == 0:
                    self.gdn_layer(l)
                else:
                    self.hyena_layer(l)
            if l + 1 < DEPTH:
                self.ada_pending = self.ada_gen(l + 1)
            if f"ffn{l}" not in self.skip:
                self.ffn_layer(l)
        self.final_out()
        kb.finish(self.out_res + [self.r_nsout])

    def next_ps(self):
        i = self.pi % self.NPS
        self.pi += 1
        return self.psb[i], self.r_ps[i]

    def next_tmp(self):
        i = self.ti % self.NTMP
        self.ti += 1
        return self.tmp[i], self.r_tmp[i]

    def next_rstd(self):
        i = self.ri % 2
        self.ri += 1
        return self.rstdb[i], self.r_rstd[i]

    def next_tmpb(self):
        i = self.tbi % self.NTMP
        self.tbi += 1
        return self.tmpb[i], self.r_tmpb[i]

    def next_wslot(self):
        i = self.wi % self.NSLOT
        self.wi += 1
        return self.wslots[i], self.r_wslot[i]

    def mm(self, out, lhsT, rhs, start, stop, reads, writes, signal=None):
        P = self.nc.tensor
        return self.kb.op("pe", lambda: P.matmul(out, lhsT, rhs, start=start, stop=stop),
                          reads=reads, writes=writes, signal=(stop if signal is None else signal))

    def load_inputs(self):
        nc, kb = self.nc, self.kb
        V, A, P = nc.vector, nc.scalar, nc.tensor
        d = self.dr
        kb.dma("sp", self.csil[:, :, :], d["cvecT"][:, :, :], writes=[self.r_csil])
        kb.op("act", lambda: A.activation(out=self.csil[:], in_=self.csil[:], func=AF.Silu),
              reads=[self.r_csil], writes=[self.r_csil])
        kb.dma("sp", self.gvec[:, :, :], d["gvecT"][:, :, :], writes=[self.r_gvec])
        kb.dma("sp", self.adab[:, :, :], d["adabT"][:, :, :], writes=[self.r_adab])
        for tt in range(T // 128):
            b = tt // 4
            lat = tt < 8
            src = d["x_lat"][tt * 128:(tt + 1) * 128, :] if lat else d["x_ctx"][(tt - 8) * 128:(tt - 7) * 128, :]
            for half in range(2):
                t0, r0 = self.next_tmp()
                kb.dma("sp", t0[:], src[:, half * 512:(half + 1) * 512], writes=[r0])
                if lat:
                    t1, r1 = self.next_tmp()
                    kb.dma("sp", t1[:], d["pos"][tt * 128:(tt + 1) * 128, half * 512:(half + 1) * 512], writes=[r1])
                    kb.op("dve", lambda: V.tensor_add(out=t0[:], in0=t0[:], in1=t1[:]), reads=[r0, r1], writes=[r0])
                ps, rp = self.next_ps()
                for j in range(4):
                    kb.op("pe", lambda: P.transpose(ps[:, j * 128:(j + 1) * 128], t0[:, j * 128:(j + 1) * 128], self.ident[:]),
                          reads=[r0, self.r_const], writes=[rp], signal=(j == 3))
                cs = half * 4
                kb.op("act", lambda: A.copy(out=self.xT[:, cs:cs + 4, tt * 128:(tt + 1) * 128],
                                            in_=ps[:].rearrange("p (c t) -> p c t", c=4)),
                      reads=[rp], writes=[self.r_x[c][b] for c in range(cs, cs + 4)])

    def ada_gen(self, l):
        nc, kb = self.nc, self.kb
        V, A, P = nc.vector, nc.scalar, nc.tensor
        d = self.dr
        W = 256
        ps, rp = self.psb[7], self.r_ps[7]
        for ti in range(6 * D // W):
            ws, rw = self.next_wslot()
            wf = ws[:].bitcast(F32)[:, 0:NCH * W].rearrange("p (c n) -> p c n", c=NCH)
            kb.dma("sp", wf, d["ada_w"][l, :, ti * W:(ti + 1) * W].rearrange("(c p) n -> p c n", p=128), writes=[rw])
            for jj in range(W // 128):
                j = ti * (W // 128) + jj
                for c in range(NCH):
                    self.mm(ps[:, 2 * j:2 * j + 2], wf[:, c, jj * 128:(jj + 1) * 128], self.csil[:, c, :],
                            start=(c == 0), stop=(c == NCH - 1), reads=[rw, self.r_csil], writes=[rp])
            yield
        kb.op("dve", lambda: V.tensor_tensor(out=self.mods[:, l, :, :], in0=ps[:, 0:96].rearrange("p (j w) -> p j w", w=2),
                                             in1=self.adab[:, l, :].unsqueeze(2).to_broadcast([128, 48, 2]), op=ALU.add),
              reads=[rp, self.r_adab], writes=[self.r_mods[l]])

    def ada(self, l):
        for _ in self.ada_gen(l):
            pass

    def ada_step(self, n):
        g = getattr(self, "ada_pending", None)
        if g is None:
            return
        for _ in range(n):
            try:
                next(g)
            except StopIteration:
                self.ada_pending = None
                return

    def mod(self, l, which, idx, c):
        return self.mods[:, l, idx * NCH + c, which:which + 1]

    def norm_mod(self, l, gidx, sh_idx, sc_idx):
        nc, kb = self.nc, self.kb
        V, A, P = nc.vector, nc.scalar, nc.tensor
        gs = self.small[:, 0:NCH, :]
        kb.op("dve", lambda: V.tensor_scalar(out=gs, in0=self.mods[:, l, sc_idx * NCH:(sc_idx + 1) * NCH, :], scalar1=1.0, scalar2=None, op0=ALU.add),
              reads=[self.r_mods[l]], writes=[self.r_small])
        kb.op("dve", lambda: V.tensor_tensor(out=gs, in0=gs, in1=self.gvec[:, gidx, :].unsqueeze(2).to_broadcast([128, NCH, 2]), op=ALU.mult),
              reads=[self.r_gvec, self.r_small], writes=[self.r_small])
        for b in range(NTB):
            w = 0 if b < 2 else 1
            ps, rp = self.next_ps()
            for c in range(NCH):
                sq, rsq = self.next_tmpb()
                kb.op("act", lambda: A.activation(out=sq[:], in_=self.xT[:, c, b * 512:(b + 1) * 512], func=AF.Square),
                      reads=[self.r_x[c][b]], writes=[rsq])
                self.mm(ps[:], self.onesb[:], sq[:], start=(c == 0), stop=(c == NCH - 1), reads=[rsq, self.r_const], writes=[rp], signal=True)
            rstd, rr = self.next_rstd()
            kb.op("dve", lambda: V.tensor_scalar(out=rstd[:], in0=ps[:], scalar1=1.0 / D, scalar2=EPS, op0=ALU.mult, op1=ALU.add),
                  reads=[rp], writes=[rr])
            kb.op("act", lambda: A.activation(out=rstd[:], in_=rstd[:], func=AF.Sqrt), reads=[rr], writes=[rr])
            kb.op("dve", lambda: V.reciprocal(out=rstd[:], in_=rstd[:]), reads=[rr], writes=[rr])
            for c in range(NCH):
                t0, r0 = self.next_tmp()
                kb.op("dve", lambda: V.scalar_tensor_tensor(out=t0[:], in0=self.xT[:, c, b * 512:(b + 1) * 512], scalar=gs[:, c, w:w + 1],
                                                            in1=rstd[:], op0=ALU.mult, op1=ALU.mult),
                      reads=[self.r_x[c][b], rr, self.r_small], writes=[r0])
                kb.op("act", lambda: A.activation(out=self.hT[:, c, b * 512:(b + 1) * 512], in_=t0[:], func=AF.Identity,
                                                  bias=self.mod(l, w, sh_idx, c), scale=1.0),
                      reads=[r0, self.r_mods[l]], writes=[self.r_h[c][b]])

    def ffn_layer(self, l):
        nc, kb = self.nc, self.kb
        V, A, P, G = nc.vector, nc.scalar, nc.tensor, nc.gpsimd
        d = self.dr
        self.norm_mod(l, 2 * l + 1, 3, 4)
        act = self.big[:, 0:NFB * T].rearrange("p (f t) -> p f t", f=NFB)
        r_act = [[Res(f"act{f}_{b}") for b in range(NTB)] for f in range(NFB)]
        for f in range(NFB):
            for b in range(NTB):
                r_act[f][b].w = self.r_big.w
                r_act[f][b].rs = list(self.r_big.rs)
        hreads = lambda b: [self.r_h[c][b] for c in range(NCH)]
        ntile = (DFF + 511) // 512
        for ti in range(ntile):
            c0 = ti * 512
            ncol = min(512, DFF - c0)
            wg, rg = self.next_wslot()
            wgv = wg[:, 0:NCH * ncol].rearrange("p (c n) -> p c n", c=NCH)
            kb.dma("pool", wgv, d["ffn_w_gu"][l, :, c0:c0 + ncol].rearrange("(c p) n -> p c n", p=128), writes=[rg])
            wu, ru = self.next_wslot()
            wuv = wu[:, 0:NCH * ncol].rearrange("p (c n) -> p c n", c=NCH)
            kb.dma("pool", wuv, d["ffn_w_gu"][l, :, DFF + c0:DFF + c0 + ncol].rearrange("(c p) n -> p c n", p=128), writes=[ru])
            self.ada_step(3)
            for jj in range(ncol // 128):
                f = ti * 4 + jj
                for b in range(NTB):
                    pg, rpg = self.next_ps()
                    for c in range(NCH):
                        self.mm(pg[:], wgv[:, c, jj * 128:(jj + 1) * 128], self.hT[:, c, b * 512:(b + 1) * 512],
                                start=(c == 0), stop=(c == NCH - 1), reads=[rg, self.r_h[c][b]], writes=[rpg])
                    pu, rpu = self.next_ps()
                    for c in range(NCH):
                        self.mm(pu[:], wuv[:, c, jj * 128:(jj + 1) * 128], self.hT[:, c, b * 512:(b + 1) * 512],
                                start=(c == 0), stop=(c == NCH - 1), reads=[ru, self.r_h[c][b]], writes=[rpu])
                    sg, rsg = self.next_tmp()
                    kb.op("act", lambda: A.activation(out=sg[:], in_=pg[:], func=AF.Silu), reads=[rpg], writes=[rsg])
                    kb.op("dve", lambda: V.tensor_tensor(out=act[:, f, b * 512:(b + 1) * 512], in0=sg[:], in1=pu[:], op=ALU.mult),
                          reads=[rsg, rpu], writes=[r_act[f][b]])
        for ti in range(D // 128):
            self.ada_step(1)
            ws, rw = self.next_wslot()
            wv = ws[:, 0:NFB * 128].rearrange("p (f n) -> p f n", f=NFB)
            kb.dma("pool", wv, d["ffn_w_down"][l, :, ti * 128:(ti + 1) * 128].rearrange("(f p) n -> p f n", p=128), writes=[rw])
            for jj in range(1):
                c = ti
                for b in range(NTB):
                    w = 0 if b < 2 else 1
                    ps, rp = self.next_ps()
                    for f in range(NFB):
                        self.mm(ps[:], wv[:, f, jj * 128:(jj + 1) * 128], act[:, f, b * 512:(b + 1) * 512],
                                start=(f == 0), stop=(f == NFB - 1), reads=[rw, r_act[f][b]], writes=[rp])
                    xs = self.xT[:, c, b * 512:(b + 1) * 512]
                    kb.op("dve", lambda: V.scalar_tensor_tensor(out=xs, in0=ps[:], scalar=self.mod(l, w, 5, c), in1=xs,
                                                                op0=ALU.mult, op1=ALU.add),
                          reads=[rp, self.r_x[c][b], self.r_mods[l]], writes=[self.r_x[c][b]])
        self.merge_res(self.r_big, [r for rr in r_act for r in rr])

    def merge_res(self, dst, srcs):
        evs = []
        for s in srcs:
            if s.w is not None:
                evs.append(s.w)
            evs.extend(s.rs)
        best = {}
        for ev in evs:
            k = id(ev[0])
            if k not in best or best[k][1] < ev[1]:
                best[k] = ev
        dst.w = None
        dst.rs = list(best.values())

    def gdn_layer(self, l):
        nc, kb = self.nc, self.kb
        V, A, P, G = nc.vector, nc.scalar, nc.tensor, nc.gpsimd
        d = self.dr
        j = l // 2
        DT = F32
        self.norm_mod(l, 2 * l, 0, 1)
        allx = [r for rr in self.r_x for r in rr]
        kb.dma("sp", d["xsp"][:, :], self.xT[:].rearrange("p c t -> p (c t)"), reads=allx)
        R1 = self.xT[:].rearrange("p c t -> p (c t)")
        pos1 = [0]

        def c1(n):
            a = R1[:, pos1[0]:pos1[0] + n]
            pos1[0] += n
            assert pos1[0] <= 12288
            return a
        R2 = self.big
        pos2 = [0]

        def c2(n):
            a = R2[:, pos2[0]:pos2[0] + n]
            pos2[0] += n
            assert pos2[0] <= 33792, pos2[0]
            return a
        PWG = T + 12
        pp = c1(PWG)
        cv = c1(T)
        oT = cv
        qT = c1(T // 2).bitcast(BF16)
        kT = c1(T // 2).bitcast(BF16)
        vT = c1(T // 2).bitcast(BF16)
        gateT = c1(T // 2).bitcast(BF16)
        sqb = pp[:, 2:2 + T // 2].bitcast(BF16)
        ktok = c1(T // 2).bitcast(BF16).rearrange("p (t c) -> p t c", t=12)
        vtok = c1(T // 2).bitcast(BF16).rearrange("p (t c) -> p t c", t=12)
        r_pp, r_cv, r_q, r_k, r_v, r_gate, r_ktok, r_vtok = (Res(n) for n in ("pp", "cv", "q", "k", "v", "gate", "ktok", "vtok"))
        r_sqb = r_pp
        r_oT = [Res(f"oT{t}") for t in range(12)]
        self.inherit([r_pp, r_cv, r_q, r_k, r_v, r_gate, r_sqb, r_ktok, r_vtok] + r_oT, allx)
        og = c2(NCH * T).rearrange("p (h t) -> p h t", h=NCH)
        r_og = [Res(f"og{h}") for h in range(NCH)]
        NCHAIN = 6
        slots = []
        extra_slots = []
        for ci in range(NCHAIN):
            ring = []
            for r_ in range(2):
                wu_ = c2(512).bitcast(F32)
                sl = dict(WU=wu_, WT=wu_[:, 0:128], U=wu_[:, 128:256], qdT=c2(256).bitcast(F32), aqkT=c2(256).bitcast(F32), ktail=c2(256).bitcast(F32),
                          gl=c2(4).bitcast(F32),
                          r=Res(f"slot{ci}_{r_}"))
                ring.append(sl)
            slots.append(ring)
        works = []
        for wi in range(2):
            wk = dict(r=Res(f"work{wi}"))
            PA, PR, PX, PX2 = c2(512).bitcast(F32), c2(512).bitcast(F32), c2(512).bitcast(F32), c2(512).bitcast(F32)
            wk["PA"], wk["PR"], wk["PX"], wk["PX2"] = PA, PR, PX, PX2
            wk["A0"], wk["AT0"] = PA[:, 0:128], PA[:, 128:256]
            wk["R"], wk["RT"] = PR[:, 0:128], PR[:, 128:256]
            wk["Dsym"], wk["egrow"] = PX[:, 0:128], PX[:, 128:256]
            wk["DMs"], wk["DMi"] = PX2[:, 0:128], PX2[:, 128:256]
            wk["kbe"], wk["vb"] = c2(256).bitcast(F32), c2(256).bitcast(F32)
            wk["dd"] = wk["Dsym"]
            wk["GR"] = c1(256)
            wk["cols"] = c2(16).bitcast(F32)
            works.append(wk)
        for wi in range(2):
            wk = dict(r=Res(f"workx{wi}"))
            base = self.wslots[wi]
            PA, PR, PX, PX2 = base[:, 0:512].bitcast(F32), base[:, 512:1024].bitcast(F32), base[:, 1024:1536].bitcast(F32), base[:, 1536:2048].bitcast(F32)
            wk["PA"], wk["PR"], wk["PX"], wk["PX2"] = PA, PR, PX, PX2
            wk["A0"], wk["AT0"] = PA[:, 0:128], PA[:, 128:256]
            wk["R"], wk["RT"] = PR[:, 0:128], PR[:, 128:256]
            wk["Dsym"], wk["egrow"] = PX[:, 0:128], PX[:, 128:256]
            wk["DMs"], wk["DMi"] = PX2[:, 0:128], PX2[:, 128:256]
            wk["kbe"], wk["vb"] = base[:, 2048:2304].bitcast(F32), base[:, 2304:2560].bitcast(F32)
            wk["dd"] = wk["Dsym"]
            wk["GR"] = base[:, 2560:3072].bitcast(F32)
            wk["cols"] = base[:, 3072:3088].bitcast(F32)
            self.inherit([wk["r"]], [self.r_wslot[wi]])
            works.append(wk)
        vnew = [self.wslots[4][:, 1792 + i * 256:1792 + (i + 1) * 256].bitcast(F32) for i in range(NCHAIN)]
        r_vnew = [Res(f"vnew{i}") for i in range(NCHAIN)]
        Sst = [c1(128) for _ in range(NCHAIN)]
        r_S = [Res(f"S{i}") for i in range(NCHAIN)]
        g_tok = c1(192).rearrange("p (t n) -> p t n", t=12)
        b_tok = c1(192).rearrange("p (t n) -> p t n", t=12)
        gccol = c1(192).rearrange("p (a t n) -> p a t n", a=2, t=12)
        rows = c1(32)
        gvec = c1(122)
        for ci in range(2):
            for r_ in range(2):
                wu_ = c1(256)
                sl = dict(WU=wu_, WT=wu_[:, 0:128], U=wu_[:, 128:256], qdT=c1(128), aqkT=c1(128), ktail=c1(128), gl=c1(2), r=Res(f"slotx{ci}_{r_}"))
                slots[ci].append(sl)
                extra_slots.append(sl["r"])
        self.inherit(extra_slots, allx)
        masks = [self.wslots[4][:, 256 + i * 256:256 + (i + 1) * 256].bitcast(F32) for i in range(6)]
        masks += [self.wslots[4][:, 3328 + i * 256:3328 + (i + 1) * 256].bitcast(F32) for i in range(3)]
        r_ab = Res("ab"); r_gpar = Res("gpar"); r_masks = Res("masks")
        allr2 = r_og + [sl["r"] for ring in slots for sl in ring]
        self.inherit(allr2, [self.r_big])
        r_works = [wk["r"] for wk in works]
        self.inherit(r_works, [self.r_big] + allx)
        self.inherit([r_ab, r_gpar] + r_S, allx)
        self.inherit([r_masks] + r_vnew, [self.r_wslot[4]])
        for ci in range(NCHAIN):
            kb.op("dve", lambda: V.memset(vnew[ci], 0.0), writes=[r_vnew[ci]])
        for i in range(9):
            kb.dma("sp", masks[i], d["gmasks"][i], writes=[r_masks])
        m_sl, m_ui, m_su, m_li, Lf, Lb, m_bd16, m_s1, m_s2 = masks
        mpair = [self.wslots[4][:, 256:768].bitcast(F32), self.wslots[4][:, 768:1280].bitcast(F32)]
        kb.dma("sp", gvec[:, 0:121], d["gdn_vecT"][j], writes=[r_gpar])
        kb.dma("sp", rows[:, 0:32], d["gdn_rows"][j:j + 1].rearrange("o a n -> o (a n)").partition_broadcast(128), writes=[r_gpar])
        kb.op("act", lambda: A.activation(out=rows[:, 0:16], in_=rows[:, 0:16], func=AF.Exp), reads=[r_gpar], writes=[r_gpar])
        kb.op("dve", lambda: V.tensor_scalar(out=rows[:, 0:16], in0=rows[:, 0:16], scalar1=-1.0, scalar2=None, op0=ALU.mult), reads=[r_gpar], writes=[r_gpar])
        cwg = lambda k, blk: gvec[:, k * 24 + blk:k * 24 + blk + 1]
        onorm = gvec[:, 120:121]
        wab_s, r_wab = self.wslots[4], self.r_wslot[4]
        wab = wab_s[:, 0:NCH * 32].rearrange("p (c n) -> p c n", c=NCH)
        kb.dma("pool", wab, d["gdn_w_in"][j, :, 4096:4128].rearrange("(c p) n -> p c n", p=128), writes=[r_wab])
        ps, rp = self.next_ps()
        for tt in range(12):
            for c in range(NCH):
                self.mm(ps[:, tt * 32:(tt + 1) * 32], self.hT[:, c, tt * 128:(tt + 1) * 128], wab[:, c, :], start=(c == 0), stop=(c == NCH - 1),
                        reads=[self.r_h[c][tt // 4], r_wab], writes=[rp], signal=(tt == 11 and c == NCH - 1))
        abt, rabt = self.next_tmp()
        ab3 = abt[:, 0:384].rearrange("p (t n) -> p t n", t=12)
        kb.op("dve", lambda: V.tensor_copy(out=abt[:, 0:384], in_=ps[:, 0:384]), reads=[rp], writes=[rabt])
        kb.op("act", lambda: A.activation(out=b_tok, in_=ab3[:, :, 16:32], func=AF.Sigmoid), reads=[rabt], writes=[r_ab])
        kb.op("dve", lambda: V.tensor_tensor(out=g_tok, in0=ab3[:, :, 0:16], in1=rows[:, 16:32].unsqueeze(1).to_broadcast([128, 12, 16]), op=ALU.add),
              reads=[rabt, r_gpar], writes=[r_ab])
        kb.op("act", lambda: A.activation(out=g_tok, in_=g_tok, func=AF.Exp), reads=[r_ab], writes=[r_ab])
        kb.op("act", lambda: A.activation(out=g_tok, in_=g_tok, func=AF.Ln, bias=1.0, scale=1.0), reads=[r_ab], writes=[r_ab])
        kb.op("dve", lambda: V.tensor_tensor(out=g_tok, in0=g_tok, in1=rows[:, 0:16].unsqueeze(1).to_broadcast([128, 12, 16]), op=ALU.mult),
              reads=[r_ab, r_gpar], writes=[r_ab])
        ps, rp = self.next_ps()
        for dr_ in range(2):
            self.mm(ps[:, dr_ * 96:(dr_ + 1) * 96].rearrange("p (t n) -> p t n", t=12), (Lf if dr_ == 0 else Lb), g_tok[:, :, dr_ * 8:(dr_ + 1) * 8], True, True,
                    reads=[r_masks, r_ab], writes=[rp], signal=(dr_ == 1))
        kb.op("dve", lambda: V.tensor_copy(out=gccol.rearrange("p a t n -> p (a t n)"), in_=ps[:, 0:192]), reads=[rp], writes=[r_ab])
        chains = [dict(tiles=list(range(8)), dir=0, seq=None), dict(tiles=list(range(7, -1, -1)), dir=1, seq=None),
                  dict(tiles=[8, 9], dir=0, seq=0), dict(tiles=[9, 8], dir=1, seq=0),
                  dict(tiles=[10, 11], dir=0, seq=1), dict(tiles=[11, 10], dir=1, seq=1)]
        evi = [0]

        def evac(out, in_, reads, writes):
            evi[0] += 1
            if evi[0] % 2 == 0:
                kb.op("act", lambda: A.copy(out=out, in_=in_), reads=reads, writes=writes)
            else:
                kb.op("dve", lambda: V.tensor_copy(out=out, in_=in_), reads=reads, writes=writes)
        wki = [0]
        segs = [(2, LLAT, 0), (1030, LCTX, LLAT), (1290, LCTX, LLAT + LCTX)]
        kb.op("dve", lambda: V.memset(pp, 0.0), writes=[r_pp])
        for h in range(NCH):
            wsl, rws = self.wslots[2 + h % 2], self.r_wslot[2 + h % 2]
            win = wsl[:, 0:4096].rearrange("p (c g n) -> p c g n", c=NCH, g=4)
            for g in range(4):
                kb.dma("pool", win[:, :, g, :], d["gdn_w_in"][j, :, g * D + h * 128:g * D + (h + 1) * 128].rearrange("(c p) n -> p c n", p=128), writes=[rws])
            self.inherit([r_cv], r_oT)
            for g in range(3):
                blk = g * 8 + h
                for b in range(NTB):
                    ps, rp = self.next_ps()
                    for c in range(NCH):
                        self.mm(ps[:], win[:, c, g, :], self.hT[:, c, b * 512:(b + 1) * 512], start=(c == 0), stop=(c == NCH - 1),
                                reads=[rws, self.r_h[c][b]], writes=[rp])
                    if b < 2:
                        evac(pp[:, 2 + b * 512:2 + (b + 1) * 512], ps[:], [rp], [r_pp])
                    else:
                        evac(pp[:, 1030:1286], ps[:, 0:256], [rp], [r_pp])
                        evac(pp[:, 1290:1546], ps[:, 256:512], [rp], [r_pp])
                for (s0, Ls, o0) in segs:
                    co = cv[:, o0:o0 + Ls]
                    kb.op("dve", lambda: V.tensor_scalar(out=co, in0=pp[:, s0 - 2:s0 - 2 + Ls], scalar1=cwg(0, blk), scalar2=None, op0=ALU.mult),
                          reads=[r_pp, r_gpar], writes=[r_cv])
                    for k in range(1, 5):
                        kb.op("dve", lambda: V.scalar_tensor_tensor(out=co, in0=pp[:, s0 - 2 + k:s0 - 2 + k + Ls], scalar=cwg(k, blk), in1=co,
                                                                    op0=ALU.mult, op1=ALU.add), reads=[r_pp, r_gpar, r_cv], writes=[r_cv])
                if g == 2:
                    kb.op("act", lambda: A.activation(out=vT, in_=cv, func=AF.Silu), reads=[r_cv], writes=[r_v])
                else:
                    dst, rdst = (qT, r_q) if g == 0 else (kT, r_k)
                    kb.op("act", lambda: A.activation(out=cv, in_=cv, func=AF.Silu), reads=[r_cv], writes=[r_cv])
                    kb.op("act", lambda: A.activation(out=sqb, in_=cv, func=AF.Square), reads=[r_cv], writes=[r_sqb])
                    for b in range(NTB):
                        ps, rp = self.next_ps()
                        self.mm(ps[:], self.onesb[:], sqb[:, b * 512:(b + 1) * 512], True, True, reads=[r_sqb, self.r_const], writes=[rp])
                        rs, rrs = self.next_rstd()
                        kb.op("dve", lambda: V.tensor_scalar(out=rs[:], in0=ps[:], scalar1=EPS, scalar2=None, op0=ALU.add), reads=[rp], writes=[rrs])
                        kb.op("act", lambda: A.activation(out=rs[:], in_=rs[:], func=AF.Sqrt), reads=[rrs], writes=[rrs])
                        kb.op("dve", lambda: V.reciprocal(out=rs[:], in_=rs[:]), reads=[rrs], writes=[rrs])
                        sc = (128.0 ** -0.5) if g == 0 else 1.0
                        kb.op("dve", lambda: V.scalar_tensor_tensor(out=dst[:, b * 512:(b + 1) * 512], in0=cv[:, b * 512:(b + 1) * 512], scalar=sc, in1=rs[:],
                                                                    op0=ALU.mult, op1=ALU.mult), reads=[r_cv, rrs], writes=[rdst])
            for b in range(NTB):
                ps, rp = self.next_ps()
                for c in range(NCH):
                    self.mm(ps[:], win[:, c, 3, :], self.hT[:, c, b * 512:(b + 1) * 512], start=(c == 0), stop=(c == NCH - 1),
                            reads=[rws, self.r_h[c][b]], writes=[rp])
                kb.op("act", lambda: A.activation(out=gateT[:, b * 512:(b + 1) * 512], in_=ps[:], func=AF.Silu), reads=[rp], writes=[r_gate])
            self.inherit(r_oT, [r_cv])
            for (src, rsrc, dst, rdst) in ((kT, r_k, ktok, r_ktok), (vT, r_v, vtok, r_vtok)):
                for g4 in range(3):
                    ps, rp = self.next_ps()
                    psb = ps[:].bitcast(BF16)
                    for i in range(4):
                        tt = g4 * 4 + i
                        kb.op("pe", lambda: P.transpose(psb[:, i * 128:(i + 1) * 128], src[:, tt * 128:(tt + 1) * 128], self.identb[:]),
                              reads=[rsrc, self.r_const], writes=[rp], signal=(i == 3))
                    evac(dst[:, g4 * 4:(g4 + 1) * 4, :], psb[:, 0:512].rearrange("p (t c) -> p t c", t=4), [rp], [rdst])
            for ci, ch in enumerate(chains):
                if ch["seq"] is None:
                    kb.dma("sp", Sst[ci], d["state_in"][j, ch["dir"], h], writes=[r_S[ci]])
                else:
                    kb.op("dve", lambda: V.memset(Sst[ci], 0.0), writes=[r_S[ci]])

            def intra(ci, step):
                ch = chains[ci]
                tt = ch["tiles"][step]
                dr_ = ch["dir"]
                sl = slots[ci][step % len(slots[ci])]
                rsl = sl["r"]
                wk = works[wki[0] % len(works)]
                wki[0] += 1
                rw = wk["r"]
                tcs = slice(tt * 128, (tt + 1) * 128)
                gcc = gccol[:, dr_, tt, h:h + 1]
                bc = b_tok[:, tt, dr_ * 8 + h:dr_ * 8 + h + 1]
                lastA, lastB = (63, 127) if dr_ == 0 else (0, 64)
                m_s = m_sl if dr_ == 0 else m_su
                m_i = m_ui if dr_ == 0 else m_li
                cols = wk["cols"]
                yield
                ps, rp = self.next_ps()
                self.mm(ps[:, 0:128], kT[:, tcs], kT[:, tcs], True, True, reads=[r_k], writes=[rp], signal=False)
                self.mm(ps[:, 128:256], kT[:, tcs], qT[:, tcs], True, True, reads=[r_k, r_q], writes=[rp])
                evac(wk["GR"], ps[:, 0:256], [rp], [rw])
                yield
                ps2, rp2 = self.next_ps()
                kb.op("pe", lambda: P.transpose(ps2[:, 0:128], gcc.to_broadcast([128, 128]), self.ident[:]), reads=[r_ab, self.r_const], writes=[rp2])
                kb.op("dve", lambda: V.tensor_scalar(out=wk["dd"], in0=ps2[:, 0:128], scalar1=gcc, scalar2=None, op0=ALU.subtract), reads=[rp2, r_ab], writes=[rw])
                kb.op("dve", lambda: V.scalar_tensor_tensor(out=wk["dd"], in0=wk["dd"], scalar=-1.0, in1=wk["dd"], op0=ALU.mult, op1=ALU.max), reads=[rw], writes=[rw])
                kb.op("act", lambda: A.activation(out=wk["Dsym"], in_=wk["dd"], func=AF.Exp, scale=-1.0), reads=[rw], writes=[rw])
                kb.op("act", lambda: A.activation(out=wk["egrow"], in_=ps2[:, 0:128], func=AF.Exp), reads=[rp2], writes=[rw])
                kb.op("dve", lambda: V.tensor_copy(out=cols[0:64, 3:4], in_=ps2[0:64, lastA:lastA + 1]), reads=[rp2], writes=[rw])
                kb.op("dve", lambda: V.tensor_copy(out=cols[64:128, 3:4], in_=ps2[64:128, lastB:lastB + 1]), reads=[rp2], writes=[rw])
                kb.op("act", lambda: A.activation(out=cols[:, 0:1], in_=gcc, func=AF.Exp), reads=[r_ab], writes=[rw])
                kb.op("dve", lambda: V.tensor_tensor(out=cols[:, 1:2], in0=cols[:, 0:1], in1=bc, op=ALU.mult), reads=[rw, r_ab], writes=[rw])
                kb.op("act", lambda: A.activation(out=cols[:, 2:3], in_=gcc, func=AF.Exp, bias=cols[:, 3:4], scale=-1.0), reads=[rw, r_ab], writes=[rw])
                kb.op("act", lambda: A.copy(out=sl["gl"][:, 0:1], in_=wk["egrow"][:, lastA:lastA + 1]), reads=[rw], writes=[rsl])
                kb.op("act", lambda: A.copy(out=sl["gl"][:, 1:2], in_=wk["egrow"][:, lastB:lastB + 1]), reads=[rw], writes=[rsl])
                mp = mpair[dr_]
                id2 = self.ident[:].unsqueeze(1).to_broadcast([128, 2, 128])
                v3 = lambda ap: ap.rearrange("p (a c) -> p a c", a=2)
                kb.op("pool", lambda: G.tensor_tensor(out=v3(wk["PX2"]), in0=wk["Dsym"].unsqueeze(1).to_broadcast([128, 2, 128]), in1=v3(mp), op=ALU.mult),
                      reads=[rw, r_masks], writes=[rw])
                yield
                kb.op("dve", lambda: V.scalar_tensor_tensor(out=wk["A0"], in0=wk["GR"][:, 0:128], scalar=bc, in1=wk["DMs"], op0=ALU.mult, op1=ALU.mult),
                      reads=[rw, r_ab], writes=[rw])
                yield
                kb.op("dve", lambda: V.tensor_tensor(out=sl["aqkT"], in0=wk["GR"][:, 128:256], in1=wk["DMi"], op=ALU.mult), reads=[rw], writes=[rsl])
                yield
                kb.op("dve", lambda: V.tensor_tensor(out=sl["qdT"], in0=qT[:, tcs], in1=wk["egrow"], op=ALU.mult), reads=[rw, r_q], writes=[rsl])
                yield
                kb.op("act", lambda: A.activation(out=wk["kbe"], in_=ktok[:, tt, :], func=AF.Copy, scale=cols[:, 1:2]), reads=[rw, r_ktok], writes=[rw])
                yield
                kb.op("act", lambda: A.activation(out=wk["vb"], in_=vtok[:, tt, :], func=AF.Copy, scale=bc), reads=[rw, r_vtok, r_ab], writes=[rw])
                yield
                kb.op("act", lambda: A.activation(out=sl["ktail"], in_=ktok[:, tt, :], func=AF.Copy, scale=cols[:, 2:3]), reads=[rw, r_ktok], writes=[rsl])
                yield
                ps3, rp3 = self.next_ps()
                kb.op("pe", lambda: P.transpose(ps3[:, 0:128], wk["A0"], self.ident[:]), reads=[rw, self.r_const], writes=[rp3])
                kb.op("act", lambda: A.copy(out=wk["AT0"], in_=ps3[:, 0:128]), reads=[rp3], writes=[rw])
                PX, PX2, PR, PA = wk["PX"], wk["PX2"], wk["PR"], wk["PA"]
                bd2 = m_bd16.unsqueeze(1).to_broadcast([128, 2, 128])
                kb.op("pool", lambda: G.tensor_tensor(out=v3(PX), in0=v3(PA), in1=bd2, op=ALU.mult), reads=[rw, r_masks], writes=[rw])
                yield
                kb.op("dve", lambda: V.tensor_tensor(out=v3(PR), in0=id2, in1=v3(PX), op=ALU.subtract), reads=[rw, self.r_const], writes=[rw])
                for it in range(3):
                    X, XT = PX[:, 0:128], PX[:, 128:256]
                    yield
                    psA, rpA = self.next_ps()
                    self.mm(psA[:, 0:128], XT, X, True, True, reads=[rw], writes=[rpA], signal=False)
                    self.mm(psA[:, 128:256], X, XT, True, True, reads=[rw], writes=[rpA])
                    kb.op("act", lambda: A.copy(out=PX2, in_=psA[:, 0:256]), reads=[rpA], writes=[rw])
                    X2, XT2 = PX2[:, 0:128], PX2[:, 128:256]
                    yield
                    psB, rpB = self.next_ps()
                    self.mm(psB[:, 0:128], XT2, PR[:, 0:128], True, True, reads=[rw], writes=[rpB], signal=False)
                    self.mm(psB[:, 128:256], X2, PR[:, 128:256], True, True, reads=[rw], writes=[rpB])
                    kb.op("dve", lambda: V.tensor_tensor(out=PR, in0=PR, in1=psB[:, 0:256], op=ALU.add), reads=[rpB, rw], writes=[rw])
                    PX, PX2 = PX2, PX
                for msk in (m_s1, m_s2):
                    ms2 = msk.unsqueeze(1).to_broadcast([128, 2, 128])
                    kb.op("pool", lambda: G.tensor_tensor(out=v3(PX), in0=v3(PA), in1=ms2, op=ALU.mult), reads=[rw, r_masks], writes=[rw])
                    X, XT = PX[:, 0:128], PX[:, 128:256]
                    yield
                    ps1, rp1 = self.next_ps()
                    self.mm(ps1[:, 0:128], XT, PR[:, 0:128], True, True, reads=[rw], writes=[rp1], signal=False)
                    self.mm(ps1[:, 128:256], X, PR[:, 128:256], True, True, reads=[rw], writes=[rp1])
                    kb.op("act", lambda: A.copy(out=PX2, in_=ps1[:, 0:256]), reads=[rp1], writes=[rw])
                    yield
                    ps2_, rp2_ = self.next_ps()
                    self.mm(ps2_[:, 0:128], PR[:, 128:256], PX2[:, 0:128], True, True, reads=[rw], writes=[rp2_], signal=False)
                    self.mm(ps2_[:, 128:256], PR[:, 0:128], PX2[:, 128:256], True, True, reads=[rw], writes=[rp2_])
                    kb.op("dve", lambda: V.tensor_tensor(out=PR, in0=PR, in1=ps2_[:, 0:256], op=ALU.subtract), reads=[rp2_, rw], writes=[rw])
                yield
                psW, rpW = self.next_ps()
                self.mm(psW[:, 0:128], wk["kbe"], wk["RT"], True, True, reads=[rw], writes=[rpW], signal=False)
                self.mm(psW[:, 128:256], wk["RT"], wk["vb"], True, True, reads=[rw], writes=[rpW])
                kb.op("act", lambda: A.copy(out=sl["WU"], in_=psW[:, 0:256]), reads=[rpW], writes=[rsl])

            def scan(ci, step):
                ch = chains[ci]
                tt = ch["tiles"][step]
                sl = slots[ci][step % len(slots[ci])]
                rsl = sl["r"]
                tcs = slice(tt * 128, (tt + 1) * 128)
                for half in ((0, 1) if ch["dir"] == 0 else (1, 0)):
                    hs = slice(half * 64, (half + 1) * 64)
                    hcs = slice(tt * 128 + half * 64, tt * 128 + (half + 1) * 64)
                    yield
                    psP, rpP = self.next_ps()
                    self.mm(psP[:, 0:128], sl["WT"], Sst[ci], True, True, reads=[rsl, r_S[ci]], writes=[rpP])
                    kb.op("dve", lambda: V.tensor_tensor(out=vnew[ci][hs, :], in0=sl["U"][hs, :], in1=psP[hs, 0:128], op=ALU.subtract),
                          reads=[rsl, rpP], writes=[r_vnew[ci]])
                    yield
                    psO, rpO = self.next_ps()
                    self.mm(psO[:, 0:64], Sst[ci], sl["qdT"][:, hs], True, False, reads=[rsl, r_S[ci]], writes=[rpO])
                    self.mm(psO[:, 0:64], vnew[ci][hs, :], sl["aqkT"][hs, hs], False, True, reads=[rsl, r_vnew[ci]], writes=[rpO])
                    psS, rpS = self.next_ps()
                    self.mm(psS[:, 0:128], sl["ktail"][hs, :], vnew[ci][hs, :], True, True, reads=[rsl, r_vnew[ci]], writes=[rpS])
                    okey = (tt, half)
                    if okey not in owritten:
                        owritten.add(okey)
                        kb.op("act", lambda: A.copy(out=oT[:, hcs], in_=psO[:, 0:64]), reads=[rpO], writes=[r_oT[tt]])
                    else:
                        kb.op("dve", lambda: V.tensor_tensor(out=oT[:, hcs], in0=oT[:, hcs], in1=psO[:, 0:64], op=ALU.add), reads=[rpO, r_oT[tt]], writes=[r_oT[tt]])
                    kb.op("dve", lambda: V.scalar_tensor_tensor(out=Sst[ci], in0=Sst[ci], scalar=sl["gl"][:, half:half + 1], in1=psS[:, 0:128], op0=ALU.mult, op1=ALU.add),
                          reads=[rpS, rsl, r_S[ci]], writes=[r_S[ci]])
                if ch["seq"] is not None and step == len(ch["tiles"]) - 1:
                    kb.dma("sp", d["ns_out"][ch["seq"], j, ch["dir"], h], Sst[ci], reads=[r_S[ci]], writes=[self.r_nsout])

            owritten = set()

            def run_interleaved(gens):
                active = list(gens)
                while active:
                    for g_ in list(active):
                        try:
                            next(g_)
                        except StopIteration:
                            active.remove(g_)
            NW = len(works)
            nI = [0] * NCHAIN
            nS = [0] * NCHAIN
            nT = [len(chains[ci]["tiles"]) for ci in range(NCHAIN)]
            while any(nS[ci] < nT[ci] for ci in range(NCHAIN)):
                gens = []
                scan_ready = [ci for ci in range(NCHAIN) if nS[ci] < nI[ci]]
                cand = [ci for ci in range(NCHAIN) if nI[ci] < nT[ci] and nI[ci] - nS[ci] < len(slots[ci])]
                cand.sort(key=lambda ci: (nI[ci] - nS[ci], ci))
                picked = cand[:NW]
                for ci in picked:
                    gens.append(intra(ci, nI[ci]))
                for ci in scan_ready:
                    gens.append(scan(ci, nS[ci]))
                run_interleaved(gens)
                for ci in picked:
                    nI[ci] += 1
                for ci in scan_ready:
                    nS[ci] += 1
            kb.op("act", lambda: A.activation(out=sqb, in_=oT, func=AF.Square), reads=r_oT, writes=[r_sqb])
            for b in range(NTB):
                ps, rp = self.next_ps()
                self.mm(ps[:], self.onesb[:], sqb[:, b * 512:(b + 1) * 512], True, True, reads=[r_sqb, self.r_const], writes=[rp])
                rs, rrs = self.next_rstd()
                kb.op("dve", lambda: V.tensor_scalar(out=rs[:], in0=ps[:], scalar1=1.0 / 128.0, scalar2=EPS, op0=ALU.mult, op1=ALU.add), reads=[rp], writes=[rrs])
                kb.op("act", lambda: A.activation(out=rs[:], in_=rs[:], func=AF.Sqrt), reads=[rrs], writes=[rrs])
                kb.op("dve", lambda: V.reciprocal(out=rs[:], in_=rs[:]), reads=[rrs], writes=[rrs])
                t0, r0 = self.next_tmp()
                kb.op("dve", lambda: V.scalar_tensor_tensor(out=t0[:], in0=oT[:, b * 512:(b + 1) * 512], scalar=onorm, in1=rs[:], op0=ALU.mult, op1=ALU.mult),
                      reads=r_oT[b * 4:(b + 1) * 4] + [rrs, r_gpar], writes=[r0])
                kb.op("dve", lambda: V.tensor_tensor(out=og[:, h, b * 512:(b + 1) * 512], in0=t0[:], in1=gateT[:, b * 512:(b + 1) * 512], op=ALU.mult),
                      reads=[r0, r_gate], writes=[r_og[h]])
        self.inherit(allx, [r_pp, r_cv, r_q, r_k, r_v, r_gate, r_sqb, r_ktok, r_vtok, r_ab, r_gpar] + r_oT + r_S + r_works + extra_slots)
        if False:
            for b in range(NTB):
                t0, r0 = self.next_tmp()
                kb.op("dve", lambda: V.tensor_copy(out=t0[:], in_=og[:, 0, b * 512:(b + 1) * 512]), reads=[r_og[0]], writes=[r0])
                kb.dma("sp", d["dbg"][5, :, b * 512:(b + 1) * 512], t0[:], reads=[r0], writes=[self.r_nsout])
        self.inherit([self.r_wslot[0]], [works[2]["r"]])
        self.inherit([self.r_wslot[1]], [works[3]["r"]])
        self.out_proj(l, d["gdn_w_out"][j], og, r_og, None, None)
        if False:
            pass
        self.inherit([self.r_big], allr2 + r_works)
        self.inherit([self.r_wslot[4]], [r_masks] + r_vnew)


    def out_proj(self, l, wdram, src, r_src, bias_fn, r_bias):
        nc, kb = self.nc, self.kb
        V, A, P = nc.vector, nc.scalar, nc.tensor
        d = self.dr
        allx = [r for rr in self.r_x for r in rr]
        kb.dma("sp", self.xT[:].rearrange("p c t -> p (c t)"), d["xsp"][:, :], writes=allx)
        wo = []
        for hlf in range(2):
            wv = self.wslots[hlf][:, 0:NCH * 512].rearrange("p (c n) -> p c n", c=NCH)
            kb.dma("pool", wv, wdram[:, hlf * 512:(hlf + 1) * 512].rearrange("(c p) n -> p c n", p=128), writes=[self.r_wslot[hlf]])
            wo.append(wv)
        for c in range(NCH):
            for b in range(NTB):
                w = 0 if b < 2 else 1
                ps, rp = self.next_ps()
                for cb in range(NCH):
                    self.mm(ps[:], wo[c // 4][:, cb, (c % 4) * 128:(c % 4 + 1) * 128], src[:, cb, b * 512:(b + 1) * 512],
                            start=(cb == 0), stop=(cb == NCH - 1), reads=[self.r_wslot[c // 4], r_src[cb]], writes=[rp])
                t0, r0 = self.next_tmp()
                if bias_fn is not None:
                    kb.op("dve", lambda: V.tensor_scalar(out=t0[:], in0=ps[:], scalar1=bias_fn(c), scalar2=self.mod(l, w, 2, c), op0=ALU.add, op1=ALU.mult),
                          reads=[rp, r_bias, self.r_mods[l]], writes=[r0])
                else:
                    kb.op("dve", lambda: V.tensor_scalar(out=t0[:], in0=ps[:], scalar1=self.mod(l, w, 2, c), scalar2=None, op0=ALU.mult),
                          reads=[rp, self.r_mods[l]], writes=[r0])
                xs = self.xT[:, c, b * 512:(b + 1) * 512]
                kb.op("dve", lambda: V.tensor_tensor(out=xs, in0=xs, in1=t0[:], op=ALU.add), reads=[r0, self.r_x[c][b]], writes=[self.r_x[c][b]])

    def inherit(self, dsts, srcs):
        evs = []
        for sres in srcs:
            if sres.w is not None:
                evs.append(sres.w)
            evs.extend(sres.rs)
        best = {}
        for ev in evs:
            k = id(ev[0])
            if k not in best or best[k][1] < ev[1]:
                best[k] = ev
        for dres in dsts:
            own = ([dres.w] if dres.w is not None else []) + list(dres.rs)
            b2 = dict(best)
            for ev in own:
                k = id(ev[0])
                if k not in b2 or b2[k][1] < ev[1]:
                    b2[k] = ev
            dres.w = None
            dres.rs = list(b2.values())

    def sin3(self, buf, rbuf, npart, ncol):
        nc, kb = self.nc, self.kb
        V, A = nc.vector, nc.scalar
        q, rq = self.next_tmp()
        bv = buf[0:npart, 0:ncol]
        qv = q[0:npart, 0:ncol]
        kb.op("act", lambda: A.activation(out=bv, in_=bv, func=AF.Sin, scale=1.0 / 3.0), reads=[rbuf], writes=[rbuf])
        kb.op("dve", lambda: V.tensor_tensor(out=qv, in0=bv, in1=bv, op=ALU.mult), reads=[rbuf], writes=[rq])
        kb.op("dve", lambda: V.tensor_scalar(out=qv, in0=qv, scalar1=-4.0, scalar2=3.0, op0=ALU.mult, op1=ALU.add), reads=[rq], writes=[rq])
        kb.op("dve", lambda: V.tensor_tensor(out=bv, in0=bv, in1=qv, op=ALU.mult), reads=[rq, rbuf], writes=[rbuf])

    def hyena_layer(self, l):
        nc, kb = self.nc, self.kb
        V, A, P, G = nc.vector, nc.scalar, nc.tensor, nc.gpsimd
        d = self.dr
        j = l // 2
        self.norm_mod(l, 2 * l, 0, 1)
        allx = [r for rr in self.r_x for r in rr]
        kb.dma("sp", d["xsp"][:, :], self.xT[:].rearrange("p c t -> p (c t)"), reads=allx)
        R1 = self.xT[:].rearrange("p c t -> p (c t)")
        PW = 1544
        pp = [R1[:, i * PW:(i + 1) * PW] for i in range(3)]
        o1 = 3 * PW
        u = [R1[:, o1 + i * T:o1 + (i + 1) * T] for i in range(3)]
        o1 += 3 * T
        zt = R1[:, o1:o1 + 768].bitcast(BF16).rearrange("p (t c) -> p t c", t=12)
        r_pp = [Res(f"pp{i}") for i in range(3)]
        r_u = [Res(f"u{i}") for i in range(3)]
        r_zt = Res("zt")
        self.inherit(r_pp + r_u + [r_zt], allx)
        R2 = self.big
        zfin = R2[:, 0:NCH * T].rearrange("p (c t) -> p c t", c=NCH)
        o2 = NCH * T
        hid_lat = R2[:, o2:o2 + 2048].bitcast(F32); o2 += 2048
        hid_ctx = R2[:, o2:o2 + 512].bitcast(F32); o2 += 512
        delta_b = R2[:, o2:o2 + 2048].bitcast(F32); o2 += 2048
        fv = {}
        for nm in ("A0", "B0", "hf0", "hf1", "nhf1", "hb0", "nhb0", "hb1"):
            fv[nm] = R2[:, o2:o2 + 512].rearrange("p (q c) -> p q c", q=4); o2 += 512
        for nm in ("Ac", "Bc"):
            fv[nm] = R2[:, o2:o2 + 256].rearrange("p (q c) -> p q c", q=2); o2 += 256
        spec = R2[:, o2:o2 + 12288].bitcast(F32)
        o2 += 12288
        assert o2 <= 33792
        SP = [spec[:, i * 512:(i + 1) * 512] for i in range(10)]
        Yb = spec[:, 5120:6144].bitcast(BF16)
        Y = [Yb[:, i * 512:(i + 1) * 512] for i in range(4)]
        r_SP = [Res(f"sp{i}") for i in range(10)]
        r_Y = [Res(f"Y{i}") for i in range(4)]
        r_zfin = [Res(f"zfin{c}") for c in range(NCH)]
        r_hid = Res("hid"); r_delta = Res("delta"); r_fv = Res("fv")
        self.inherit(r_SP + r_Y + r_zfin + [r_hid, r_delta, r_fv], [self.r_big])
        if not hasattr(self, "hy_alloc"):
            self.hy_alloc = dict(
                vecT=kb.sb("s_hyvecT", [128, 120], F32), vec64=kb.sb("s_hyvec64", [64, 4], F32),
                w1=kb.sb("s_hyw1", [33, 64], F32), w2=kb.sb("s_hyw2", [64, 64], F32),
                tneg=kb.sb("s_hytneg", [128, 10], F32), mask0=kb.sb("s_hymask0", [128, 1], F32),
                w3cb=[kb.sb(f"s_hyw3cb_{i}", [64, 4, 128], F32) for i in range(2)],
                r_w3cb=[Res("w3cb0"), Res("w3cb1")], r_par=Res("hypar"))
        ha = self.hy_alloc
        vecT, vec64, w1, w2, tneg, mask0, w3cb, r_w3cb, r_par = (ha[k] for k in
            ("vecT", "vec64", "w1", "w2", "tneg", "mask0", "w3cb", "r_w3cb", "r_par"))
        kb.dma("sp", vecT[:], d["hy_vecT"][j], writes=[r_par])
        kb.dma("sp", vec64[:, 0:3], d["hy_vec64"][j], writes=[r_par])
        kb.dma("sp", w1[:], d["hy_f_w1"][j], writes=[r_par])
        kb.dma("sp", w2[:], d["hy_f_w2"][j], writes=[r_par])
        kb.dma("sp", tneg[:], d["hytneg"][:, :], writes=[r_par])
        kb.dma("sp", mask0[:], d["mask0"][:, :], writes=[r_par])
        kb.dma("sp", delta_b, d["hydelta"][0:1, :].partition_broadcast(128), writes=[r_delta])
        kb.op("dve", lambda: V.tensor_tensor(out=vec64[:, 3:4], in0=vec64[:, 0:1], in1=vec64[:, 2:3], op=ALU.mult), reads=[r_par], writes=[r_par])
        b_in = lambda blk: vecT[:, blk:blk + 1]
        cw = lambda k, blk: vecT[:, 24 + k * 24 + blk:24 + k * 24 + blk + 1]
        skp = lambda n, c: vecT[:, 96 + n * 8 + c:96 + n * 8 + c + 1]
        b_out = lambda c: vecT[:, 112 + c:113 + c]
        tabs = []
        for i in range(6):
            sl = self.wslots[i // 2]
            tv = sl[:, (i % 2) * 2048:(i % 2 + 1) * 2048].rearrange("p (q k) -> p q k", q=4)
            kb.dma("pool", tv, d["hytab"][i].rearrange("(q p) k -> p q k", p=128), writes=[self.r_wslot[i // 2]])
            tabs.append(tv)
        Cf, Sf, Cr, Sr, Ci, Si = tabs
        r_tab = [self.r_wslot[0], self.r_wslot[0], self.r_wslot[1], self.r_wslot[1], self.r_wslot[2], self.r_wslot[2]]
        rCf, rSf, rCr, rSr, rCi, rSi = r_tab
        for (zname, L, hid) in (("zemb_lat", LLAT, hid_lat), ("zemb_ctx", LCTX, hid_ctx)):
            for c0 in range(0, L, 512):
                ncol = min(512, L - c0)
                ze, rze = self.next_tmp()
                kb.dma("sp", ze[0:33, 0:ncol], d[zname][:, c0:c0 + ncol], writes=[rze])
                ps, rp = self.next_ps()
                self.mm(ps[0:64, 0:ncol], w1[:, :], ze[0:33, 0:ncol], True, True, reads=[rze, r_par], writes=[rp])
                h1, rh1 = self.next_tmp()
                kb.op("dve", lambda: V.tensor_scalar(out=h1[0:64, 0:ncol], in0=ps[0:64, 0:ncol], scalar1=vec64[:, 0:1], scalar2=vec64[:, 2:3],
                                                     op0=ALU.add, op1=ALU.mult), reads=[rp, r_par], writes=[rh1])
                self.sin3(h1, rh1, 64, ncol)
                ps2, rp2 = self.next_ps()
                self.mm(ps2[0:64, 0:ncol], w2[:, :], h1[0:64, 0:ncol], True, True, reads=[rh1, r_par], writes=[rp2])
                hv = hid[0:64, c0:c0 + ncol]
                kb.op("dve", lambda: V.tensor_scalar(out=hv, in0=ps2[0:64, 0:ncol], scalar1=vec64[:, 1:2], scalar2=vec64[:, 2:3],
                                                     op0=ALU.add, op1=ALU.mult), reads=[rp2, r_par], writes=[r_hid])
                kb.op("act", lambda: A.activation(out=hv, in_=hv, func=AF.Sin, scale=1.0 / 3.0), reads=[r_hid], writes=[r_hid])
                q, rq = self.next_tmp()
                qv = q[0:64, 0:ncol]
                kb.op("dve", lambda: V.tensor_tensor(out=qv, in0=hv, in1=hv, op=ALU.mult), reads=[r_hid], writes=[rq])
                kb.op("dve", lambda: V.tensor_scalar(out=qv, in0=qv, scalar1=-4.0, scalar2=3.0, op0=ALU.mult, op1=ALU.add), reads=[rq], writes=[rq])
                kb.op("dve", lambda: V.tensor_tensor(out=hv, in0=hv, in1=qv, op=ALU.mult), reads=[rq, r_hid], writes=[r_hid])
        for i in range(3):
            kb.op("dve", lambda: V.memset(pp[i], 0.0), writes=[r_pp[i]])
        segs = [(1, LLAT, 0), (1027, LCTX, LLAT), (1285, LCTX, LLAT + LCTX)]
        evi = [0]

        def evac(out, in_, reads, writes):
            evi[0] += 1
            if evi[0] % 2 == 0:
                kb.op("act", lambda: A.copy(out=out, in_=in_), reads=reads, writes=writes)
            else:
                kb.op("dve", lambda: V.tensor_copy(out=out, in_=in_), reads=reads, writes=writes)

        def spectrum(dst, rdst, terms):
            ps, rp = self.next_ps()
            tot = sum(t[4] for t in terms)
            for kb_ in range(4):
                i = 0
                for (tab, rtab, dat, rdat, nq) in terms:
                    for q in range(nq):
                        self.mm(ps[:, kb_ * 128:(kb_ + 1) * 128], tab[:, q, kb_ * 128:(kb_ + 1) * 128], dat[:, q, :],
                                start=(i == 0), stop=(i == tot - 1), reads=[rtab, rdat], writes=[rp], signal=(kb_ == 3 and i == tot - 1))
                        i += 1
            evac(dst, ps[:], [rp], [rdst])

        def cprod(dst, rdst, terms):
            acc, racc = self.next_tmp()
            n = len(terms)
            for i, (a, ra, b, rb, sg) in enumerate(terms):
                if i == 0:
                    kb.op("dve", lambda: V.tensor_tensor(out=acc[:], in0=a, in1=b, op=ALU.mult), reads=[ra, rb], writes=[racc])
                else:
                    t2, r2 = self.next_tmp()
                    kb.op("dve", lambda: V.tensor_tensor(out=t2[:], in0=a, in1=b, op=ALU.mult), reads=[ra, rb], writes=[r2])
                    o = dst if i == n - 1 else acc[:]
                    ro = rdst if i == n - 1 else racc
                    kb.op("dve", lambda: V.tensor_tensor(out=o, in0=acc[:], in1=t2[:], op=(ALU.add if sg > 0 else ALU.subtract)),
                          reads=[racc, r2], writes=[ro])

        for cb in range(NCH):
            wsl = self.wslots[3 + cb % 2]
            rws = self.r_wslot[3 + cb % 2]
            win = wsl[:, 0:NCH * 3 * 128].rearrange("p (c g n) -> p c g n", c=NCH, g=3)
            for g in range(3):
                kb.dma("pool", win[:, :, g, :], d["hy_w_in"][j, :, g * D + cb * 128:g * D + (cb + 1) * 128].rearrange("(c p) n -> p c n", p=128), writes=[rws])
            w3 = w3cb[cb % 2]
            rw3 = r_w3cb[cb % 2]
            kb.dma("sp", w3[:], d["hy_f_w3"][j].rearrange("r (g c) -> r g c", g=4)[:, :, cb * 128:(cb + 1) * 128], writes=[rw3])
            for g in range(3):
                blk = g * 8 + cb
                for b in range(NTB):
                    ps, rp = self.next_ps()
                    for c in range(NCH):
                        self.mm(ps[:], win[:, c, g, :], self.hT[:, c, b * 512:(b + 1) * 512], start=(c == 0), stop=(c == NCH - 1),
                                reads=[rws, self.r_h[c][b]], writes=[rp])
                    if b < 2:
                        dsts = [(pp[g][:, 1 + b * 512:1 + (b + 1) * 512], ps[:])]
                    else:
                        dsts = [(pp[g][:, 1027:1027 + 256], ps[:, 0:256]), (pp[g][:, 1285:1285 + 256], ps[:, 256:512])]
                    for (o, i_) in dsts:
                        kb.op("act", lambda: A.activation(out=o, in_=i_, func=AF.Identity, bias=b_in(blk), scale=1.0),
                              reads=[rp, r_par], writes=[r_pp[g]])
                for (s0, Ls, o0) in segs:
                    uo = u[g][:, o0:o0 + Ls]
                    kb.op("dve", lambda: V.tensor_scalar(out=uo, in0=pp[g][:, s0 - 1:s0 - 1 + Ls], scalar1=cw(0, blk), scalar2=None, op0=ALU.mult),
                          reads=[r_pp[g], r_par], writes=[r_u[g]])
                    for k in (1, 2):
                        kb.op("dve", lambda: V.scalar_tensor_tensor(out=uo, in0=pp[g][:, s0 - 1 + k:s0 - 1 + k + Ls], scalar=cw(k, blk), in1=uo,
                                                                    op0=ALU.mult, op1=ALU.add), reads=[r_pp[g], r_par, r_u[g]], writes=[r_u[g]])
            for n in range(2):
                z, rz = u[n], r_u[n]
                for g4 in range(3):
                    ps, rp = self.next_ps()
                    for i in range(4):
                        tt = g4 * 4 + i
                        kb.op("pe", lambda: P.transpose(ps[:, i * 128:(i + 1) * 128], z[:, tt * 128:(tt + 1) * 128], self.ident[:]),
                              reads=[rz, self.r_const], writes=[rp], signal=(i == 3))
                    evac(zt[:, g4 * 4:(g4 + 1) * 4, :], ps[:].rearrange("p (t c) -> p t c", t=4), [rp], [r_zt])
                for q in range(10):
                    lat = q < 8
                    hidv = hid_lat[0:64, q * 128:(q + 1) * 128] if lat else hid_ctx[0:64, (q - 8) * 128:(q - 7) * 128]
                    ps, rp = self.next_ps()
                    for dr_ in range(2):
                        self.mm(ps[:, dr_ * 128:(dr_ + 1) * 128], hidv, w3[:, n * 2 + dr_, :], True, True, reads=[r_hid, rw3], writes=[rp], signal=(dr_ == 1))
                    win_, rwin = self.next_tmp()
                    kb.op("act", lambda: A.activation(out=win_[:, 0:128], in_=delta_b[:, cb * 128:(cb + 1) * 128], func=AF.Exp, scale=tneg[:, q:q + 1]),
                          reads=[r_delta, r_par], writes=[rwin])
                    hh, rhh = self.next_tmp()
                    kb.op("dve", lambda: V.tensor_tensor(out=hh[:, 0:256].rearrange("p (a c) -> p a c", a=2), in0=ps[:, 0:256].rearrange("p (a c) -> p a c", a=2),
                                                         in1=win_[:, 0:128].unsqueeze(1).to_broadcast([128, 2, 128]), op=ALU.mult),
                          reads=[rp, rwin], writes=[rhh])
                    hf = hh[:, 0:128]
                    hb = hh[:, 128:256]
                    if q == 0 or q == 8:
                        kb.op("dve", lambda: V.tensor_scalar(out=hb, in0=hb, scalar1=mask0[:, 0:1], scalar2=None, op0=ALU.mult), reads=[rhh, r_par], writes=[rhh])
                    def wr(nm, qq, fn):
                        kb.op("dve", fn(fv[nm][:, qq, :]), reads=[rhh], writes=[r_fv])
                    if q < 4 or q >= 8:
                        nmA, nmB, qq = ("A0", "B0", q) if lat else ("Ac", "Bc", q - 8)
                        kb.op("dve", lambda: V.tensor_tensor(out=fv[nmA][:, qq, :], in0=hf, in1=hb, op=ALU.add), reads=[rhh], writes=[r_fv])
                        kb.op("dve", lambda: V.tensor_tensor(out=fv[nmB][:, qq, :], in0=hb, in1=hf, op=ALU.subtract), reads=[rhh], writes=[r_fv])
                        if lat:
                            kb.op("act", lambda: A.copy(out=fv["hf0"][:, q, :], in_=hf), reads=[rhh], writes=[r_fv])
                            kb.op("act", lambda: A.copy(out=fv["hb0"][:, q, :], in_=hb), reads=[rhh], writes=[r_fv])
                            kb.op("act", lambda: A.mul(out=fv["nhb0"][:, q, :], in_=hb, mul=-1.0), reads=[rhh], writes=[r_fv])
                    else:
                        kb.op("act", lambda: A.copy(out=fv["hf1"][:, q - 4, :], in_=hf), reads=[rhh], writes=[r_fv])
                        kb.op("act", lambda: A.copy(out=fv["hb1"][:, q - 4, :], in_=hb), reads=[rhh], writes=[r_fv])
                        kb.op("act", lambda: A.mul(out=fv["nhf1"][:, q - 4, :], in_=hf, mul=-1.0), reads=[rhh], writes=[r_fv])
                zl = [zt[:, 0:4, :], zt[:, 4:8, :]]
                spectrum(SP[0], r_SP[0], [(Cf, rCf, zl[0], r_zt, 4)])
                spectrum(SP[1], r_SP[1], [(Sf, rSf, zl[0], r_zt, 4)])
                spectrum(SP[2], r_SP[2], [(Cf, rCf, zl[1], r_zt, 4)])
                spectrum(SP[3], r_SP[3], [(Sf, rSf, zl[1], r_zt, 4)])
                spectrum(SP[4], r_SP[4], [(Cf, rCf, fv["A0"], r_fv, 4)])
                spectrum(SP[5], r_SP[5], [(Sf, rSf, fv["B0"], r_fv, 4)])
                spectrum(SP[6], r_SP[6], [(Cf, rCf, fv["hf1"], r_fv, 4), (Cr, rCr, fv["hf0"], r_fv, 4)])
                spectrum(SP[7], r_SP[7], [(Sf, rSf, fv["nhf1"], r_fv, 4), (Sr, rSr, fv["hf0"], r_fv, 4)])
                spectrum(SP[8], r_SP[8], [(Cf, rCf, fv["hb1"], r_fv, 4), (Cr, rCr, fv["hb0"], r_fv, 4)])
                spectrum(SP[9], r_SP[9], [(Sf, rSf, fv["hb1"], r_fv, 4), (Sr, rSr, fv["nhb0"], r_fv, 4)])
                S_ = lambda i: (SP[i], r_SP[i])
                def T4(a, b, sg):
                    return (SP[a], r_SP[a], SP[b], r_SP[b], sg)
                cprod(Y[0], r_Y[0], [T4(0, 4, 1), T4(1, 5, 1), T4(2, 8, 1), T4(3, 9, 1)])
                cprod(Y[1], r_Y[1], [T4(0, 5, 1), T4(1, 4, -1), T4(2, 9, 1), T4(3, 8, -1)])
                cprod(Y[2], r_Y[2], [T4(0, 6, 1), T4(1, 7, 1), T4(2, 4, 1), T4(3, 5, 1)])
                cprod(Y[3], r_Y[3], [T4(0, 7, 1), T4(1, 6, -1), T4(2, 5, 1), T4(3, 4, -1)])

                def inverse_and_gate(yre, ryre, yim, ryim, ncol, col0):
                    ps, rp = self.next_ps()
                    yre3 = yre.rearrange("p (q c) -> p q c", q=4)
                    yim3 = yim.rearrange("p (q c) -> p q c", q=4)
                    for q in range(4):
                        self.mm(ps[:, 0:ncol], yre3[:, q, :], Ci[:, q, 0:ncol], start=(q == 0), stop=False, reads=[ryre, rCi], writes=[rp])
                    for q in range(4):
                        self.mm(ps[:, 0:ncol], yim3[:, q, :], Si[:, q, 0:ncol], start=False, stop=(q == 3), reads=[ryim, rSi], writes=[rp])
                    t0, r0 = self.next_tmp()
                    zs = z[:, col0:col0 + ncol]
                    kb.op("dve", lambda: V.scalar_tensor_tensor(out=t0[:, 0:ncol], in0=zs, scalar=skp(n, cb), in1=ps[:, 0:ncol], op0=ALU.mult, op1=ALU.add),
                          reads=[rz, rp, r_par], writes=[r0])
                    un = u[n + 1][:, col0:col0 + ncol]
                    kb.op("dve", lambda: V.tensor_tensor(out=un, in0=t0[:, 0:ncol], in1=un, op=ALU.mult), reads=[r0, r_u[n + 1]], writes=[r_u[n + 1]])

                inverse_and_gate(Y[0], r_Y[0], Y[1], r_Y[1], 512, 0)
                inverse_and_gate(Y[2], r_Y[2], Y[3], r_Y[3], 512, 512)
                zc = [zt[:, 8:10, :], zt[:, 10:12, :]]
                spectrum(SP[0], r_SP[0], [(Cf, rCf, zc[0], r_zt, 2)])
                spectrum(SP[1], r_SP[1], [(Sf, rSf, zc[0], r_zt, 2)])
                spectrum(SP[2], r_SP[2], [(Cf, rCf, zc[1], r_zt, 2)])
                spectrum(SP[3], r_SP[3], [(Sf, rSf, zc[1], r_zt, 2)])
                spectrum(SP[4], r_SP[4], [(Cf, rCf, fv["Ac"], r_fv, 2)])
                spectrum(SP[5], r_SP[5], [(Sf, rSf, fv["Bc"], r_fv, 2)])
                cprod(Y[0], r_Y[0], [T4(0, 4, 1), T4(1, 5, 1)])
                cprod(Y[1], r_Y[1], [T4(0, 5, 1), T4(1, 4, -1)])
                cprod(Y[2], r_Y[2], [T4(2, 4, 1), T4(3, 5, 1)])
                cprod(Y[3], r_Y[3], [T4(2, 5, 1), T4(3, 4, -1)])
                inverse_and_gate(Y[0], r_Y[0], Y[1], r_Y[1], 256, LLAT)
                inverse_and_gate(Y[2], r_Y[2], Y[3], r_Y[3], 256, LLAT + LCTX)
            kb.op("act", lambda: A.copy(out=zfin[:, cb, :], in_=u[2]), reads=[r_u[2]], writes=[r_zfin[cb]])
        self.inherit(allx, r_pp + r_u + [r_zt])
        self.out_proj(l, d["hy_w_out"][j], zfin, r_zfin, b_out, r_par)
        self.inherit([self.r_big], r_SP + r_Y + r_zfin + [r_hid, r_delta, r_fv])

    def final_out(self):
        nc, kb = self.nc, self.kb
        V, A, P = nc.vector, nc.scalar, nc.tensor
        d = self.dr
        self.out_res = []
        gidx = 2 * DEPTH
        for b in range(NTB):
            ps, rp = self.next_ps()
            for c in range(NCH):
                sq, rsq = self.next_tmpb()
                kb.op("act", lambda: A.activation(out=sq[:], in_=self.xT[:, c, b * 512:(b + 1) * 512], func=AF.Square),
                      reads=[self.r_x[c][b]], writes=[rsq])
                self.mm(ps[:], self.onesb[:], sq[:], start=(c == 0), stop=(c == NCH - 1), reads=[rsq, self.r_const], writes=[rp], signal=True)
            rstd, rr = self.next_rstd()
            kb.op("dve", lambda: V.tensor_scalar(out=rstd[:], in0=ps[:], scalar1=1.0 / D, scalar2=EPS, op0=ALU.mult, op1=ALU.add),
                  reads=[rp], writes=[rr])
            kb.op("act", lambda: A.activation(out=rstd[:], in_=rstd[:], func=AF.Sqrt), reads=[rr], writes=[rr])
            kb.op("dve", lambda: V.reciprocal(out=rstd[:], in_=rstd[:]), reads=[rr], writes=[rr])
            for c in range(NCH):
                xs = self.xT[:, c, b * 512:(b + 1) * 512]
                kb.op("dve", lambda: V.scalar_tensor_tensor(out=xs, in0=xs, scalar=self.gvec[:, gidx, c:c + 1],
                                                            in1=rstd[:], op0=ALU.mult, op1=ALU.mult),
                      reads=[self.r_x[c][b], rr, self.r_gvec], writes=[self.r_x[c][b]])
        for tt in range(T // 128):
            b = tt // 4
            lat = tt < 8
            dst = d["y_lat"][tt * 128:(tt + 1) * 128, :] if lat else d["y_ctx"][(tt - 8) * 128:(tt - 7) * 128, :]
            for half in range(2):
                ps, rp = self.next_ps()
                for j in range(4):
                    c = half * 4 + j
                    kb.op("pe", lambda: P.transpose(ps[:, j * 128:(j + 1) * 128], self.xT[:, c, tt * 128:(tt + 1) * 128], self.ident[:]),
                          reads=[self.r_x[c][b], self.r_const], writes=[rp], signal=(j == 3))
                t0, r0 = self.next_tmp()
                if half == 0:
                    kb.op("act", lambda: A.copy(out=t0[:], in_=ps[:]), reads=[rp], writes=[r0])
                else:
                    kb.op("dve", lambda: V.tensor_copy(out=t0[:], in_=ps[:]), reads=[rp], writes=[r0])
                ro = Res("out")
                kb.dma("sp", dst[:, half * 512:(half + 1) * 512], t0[:], reads=[r0], writes=[ro])
                self.out_res.append(ro)


_CACHE = {}
LAST_DBG = None


def get_prog(skip=()):
    key = tuple(sorted(skip))
    if key not in _CACHE:
        _CACHE[key] = Prog(skip)
    return _CACHE[key]


def make_in_maps(inp, n=8):
    hc = host_consts()
    f = lambda a: np.ascontiguousarray(np.asarray(a, dtype=np.float32))
    shared = {
        "pos": hc["pos"], "ident": hc["ident"],
        "ada_w": f(inp["ada_w"]),
        "ffn_w_gu": f(inp["ffn_w_gu"]), "ffn_w_down": f(inp["ffn_w_down"]),
    }
    shared["gmasks"] = hc["gmasks"]
    for kk in ("gdn_w_in", "gdn_w_out"):
        shared[kk] = f(inp[kk])
    gv = []
    for j in range(2):
        gv.append(np.concatenate([f(inp["gdn_conv"][j]).reshape(5 * 24, 128).T, f(inp["gdn_onorm"][j]).reshape(128, 1)], axis=1))
    shared["gdn_vecT"] = np.ascontiguousarray(np.stack(gv, axis=0))
    shared["gdn_rows"] = np.ascontiguousarray(np.stack([f(inp["gdn_a_log"]).reshape(2, 16), f(inp["gdn_dt_bias"]).reshape(2, 16)], axis=1))
    for kk in ("hytab", "zemb_lat", "zemb_ctx", "hydelta", "hytneg", "mask0"):
        shared[kk] = hc[kk]
    for kk in ("hy_w_in", "hy_w_out", "hy_f_w1", "hy_f_w2", "hy_f_w3"):
        shared[kk] = f(inp[kk])
    shared["hy_vec64"] = np.ascontiguousarray(np.stack([f(inp["hy_f_b1"]), f(inp["hy_f_b2"]), f(inp["hy_freq"])], axis=2))
    hv = []
    for j in range(2):
        parts = [f(inp["hy_b_in"][j]).reshape(24, 128).T]
        parts.append(f(inp["hy_conv"][j]).reshape(3 * 24, 128).T)
        parts.append(f(inp["hy_skip"][j]).reshape(2 * 8, 128).T)
        parts.append(f(inp["hy_b_out"][j]).reshape(8, 128).T)
        hv.append(np.concatenate(parts, axis=1))
    shared["hy_vecT"] = np.ascontiguousarray(np.stack(hv, axis=0))
    fm = lambda v: np.ascontiguousarray(f(v).reshape(-1, 128).T)
    gl = []
    for l in range(DEPTH):
        gl.append(fm(inp["norm1_g"][l]))
        gl.append(fm(inp["norm2_g"][l]))
    gl.append(fm(inp["final_g"]))
    shared["gvecT"] = np.ascontiguousarray(np.stack(gl, axis=1))
    shared["adabT"] = np.ascontiguousarray(np.stack([fm(inp["ada_b"][l]) for l in range(DEPTH)], axis=1))
    maps = []
    xp = f(inp["x_prompt"])
    xs = f(inp["x_sample"])
    cc = f(inp["c"])
    cctx = f(inp["c_ctx"])
    for i in range(n):
        m = dict(shared)
        m["x_lat"] = np.ascontiguousarray(xs[i])
        m["x_ctx"] = np.ascontiguousarray(xp[2 * i:2 * i + 2].reshape(2 * LCTX, D))
        m["state_in"] = np.ascontiguousarray(f(inp["state_delta"])[i])
        m["cvecT"] = np.ascontiguousarray(np.stack([fm(cc[i]), fm(cctx)], axis=2))
        maps.append(m)
    return maps


def run(inp, skip=(), trace=False):
    prog = get_prog(skip)
    maps = make_in_maps(inp)
    res = run_bass_kernel_spmd(prog.nc, maps, core_ids=list(range(8)), trace=trace)
    y_lat = np.stack([r["y_lat"] for r in res.results], axis=0)
    y_ctx = np.concatenate([r["y_ctx"].reshape(2, LCTX, D) for r in res.results], axis=0)
    ns = np.concatenate([r["ns_out"] for r in res.results], axis=0)
    global LAST_DBG
    LAST_DBG = [r.get("dbg") for r in res.results]
    return (y_ctx.astype(np.float32), y_lat.astype(np.float32), ns.astype(np.float32)), res


def kernel(**inputs):
    (y_prompt, y_sample, new_state), _ = run(inputs)
    return (y_prompt, y_sample, new_state)
```

```python
import math
import os
import contextlib
import numpy as np
import concourse.bass as bass
import concourse.mybir as mybir
from concourse.bass_utils import run_bass_kernel_spmd

F32 = mybir.dt.float32
BF16 = mybir.dt.bfloat16
ALU = mybir.AluOpType
AF = mybir.ActivationFunctionType

D = 1024
NCH = 8
LLAT = 1024
LCTX = 256
T = LLAT + 2 * LCTX
NTB = 3
DFF = 2816
NFB = 22
DEPTH = 4
EPS = 1e-6
SAME_ENGINE_SYNC = True


class Res:
    __slots__ = ("w", "rs", "name", "excl")

    def __init__(self, name="", excl=False):
        self.w = None
        self.rs = []
        self.name = name
        self.excl = excl


class Eng:
    def __init__(self, name, h, sem):
        self.name = name
        self.h = h
        self.sem = sem
        self.count = 0
        self.waited = {}


class KB:
    def __init__(self, nc):
        self.nc = nc
        self.es = contextlib.ExitStack()
        self.engs = {}
        for name, h in (("pe", nc.tensor), ("act", nc.scalar), ("dve", nc.vector),
                        ("pool", nc.gpsimd), ("sp", nc.sync)):
            sem = self.es.enter_context(nc.semaphore("sem_" + name))
            self.engs[name] = Eng(name, h, sem)
        self.dsem = {}
        for q in ("sp", "pool"):
            sems = [self.es.enter_context(nc.semaphore(f"dsem_{q}{i}")) for i in range(20)]
            self.dsem[q] = {"sems": sems, "tot": [0] * len(sems), "i": 0}
        self.semkey = {}
        self.ninst = 0

    def sb(self, name, shape, dt):
        return self.es.enter_context(self.nc.sbuf_tensor(name, list(shape), dt))

    def ps(self, name, shape, dt):
        return self.es.enter_context(self.nc.psum_tensor(name, list(shape), dt))

    def _waits(self, eng, reads, writes):
        need = {}
        def add(ev):
            if ev is None:
                return
            sem, val, src = ev
            if src == eng.name:
                if eng.name == "pe" or not SAME_ENGINE_SYNC:
                    return
            k = id(sem)
            if k not in need or need[k][1] < val:
                need[k] = (sem, val, src)
        for r in reads:
            add(r.w)
        for w in writes:
            add(w.w)
            for ev in w.rs:
                add(ev)
        for k, (sem, val, src) in need.items():
            if eng.waited.get(k, 0) >= val:
                continue
            if src in self.engs:
                assert self.engs[src].count >= val, f"pending unsignaled dep {src} {val} > {self.engs[src].count}"
            eng.h.wait_ge(sem, val)
            eng.waited[k] = val
            self.ninst += 1

    def op(self, en, fn, reads=(), writes=(), signal=True):
        eng = self.engs[en]
        xr = [r for r in reads if r.excl]
        if xr:
            writes = list(writes) + [r for r in xr if r not in writes]
        self._waits(eng, reads, writes)
        ins = fn()
        self.ninst += 1
        if signal:
            eng.count += 1
            ins.then_inc(eng.sem, 1)
            ev = (eng.sem, eng.count, en)
        else:
            ev = (eng.sem, eng.count + 1, en)
        for r in reads:
            r.rs.append(ev)
        for w in writes:
            w.w = ev
            w.rs = []
        return ev

    def dma(self, q, out, in_, reads=(), writes=()):
        eng = self.engs[q]
        self._waits(eng, reads, writes)
        d = self.dsem[q]
        i = d["i"] % len(d["sems"])
        d["i"] += 1
        sem = d["sems"][i]
        k = id(sem)
        if d["tot"][i] > 0 and eng.waited.get(k, 0) < d["tot"][i]:
            eng.h.wait_ge(sem, d["tot"][i])
            eng.waited[k] = d["tot"][i]
        eng.h.dma_start(out=out, in_=in_).then_inc(sem, 16)
        self.ninst += 1
        d["tot"][i] += 16
        ev = (sem, d["tot"][i], "dma_" + q)
        for r in reads:
            r.rs.append(ev)
        for w in writes:
            w.w = ev
            w.rs = []
        return ev

    def finish(self, res_list):
        eng = self.engs["sp"]
        self._waits(eng, res_list, res_list)
        self.es.close()


def host_consts():
    c = {}
    c["ident"] = np.eye(128, dtype=np.float32)
    GRID_W = 64
    rows = LLAT // GRID_W
    r, col = np.meshgrid(np.arange(rows), np.arange(GRID_W), indexing="ij")
    quarter = D // 4
    omega = (1.0 / (10000.0 ** (np.arange(quarter, dtype=np.float32) / quarter))).astype(np.float32)

    def emb1d(p):
        a = p.reshape(-1, 1).astype(np.float32) * omega[None, :]
        return np.concatenate([np.sin(a), np.cos(a)], axis=-1)

    c["pos"] = np.concatenate([emb1d(r), emb1d(col)], axis=-1).astype(np.float32)
    B = 512
    k = np.arange(B, dtype=np.float64)
    om = 2.0 * np.pi * (k + 0.5) / (2 * B)
    jj = np.arange(B, dtype=np.float64)
    Cf = np.cos(jj[:, None] * om[None, :])
    Sf = np.sin(jj[:, None] * om[None, :])
    Cr = np.cos((B - jj)[:, None] * om[None, :]); Cr[0, :] = 0.0
    Sr = np.sin((B - jj)[:, None] * om[None, :]); Sr[0, :] = 0.0
    Ci = (2.0 / (2 * B)) * np.cos(om[:, None] * jj[None, :])
    Si = -(2.0 / (2 * B)) * np.sin(om[:, None] * jj[None, :])
    c["hytab"] = np.stack([Cf, Sf, Cr, Sr, Ci, Si], axis=0).astype(np.float32)
    def zemb(seq):
        t = np.linspace(0.0, 1.0, seq, dtype=np.float32)[:, None]
        wpos = (2.0 * math.pi / seq) * np.arange(seq, dtype=np.float32)[:, None]
        fb = np.linspace(1e-4, 15, 16, dtype=np.float32)[None, :]
        z = np.concatenate([t, np.cos(fb * wpos), -np.sin(fb * wpos)], axis=-1).astype(np.float32)
        return np.ascontiguousarray(z.T)
    c["zemb_lat"] = zemb(LLAT)
    c["zemb_ctx"] = zemb(LCTX)
    mx = math.log(1e-2) / 0.3
    mn = math.log(1e-2) / 1.5
    c["hydelta"] = np.abs(np.linspace(mn, mx, D, dtype=np.float32)).reshape(1, D).astype(np.float32)
    tn = np.zeros((128, 10), np.float32)
    for q in range(8):
        tn[:, q] = -(q * 128 + np.arange(128)) / np.float32(LLAT - 1)
    for q in range(2):
        tn[:, 8 + q] = -(q * 128 + np.arange(128)) / np.float32(LCTX - 1)
    tn = -np.stack([np.linspace(0.0, 1.0, LLAT, dtype=np.float32).reshape(8, 128).T] , 0)[0]
    tc = -np.linspace(0.0, 1.0, LCTX, dtype=np.float32).reshape(2, 128).T
    c["hytneg"] = np.ascontiguousarray(np.concatenate([tn, tc], axis=1).astype(np.float32))
    pi_, fi_ = np.meshgrid(np.arange(128), np.arange(128), indexing="ij")
    sb_ = (pi_ // 64) == (fi_ // 64)
    bd16 = (pi_ // 16) == (fi_ // 16)
    s1 = ((pi_ // 32) == (fi_ // 32)) & ~bd16
    s2 = ((pi_ // 64) == (fi_ // 64)) & ((pi_ // 32) != (fi_ // 32))
    c["gmasks"] = np.stack([(pi_ > fi_) & sb_, (fi_ >= pi_) & sb_, (pi_ < fi_) & sb_, (fi_ <= pi_) & sb_,
                            (pi_ <= fi_) & sb_, (pi_ >= fi_) & sb_, bd16, s1, s2], axis=0).astype(np.float32)
    m0 = np.ones((128, 1), np.float32); m0[0, 0] = 0.0
    c["mask0"] = m0
    return c


class Prog:
    def __init__(self, skip=()):
        self.skip = set(skip)
        nc = bass.Bass("TRN2", target_bir_lowering=False)
        self.nc = nc
        self.kb = KB(nc)
        self.build()

    def din(self, name, shape, dt=F32):
        return self.nc.dram_tensor(name, list(shape), dt, kind="ExternalInput").ap()

    def dout(self, name, shape, dt=F32):
        return self.nc.dram_tensor(name, list(shape), dt, kind="ExternalOutput").ap()

    def build(self):
        nc, kb = self.nc, self.kb
        V, A, P, G = nc.vector, nc.scalar, nc.tensor, nc.gpsimd
        x_lat = self.din("x_lat", [LLAT, D])
        x_ctx = self.din("x_ctx", [2 * LCTX, D])
        cvecT = self.din("cvecT", [128, NCH, 2])
        gvecT = self.din("gvecT", [128, 2 * DEPTH + 1, NCH])
        adabT = self.din("adabT", [128, DEPTH, 48])
        pos = self.din("pos", [LLAT, D])
        identd = self.din("ident", [128, 128])
        ada_w = self.din("ada_w", [DEPTH, D, 6 * D])
        ffn_w_gu = self.din("ffn_w_gu", [DEPTH, D, 2 * DFF])
        ffn_w_down = self.din("ffn_w_down", [DEPTH, DFF, D])
        hytab = self.din("hytab", [6, 512, 512])
        zemb_lat = self.din("zemb_lat", [33, LLAT])
        zemb_ctx = self.din("zemb_ctx", [33, LCTX])
        hydelta = self.din("hydelta", [1, D])
        hytneg = self.din("hytneg", [128, 10])
        mask0 = self.din("mask0", [128, 1])
        hy_w_in = self.din("hy_w_in", [2, D, 3 * D])
        hy_w_out = self.din("hy_w_out", [2, D, D])
        hy_f_w1 = self.din("hy_f_w1", [2, 33, 64])
        hy_f_w2 = self.din("hy_f_w2", [2, 64, 64])
        hy_f_w3 = self.din("hy_f_w3", [2, 64, 4 * D])
        hy_vec64 = self.din("hy_vec64", [2, 64, 3])
        hy_vecT = self.din("hy_vecT", [2, 128, 24 + 72 + 16 + 8])
        gmasks = self.din("gmasks", [9, 128, 128])
        gdn_w_in = self.din("gdn_w_in", [2, D, 4128])
        gdn_w_out = self.din("gdn_w_out", [2, D, D])
        gdn_vecT = self.din("gdn_vecT", [2, 128, 5 * 24 + 1])
        gdn_rows = self.din("gdn_rows", [2, 2, 16])
        state_in = self.din("state_in", [2, 2, 8, 128, 128])
        ns_out = self.dout("ns_out", [2, 2, 2, 8, 128, 128])
        import os
        self.debug = bool(os.environ.get('GDN_DEBUG'))
        dbg = self.dout("dbg", [8, 128, T]) if self.debug else None
        xsp = self.nc.dram_tensor("xsp", [128, NCH * T], F32, kind="Internal").ap()
        y_lat = self.dout("y_lat", [LLAT, D])
        y_ctx = self.dout("y_ctx", [2 * LCTX, D])
        self.dr = dict(locals())

        xT = kb.sb("xT", [128, NCH, T], F32)
        hT = kb.sb("hT", [128, NCH, T], BF16)
        self.xT, self.hT = xT, hT
        self.r_x = [[Res(f"x{c}_{b}") for b in range(NTB)] for c in range(NCH)]
        self.r_h = [[Res(f"h{c}_{b}") for b in range(NTB)] for c in range(NCH)]
        big = kb.sb("big", [128, 33792], BF16)
        self.big = big
        self.r_big = Res("big")
        ident = kb.sb("identf", [128, 128], F32)
        identb = kb.sb("identb", [128, 128], BF16)
        onesb = kb.sb("onesb", [128, 128], BF16)
        self.ident, self.identb, self.onesb = ident, identb, onesb
        r_const = Res("const")
        self.r_const = r_const
        kb.dma("sp", ident[:], identd[:, :], writes=[r_const])
        kb.op("dve", lambda: V.tensor_copy(out=identb[:], in_=ident[:]), reads=[r_const], writes=[r_const])
        kb.op("dve", lambda: V.memset(onesb[:], 1.0), writes=[r_const])
        self.NSLOT = 5
        self.wslots = [kb.sb(f"wslot{i}", [128, 4096], BF16) for i in range(self.NSLOT)]
        self.r_wslot = [Res(f"wslot{i}") for i in range(self.NSLOT)]
        self.wi = 0
        self.NPS = 7
        self.psb = [kb.ps(f"psb{i}", [128, 512], F32) for i in range(8)]
        self.r_ps = [Res(f"ps{i}", excl=True) for i in range(8)]
        self.pi = 0
        self.NTMP = 4
        self.tmp = [kb.sb(f"tmp{i}", [128, 512], F32) for i in range(self.NTMP)]
        self.r_tmp = [Res(f"tmp{i}") for i in range(self.NTMP)]
        self.ti = 0
        self.rstdb = [kb.sb(f"rstd{i}", [128, 512], F32) for i in range(2)]
        self.r_rstd = [Res(f"rstd{i}") for i in range(2)]
        self.ri = 0
        self.tmpb = [kb.sb(f"tmpb{i}", [128, 512], BF16) for i in range(self.NTMP)]
        self.r_tmpb = [Res(f"tmpb{i}") for i in range(self.NTMP)]
        self.tbi = 0
        self.mods = kb.sb("mods", [128, DEPTH, 48, 2], F32)
        self.r_mods = [Res(f"mods{l}") for l in range(DEPTH)]
        self.csil = kb.sb("csil", [128, NCH, 2], F32)
        self.csilb = kb.sb("csilb", [128, NCH, 2], BF16)
        self.r_csil = Res("csil")
        self.small = kb.sb("small", [128, 64, 2], F32)
        self.r_small = Res("small")
        self.gvec = kb.sb("gvec", [128, 2 * DEPTH + 1, NCH], F32)
        self.r_gvec = Res("gvec")
        self.adab = kb.sb("adab", [128, DEPTH, 48], F32)
        self.r_adab = Res("adab")

        self.r_nsout = Res("nsout")
        self.load_inputs()
        self.ada_pending = None
        for l in range(DEPTH):
            if l == 0:
                self.ada(l)
            else:
                self.ada_step(1000)
            if f"mix{l}" not in self.skip:
                if l % 2 == 0:
                    self.gdn_layer(l)
                else:
                    self.hyena_layer(l)
            if l + 1 < DEPTH:
                self.ada_pending = self.ada_gen(l + 1)
            if f"ffn{l}" not in self.skip:
                self.ffn_layer(l)
        self.final_out()
        kb.finish(self.out_res + [self.r_nsout])

    def next_ps(self):
        i = self.pi % self.NPS
        self.pi += 1
        return self.psb[i], self.r_ps[i]

    def next_tmp(self):
        i = self.ti % self.NTMP
        self.ti += 1
        return self.tmp[i], self.r_tmp[i]

    def next_rstd(self):
        i = self.ri % 2
        self.ri += 1
        return self.rstdb[i], self.r_rstd[i]

    def next_tmpb(self):
        i = self.tbi % self.NTMP
        self.tbi += 1
        return self.tmpb[i], self.r_tmpb[i]

    def next_wslot(self):
        i = self.wi % self.NSLOT
        self.wi += 1
        return self.wslots[i], self.r_wslot[i]

    def mm(self, out, lhsT, rhs, start, stop, reads, writes, signal=None):
        P = self.nc.tensor
        return self.kb.op("pe", lambda: P.matmul(out, lhsT, rhs, start=start, stop=stop),
                          reads=reads, writes=writes, signal=(stop if signal is None else signal))

    def load_inputs(self):
        nc, kb = self.nc, self.kb
        V, A, P = nc.vector, nc.scalar, nc.tensor
        d = self.dr
        kb.dma("sp", self.csil[:, :, :], d["cvecT"][:, :, :], writes=[self.r_csil])
        kb.op("act", lambda: A.activation(out=self.csil[:], in_=self.csil[:], func=AF.Silu),
              reads=[self.r_csil], writes=[self.r_csil])
        kb.op("dve", lambda: V.tensor_copy(out=self.csilb[:], in_=self.csil[:]), reads=[self.r_csil], writes=[self.r_csil])
        kb.dma("sp", self.gvec[:, :, :], d["gvecT"][:, :, :], writes=[self.r_gvec])
        kb.dma("sp", self.adab[:, :, :], d["adabT"][:, :, :], writes=[self.r_adab])
        for tt in range(T // 128):
            b = tt // 4
            lat = tt < 8
            src = d["x_lat"][tt * 128:(tt + 1) * 128, :] if lat else d["x_ctx"][(tt - 8) * 128:(tt - 7) * 128, :]
            for half in range(2):
                t0, r0 = self.next_tmp()
                kb.dma("sp", t0[:], src[:, half * 512:(half + 1) * 512], writes=[r0])
                if lat:
                    t1, r1 = self.next_tmp()
                    kb.dma("sp", t1[:], d["pos"][tt * 128:(tt + 1) * 128, half * 512:(half + 1) * 512], writes=[r1])
                    kb.op("dve", lambda: V.tensor_add(out=t0[:], in0=t0[:], in1=t1[:]), reads=[r0, r1], writes=[r0])
                ps, rp = self.next_ps()
                for j in range(4):
                    kb.op("pe", lambda: P.transpose(ps[:, j * 128:(j + 1) * 128], t0[:, j * 128:(j + 1) * 128], self.ident[:]),
                          reads=[r0, self.r_const], writes=[rp], signal=(j == 3))
                cs = half * 4
                kb.op("act", lambda: A.copy(out=self.xT[:, cs:cs + 4, tt * 128:(tt + 1) * 128],
                                            in_=ps[:].rearrange("p (c t) -> p c t", c=4)),
                      reads=[rp], writes=[self.r_x[c][b] for c in range(cs, cs + 4)])

    def ada_gen(self, l):
        nc, kb = self.nc, self.kb
        V, A, P = nc.vector, nc.scalar, nc.tensor
        d = self.dr
        W = 512
        ps, rp = self.psb[7], self.r_ps[7]
        for ti in range(6 * D // W):
            ws, rw = self.next_wslot()
            wf = ws[:, 0:NCH * W].rearrange("p (c n) -> p c n", c=NCH)
            kb.dma("pool", wf, d["ada_w"][l, :, ti * W:(ti + 1) * W].rearrange("(c p) n -> p c n", p=128), writes=[rw])
            for jj in range(W // 128):
                j = ti * (W // 128) + jj
                for c in range(NCH):
                    self.mm(ps[:, 2 * j:2 * j + 2], wf[:, c, jj * 128:(jj + 1) * 128], self.csilb[:, c, :],
                            start=(c == 0), stop=(c == NCH - 1), reads=[rw, self.r_csil], writes=[rp])
            yield
        kb.op("dve", lambda: V.tensor_tensor(out=self.mods[:, l, :, :], in0=ps[:, 0:96].rearrange("p (j w) -> p j w", w=2),
                                             in1=self.adab[:, l, :].unsqueeze(2).to_broadcast([128, 48, 2]), op=ALU.add),
              reads=[rp, self.r_adab], writes=[self.r_mods[l]])

    def ada(self, l):
        for _ in self.ada_gen(l):
            pass

    def ada_step(self, n):
        g = getattr(self, "ada_pending", None)
        if g is None:
            return
        for _ in range(n):
            try:
                next(g)
            except StopIteration:
                self.ada_pending = None
                return

    def mod(self, l, which, idx, c):
        return self.mods[:, l, idx * NCH + c, which:which + 1]

    def norm_mod(self, l, gidx, sh_idx, sc_idx):
        nc, kb = self.nc, self.kb
        V, A, P = nc.vector, nc.scalar, nc.tensor
        gs = self.small[:, 0:NCH, :]
        kb.op("dve", lambda: V.tensor_scalar(out=gs, in0=self.mods[:, l, sc_idx * NCH:(sc_idx + 1) * NCH, :], scalar1=1.0, scalar2=None, op0=ALU.add),
              reads=[self.r_mods[l]], writes=[self.r_small])
        kb.op("dve", lambda: V.tensor_tensor(out=gs, in0=gs, in1=self.gvec[:, gidx, :].unsqueeze(2).to_broadcast([128, NCH, 2]), op=ALU.mult),
              reads=[self.r_gvec, self.r_small], writes=[self.r_small])
        for b in range(NTB):
            w = 0 if b < 2 else 1
            ps, rp = self.next_ps()
            for c in range(NCH):
                sq, rsq = self.next_tmpb()
                kb.op("act", lambda: A.activation(out=sq[:], in_=self.xT[:, c, b * 512:(b + 1) * 512], func=AF.Square),
                      reads=[self.r_x[c][b]], writes=[rsq])
                self.mm(ps[:], self.onesb[:], sq[:], start=(c == 0), stop=(c == NCH - 1), reads=[rsq, self.r_const], writes=[rp], signal=True)
            rstd, rr = self.next_rstd()
            kb.op("dve", lambda: V.tensor_scalar(out=rstd[:], in0=ps[:], scalar1=1.0 / D, scalar2=EPS, op0=ALU.mult, op1=ALU.add),
                  reads=[rp], writes=[rr])
            kb.op("act", lambda: A.activation(out=rstd[:], in_=rstd[:], func=AF.Sqrt), reads=[rr], writes=[rr])
            kb.op("dve", lambda: V.reciprocal(out=rstd[:], in_=rstd[:]), reads=[rr], writes=[rr])
            for c in range(NCH):
                t0, r0 = self.next_tmp()
                kb.op("dve", lambda: V.scalar_tensor_tensor(out=t0[:], in0=self.xT[:, c, b * 512:(b + 1) * 512], scalar=gs[:, c, w:w + 1],
                                                            in1=rstd[:], op0=ALU.mult, op1=ALU.mult),
                      reads=[self.r_x[c][b], rr, self.r_small], writes=[r0])
                kb.op("act", lambda: A.activation(out=self.hT[:, c, b * 512:(b + 1) * 512], in_=t0[:], func=AF.Identity,
                                                  bias=self.mod(l, w, sh_idx, c), scale=1.0),
                      reads=[r0, self.r_mods[l]], writes=[self.r_h[c][b]])

    def ffn_layer(self, l):
        nc, kb = self.nc, self.kb
        V, A, P, G = nc.vector, nc.scalar, nc.tensor, nc.gpsimd
        d = self.dr
        self.norm_mod(l, 2 * l + 1, 3, 4)
        act = self.big[:, 0:NFB * T].rearrange("p (f t) -> p f t", f=NFB)
        r_act = [[Res(f"act{f}_{b}") for b in range(NTB)] for f in range(NFB)]
        for f in range(NFB):
            for b in range(NTB):
                r_act[f][b].w = self.r_big.w
                r_act[f][b].rs = list(self.r_big.rs)
        hreads = lambda b: [self.r_h[c][b] for c in range(NCH)]
        ntile = (DFF + 511) // 512
        for ti in range(ntile):
            c0 = ti * 512
            ncol = min(512, DFF - c0)
            wg, rg = self.next_wslot()
            wgv = wg[:, 0:NCH * ncol].rearrange("p (c n) -> p c n", c=NCH)
            kb.dma("pool", wgv, d["ffn_w_gu"][l, :, c0:c0 + ncol].rearrange("(c p) n -> p c n", p=128), writes=[rg])
            wu, ru = self.next_wslot()
            wuv = wu[:, 0:NCH * ncol].rearrange("p (c n) -> p c n", c=NCH)
            kb.dma("pool", wuv, d["ffn_w_gu"][l, :, DFF + c0:DFF + c0 + ncol].rearrange("(c p) n -> p c n", p=128), writes=[ru])
            self.ada_step(2)
            for jj in range(ncol // 128):
                f = ti * 4 + jj
                for b in range(NTB):
                    pg, rpg = self.next_ps()
                    for c in range(NCH):
                        self.mm(pg[:], wgv[:, c, jj * 128:(jj + 1) * 128], self.hT[:, c, b * 512:(b + 1) * 512],
                                start=(c == 0), stop=(c == NCH - 1), reads=[rg, self.r_h[c][b]], writes=[rpg])
                    pu, rpu = self.next_ps()
                    for c in range(NCH):
                        self.mm(pu[:], wuv[:, c, jj * 128:(jj + 1) * 128], self.hT[:, c, b * 512:(b + 1) * 512],
                                start=(c == 0), stop=(c == NCH - 1), reads=[ru, self.r_h[c][b]], writes=[rpu])
                    sg, rsg = self.next_tmp()
                    kb.op("act", lambda: A.activation(out=sg[:], in_=pg[:], func=AF.Silu), reads=[rpg], writes=[rsg])
                    kb.op("dve", lambda: V.tensor_tensor(out=act[:, f, b * 512:(b + 1) * 512], in0=sg[:], in1=pu[:], op=ALU.mult),
                          reads=[rsg, rpu], writes=[r_act[f][b]])
        for ti in range(D // 128):
            self.ada_step(1)
            ws, rw = self.next_wslot()
            wv = ws[:, 0:NFB * 128].rearrange("p (f n) -> p f n", f=NFB)
            kb.dma("pool", wv, d["ffn_w_down"][l, :, ti * 128:(ti + 1) * 128].rearrange("(f p) n -> p f n", p=128), writes=[rw])
            for jj in range(1):
                c = ti
                for b in range(NTB):
                    w = 0 if b < 2 else 1
                    ps, rp = self.next_ps()
                    for f in range(NFB):
                        self.mm(ps[:], wv[:, f, jj * 128:(jj + 1) * 128], act[:, f, b * 512:(b + 1) * 512],
                                start=(f == 0), stop=(f == NFB - 1), reads=[rw, r_act[f][b]], writes=[rp])
                    xs = self.xT[:, c, b * 512:(b + 1) * 512]
                    kb.op("dve", lambda: V.scalar_tensor_tensor(out=xs, in0=ps[:], scalar=self.mod(l, w, 5, c), in1=xs,
                                                                op0=ALU.mult, op1=ALU.add),
                          reads=[rp, self.r_x[c][b], self.r_mods[l]], writes=[self.r_x[c][b]])
        self.merge_res(self.r_big, [r for rr in r_act for r in rr])

    def merge_res(self, dst, srcs):
        evs = []
        for s in srcs:
            if s.w is not None:
                evs.append(s.w)
            evs.extend(s.rs)
        best = {}
        for ev in evs:
            k = id(ev[0])
            if k not in best or best[k][1] < ev[1]:
                best[k] = ev
        dst.w = None
        dst.rs = list(best.values())

    def gdn_layer(self, l):
        nc, kb = self.nc, self.kb
        V, A, P, G = nc.vector, nc.scalar, nc.tensor, nc.gpsimd
        d = self.dr
        j = l // 2
        DT = F32
        self.norm_mod(l, 2 * l, 0, 1)
        allx = [r for rr in self.r_x for r in rr]
        kb.dma("sp", d["xsp"][:, :], self.xT[:].rearrange("p c t -> p (c t)"), reads=allx)
        R1 = self.xT[:].rearrange("p c t -> p (c t)")
        pos1 = [0]

        def c1(n):
            a = R1[:, pos1[0]:pos1[0] + n]
            pos1[0] += n
            assert pos1[0] <= 12288
            return a
        R2 = self.big
        pos2 = [0]

        def c2(n):
            a = R2[:, pos2[0]:pos2[0] + n]
            pos2[0] += n
            assert pos2[0] <= 33792, pos2[0]
            return a
        PWG = T + 12
        pp = c1(PWG)
        cv = c1(T)
        oT = cv
        qT = c1(T // 2).bitcast(BF16)
        kT = c1(T // 2).bitcast(BF16)
        vT = c1(T // 2).bitcast(BF16)
        gateT = c1(T // 2).bitcast(BF16)
        sqb = pp[:, 2:2 + T // 2].bitcast(BF16)
        ktok = c1(T // 2).bitcast(BF16).rearrange("p (t c) -> p t c", t=12)
        vtok = c1(T // 2).bitcast(BF16).rearrange("p (t c) -> p t c", t=12)
        r_pp, r_cv, r_q, r_k, r_v, r_gate, r_ktok, r_vtok = (Res(n) for n in ("pp", "cv", "q", "k", "v", "gate", "ktok", "vtok"))
        r_sqb = r_pp
        r_oT = [Res(f"oT{t}") for t in range(12)]
        self.inherit([r_pp, r_cv, r_q, r_k, r_v, r_gate, r_sqb, r_ktok, r_vtok] + r_oT, allx)
        og = c2(NCH * T).rearrange("p (h t) -> p h t", h=NCH)
        r_og = [Res(f"og{h}") for h in range(NCH)]
        NCHAIN = 6
        slots = []
        extra_slots = []
        for ci in range(NCHAIN):
            ring = []
            for r_ in range(2):
                wu_ = c2(512).bitcast(F32)
                sl = dict(WU=wu_, WT=wu_[:, 0:128], U=wu_[:, 128:256], qdT=c2(256).bitcast(F32), aqkT=c2(256).bitcast(F32), ktail=c2(256).bitcast(F32),
                          gl=c2(4).bitcast(F32),
                          r=Res(f"slot{ci}_{r_}"))
                ring.append(sl)
            slots.append(ring)
        works = []
        for wi in range(2):
            wk = dict(r=Res(f"work{wi}"))
            PA, PR, PX, PX2 = c2(512).bitcast(F32), c2(512).bitcast(F32), c2(512).bitcast(F32), c2(512).bitcast(F32)
            wk["PA"], wk["PR"], wk["PX"], wk["PX2"] = PA, PR, PX, PX2
            wk["A0"], wk["AT0"] = PA[:, 0:128], PA[:, 128:256]
            wk["R"], wk["RT"] = PR[:, 0:128], PR[:, 128:256]
            wk["Dsym"], wk["egrow"] = PX[:, 0:128], PX[:, 128:256]
            wk["DMs"], wk["DMi"] = PX2[:, 0:128], PX2[:, 128:256]
            wk["kbe"], wk["vb"] = c2(256).bitcast(F32), c2(256).bitcast(F32)
            wk["dd"] = wk["Dsym"]
            wk["GR"] = c1(256)
            wk["cols"] = c2(16).bitcast(F32)
            works.append(wk)
        for wi in range(2):
            wk = dict(r=Res(f"workx{wi}"))
            base = self.wslots[wi]
            PA, PR, PX, PX2 = base[:, 0:512].bitcast(F32), base[:, 512:1024].bitcast(F32), base[:, 1024:1536].bitcast(F32), base[:, 1536:2048].bitcast(F32)
            wk["PA"], wk["PR"], wk["PX"], wk["PX2"] = PA, PR, PX, PX2
            wk["A0"], wk["AT0"] = PA[:, 0:128], PA[:, 128:256]
            wk["R"], wk["RT"] = PR[:, 0:128], PR[:, 128:256]
            wk["Dsym"], wk["egrow"] = PX[:, 0:128], PX[:, 128:256]
            wk["DMs"], wk["DMi"] = PX2[:, 0:128], PX2[:, 128:256]
            wk["kbe"], wk["vb"] = base[:, 2048:2304].bitcast(F32), base[:, 2304:2560].bitcast(F32)
            wk["dd"] = wk["Dsym"]
            wk["GR"] = base[:, 2560:3072].bitcast(F32)
            wk["cols"] = base[:, 3072:3088].bitcast(F32)
            self.inherit([wk["r"]], [self.r_wslot[wi]])
            works.append(wk)
        vnew = [self.wslots[4][:, 1792 + i * 256:1792 + (i + 1) * 256].bitcast(F32) for i in range(NCHAIN)]
        r_vnew = [Res(f"vnew{i}") for i in range(NCHAIN)]
        Sst = [c1(128) for _ in range(NCHAIN)]
        r_S = [Res(f"S{i}") for i in range(NCHAIN)]
        g_tok = c1(192).rearrange("p (t n) -> p t n", t=12)
        b_tok = c1(192).rearrange("p (t n) -> p t n", t=12)
        gccol = c1(192).rearrange("p (a t n) -> p a t n", a=2, t=12)
        rows = c1(32)
        gvec = c1(122)
        for ci in range(2):
            for r_ in range(2):
                wu_ = c1(256)
                sl = dict(WU=wu_, WT=wu_[:, 0:128], U=wu_[:, 128:256], qdT=c1(128), aqkT=c1(128), ktail=c1(128), gl=c1(2), r=Res(f"slotx{ci}_{r_}"))
                slots[ci].append(sl)
                extra_slots.append(sl["r"])
        self.inherit(extra_slots, allx)
        masks = [self.wslots[4][:, 256 + i * 256:256 + (i + 1) * 256].bitcast(F32) for i in range(6)]
        masks += [self.wslots[4][:, 3328 + i * 256:3328 + (i + 1) * 256].bitcast(F32) for i in range(3)]
        r_ab = Res("ab"); r_gpar = Res("gpar"); r_masks = Res("masks")
        allr2 = r_og + [sl["r"] for ring in slots for sl in ring]
        self.inherit(allr2, [self.r_big])
        r_works = [wk["r"] for wk in works]
        self.inherit(r_works, [self.r_big] + allx)
        self.inherit([r_ab, r_gpar] + r_S, allx)
        self.inherit([r_masks] + r_vnew, [self.r_wslot[4]])
        for ci in range(NCHAIN):
            kb.op("dve", lambda: V.memset(vnew[ci], 0.0), writes=[r_vnew[ci]])
        for i in range(9):
            kb.dma("sp", masks[i], d["gmasks"][i], writes=[r_masks])
        m_sl, m_ui, m_su, m_li, Lf, Lb, m_bd16, m_s1, m_s2 = masks
        mpair = [self.wslots[4][:, 256:768].bitcast(F32), self.wslots[4][:, 768:1280].bitcast(F32)]
        kb.dma("sp", gvec[:, 0:121], d["gdn_vecT"][j], writes=[r_gpar])
        kb.dma("sp", rows[:, 0:32], d["gdn_rows"][j:j + 1].rearrange("o a n -> o (a n)").partition_broadcast(128), writes=[r_gpar])
        kb.op("act", lambda: A.activation(out=rows[:, 0:16], in_=rows[:, 0:16], func=AF.Exp), reads=[r_gpar], writes=[r_gpar])
        kb.op("dve", lambda: V.tensor_scalar(out=rows[:, 0:16], in0=rows[:, 0:16], scalar1=-1.0, scalar2=None, op0=ALU.mult), reads=[r_gpar], writes=[r_gpar])
        cwg = lambda k, blk: gvec[:, k * 24 + blk:k * 24 + blk + 1]
        onorm = gvec[:, 120:121]
        wab_s, r_wab = self.wslots[4], self.r_wslot[4]
        wab = wab_s[:, 0:NCH * 32].rearrange("p (c n) -> p c n", c=NCH)
        kb.dma("pool", wab, d["gdn_w_in"][j, :, 4096:4128].rearrange("(c p) n -> p c n", p=128), writes=[r_wab])
        ps, rp = self.next_ps()
        for tt in range(12):
            for c in range(NCH):
                self.mm(ps[:, tt * 32:(tt + 1) * 32], self.hT[:, c, tt * 128:(tt + 1) * 128], wab[:, c, :], start=(c == 0), stop=(c == NCH - 1),
                        reads=[self.r_h[c][tt // 4], r_wab], writes=[rp], signal=(tt == 11 and c == NCH - 1))
        abt, rabt = self.next_tmp()
        ab3 = abt[:, 0:384].rearrange("p (t n) -> p t n", t=12)
        kb.op("dve", lambda: V.tensor_copy(out=abt[:, 0:384], in_=ps[:, 0:384]), reads=[rp], writes=[rabt])
        kb.op("act", lambda: A.activation(out=b_tok, in_=ab3[:, :, 16:32], func=AF.Sigmoid), reads=[rabt], writes=[r_ab])
        kb.op("dve", lambda: V.tensor_tensor(out=g_tok, in0=ab3[:, :, 0:16], in1=rows[:, 16:32].unsqueeze(1).to_broadcast([128, 12, 16]), op=ALU.add),
              reads=[rabt, r_gpar], writes=[r_ab])
        kb.op("act", lambda: A.activation(out=g_tok, in_=g_tok, func=AF.Exp), reads=[r_ab], writes=[r_ab])
        kb.op("act", lambda: A.activation(out=g_tok, in_=g_tok, func=AF.Ln, bias=1.0, scale=1.0), reads=[r_ab], writes=[r_ab])
        kb.op("dve", lambda: V.tensor_tensor(out=g_tok, in0=g_tok, in1=rows[:, 0:16].unsqueeze(1).to_broadcast([128, 12, 16]), op=ALU.mult),
              reads=[r_ab, r_gpar], writes=[r_ab])
        ps, rp = self.next_ps()
        for dr_ in range(2):
            self.mm(ps[:, dr_ * 96:(dr_ + 1) * 96].rearrange("p (t n) -> p t n", t=12), (Lf if dr_ == 0 else Lb), g_tok[:, :, dr_ * 8:(dr_ + 1) * 8], True, True,
                    reads=[r_masks, r_ab], writes=[rp], signal=(dr_ == 1))
        kb.op("dve", lambda: V.tensor_copy(out=gccol.rearrange("p a t n -> p (a t n)"), in_=ps[:, 0:192]), reads=[rp], writes=[r_ab])
        chains = [dict(tiles=list(range(8)), dir=0, seq=None), dict(tiles=list(range(7, -1, -1)), dir=1, seq=None),
                  dict(tiles=[8, 9], dir=0, seq=0), dict(tiles=[9, 8], dir=1, seq=0),
                  dict(tiles=[10, 11], dir=0, seq=1), dict(tiles=[11, 10], dir=1, seq=1)]
        evi = [0]

        def evac(out, in_, reads, writes):
            evi[0] += 1
            if evi[0] % 2 == 0:
                kb.op("act", lambda: A.copy(out=out, in_=in_), reads=reads, writes=writes)
            else:
                kb.op("dve", lambda: V.tensor_copy(out=out, in_=in_), reads=reads, writes=writes)
        wki = [0]
        segs = [(2, LLAT, 0), (1030, LCTX, LLAT), (1290, LCTX, LLAT + LCTX)]
        kb.op("dve", lambda: V.memset(pp, 0.0), writes=[r_pp])
        for h in range(NCH):
            wsl, rws = self.wslots[2 + h % 2], self.r_wslot[2 + h % 2]
            win = wsl[:, 0:4096].rearrange("p (c g n) -> p c g n", c=NCH, g=4)
            for g in range(4):
                kb.dma("pool", win[:, :, g, :], d["gdn_w_in"][j, :, g * D + h * 128:g * D + (h + 1) * 128].rearrange("(c p) n -> p c n", p=128), writes=[rws])
            self.inherit([r_cv], r_oT)
            for g in range(3):
                blk = g * 8 + h
                for b in range(NTB):
                    ps, rp = self.next_ps()
                    for c in range(NCH):
                        self.mm(ps[:], win[:, c, g, :], self.hT[:, c, b * 512:(b + 1) * 512], start=(c == 0), stop=(c == NCH - 1),
                                reads=[rws, self.r_h[c][b]], writes=[rp])
                    if b < 2:
                        evac(pp[:, 2 + b * 512:2 + (b + 1) * 512], ps[:], [rp], [r_pp])
                    else:
                        evac(pp[:, 1030:1286], ps[:, 0:256], [rp], [r_pp])
                        evac(pp[:, 1290:1546], ps[:, 256:512], [rp], [r_pp])
                for (s0, Ls, o0) in segs:
                    co = cv[:, o0:o0 + Ls]
                    kb.op("dve", lambda: V.tensor_scalar(out=co, in0=pp[:, s0 - 2:s0 - 2 + Ls], scalar1=cwg(0, blk), scalar2=None, op0=ALU.mult),
                          reads=[r_pp, r_gpar], writes=[r_cv])
                    for k in range(1, 5):
                        kb.op("dve", lambda: V.scalar_tensor_tensor(out=co, in0=pp[:, s0 - 2 + k:s0 - 2 + k + Ls], scalar=cwg(k, blk), in1=co,
                                                                    op0=ALU.mult, op1=ALU.add), reads=[r_pp, r_gpar, r_cv], writes=[r_cv])
                if g == 2:
                    kb.op("act", lambda: A.activation(out=vT, in_=cv, func=AF.Silu), reads=[r_cv], writes=[r_v])
                else:
                    dst, rdst = (qT, r_q) if g == 0 else (kT, r_k)
                    kb.op("act", lambda: A.activation(out=cv, in_=cv, func=AF.Silu), reads=[r_cv], writes=[r_cv])
                    kb.op("act", lambda: A.activation(out=sqb, in_=cv, func=AF.Square), reads=[r_cv], writes=[r_sqb])
                    for b in range(NTB):
                        ps, rp = self.next_ps()
                        self.mm(ps[:], self.onesb[:], sqb[:, b * 512:(b + 1) * 512], True, True, reads=[r_sqb, self.r_const], writes=[rp])
                        rs, rrs = self.next_rstd()
                        kb.op("dve", lambda: V.tensor_scalar(out=rs[:], in0=ps[:], scalar1=EPS, scalar2=None, op0=ALU.add), reads=[rp], writes=[rrs])
                        kb.op("act", lambda: A.activation(out=rs[:], in_=rs[:], func=AF.Sqrt), reads=[rrs], writes=[rrs])
                        kb.op("dve", lambda: V.reciprocal(out=rs[:], in_=rs[:]), reads=[rrs], writes=[rrs])
                        sc = (128.0 ** -0.5) if g == 0 else 1.0
                        kb.op("dve", lambda: V.scalar_tensor_tensor(out=dst[:, b * 512:(b + 1) * 512], in0=cv[:, b * 512:(b + 1) * 512], scalar=sc, in1=rs[:],
                                                                    op0=ALU.mult, op1=ALU.mult), reads=[r_cv, rrs], writes=[rdst])
            for b in range(NTB):
                ps, rp = self.next_ps()
                for c in range(NCH):
                    self.mm(ps[:], win[:, c, 3, :], self.hT[:, c, b * 512:(b + 1) * 512], start=(c == 0), stop=(c == NCH - 1),
                            reads=[rws, self.r_h[c][b]], writes=[rp])
                kb.op("act", lambda: A.activation(out=gateT[:, b * 512:(b + 1) * 512], in_=ps[:], func=AF.Silu), reads=[rp], writes=[r_gate])
            self.inherit(r_oT, [r_cv])
            for (src, rsrc, dst, rdst) in ((kT, r_k, ktok, r_ktok), (vT, r_v, vtok, r_vtok)):
                for g4 in range(3):
                    ps, rp = self.next_ps()
                    psb = ps[:].bitcast(BF16)
                    for i in range(4):
                        tt = g4 * 4 + i
                        kb.op("pe", lambda: P.transpose(psb[:, i * 128:(i + 1) * 128], src[:, tt * 128:(tt + 1) * 128], self.identb[:]),
                              reads=[rsrc, self.r_const], writes=[rp], signal=(i == 3))
                    evac(dst[:, g4 * 4:(g4 + 1) * 4, :], psb[:, 0:512].rearrange("p (t c) -> p t c", t=4), [rp], [rdst])
            for ci, ch in enumerate(chains):
                if ch["seq"] is None:
                    kb.dma("sp", Sst[ci], d["state_in"][j, ch["dir"], h], writes=[r_S[ci]])
                else:
                    kb.op("dve", lambda: V.memset(Sst[ci], 0.0), writes=[r_S[ci]])

            def intra(ci, step):
                ch = chains[ci]
                tt = ch["tiles"][step]
                dr_ = ch["dir"]
                sl = slots[ci][step % len(slots[ci])]
                rsl = sl["r"]
                wk = works[wki[0] % len(works)]
                wki[0] += 1
                rw = wk["r"]
                tcs = slice(tt * 128, (tt + 1) * 128)
                gcc = gccol[:, dr_, tt, h:h + 1]
                bc = b_tok[:, tt, dr_ * 8 + h:dr_ * 8 + h + 1]
                lastA, lastB = (63, 127) if dr_ == 0 else (0, 64)
                m_s = m_sl if dr_ == 0 else m_su
                m_i = m_ui if dr_ == 0 else m_li
                cols = wk["cols"]
                yield
                ps, rp = self.next_ps()
                self.mm(ps[:, 0:128], kT[:, tcs], kT[:, tcs], True, True, reads=[r_k], writes=[rp], signal=False)
                self.mm(ps[:, 128:256], kT[:, tcs], qT[:, tcs], True, True, reads=[r_k, r_q], writes=[rp])
                evac(wk["GR"], ps[:, 0:256], [rp], [rw])
                yield
                ps2, rp2 = self.next_ps()
                kb.op("pe", lambda: P.transpose(ps2[:, 0:128], gcc.to_broadcast([128, 128]), self.ident[:]), reads=[r_ab, self.r_const], writes=[rp2])
                kb.op("dve", lambda: V.tensor_scalar(out=wk["dd"], in0=ps2[:, 0:128], scalar1=gcc, scalar2=None, op0=ALU.subtract), reads=[rp2, r_ab], writes=[rw])
                kb.op("dve", lambda: V.scalar_tensor_tensor(out=wk["dd"], in0=wk["dd"], scalar=-1.0, in1=wk["dd"], op0=ALU.mult, op1=ALU.max), reads=[rw], writes=[rw])
                kb.op("act", lambda: A.activation(out=wk["Dsym"], in_=wk["dd"], func=AF.Exp, scale=-1.0), reads=[rw], writes=[rw])
                kb.op("act", lambda: A.activation(out=wk["egrow"], in_=ps2[:, 0:128], func=AF.Exp), reads=[rp2], writes=[rw])
                kb.op("dve", lambda: V.tensor_copy(out=cols[0:64, 3:4], in_=ps2[0:64, lastA:lastA + 1]), reads=[rp2], writes=[rw])
                kb.op("dve", lambda: V.tensor_copy(out=cols[64:128, 3:4], in_=ps2[64:128, lastB:lastB + 1]), reads=[rp2], writes=[rw])
                kb.op("act", lambda: A.activation(out=cols[:, 0:1], in_=gcc, func=AF.Exp), reads=[r_ab], writes=[rw])
                kb.op("dve", lambda: V.tensor_tensor(out=cols[:, 1:2], in0=cols[:, 0:1], in1=bc, op=ALU.mult), reads=[rw, r_ab], writes=[rw])
                kb.op("act", lambda: A.activation(out=cols[:, 2:3], in_=gcc, func=AF.Exp, bias=cols[:, 3:4], scale=-1.0), reads=[rw, r_ab], writes=[rw])
                kb.op("act", lambda: A.copy(out=sl["gl"][:, 0:1], in_=wk["egrow"][:, lastA:lastA + 1]), reads=[rw], writes=[rsl])
                kb.op("act", lambda: A.copy(out=sl["gl"][:, 1:2], in_=wk["egrow"][:, lastB:lastB + 1]), reads=[rw], writes=[rsl])
                mp = mpair[dr_]
                id2 = self.ident[:].unsqueeze(1).to_broadcast([128, 2, 128])
                v3 = lambda ap: ap.rearrange("p (a c) -> p a c", a=2)
                kb.op("pool", lambda: G.tensor_tensor(out=v3(wk["PX2"]), in0=wk["Dsym"].unsqueeze(1).to_broadcast([128, 2, 128]), in1=v3(mp), op=ALU.mult),
                      reads=[rw, r_masks], writes=[rw])
                yield
                kb.op("dve", lambda: V.scalar_tensor_tensor(out=wk["A0"], in0=wk["GR"][:, 0:128], scalar=bc, in1=wk["DMs"], op0=ALU.mult, op1=ALU.mult),
                      reads=[rw, r_ab], writes=[rw])
                yield
                kb.op("dve", lambda: V.tensor_tensor(out=sl["aqkT"], in0=wk["GR"][:, 128:256], in1=wk["DMi"], op=ALU.mult), reads=[rw], writes=[rsl])
                yield
                kb.op("dve", lambda: V.tensor_tensor(out=sl["qdT"], in0=qT[:, tcs], in1=wk["egrow"], op=ALU.mult), reads=[rw, r_q], writes=[rsl])
                yield
                kb.op("act", lambda: A.activation(out=wk["kbe"], in_=ktok[:, tt, :], func=AF.Copy, scale=cols[:, 1:2]), reads=[rw, r_ktok], writes=[rw])
                yield
                kb.op("act", lambda: A.activation(out=wk["vb"], in_=vtok[:, tt, :], func=AF.Copy, scale=bc), reads=[rw, r_vtok, r_ab], writes=[rw])
                yield
                kb.op("act", lambda: A.activation(out=sl["ktail"], in_=ktok[:, tt, :], func=AF.Copy, scale=cols[:, 2:3]), reads=[rw, r_ktok], writes=[rsl])
                yield
                ps3, rp3 = self.next_ps()
                kb.op("pe", lambda: P.transpose(ps3[:, 0:128], wk["A0"], self.ident[:]), reads=[rw, self.r_const], writes=[rp3])
                kb.op("act", lambda: A.copy(out=wk["AT0"], in_=ps3[:, 0:128]), reads=[rp3], writes=[rw])
                PX, PX2, PR, PA = wk["PX"], wk["PX2"], wk["PR"], wk["PA"]
                bd2 = m_bd16.unsqueeze(1).to_broadcast([128, 2, 128])
                kb.op("pool", lambda: G.tensor_tensor(out=v3(PX), in0=v3(PA), in1=bd2, op=ALU.mult), reads=[rw, r_masks], writes=[rw])
                yield
                kb.op("dve", lambda: V.tensor_tensor(out=v3(PR), in0=id2, in1=v3(PX), op=ALU.subtract), reads=[rw, self.r_const], writes=[rw])
                for it in range(3):
                    X, XT = PX[:, 0:128], PX[:, 128:256]
                    yield
                    psA, rpA = self.next_ps()
                    self.mm(psA[:, 0:128], XT, X, True, True, reads=[rw], writes=[rpA], signal=False)
                    self.mm(psA[:, 128:256], X, XT, True, True, reads=[rw], writes=[rpA])
                    kb.op("act", lambda: A.copy(out=PX2, in_=psA[:, 0:256]), reads=[rpA], writes=[rw])
                    X2, XT2 = PX2[:, 0:128], PX2[:, 128:256]
                    yield
                    psB, rpB = self.next_ps()
                    self.mm(psB[:, 0:128], XT2, PR[:, 0:128], True, True, reads=[rw], writes=[rpB], signal=False)
                    self.mm(psB[:, 128:256], X2, PR[:, 128:256], True, True, reads=[rw], writes=[rpB])
                    kb.op("dve", lambda: V.tensor_tensor(out=PR, in0=PR, in1=psB[:, 0:256], op=ALU.add), reads=[rpB, rw], writes=[rw])
                    PX, PX2 = PX2, PX
                for msk in (m_s1, m_s2):
                    ms2 = msk.unsqueeze(1).to_broadcast([128, 2, 128])
                    kb.op("pool", lambda: G.tensor_tensor(out=v3(PX), in0=v3(PA), in1=ms2, op=ALU.mult), reads=[rw, r_masks], writes=[rw])
                    X, XT = PX[:, 0:128], PX[:, 128:256]
                    yield
                    ps1, rp1 = self.next_ps()
                    self.mm(ps1[:, 0:128], XT, PR[:, 0:128], True, True, reads=[rw], writes=[rp1], signal=False)
                    self.mm(ps1[:, 128:256], X, PR[:, 128:256], True, True, reads=[rw], writes=[rp1])
                    kb.op("act", lambda: A.copy(out=PX2, in_=ps1[:, 0:256]), reads=[rp1], writes=[rw])
                    yield
                    ps2_, rp2_ = self.next_ps()
                    self.mm(ps2_[:, 0:128], PR[:, 128:256], PX2[:, 0:128], True, True, reads=[rw], writes=[rp2_], signal=False)
                    self.mm(ps2_[:, 128:256], PR[:, 0:128], PX2[:, 128:256], True, True, reads=[rw], writes=[rp2_])
                    kb.op("dve", lambda: V.tensor_tensor(out=PR, in0=PR, in1=ps2_[:, 0:256], op=ALU.subtract), reads=[rp2_, rw], writes=[rw])
                yield
                psW, rpW = self.next_ps()
                self.mm(psW[:, 0:128], wk["kbe"], wk["RT"], True, True, reads=[rw], writes=[rpW], signal=False)
                self.mm(psW[:, 128:256], wk["RT"], wk["vb"], True, True, reads=[rw], writes=[rpW])
                kb.op("act", lambda: A.copy(out=sl["WU"], in_=psW[:, 0:256]), reads=[rpW], writes=[rsl])

            def scan(ci, step):
                ch = chains[ci]
                tt = ch["tiles"][step]
                sl = slots[ci][step % len(slots[ci])]
                rsl = sl["r"]
                tcs = slice(tt * 128, (tt + 1) * 128)
                for half in ((0, 1) if ch["dir"] == 0 else (1, 0)):
                    hs = slice(half * 64, (half + 1) * 64)
                    hcs = slice(tt * 128 + half * 64, tt * 128 + (half + 1) * 64)
                    yield
                    psP, rpP = self.next_ps()
                    self.mm(psP[:, 0:128], sl["WT"], Sst[ci], True, True, reads=[rsl, r_S[ci]], writes=[rpP])
                    kb.op("dve", lambda: V.tensor_tensor(out=vnew[ci][hs, :], in0=sl["U"][hs, :], in1=psP[hs, 0:128], op=ALU.subtract),
                          reads=[rsl, rpP], writes=[r_vnew[ci]])
                    yield
                    psO, rpO = self.next_ps()
                    self.mm(psO[:, 0:64], Sst[ci], sl["qdT"][:, hs], True, False, reads=[rsl, r_S[ci]], writes=[rpO])
                    self.mm(psO[:, 0:64], vnew[ci][hs, :], sl["aqkT"][hs, hs], False, True, reads=[rsl, r_vnew[ci]], writes=[rpO])
                    psS, rpS = self.next_ps()
                    self.mm(psS[:, 0:128], sl["ktail"][hs, :], vnew[ci][hs, :], True, True, reads=[rsl, r_vnew[ci]], writes=[rpS])
                    okey = (tt, half)
                    if okey not in owritten:
                        owritten.add(okey)
                        kb.op("act", lambda: A.copy(out=oT[:, hcs], in_=psO[:, 0:64]), reads=[rpO], writes=[r_oT[tt]])
                    else:
                        kb.op("dve", lambda: V.tensor_tensor(out=oT[:, hcs], in0=oT[:, hcs], in1=psO[:, 0:64], op=ALU.add), reads=[rpO, r_oT[tt]], writes=[r_oT[tt]])
                    kb.op("dve", lambda: V.scalar_tensor_tensor(out=Sst[ci], in0=Sst[ci], scalar=sl["gl"][:, half:half + 1], in1=psS[:, 0:128], op0=ALU.mult, op1=ALU.add),
                          reads=[rpS, rsl, r_S[ci]], writes=[r_S[ci]])
                if ch["seq"] is not None and step == len(ch["tiles"]) - 1:
                    kb.dma("sp", d["ns_out"][ch["seq"], j, ch["dir"], h], Sst[ci], reads=[r_S[ci]], writes=[self.r_nsout])

            owritten = set()

            def run_interleaved(gens):
                active = list(gens)
                while active:
                    for g_ in list(active):
                        try:
                            next(g_)
                        except StopIteration:
                            active.remove(g_)
            NW = len(works)
            nI = [0] * NCHAIN
            nS = [0] * NCHAIN
            nT = [len(chains[ci]["tiles"]) for ci in range(NCHAIN)]
            while any(nS[ci] < nT[ci] for ci in range(NCHAIN)):
                gens = []
                scan_ready = [ci for ci in range(NCHAIN) if nS[ci] < nI[ci]]
                cand = [ci for ci in range(NCHAIN) if nI[ci] < nT[ci] and nI[ci] - nS[ci] < len(slots[ci])]
                cand.sort(key=lambda ci: (nI[ci] - nS[ci], ci))
                picked = cand[:NW]
                for ci in picked:
                    gens.append(intra(ci, nI[ci]))
                for ci in scan_ready:
                    gens.append(scan(ci, nS[ci]))
                run_interleaved(gens)
                for ci in picked:
                    nI[ci] += 1
                for ci in scan_ready:
                    nS[ci] += 1
            kb.op("act", lambda: A.activation(out=sqb, in_=oT, func=AF.Square), reads=r_oT, writes=[r_sqb])
            for b in range(NTB):
                ps, rp = self.next_ps()
                self.mm(ps[:], self.onesb[:], sqb[:, b * 512:(b + 1) * 512], True, True, reads=[r_sqb, self.r_const], writes=[rp])
                rs, rrs = self.next_rstd()
                kb.op("dve", lambda: V.tensor_scalar(out=rs[:], in0=ps[:], scalar1=1.0 / 128.0, scalar2=EPS, op0=ALU.mult, op1=ALU.add), reads=[rp], writes=[rrs])
                kb.op("act", lambda: A.activation(out=rs[:], in_=rs[:], func=AF.Sqrt), reads=[rrs], writes=[rrs])
                kb.op("dve", lambda: V.reciprocal(out=rs[:], in_=rs[:]), reads=[rrs], writes=[rrs])
                t0, r0 = self.next_tmp()
                kb.op("dve", lambda: V.scalar_tensor_tensor(out=t0[:], in0=oT[:, b * 512:(b + 1) * 512], scalar=onorm, in1=rs[:], op0=ALU.mult, op1=ALU.mult),
                      reads=r_oT[b * 4:(b + 1) * 4] + [rrs, r_gpar], writes=[r0])
                kb.op("dve", lambda: V.tensor_tensor(out=og[:, h, b * 512:(b + 1) * 512], in0=t0[:], in1=gateT[:, b * 512:(b + 1) * 512], op=ALU.mult),
                      reads=[r0, r_gate], writes=[r_og[h]])
        self.inherit(allx, [r_pp, r_cv, r_q, r_k, r_v, r_gate, r_sqb, r_ktok, r_vtok, r_ab, r_gpar] + r_oT + r_S + r_works + extra_slots)
        if False:
            for b in range(NTB):
                t0, r0 = self.next_tmp()
                kb.op("dve", lambda: V.tensor_copy(out=t0[:], in_=og[:, 0, b * 512:(b + 1) * 512]), reads=[r_og[0]], writes=[r0])
                kb.dma("sp", d["dbg"][5, :, b * 512:(b + 1) * 512], t0[:], reads=[r0], writes=[self.r_nsout])
        self.inherit([self.r_wslot[0]], [works[2]["r"]])
        self.inherit([self.r_wslot[1]], [works[3]["r"]])
        self.out_proj(l, d["gdn_w_out"][j], og, r_og, None, None)
        if False:
            pass
        self.inherit([self.r_big], allr2 + r_works)
        self.inherit([self.r_wslot[4]], [r_masks] + r_vnew)


    def out_proj(self, l, wdram, src, r_src, bias_fn, r_bias):
        nc, kb = self.nc, self.kb
        V, A, P = nc.vector, nc.scalar, nc.tensor
        d = self.dr
        allx = [r for rr in self.r_x for r in rr]
        kb.dma("sp", self.xT[:].rearrange("p c t -> p (c t)"), d["xsp"][:, :], writes=allx)
        wo = []
        for hlf in range(2):
            wv = self.wslots[hlf][:, 0:NCH * 512].rearrange("p (c n) -> p c n", c=NCH)
            kb.dma("pool", wv, wdram[:, hlf * 512:(hlf + 1) * 512].rearrange("(c p) n -> p c n", p=128), writes=[self.r_wslot[hlf]])
            wo.append(wv)
        for c in range(NCH):
            for b in range(NTB):
                w = 0 if b < 2 else 1
                ps, rp = self.next_ps()
                for cb in range(NCH):
                    self.mm(ps[:], wo[c // 4][:, cb, (c % 4) * 128:(c % 4 + 1) * 128], src[:, cb, b * 512:(b + 1) * 512],
                            start=(cb == 0), stop=(cb == NCH - 1), reads=[self.r_wslot[c // 4], r_src[cb]], writes=[rp])
                t0, r0 = self.next_tmp()
                if bias_fn is not None:
                    kb.op("dve", lambda: V.tensor_scalar(out=t0[:], in0=ps[:], scalar1=bias_fn(c), scalar2=self.mod(l, w, 2, c), op0=ALU.add, op1=ALU.mult),
                          reads=[rp, r_bias, self.r_mods[l]], writes=[r0])
                else:
                    kb.op("dve", lambda: V.tensor_scalar(out=t0[:], in0=ps[:], scalar1=self.mod(l, w, 2, c), scalar2=None, op0=ALU.mult),
                          reads=[rp, self.r_mods[l]], writes=[r0])
                xs = self.xT[:, c, b * 512:(b + 1) * 512]
                kb.op("dve", lambda: V.tensor_tensor(out=xs, in0=xs, in1=t0[:], op=ALU.add), reads=[r0, self.r_x[c][b]], writes=[self.r_x[c][b]])

    def inherit(self, dsts, srcs):
        evs = []
        for sres in srcs:
            if sres.w is not None:
                evs.append(sres.w)
            evs.extend(sres.rs)
        best = {}
        for ev in evs:
            k = id(ev[0])
            if k not in best or best[k][1] < ev[1]:
                best[k] = ev
        for dres in dsts:
            own = ([dres.w] if dres.w is not None else []) + list(dres.rs)
            b2 = dict(best)
            for ev in own:
                k = id(ev[0])
                if k not in b2 or b2[k][1] < ev[1]:
                    b2[k] = ev
            dres.w = None
            dres.rs = list(b2.values())

    def sin3(self, buf, rbuf, npart, ncol):
        nc, kb = self.nc, self.kb
        V, A = nc.vector, nc.scalar
        q, rq = self.next_tmp()
        bv = buf[0:npart, 0:ncol]
        qv = q[0:npart, 0:ncol]
        kb.op("act", lambda: A.activation(out=bv, in_=bv, func=AF.Sin, scale=1.0 / 3.0), reads=[rbuf], writes=[rbuf])
        kb.op("dve", lambda: V.tensor_tensor(out=qv, in0=bv, in1=bv, op=ALU.mult), reads=[rbuf], writes=[rq])
        kb.op("dve", lambda: V.tensor_scalar(out=qv, in0=qv, scalar1=-4.0, scalar2=3.0, op0=ALU.mult, op1=ALU.add), reads=[rq], writes=[rq])
        kb.op("dve", lambda: V.tensor_tensor(out=bv, in0=bv, in1=qv, op=ALU.mult), reads=[rq, rbuf], writes=[rbuf])

    def hyena_layer(self, l):
        nc, kb = self.nc, self.kb
        V, A, P, G = nc.vector, nc.scalar, nc.tensor, nc.gpsimd
        d = self.dr
        j = l // 2
        self.norm_mod(l, 2 * l, 0, 1)
        allx = [r for rr in self.r_x for r in rr]
        kb.dma("sp", d["xsp"][:, :], self.xT[:].rearrange("p c t -> p (c t)"), reads=allx)
        R1 = self.xT[:].rearrange("p c t -> p (c t)")
        PW = 1544
        pp = [R1[:, i * PW:(i + 1) * PW] for i in range(3)]
        o1 = 3 * PW
        u = [R1[:, o1 + i * T:o1 + (i + 1) * T] for i in range(3)]
        o1 += 3 * T
        zt = R1[:, o1:o1 + 768].bitcast(BF16).rearrange("p (t c) -> p t c", t=12)
        r_pp = [Res(f"pp{i}") for i in range(3)]
        r_u = [Res(f"u{i}") for i in range(3)]
        r_zt = Res("zt")
        self.inherit(r_pp + r_u + [r_zt], allx)
        R2 = self.big
        zfin = R2[:, 0:NCH * T].rearrange("p (c t) -> p c t", c=NCH)
        o2 = NCH * T
        hid_lat = R2[:, o2:o2 + 2048].bitcast(F32); o2 += 2048
        hid_ctx = R2[:, o2:o2 + 512].bitcast(F32); o2 += 512
        delta_b = R2[:, o2:o2 + 2048].bitcast(F32); o2 += 2048
        fv = {}
        for nm in ("A0", "B0", "hf0", "hf1", "nhf1", "hb0", "nhb0", "hb1"):
            fv[nm] = R2[:, o2:o2 + 512].rearrange("p (q c) -> p q c", q=4); o2 += 512
        for nm in ("Ac", "Bc"):
            fv[nm] = R2[:, o2:o2 + 256].rearrange("p (q c) -> p q c", q=2); o2 += 256
        spec = R2[:, o2:o2 + 12288].bitcast(F32)
        o2 += 12288
        assert o2 <= 33792
        SP = [spec[:, i * 512:(i + 1) * 512] for i in range(10)]
        Yb = spec[:, 5120:6144].bitcast(BF16)
        Y = [Yb[:, i * 512:(i + 1) * 512] for i in range(4)]
        r_SP = [Res(f"sp{i}") for i in range(10)]
        r_Y = [Res(f"Y{i}") for i in range(4)]
        r_zfin = [Res(f"zfin{c}") for c in range(NCH)]
        r_hid = Res("hid"); r_delta = Res("delta"); r_fv = Res("fv")
        self.inherit(r_SP + r_Y + r_zfin + [r_hid, r_delta, r_fv], [self.r_big])
        if not hasattr(self, "hy_alloc"):
            self.hy_alloc = dict(
                vecT=kb.sb("s_hyvecT", [128, 120], F32), vec64=kb.sb("s_hyvec64", [64, 4], F32),
                w1=kb.sb("s_hyw1", [33, 64], F32), w2=kb.sb("s_hyw2", [64, 64], F32),
                tneg=kb.sb("s_hytneg", [128, 10], F32), mask0=kb.sb("s_hymask0", [128, 1], F32),
                w3cb=[kb.sb(f"s_hyw3cb_{i}", [64, 4, 128], F32) for i in range(2)],
                r_w3cb=[Res("w3cb0"), Res("w3cb1")], r_par=Res("hypar"))
        ha = self.hy_alloc
        vecT, vec64, w1, w2, tneg, mask0, w3cb, r_w3cb, r_par = (ha[k] for k in
            ("vecT", "vec64", "w1", "w2", "tneg", "mask0", "w3cb", "r_w3cb", "r_par"))
        kb.dma("sp", vecT[:], d["hy_vecT"][j], writes=[r_par])
        kb.dma("sp", vec64[:, 0:3], d["hy_vec64"][j], writes=[r_par])
        kb.dma("sp", w1[:], d["hy_f_w1"][j], writes=[r_par])
        kb.dma("sp", w2[:], d["hy_f_w2"][j], writes=[r_par])
        kb.dma("sp", tneg[:], d["hytneg"][:, :], writes=[r_par])
        kb.dma("sp", mask0[:], d["mask0"][:, :], writes=[r_par])
        kb.dma("sp", delta_b, d["hydelta"][0:1, :].partition_broadcast(128), writes=[r_delta])
        kb.op("dve", lambda: V.tensor_tensor(out=vec64[:, 3:4], in0=vec64[:, 0:1], in1=vec64[:, 2:3], op=ALU.mult), reads=[r_par], writes=[r_par])
        b_in = lambda blk: vecT[:, blk:blk + 1]
        cw = lambda k, blk: vecT[:, 24 + k * 24 + blk:24 + k * 24 + blk + 1]
        skp = lambda n, c: vecT[:, 96 + n * 8 + c:96 + n * 8 + c + 1]
        b_out = lambda c: vecT[:, 112 + c:113 + c]
        tabs = []
        for i in range(6):
            sl = self.wslots[i // 2]
            tv = sl[:, (i % 2) * 2048:(i % 2 + 1) * 2048].rearrange("p (q k) -> p q k", q=4)
            kb.dma("pool", tv, d["hytab"][i].rearrange("(q p) k -> p q k", p=128), writes=[self.r_wslot[i // 2]])
            tabs.append(tv)
        Cf, Sf, Cr, Sr, Ci, Si = tabs
        r_tab = [self.r_wslot[0], self.r_wslot[0], self.r_wslot[1], self.r_wslot[1], self.r_wslot[2], self.r_wslot[2]]
        rCf, rSf, rCr, rSr, rCi, rSi = r_tab
        for (zname, L, hid) in (("zemb_lat", LLAT, hid_lat), ("zemb_ctx", LCTX, hid_ctx)):
            for c0 in range(0, L, 512):
                ncol = min(512, L - c0)
                ze, rze = self.next_tmp()
                kb.dma("sp", ze[0:33, 0:ncol], d[zname][:, c0:c0 + ncol], writes=[rze])
                ps, rp = self.next_ps()
                self.mm(ps[0:64, 0:ncol], w1[:, :], ze[0:33, 0:ncol], True, True, reads=[rze, r_par], writes=[rp])
                h1, rh1 = self.next_tmp()
                kb.op("dve", lambda: V.tensor_scalar(out=h1[0:64, 0:ncol], in0=ps[0:64, 0:ncol], scalar1=vec64[:, 0:1], scalar2=vec64[:, 2:3],
                                                     op0=ALU.add, op1=ALU.mult), reads=[rp, r_par], writes=[rh1])
                self.sin3(h1, rh1, 64, ncol)
                ps2, rp2 = self.next_ps()
                self.mm(ps2[0:64, 0:ncol], w2[:, :], h1[0:64, 0:ncol], True, True, reads=[rh1, r_par], writes=[rp2])
                hv = hid[0:64, c0:c0 + ncol]
                kb.op("dve", lambda: V.tensor_scalar(out=hv, in0=ps2[0:64, 0:ncol], scalar1=vec64[:, 1:2], scalar2=vec64[:, 2:3],
                                                     op0=ALU.add, op1=ALU.mult), reads=[rp2, r_par], writes=[r_hid])
                kb.op("act", lambda: A.activation(out=hv, in_=hv, func=AF.Sin, scale=1.0 / 3.0), reads=[r_hid], writes=[r_hid])
                q, rq = self.next_tmp()
                qv = q[0:64, 0:ncol]
                kb.op("dve", lambda: V.tensor_tensor(out=qv, in0=hv, in1=hv, op=ALU.mult), reads=[r_hid], writes=[rq])
                kb.op("dve", lambda: V.tensor_scalar(out=qv, in0=qv, scalar1=-4.0, scalar2=3.0, op0=ALU.mult, op1=ALU.add), reads=[rq], writes=[rq])
                kb.op("dve", lambda: V.tensor_tensor(out=hv, in0=hv, in1=qv, op=ALU.mult), reads=[rq, r_hid], writes=[r_hid])
        for i in range(3):
            kb.op("dve", lambda: V.memset(pp[i], 0.0), writes=[r_pp[i]])
        segs = [(1, LLAT, 0), (1027, LCTX, LLAT), (1285, LCTX, LLAT + LCTX)]
        evi = [0]

        def evac(out, in_, reads, writes):
            evi[0] += 1
            if evi[0] % 2 == 0:
                kb.op("act", lambda: A.copy(out=out, in_=in_), reads=reads, writes=writes)
            else:
                kb.op("dve", lambda: V.tensor_copy(out=out, in_=in_), reads=reads, writes=writes)

        def spectrum(dst, rdst, terms):
            ps, rp = self.next_ps()
            tot = sum(t[4] for t in terms)
            for kb_ in range(4):
                i = 0
                for (tab, rtab, dat, rdat, nq) in terms:
                    for q in range(nq):
                        self.mm(ps[:, kb_ * 128:(kb_ + 1) * 128], tab[:, q, kb_ * 128:(kb_ + 1) * 128], dat[:, q, :],
                                start=(i == 0), stop=(i == tot - 1), reads=[rtab, rdat], writes=[rp], signal=(kb_ == 3 and i == tot - 1))
                        i += 1
            evac(dst, ps[:], [rp], [rdst])

        def cprod(dst, rdst, terms):
            acc, racc = self.next_tmp()
            n = len(terms)
            for i, (a, ra, b, rb, sg) in enumerate(terms):
                if i == 0:
                    kb.op("dve", lambda: V.tensor_tensor(out=acc[:], in0=a, in1=b, op=ALU.mult), reads=[ra, rb], writes=[racc])
                else:
                    t2, r2 = self.next_tmp()
                    kb.op("dve", lambda: V.tensor_tensor(out=t2[:], in0=a, in1=b, op=ALU.mult), reads=[ra, rb], writes=[r2])
                    o = dst if i == n - 1 else acc[:]
                    ro = rdst if i == n - 1 else racc
                    kb.op("dve", lambda: V.tensor_tensor(out=o, in0=acc[:], in1=t2[:], op=(ALU.add if sg > 0 else ALU.subtract)),
                          reads=[racc, r2], writes=[ro])

        for cb in range(NCH):
            wsl = self.wslots[3 + cb % 2]
            rws = self.r_wslot[3 + cb % 2]
            win = wsl[:, 0:NCH * 3 * 128].rearrange("p (c g n) -> p c g n", c=NCH, g=3)
            for g in range(3):
                kb.dma("pool", win[:, :, g, :], d["hy_w_in"][j, :, g * D + cb * 128:g * D + (cb + 1) * 128].rearrange("(c p) n -> p c n", p=128), writes=[rws])
            w3 = w3cb[cb % 2]
            rw3 = r_w3cb[cb % 2]
            kb.dma("sp", w3[:], d["hy_f_w3"][j].rearrange("r (g c) -> r g c", g=4)[:, :, cb * 128:(cb + 1) * 128], writes=[rw3])
            for g in range(3):
                blk = g * 8 + cb
                for b in range(NTB):
                    ps, rp = self.next_ps()
                    for c in range(NCH):
                        self.mm(ps[:], win[:, c, g, :], self.hT[:, c, b * 512:(b + 1) * 512], start=(c == 0), stop=(c == NCH - 1),
                                reads=[rws, self.r_h[c][b]], writes=[rp])
                    if b < 2:
                        dsts = [(pp[g][:, 1 + b * 512:1 + (b + 1) * 512], ps[:])]
                    else:
                        dsts = [(pp[g][:, 1027:1027 + 256], ps[:, 0:256]), (pp[g][:, 1285:1285 + 256], ps[:, 256:512])]
                    for (o, i_) in dsts:
                        kb.op("act", lambda: A.activation(out=o, in_=i_, func=AF.Identity, bias=b_in(blk), scale=1.0),
                              reads=[rp, r_par], writes=[r_pp[g]])
                for (s0, Ls, o0) in segs:
                    uo = u[g][:, o0:o0 + Ls]
                    kb.op("dve", lambda: V.tensor_scalar(out=uo, in0=pp[g][:, s0 - 1:s0 - 1 + Ls], scalar1=cw(0, blk), scalar2=None, op0=ALU.mult),
                          reads=[r_pp[g], r_par], writes=[r_u[g]])
                    for k in (1, 2):
                        kb.op("dve", lambda: V.scalar_tensor_tensor(out=uo, in0=pp[g][:, s0 - 1 + k:s0 - 1 + k + Ls], scalar=cw(k, blk), in1=uo,
                                                                    op0=ALU.mult, op1=ALU.add), reads=[r_pp[g], r_par, r_u[g]], writes=[r_u[g]])
            for n in range(2):
                z, rz = u[n], r_u[n]
                for g4 in range(3):
                    ps, rp = self.next_ps()
                    for i in range(4):
                        tt = g4 * 4 + i
                        kb.op("pe", lambda: P.transpose(ps[:, i * 128:(i + 1) * 128], z[:, tt * 128:(tt + 1) * 128], self.ident[:]),
                              reads=[rz, self.r_const], writes=[rp], signal=(i == 3))
                    evac(zt[:, g4 * 4:(g4 + 1) * 4, :], ps[:].rearrange("p (t c) -> p t c", t=4), [rp], [r_zt])
                for q in range(10):
                    lat = q < 8
                    hidv = hid_lat[0:64, q * 128:(q + 1) * 128] if lat else hid_ctx[0:64, (q - 8) * 128:(q - 7) * 128]
                    ps, rp = self.next_ps()
                    for dr_ in range(2):
                        self.mm(ps[:, dr_ * 128:(dr_ + 1) * 128], hidv, w3[:, n * 2 + dr_, :], True, True, reads=[r_hid, rw3], writes=[rp], signal=(dr_ == 1))
                    win_, rwin = self.next_tmp()
                    kb.op("act", lambda: A.activation(out=win_[:, 0:128], in_=delta_b[:, cb * 128:(cb + 1) * 128], func=AF.Exp, scale=tneg[:, q:q + 1]),
                          reads=[r_delta, r_par], writes=[rwin])
                    hh, rhh = self.next_tmp()
                    kb.op("dve", lambda: V.tensor_tensor(out=hh[:, 0:256].rearrange("p (a c) -> p a c", a=2), in0=ps[:, 0:256].rearrange("p (a c) -> p a c", a=2),
                                                         in1=win_[:, 0:128].unsqueeze(1).to_broadcast([128, 2, 128]), op=ALU.mult),
                          reads=[rp, rwin], writes=[rhh])
                    hf = hh[:, 0:128]
                    hb = hh[:, 128:256]
                    if q == 0 or q == 8:
                        kb.op("dve", lambda: V.tensor_scalar(out=hb, in0=hb, scalar1=mask0[:, 0:1], scalar2=None, op0=ALU.mult), reads=[rhh, r_par], writes=[rhh])
                    def wr(nm, qq, fn):
                        kb.op("dve", fn(fv[nm][:, qq, :]), reads=[rhh], writes=[r_fv])
                    if q < 4 or q >= 8:
                        nmA, nmB, qq = ("A0", "B0", q) if lat else ("Ac", "Bc", q - 8)
                        kb.op("dve", lambda: V.tensor_tensor(out=fv[nmA][:, qq, :], in0=hf, in1=hb, op=ALU.add), reads=[rhh], writes=[r_fv])
                        kb.op("dve", lambda: V.tensor_tensor(out=fv[nmB][:, qq, :], in0=hb, in1=hf, op=ALU.subtract), reads=[rhh], writes=[r_fv])
                        if lat:
                            kb.op("act", lambda: A.copy(out=fv["hf0"][:, q, :], in_=hf), reads=[rhh], writes=[r_fv])
                            kb.op("act", lambda: A.copy(out=fv["hb0"][:, q, :], in_=hb), reads=[rhh], writes=[r_fv])
                            kb.op("act", lambda: A.mul(out=fv["nhb0"][:, q, :], in_=hb, mul=-1.0), reads=[rhh], writes=[r_fv])
                    else:
                        kb.op("act", lambda: A.copy(out=fv["hf1"][:, q - 4, :], in_=hf), reads=[rhh], writes=[r_fv])
                        kb.op("act", lambda: A.copy(out=fv["hb1"][:, q - 4, :], in_=hb), reads=[rhh], writes=[r_fv])
                        kb.op("act", lambda: A.mul(out=fv["nhf1"][:, q - 4, :], in_=hf, mul=-1.0), reads=[rhh], writes=[r_fv])
                zl = [zt[:, 0:4, :], zt[:, 4:8, :]]
                spectrum(SP[0], r_SP[0], [(Cf, rCf, zl[0], r_zt, 4)])
                spectrum(SP[1], r_SP[1], [(Sf, rSf, zl[0], r_zt, 4)])
                spectrum(SP[2], r_SP[2], [(Cf, rCf, zl[1], r_zt, 4)])
                spectrum(SP[3], r_SP[3], [(Sf, rSf, zl[1], r_zt, 4)])
                spectrum(SP[4], r_SP[4], [(Cf, rCf, fv["A0"], r_fv, 4)])
                spectrum(SP[5], r_SP[5], [(Sf, rSf, fv["B0"], r_fv, 4)])
                spectrum(SP[6], r_SP[6], [(Cf, rCf, fv["hf1"], r_fv, 4), (Cr, rCr, fv["hf0"], r_fv, 4)])
                spectrum(SP[7], r_SP[7], [(Sf, rSf, fv["nhf1"], r_fv, 4), (Sr, rSr, fv["hf0"], r_fv, 4)])
                spectrum(SP[8], r_SP[8], [(Cf, rCf, fv["hb1"], r_fv, 4), (Cr, rCr, fv["hb0"], r_fv, 4)])
                spectrum(SP[9], r_SP[9], [(Sf, rSf, fv["hb1"], r_fv, 4), (Sr, rSr, fv["nhb0"], r_fv, 4)])
                S_ = lambda i: (SP[i], r_SP[i])
                def T4(a, b, sg):
                    return (SP[a], r_SP[a], SP[b], r_SP[b], sg)
                cprod(Y[0], r_Y[0], [T4(0, 4, 1), T4(1, 5, 1), T4(2, 8, 1), T4(3, 9, 1)])
                cprod(Y[1], r_Y[1], [T4(0, 5, 1), T4(1, 4, -1), T4(2, 9, 1), T4(3, 8, -1)])
                cprod(Y[2], r_Y[2], [T4(0, 6, 1), T4(1, 7, 1), T4(2, 4, 1), T4(3, 5, 1)])
                cprod(Y[3], r_Y[3], [T4(0, 7, 1), T4(1, 6, -1), T4(2, 5, 1), T4(3, 4, -1)])

                def inverse_and_gate(yre, ryre, yim, ryim, ncol, col0):
                    ps, rp = self.next_ps()
                    yre3 = yre.rearrange("p (q c) -> p q c", q=4)
                    yim3 = yim.rearrange("p (q c) -> p q c", q=4)
                    for q in range(4):
                        self.mm(ps[:, 0:ncol], yre3[:, q, :], Ci[:, q, 0:ncol], start=(q == 0), stop=False, reads=[ryre, rCi], writes=[rp])
                    for q in range(4):
                        self.mm(ps[:, 0:ncol], yim3[:, q, :], Si[:, q, 0:ncol], start=False, stop=(q == 3), reads=[ryim, rSi], writes=[rp])
                    t0, r0 = self.next_tmp()
                    zs = z[:, col0:col0 + ncol]
                    kb.op("dve", lambda: V.scalar_tensor_tensor(out=t0[:, 0:ncol], in0=zs, scalar=skp(n, cb), in1=ps[:, 0:ncol], op0=ALU.mult, op1=ALU.add),
                          reads=[rz, rp, r_par], writes=[r0])
                    un = u[n + 1][:, col0:col0 + ncol]
                    kb.op("dve", lambda: V.tensor_tensor(out=un, in0=t0[:, 0:ncol], in1=un, op=ALU.mult), reads=[r0, r_u[n + 1]], writes=[r_u[n + 1]])

                inverse_and_gate(Y[0], r_Y[0], Y[1], r_Y[1], 512, 0)
                inverse_and_gate(Y[2], r_Y[2], Y[3], r_Y[3], 512, 512)
                zc = [zt[:, 8:10, :], zt[:, 10:12, :]]
                spectrum(SP[0], r_SP[0], [(Cf, rCf, zc[0], r_zt, 2)])
                spectrum(SP[1], r_SP[1], [(Sf, rSf, zc[0], r_zt, 2)])
                spectrum(SP[2], r_SP[2], [(Cf, rCf, zc[1], r_zt, 2)])
                spectrum(SP[3], r_SP[3], [(Sf, rSf, zc[1], r_zt, 2)])
                spectrum(SP[4], r_SP[4], [(Cf, rCf, fv["Ac"], r_fv, 2)])
                spectrum(SP[5], r_SP[5], [(Sf, rSf, fv["Bc"], r_fv, 2)])
                cprod(Y[0], r_Y[0], [T4(0, 4, 1), T4(1, 5, 1)])
                cprod(Y[1], r_Y[1], [T4(0, 5, 1), T4(1, 4, -1)])
                cprod(Y[2], r_Y[2], [T4(2, 4, 1), T4(3, 5, 1)])
                cprod(Y[3], r_Y[3], [T4(2, 5, 1), T4(3, 4, -1)])
                inverse_and_gate(Y[0], r_Y[0], Y[1], r_Y[1], 256, LLAT)
                inverse_and_gate(Y[2], r_Y[2], Y[3], r_Y[3], 256, LLAT + LCTX)
            kb.op("act", lambda: A.copy(out=zfin[:, cb, :], in_=u[2]), reads=[r_u[2]], writes=[r_zfin[cb]])
        self.inherit(allx, r_pp + r_u + [r_zt])
        self.out_proj(l, d["hy_w_out"][j], zfin, r_zfin, b_out, r_par)
        self.inherit([self.r_big], r_SP + r_Y + r_zfin + [r_hid, r_delta, r_fv])

    def final_out(self):
        nc, kb = self.nc, self.kb
        V, A, P = nc.vector, nc.scalar, nc.tensor
        d = self.dr
        self.out_res = []
        gidx = 2 * DEPTH
        for b in range(NTB):
            ps, rp = self.next_ps()
            for c in range(NCH):
                sq, rsq = self.next_tmpb()
                kb.op("act", lambda: A.activation(out=sq[:], in_=self.xT[:, c, b * 512:(b + 1) * 512], func=AF.Square),
                      reads=[self.r_x[c][b]], writes=[rsq])
                self.mm(ps[:], self.onesb[:], sq[:], start=(c == 0), stop=(c == NCH - 1), reads=[rsq, self.r_const], writes=[rp], signal=True)
            rstd, rr = self.next_rstd()
            kb.op("dve", lambda: V.tensor_scalar(out=rstd[:], in0=ps[:], scalar1=1.0 / D, scalar2=EPS, op0=ALU.mult, op1=ALU.add),
                  reads=[rp], writes=[rr])
            kb.op("act", lambda: A.activation(out=rstd[:], in_=rstd[:], func=AF.Sqrt), reads=[rr], writes=[rr])
            kb.op("dve", lambda: V.reciprocal(out=rstd[:], in_=rstd[:]), reads=[rr], writes=[rr])
            for c in range(NCH):
                xs = self.xT[:, c, b * 512:(b + 1) * 512]
                kb.op("dve", lambda: V.scalar_tensor_tensor(out=xs, in0=xs, scalar=self.gvec[:, gidx, c:c + 1],
                                                            in1=rstd[:], op0=ALU.mult, op1=ALU.mult),
                      reads=[self.r_x[c][b], rr, self.r_gvec], writes=[self.r_x[c][b]])
        for tt in range(T // 128):
            b = tt // 4
            lat = tt < 8
            dst = d["y_lat"][tt * 128:(tt + 1) * 128, :] if lat else d["y_ctx"][(tt - 8) * 128:(tt - 7) * 128, :]
            for half in range(2):
                ps, rp = self.next_ps()
                for j in range(4):
                    c = half * 4 + j
                    kb.op("pe", lambda: P.transpose(ps[:, j * 128:(j + 1) * 128], self.xT[:, c, tt * 128:(tt + 1) * 128], self.ident[:]),
                          reads=[self.r_x[c][b], self.r_const], writes=[rp], signal=(j == 3))
                t0, r0 = self.next_tmp()
                if half == 0:
                    kb.op("act", lambda: A.copy(out=t0[:], in_=ps[:]), reads=[rp], writes=[r0])
                else:
                    kb.op("dve", lambda: V.tensor_copy(out=t0[:], in_=ps[:]), reads=[rp], writes=[r0])
                ro = Res("out")
                kb.dma("sp", dst[:, half * 512:(half + 1) * 512], t0[:], reads=[r0], writes=[ro])
                self.out_res.append(ro)


_CACHE = {}
LAST_DBG = None


def get_prog(skip=()):
    key = tuple(sorted(skip))
    if key not in _CACHE:
        _CACHE[key] = Prog(skip)
    return _CACHE[key]


def make_in_maps(inp, n=8):
    hc = host_consts()
    f = lambda a: np.ascontiguousarray(np.asarray(a, dtype=np.float32))
    shared = {
        "pos": hc["pos"], "ident": hc["ident"],
        "ada_w": f(inp["ada_w"]),
        "ffn_w_gu": f(inp["ffn_w_gu"]), "ffn_w_down": f(inp["ffn_w_down"]),
    }
    shared["gmasks"] = hc["gmasks"]
    for kk in ("gdn_w_in", "gdn_w_out"):
        shared[kk] = f(inp[kk])
    gv = []
    for j in range(2):
        gv.append(np.concatenate([f(inp["gdn_conv"][j]).reshape(5 * 24, 128).T, f(inp["gdn_onorm"][j]).reshape(128, 1)], axis=1))
    shared["gdn_vecT"] = np.ascontiguousarray(np.stack(gv, axis=0))
    shared["gdn_rows"] = np.ascontiguousarray(np.stack([f(inp["gdn_a_log"]).reshape(2, 16), f(inp["gdn_dt_bias"]).reshape(2, 16)], axis=1))
    for kk in ("hytab", "zemb_lat", "zemb_ctx", "hydelta", "hytneg", "mask0"):
        shared[kk] = hc[kk]
    for kk in ("hy_w_in", "hy_w_out", "hy_f_w1", "hy_f_w2", "hy_f_w3"):
        shared[kk] = f(inp[kk])
    shared["hy_vec64"] = np.ascontiguousarray(np.stack([f(inp["hy_f_b1"]), f(inp["hy_f_b2"]), f(inp["hy_freq"])], axis=2))
    hv = []
    for j in range(2):
        parts = [f(inp["hy_b_in"][j]).reshape(24, 128).T]
        parts.append(f(inp["hy_conv"][j]).reshape(3 * 24, 128).T)
        parts.append(f(inp["hy_skip"][j]).reshape(2 * 8, 128).T)
        parts.append(f(inp["hy_b_out"][j]).reshape(8, 128).T)
        hv.append(np.concatenate(parts, axis=1))
    shared["hy_vecT"] = np.ascontiguousarray(np.stack(hv, axis=0))
    fm = lambda v: np.ascontiguousarray(f(v).reshape(-1, 128).T)
    gl = []
    for l in range(DEPTH):
        gl.append(fm(inp["norm1_g"][l]))
        gl.append(fm(inp["norm2_g"][l]))
    gl.append(fm(inp["final_g"]))
    shared["gvecT"] = np.ascontiguousarray(np.stack(gl, axis=1))
    shared["adabT"] = np.ascontiguousarray(np.stack([fm(inp["ada_b"][l]) for l in range(DEPTH)], axis=1))
    maps = []
    xp = f(inp["x_prompt"])
    xs = f(inp["x_sample"])
    cc = f(inp["c"])
    cctx = f(inp["c_ctx"])
    for i in range(n):
        m = dict(shared)
        m["x_lat"] = np.ascontiguousarray(xs[i])
        m["x_ctx"] = np.ascontiguousarray(xp[2 * i:2 * i + 2].reshape(2 * LCTX, D))
        m["state_in"] = np.ascontiguousarray(f(inp["state_delta"])[i])
        m["cvecT"] = np.ascontiguousarray(np.stack([fm(cc[i]), fm(cctx)], axis=2))
        maps.append(m)
    return maps


def run(inp, skip=(), trace=False):
    prog = get_prog(skip)
    maps = make_in_maps(inp)
    res = run_bass_kernel_spmd(prog.nc, maps, core_ids=list(range(8)), trace=trace)
    y_lat = np.stack([r["y_lat"] for r in res.results], axis=0)
    y_ctx = np.concatenate([r["y_ctx"].reshape(2, LCTX, D) for r in res.results], axis=0)
    ns = np.concatenate([r["ns_out"] for r in res.results], axis=0)
    global LAST_DBG
    LAST_DBG = [r.get("dbg") for r in res.results]
    return (y_ctx.astype(np.float32), y_lat.astype(np.float32), ns.astype(np.float32)), res


def kernel(**inputs):
    (y_prompt, y_sample, new_state), _ = run(inputs)
    return (y_prompt, y_sample, new_state)
```

```python
import math
import os
import contextlib
import numpy as np
import concourse.bass as bass
import concourse.mybir as mybir
from concourse.bass_utils import run_bass_kernel_spmd

F32 = mybir.dt.float32
BF16 = mybir.dt.bfloat16
ALU = mybir.AluOpType
AF = mybir.ActivationFunctionType

D = 1024
NCH = 8
LLAT = 1024
LCTX = 256
T = LLAT + 2 * LCTX
NTB = 3
DFF = 2816
NFB = 22
DEPTH = 4
EPS = 1e-6
SAME_ENGINE_SYNC = True


class Res:
    __slots__ = ("w", "rs", "name", "excl")

    def __init__(self, name="", excl=False):
        self.w = None
        self.rs = []
        self.name = name
        self.excl = excl


class Eng:
    def __init__(self, name, h, sem):
        self.name = name
        self.h = h
        self.sem = sem
        self.count = 0
        self.waited = {}


class KB:
    def __init__(self, nc):
        self.nc = nc
        self.es = contextlib.ExitStack()
        self.engs = {}
        for name, h in (("pe", nc.tensor), ("act", nc.scalar), ("dve", nc.vector),
                        ("pool", nc.gpsimd), ("sp", nc.sync)):
            sem = self.es.enter_context(nc.semaphore("sem_" + name))
            self.engs[name] = Eng(name, h, sem)
        self.dsem = {}
        for q in ("sp", "pool"):
            sems = [self.es.enter_context(nc.semaphore(f"dsem_{q}{i}")) for i in range(20)]
            self.dsem[q] = {"sems": sems, "tot": [0] * len(sems), "i": 0}
        self.semkey = {}
        self.ninst = 0

    def sb(self, name, shape, dt):
        return self.es.enter_context(self.nc.sbuf_tensor(name, list(shape), dt))

    def ps(self, name, shape, dt):
        return self.es.enter_context(self.nc.psum_tensor(name, list(shape), dt))

    def _waits(self, eng, reads, writes):
        need = {}
        def add(ev):
            if ev is None:
                return
            sem, val, src = ev
            if src == eng.name:
                if eng.name == "pe" or not SAME_ENGINE_SYNC:
                    return
            k = id(sem)
            if k not in need or need[k][1] < val:
                need[k] = (sem, val, src)
        for r in reads:
            add(r.w)
        for w in writes:
            add(w.w)
            for ev in w.rs:
                add(ev)
        for k, (sem, val, src) in need.items():
            if eng.waited.get(k, 0) >= val:
                continue
            if src in self.engs:
                assert self.engs[src].count >= val, f"pending unsignaled dep {src} {val} > {self.engs[src].count}"
            eng.h.wait_ge(sem, val)
            eng.waited[k] = val
            self.ninst += 1

    def op(self, en, fn, reads=(), writes=(), signal=True):
        eng = self.engs[en]
        xr = [r for r in reads if r.excl]
        if xr:
            writes = list(writes) + [r for r in xr if r not in writes]
        self._waits(eng, reads, writes)
        ins = fn()
        self.ninst += 1
        if signal:
            eng.count += 1
            ins.then_inc(eng.sem, 1)
            ev = (eng.sem, eng.count, en)
        else:
            ev = (eng.sem, eng.count + 1, en)
        for r in reads:
            r.rs.append(ev)
        for w in writes:
            w.w = ev
            w.rs = []
        return ev

    def dma(self, q, out, in_, reads=(), writes=()):
        eng = self.engs[q]
        self._waits(eng, reads, writes)
        d = self.dsem[q]
        i = d["i"] % len(d["sems"])
        d["i"] += 1
        sem = d["sems"][i]
        k = id(sem)
        if d["tot"][i] > 0 and eng.waited.get(k, 0) < d["tot"][i]:
            eng.h.wait_ge(sem, d["tot"][i])
            eng.waited[k] = d["tot"][i]
        eng.h.dma_start(out=out, in_=in_).then_inc(sem, 16)
        self.ninst += 1
        d["tot"][i] += 16
        ev = (sem, d["tot"][i], "dma_" + q)
        for r in reads:
            r.rs.append(ev)
        for w in writes:
            w.w = ev
            w.rs = []
        return ev

    def finish(self, res_list):
        eng = self.engs["sp"]
        self._waits(eng, res_list, res_list)
        self.es.close()


def host_consts():
    c = {}
    c["ident"] = np.eye(128, dtype=np.float32)
    GRID_W = 64
    rows = LLAT // GRID_W
    r, col = np.meshgrid(np.arange(rows), np.arange(GRID_W), indexing="ij")
    quarter = D // 4
    omega = (1.0 / (10000.0 ** (np.arange(quarter, dtype=np.float32) / quarter))).astype(np.float32)

    def emb1d(p):
        a = p.reshape(-1, 1).astype(np.float32) * omega[None, :]
        return np.concatenate([np.sin(a), np.cos(a)], axis=-1)

    c["pos"] = np.concatenate([emb1d(r), emb1d(col)], axis=-1).astype(np.float32)
    B = 512
    k = np.arange(B, dtype=np.float64)
    om = 2.0 * np.pi * (k + 0.5) / (2 * B)
    jj = np.arange(B, dtype=np.float64)
    Cf = np.cos(jj[:, None] * om[None, :])
    Sf = np.sin(jj[:, None] * om[None, :])
    Cr = np.cos((B - jj)[:, None] * om[None, :]); Cr[0, :] = 0.0
    Sr = np.sin((B - jj)[:, None] * om[None, :]); Sr[0, :] = 0.0
    Ci = (2.0 / (2 * B)) * np.cos(om[:, None] * jj[None, :])
    Si = -(2.0 / (2 * B)) * np.sin(om[:, None] * jj[None, :])
    c["hytab"] = np.stack([Cf, Sf, Cr, Sr, Ci, Si], axis=0).astype(np.float32)
    def zemb(seq):
        t = np.linspace(0.0, 1.0, seq, dtype=np.float32)[:, None]
        wpos = (2.0 * math.pi / seq) * np.arange(seq, dtype=np.float32)[:, None]
        fb = np.linspace(1e-4, 15, 16, dtype=np.float32)[None, :]
        z = np.concatenate([t, np.cos(fb * wpos), -np.sin(fb * wpos)], axis=-1).astype(np.float32)
        return np.ascontiguousarray(z.T)
    c["zemb_lat"] = zemb(LLAT)
    c["zemb_ctx"] = zemb(LCTX)
    mx = math.log(1e-2) / 0.3
    mn = math.log(1e-2) / 1.5
    c["hydelta"] = np.abs(np.linspace(mn, mx, D, dtype=np.float32)).reshape(1, D).astype(np.float32)
    tn = np.zeros((128, 10), np.float32)
    for q in range(8):
        tn[:, q] = -(q * 128 + np.arange(128)) / np.float32(LLAT - 1)
    for q in range(2):
        tn[:, 8 + q] = -(q * 128 + np.arange(128)) / np.float32(LCTX - 1)
    tn = -np.stack([np.linspace(0.0, 1.0, LLAT, dtype=np.float32).reshape(8, 128).T] , 0)[0]
    tc = -np.linspace(0.0, 1.0, LCTX, dtype=np.float32).reshape(2, 128).T
    c["hytneg"] = np.ascontiguousarray(np.concatenate([tn, tc], axis=1).astype(np.float32))
    pi_, fi_ = np.meshgrid(np.arange(128), np.arange(128), indexing="ij")
    sb_ = (pi_ // 64) == (fi_ // 64)
    bd16 = (pi_ // 16) == (fi_ // 16)
    s1 = ((pi_ // 32) == (fi_ // 32)) & ~bd16
    s2 = ((pi_ // 64) == (fi_ // 64)) & ((pi_ // 32) != (fi_ // 32))
    c["gmasks"] = np.stack([(pi_ > fi_) & sb_, (fi_ >= pi_) & sb_, (pi_ < fi_) & sb_, (fi_ <= pi_) & sb_,
                            (pi_ <= fi_) & sb_, (pi_ >= fi_) & sb_, bd16, s1, s2], axis=0).astype(np.float32)
    m0 = np.ones((128, 1), np.float32); m0[0, 0] = 0.0
    c["mask0"] = m0
    return c


class Prog:
    def __init__(self, skip=()):
        self.skip = set(skip)
        nc = bass.Bass("TRN2", target_bir_lowering=False)
        self.nc = nc
        self.kb = KB(nc)
        self.build()

    def din(self, name, shape, dt=F32):
        return self.nc.dram_tensor(name, list(shape), dt, kind="ExternalInput").ap()

    def dout(self, name, shape, dt=F32):
        return self.nc.dram_tensor(name, list(shape), dt, kind="ExternalOutput").ap()

    def build(self):
        nc, kb = self.nc, self.kb
        V, A, P, G = nc.vector, nc.scalar, nc.tensor, nc.gpsimd
        x_lat = self.din("x_lat", [LLAT, D])
        x_ctx = self.din("x_ctx", [2 * LCTX, D])
        cvecT = self.din("cvecT", [128, NCH, 2])
        gvecT = self.din("gvecT", [128, 2 * DEPTH + 1, NCH])
        adabT = self.din("adabT", [128, DEPTH, 48])
        pos = self.din("pos", [LLAT, D])
        identd = self.din("ident", [128, 128])
        ada_w = self.din("ada_w", [DEPTH, D, 6 * D])
        ffn_w_gu = self.din("ffn_w_gu", [DEPTH, D, 2 * DFF])
        ffn_w_down = self.din("ffn_w_down", [DEPTH, DFF, D])
        hytab = self.din("hytab", [6, 512, 512])
        zemb_lat = self.din("zemb_lat", [33, LLAT])
        zemb_ctx = self.din("zemb_ctx", [33, LCTX])
        hydelta = self.din("hydelta", [1, D])
        hytneg = self.din("hytneg", [128, 10])
        mask0 = self.din("mask0", [128, 1])
        hy_w_in = self.din("hy_w_in", [2, D, 3 * D])
        hy_w_out = self.din("hy_w_out", [2, D, D])
        hy_f_w1 = self.din("hy_f_w1", [2, 33, 64])
        hy_f_w2 = self.din("hy_f_w2", [2, 64, 64])
        hy_f_w3 = self.din("hy_f_w3", [2, 64, 4 * D])
        hy_vec64 = self.din("hy_vec64", [2, 64, 3])
        hy_vecT = self.din("hy_vecT", [2, 128, 24 + 72 + 16 + 8])
        gmasks = self.din("gmasks", [9, 128, 128])
        gdn_w_in = self.din("gdn_w_in", [2, D, 4128])
        gdn_w_out = self.din("gdn_w_out", [2, D, D])
        gdn_vecT = self.din("gdn_vecT", [2, 128, 5 * 24 + 1])
        gdn_rows = self.din("gdn_rows", [2, 2, 16])
        state_in = self.din("state_in", [2, 2, 8, 128, 128])
        ns_out = self.dout("ns_out", [2, 2, 2, 8, 128, 128])
        import os
        self.debug = bool(os.environ.get('GDN_DEBUG'))
        dbg = self.dout("dbg", [8, 128, T]) if self.debug else None
        xsp = self.nc.dram_tensor("xsp", [128, NCH * T], F32, kind="Internal").ap()
        y_lat = self.dout("y_lat", [LLAT, D])
        y_ctx = self.dout("y_ctx", [2 * LCTX, D])
        self.dr = dict(locals())

        xT = kb.sb("xT", [128, NCH, T], F32)
        hT = kb.sb("hT", [128, NCH, T], BF16)
        self.xT, self.hT = xT, hT
        self.r_x = [[Res(f"x{c}_{b}") for b in range(NTB)] for c in range(NCH)]
        self.r_h = [[Res(f"h{c}_{b}") for b in range(NTB)] for c in range(NCH)]
        big = kb.sb("big", [128, 33792], BF16)
        self.big = big
        self.r_big = Res("big")
        ident = kb.sb("identf", [128, 128], F32)
        identb = kb.sb("identb", [128, 128], BF16)
        onesb = kb.sb("onesb", [128, 128], BF16)
        self.ident, self.identb, self.onesb = ident, identb, onesb
        r_const = Res("const")
        self.r_const = r_const
        kb.dma("sp", ident[:], identd[:, :], writes=[r_const])
        kb.op("dve", lambda: V.tensor_copy(out=identb[:], in_=ident[:]), reads=[r_const], writes=[r_const])
        kb.op("dve", lambda: V.memset(onesb[:], 1.0), writes=[r_const])
        self.NSLOT = 5
        self.wslots = [kb.sb(f"wslot{i}", [128, 4096], BF16) for i in range(self.NSLOT)]
        self.r_wslot = [Res(f"wslot{i}") for i in range(self.NSLOT)]
        self.wi = 0
        self.NPS = 7
        self.psb = [kb.ps(f"psb{i}", [128, 512], F32) for i in range(8)]
        self.r_ps = [Res(f"ps{i}", excl=True) for i in range(8)]
        self.pi = 0
        self.NTMP = 4
        self.tmp = [kb.sb(f"tmp{i}", [128, 512], F32) for i in range(self.NTMP)]
        self.r_tmp = [Res(f"tmp{i}") for i in range(self.NTMP)]
        self.ti = 0
        self.rstdb = [kb.sb(f"rstd{i}", [128, 512], F32) for i in range(2)]
        self.r_rstd = [Res(f"rstd{i}") for i in range(2)]
        self.ri = 0
        self.tmpb = [kb.sb(f"tmpb{i}", [128, 512], BF16) for i in range(self.NTMP)]
        self.r_tmpb = [Res(f"tmpb{i}") for i in range(self.NTMP)]
        self.tbi = 0
        self.mods = kb.sb("mods", [128, DEPTH, 48, 2], F32)
        self.r_mods = [Res(f"mods{l}") for l in range(DEPTH)]
        self.csil = kb.sb("csil", [128, NCH, 2], F32)
        self.csilb = kb.sb("csilb", [128, NCH, 2], BF16)
        self.r_csil = Res("csil")
        self.small = kb.sb("small", [128, 64, 2], F32)
        self.r_small = Res("small")
        self.gvec = kb.sb("gvec", [128, 2 * DEPTH + 1, NCH], F32)
        self.r_gvec = Res("gvec")
        self.adab = kb.sb("adab", [128, DEPTH, 48], F32)
        self.r_adab = Res("adab")

        self.r_nsout = Res("nsout")
        self.load_inputs()
        self.ada_pending = None
        for l in range(DEPTH):
            if l == 0:
                self.ada(l)
            else:
                self.ada_step(1000)
            if f"mix{l}" not in self.skip:
                if l % 2 == 0:
                    self.gdn_layer(l)
                else:
                    self.hyena_layer(l)
            if l + 1 < DEPTH:
                self.ada_pending = self.ada_gen(l + 1)
            if f"ffn{l}" not in self.skip:
                self.ffn_layer(l)
        self.final_out()
        kb.finish(self.out_res + [self.r_nsout])

    def next_ps(self):
        i = self.pi % self.NPS
        self.pi += 1
        return self.psb[i], self.r_ps[i]

    def next_tmp(self):
        i = self.ti % self.NTMP
        self.ti += 1
        return self.tmp[i], self.r_tmp[i]

    def next_rstd(self):
        i = self.ri % 2
        self.ri += 1
        return self.rstdb[i], self.r_rstd[i]

    def next_tmpb(self):
        i = self.tbi % self.NTMP
        self.tbi += 1
        return self.tmpb[i], self.r_tmpb[i]

    def next_wslot(self):
        i = self.wi % self.NSLOT
        self.wi += 1
        return self.wslots[i], self.r_wslot[i]

    def mm(self, out, lhsT, rhs, start, stop, reads, writes, signal=None):
        P = self.nc.tensor
        return self.kb.op("pe", lambda: P.matmul(out, lhsT, rhs, start=start, stop=stop),
                          reads=reads, writes=writes, signal=(stop if signal is None else signal))

    def load_inputs(self):
        nc, kb = self.nc, self.kb
        V, A, P = nc.vector, nc.scalar, nc.tensor
        d = self.dr
        kb.dma("sp", self.csil[:, :, :], d["cvecT"][:, :, :], writes=[self.r_csil])
        kb.op("act", lambda: A.activation(out=self.csil[:], in_=self.csil[:], func=AF.Silu),
              reads=[self.r_csil], writes=[self.r_csil])
        kb.op("dve", lambda: V.tensor_copy(out=self.csilb[:], in_=self.csil[:]), reads=[self.r_csil], writes=[self.r_csil])
        kb.dma("sp", self.gvec[:, :, :], d["gvecT"][:, :, :], writes=[self.r_gvec])
        kb.dma("sp", self.adab[:, :, :], d["adabT"][:, :, :], writes=[self.r_adab])
        for tt in range(T // 128):
            b = tt // 4
            lat = tt < 8
            src = d["x_lat"][tt * 128:(tt + 1) * 128, :] if lat else d["x_ctx"][(tt - 8) * 128:(tt - 7) * 128, :]
            for half in range(2):
                t0, r0 = self.next_tmp()
                kb.dma("sp", t0[:], src[:, half * 512:(half + 1) * 512], writes=[r0])
                if lat:
                    t1, r1 = self.next_tmp()
                    kb.dma("sp", t1[:], d["pos"][tt * 128:(tt + 1) * 128, half * 512:(half + 1) * 512], writes=[r1])
                    kb.op("dve", lambda: V.tensor_add(out=t0[:], in0=t0[:], in1=t1[:]), reads=[r0, r1], writes=[r0])
                ps, rp = self.next_ps()
                for j in range(4):
                    kb.op("pe", lambda: P.transpose(ps[:, j * 128:(j + 1) * 128], t0[:, j * 128:(j + 1) * 128], self.ident[:]),
                          reads=[r0, self.r_const], writes=[rp], signal=(j == 3))
                cs = half * 4
                kb.op("act", lambda: A.copy(out=self.xT[:, cs:cs + 4, tt * 128:(tt + 1) * 128],
                                            in_=ps[:].rearrange("p (c t) -> p c t", c=4)),
                      reads=[rp], writes=[self.r_x[c][b] for c in range(cs, cs + 4)])

    def ada_gen(self, l):
        nc, kb = self.nc, self.kb
        V, A, P = nc.vector, nc.scalar, nc.tensor
        d = self.dr
        W = 512
        ps, rp = self.psb[7], self.r_ps[7]
        for ti in range(6 * D // W):
            ws, rw = self.next_wslot()
            wf = ws[:, 0:NCH * W].rearrange("p (c n) -> p c n", c=NCH)
            kb.dma("pool", wf, d["ada_w"][l, :, ti * W:(ti + 1) * W].rearrange("(c p) n -> p c n", p=128), writes=[rw])
            for jj in range(W // 128):
                j = ti * (W // 128) + jj
                for c in range(NCH):
                    self.mm(ps[:, 2 * j:2 * j + 2], wf[:, c, jj * 128:(jj + 1) * 128], self.csilb[:, c, :],
                            start=(c == 0), stop=(c == NCH - 1), reads=[rw, self.r_csil], writes=[rp])
            yield
        kb.op("dve", lambda: V.tensor_tensor(out=self.mods[:, l, :, :], in0=ps[:, 0:96].rearrange("p (j w) -> p j w", w=2),
                                             in1=self.adab[:, l, :].unsqueeze(2).to_broadcast([128, 48, 2]), op=ALU.add),
              reads=[rp, self.r_adab], writes=[self.r_mods[l]])

    def ada(self, l):
        for _ in self.ada_gen(l):
            pass

    def ada_step(self, n):
        g = getattr(self, "ada_pending", None)
        if g is None:
            return
        for _ in range(n):
            try:
                next(g)
            except StopIteration:
                self.ada_pending = None
                return

    def mod(self, l, which, idx, c):
        return self.mods[:, l, idx * NCH + c, which:which + 1]

    def norm_mod(self, l, gidx, sh_idx, sc_idx):
        nc, kb = self.nc, self.kb
        V, A, P = nc.vector, nc.scalar, nc.tensor
        gs = self.small[:, 0:NCH, :]
        kb.op("dve", lambda: V.tensor_scalar(out=gs, in0=self.mods[:, l, sc_idx * NCH:(sc_idx + 1) * NCH, :], scalar1=1.0, scalar2=None, op0=ALU.add),
              reads=[self.r_mods[l]], writes=[self.r_small])
        kb.op("dve", lambda: V.tensor_tensor(out=gs, in0=gs, in1=self.gvec[:, gidx, :].unsqueeze(2).to_broadcast([128, NCH, 2]), op=ALU.mult),
              reads=[self.r_gvec, self.r_small], writes=[self.r_small])
        for b in range(NTB):
            w = 0 if b < 2 else 1
            ps, rp = self.next_ps()
            for c in range(NCH):
                sq, rsq = self.next_tmpb()
                kb.op("act", lambda: A.activation(out=sq[:], in_=self.xT[:, c, b * 512:(b + 1) * 512], func=AF.Square),
                      reads=[self.r_x[c][b]], writes=[rsq])
                self.mm(ps[:], self.onesb[:], sq[:], start=(c == 0), stop=(c == NCH - 1), reads=[rsq, self.r_const], writes=[rp], signal=True)
            rstd, rr = self.next_rstd()
            kb.op("dve", lambda: V.tensor_scalar(out=rstd[:], in0=ps[:], scalar1=1.0 / D, scalar2=EPS, op0=ALU.mult, op1=ALU.add),
                  reads=[rp], writes=[rr])
            kb.op("act", lambda: A.activation(out=rstd[:], in_=rstd[:], func=AF.Sqrt), reads=[rr], writes=[rr])
            kb.op("dve", lambda: V.reciprocal(out=rstd[:], in_=rstd[:]), reads=[rr], writes=[rr])
            for c in range(NCH):
                t0, r0 = self.next_tmp()
                kb.op("dve", lambda: V.scalar_tensor_tensor(out=t0[:], in0=self.xT[:, c, b * 512:(b + 1) * 512], scalar=gs[:, c, w:w + 1],
                                                            in1=rstd[:], op0=ALU.mult, op1=ALU.mult),
                      reads=[self.r_x[c][b], rr, self.r_small], writes=[r0])
                kb.op("act", lambda: A.activation(out=self.hT[:, c, b * 512:(b + 1) * 512], in_=t0[:], func=AF.Identity,
                                                  bias=self.mod(l, w, sh_idx, c), scale=1.0),
                      reads=[r0, self.r_mods[l]], writes=[self.r_h[c][b]])

    def ffn_layer(self, l):
        nc, kb = self.nc, self.kb
        V, A, P, G = nc.vector, nc.scalar, nc.tensor, nc.gpsimd
        d = self.dr
        self.norm_mod(l, 2 * l + 1, 3, 4)
        act = self.big[:, 0:NFB * T].rearrange("p (f t) -> p f t", f=NFB)
        r_act = [[Res(f"act{f}_{b}") for b in range(NTB)] for f in range(NFB)]
        for f in range(NFB):
            for b in range(NTB):
                r_act[f][b].w = self.r_big.w
                r_act[f][b].rs = list(self.r_big.rs)
        hreads = lambda b: [self.r_h[c][b] for c in range(NCH)]
        ntile = (DFF + 511) // 512
        for ti in range(ntile):
            c0 = ti * 512
            ncol = min(512, DFF - c0)
            wg, rg = self.next_wslot()
            wgv = wg[:, 0:NCH * ncol].rearrange("p (c n) -> p c n", c=NCH)
            kb.dma("pool", wgv, d["ffn_w_gu"][l, :, c0:c0 + ncol].rearrange("(c p) n -> p c n", p=128), writes=[rg])
            wu, ru = self.next_wslot()
            wuv = wu[:, 0:NCH * ncol].rearrange("p (c n) -> p c n", c=NCH)
            kb.dma("pool", wuv, d["ffn_w_gu"][l, :, DFF + c0:DFF + c0 + ncol].rearrange("(c p) n -> p c n", p=128), writes=[ru])
            self.ada_step(2)
            for jj in range(ncol // 128):
                f = ti * 4 + jj
                for b in range(NTB):
                    pg, rpg = self.next_ps()
                    for c in range(NCH):
                        self.mm(pg[:], wgv[:, c, jj * 128:(jj + 1) * 128], self.hT[:, c, b * 512:(b + 1) * 512],
                                start=(c == 0), stop=(c == NCH - 1), reads=[rg, self.r_h[c][b]], writes=[rpg])
                    pu, rpu = self.next_ps()
                    for c in range(NCH):
                        self.mm(pu[:], wuv[:, c, jj * 128:(jj + 1) * 128], self.hT[:, c, b * 512:(b + 1) * 512],
                                start=(c == 0), stop=(c == NCH - 1), reads=[ru, self.r_h[c][b]], writes=[rpu])
                    sg, rsg = self.next_tmp()
                    kb.op("act", lambda: A.activation(out=sg[:], in_=pg[:], func=AF.Silu), reads=[rpg], writes=[rsg])
                    kb.op("dve", lambda: V.tensor_tensor(out=act[:, f, b * 512:(b + 1) * 512], in0=sg[:], in1=pu[:], op=ALU.mult),
                          reads=[rsg, rpu], writes=[r_act[f][b]])
        for ti in range(D // 128):
            self.ada_step(1)
            ws, rw = self.next_wslot()
            wv = ws[:, 0:NFB * 128].rearrange("p (f n) -> p f n", f=NFB)
            kb.dma("pool", wv, d["ffn_w_down"][l, :, ti * 128:(ti + 1) * 128].rearrange("(f p) n -> p f n", p=128), writes=[rw])
            for jj in range(1):
                c = ti
                for b in range(NTB):
                    w = 0 if b < 2 else 1
                    ps, rp = self.next_ps()
                    for f in range(NFB):
                        self.mm(ps[:], wv[:, f, jj * 128:(jj + 1) * 128], act[:, f, b * 512:(b + 1) * 512],
                                start=(f == 0), stop=(f == NFB - 1), reads=[rw, r_act[f][b]], writes=[rp])
                    xs = self.xT[:, c, b * 512:(b + 1) * 512]
                    kb.op("dve", lambda: V.scalar_tensor_tensor(out=xs, in0=ps[:], scalar=self.mod(l, w, 5, c), in1=xs,
                                                                op0=ALU.mult, op1=ALU.add),
                          reads=[rp, self.r_x[c][b], self.r_mods[l]], writes=[self.r_x[c][b]])
        self.merge_res(self.r_big, [r for rr in r_act for r in rr])

    def merge_res(self, dst, srcs):
        evs = []
        for s in srcs:
            if s.w is not None:
                evs.append(s.w)
            evs.extend(s.rs)
        best = {}
        for ev in evs:
            k = id(ev[0])
            if k not in best or best[k][1] < ev[1]:
                best[k] = ev
        dst.w = None
        dst.rs = list(best.values())

    def gdn_layer(self, l):
        nc, kb = self.nc, self.kb
        V, A, P, G = nc.vector, nc.scalar, nc.tensor, nc.gpsimd
        d = self.dr
        j = l // 2
        DT = F32
        self.norm_mod(l, 2 * l, 0, 1)
        allx = [r for rr in self.r_x for r in rr]
        kb.dma("sp", d["xsp"][:, :], self.xT[:].rearrange("p c t -> p (c t)"), reads=allx)
        R1 = self.xT[:].rearrange("p c t -> p (c t)")
        pos1 = [0]

        def c1(n):
            a = R1[:, pos1[0]:pos1[0] + n]
            pos1[0] += n
            assert pos1[0] <= 12288
            return a
        R2 = self.big
        pos2 = [0]

        def c2(n):
            a = R2[:, pos2[0]:pos2[0] + n]
            pos2[0] += n
            assert pos2[0] <= 33792, pos2[0]
            return a
        PWG = T + 12
        pp = c1(PWG)
        cv = c1(T)
        oT = cv
        qT = c1(T // 2).bitcast(BF16)
        kT = c1(T // 2).bitcast(BF16)
        vT = c1(T // 2).bitcast(BF16)
        gateT = c1(T // 2).bitcast(BF16)
        sqb = pp[:, 2:2 + T // 2].bitcast(BF16)
        ktok = c1(T // 2).bitcast(BF16).rearrange("p (t c) -> p t c", t=12)
        vtok = c1(T // 2).bitcast(BF16).rearrange("p (t c) -> p t c", t=12)
        r_pp, r_cv, r_q, r_k, r_v, r_gate, r_ktok, r_vtok = (Res(n) for n in ("pp", "cv", "q", "k", "v", "gate", "ktok", "vtok"))
        r_sqb = r_pp
        r_oT = [Res(f"oT{t}") for t in range(12)]
        self.inherit([r_pp, r_cv, r_q, r_k, r_v, r_gate, r_sqb, r_ktok, r_vtok] + r_oT, allx)
        og = c2(NCH * T).rearrange("p (h t) -> p h t", h=NCH)
        r_og = [Res(f"og{h}") for h in range(NCH)]
        NCHAIN = 6
        slots = []
        extra_slots = []
        for ci in range(NCHAIN):
            ring = []
            for r_ in range(2):
                wu_ = c2(512).bitcast(F32)
                sl = dict(WU=wu_, WT=wu_[:, 0:128], U=wu_[:, 128:256], qdT=c2(256).bitcast(F32), aqkT=c2(256).bitcast(F32), ktail=c2(256).bitcast(F32),
                          gl=c2(4).bitcast(F32),
                          r=Res(f"slot{ci}_{r_}"))
                ring.append(sl)
            slots.append(ring)
        works = []
        for wi in range(2):
            wk = dict(r=Res(f"work{wi}"))
            PA, PR, PX, PX2 = c2(512).bitcast(F32), c2(512).bitcast(F32), c2(512).bitcast(F32), c2(512).bitcast(F32)
            wk["PA"], wk["PR"], wk["PX"], wk["PX2"] = PA, PR, PX, PX2
            wk["A0"], wk["AT0"] = PA[:, 0:128], PA[:, 128:256]
            wk["R"], wk["RT"] = PR[:, 0:128], PR[:, 128:256]
            wk["Dsym"], wk["egrow"] = PX[:, 0:128], PX[:, 128:256]
            wk["DMs"], wk["DMi"] = PX2[:, 0:128], PX2[:, 128:256]
            wk["kbe"], wk["vb"] = c2(256).bitcast(F32), c2(256).bitcast(F32)
            wk["dd"] = wk["Dsym"]
            wk["GR"] = c1(256)
            wk["cols"] = c2(16).bitcast(F32)
            works.append(wk)
        for wi in range(2):
            wk = dict(r=Res(f"workx{wi}"))
            base = self.wslots[wi]
            PA, PR, PX, PX2 = base[:, 0:512].bitcast(F32), base[:, 512:1024].bitcast(F32), base[:, 1024:1536].bitcast(F32), base[:, 1536:2048].bitcast(F32)
            wk["PA"], wk["PR"], wk["PX"], wk["PX2"] = PA, PR, PX, PX2
            wk["A0"], wk["AT0"] = PA[:, 0:128], PA[:, 128:256]
            wk["R"], wk["RT"] = PR[:, 0:128], PR[:, 128:256]
            wk["Dsym"], wk["egrow"] = PX[:, 0:128], PX[:, 128:256]
            wk["DMs"], wk["DMi"] = PX2[:, 0:128], PX2[:, 128:256]
            wk["kbe"], wk["vb"] = base[:, 2048:2304].bitcast(F32), base[:, 2304:2560].bitcast(F32)
            wk["dd"] = wk["Dsym"]
            wk["GR"] = base[:, 2560:3072].bitcast(F32)
            wk["cols"] = base[:, 3072:3088].bitcast(F32)
            self.inherit([wk["r"]], [self.r_wslot[wi]])
            works.append(wk)
        vnew = [self.wslots[4][:, 1792 + i * 256:1792 + (i + 1) * 256].bitcast(F32) for i in range(NCHAIN)]
        r_vnew = [Res(f"vnew{i}") for i in range(NCHAIN)]
        Sst = [c1(128) for _ in range(NCHAIN)]
        r_S = [Res(f"S{i}") for i in range(NCHAIN)]
        g_tok = c1(192).rearrange("p (t n) -> p t n", t=12)
        b_tok = c1(192).rearrange("p (t n) -> p t n", t=12)
        gccol = c1(192).rearrange("p (a t n) -> p a t n", a=2, t=12)
        rows = c1(32)
        gvec = c1(122)
        for ci in range(2):
            for r_ in range(2):
                wu_ = c1(256)
                sl = dict(WU=wu_, WT=wu_[:, 0:128], U=wu_[:, 128:256], qdT=c1(128), aqkT=c1(128), ktail=c1(128), gl=c1(2), r=Res(f"slotx{ci}_{r_}"))
                slots[ci].append(sl)
                extra_slots.append(sl["r"])
        self.inherit(extra_slots, allx)
        masks = [self.wslots[4][:, 256 + i * 256:256 + (i + 1) * 256].bitcast(F32) for i in range(6)]
        masks += [self.wslots[4][:, 3328 + i * 256:3328 + (i + 1) * 256].bitcast(F32) for i in range(3)]
        r_ab = Res("ab"); r_gpar = Res("gpar"); r_masks = Res("masks")
        allr2 = r_og + [sl["r"] for ring in slots for sl in ring]
        self.inherit(allr2, [self.r_big])
        r_works = [wk["r"] for wk in works]
        self.inherit(r_works, [self.r_big] + allx)
        self.inherit([r_ab, r_gpar] + r_S, allx)
        self.inherit([r_masks] + r_vnew, [self.r_wslot[4]])
        for ci in range(NCHAIN):
            kb.op("dve", lambda: V.memset(vnew[ci], 0.0), writes=[r_vnew[ci]])
        for i in range(9):
            kb.dma("sp", masks[i], d["gmasks"][i], writes=[r_masks])
        m_sl, m_ui, m_su, m_li, Lf, Lb, m_bd16, m_s1, m_s2 = masks
        mpair = [self.wslots[4][:, 256:768].bitcast(F32), self.wslots[4][:, 768:1280].bitcast(F32)]
        kb.dma("sp", gvec[:, 0:121], d["gdn_vecT"][j], writes=[r_gpar])
        kb.dma("sp", rows[:, 0:32], d["gdn_rows"][j:j + 1].rearrange("o a n -> o (a n)").partition_broadcast(128), writes=[r_gpar])
        kb.op("act", lambda: A.activation(out=rows[:, 0:16], in_=rows[:, 0:16], func=AF.Exp), reads=[r_gpar], writes=[r_gpar])
        kb.op("dve", lambda: V.tensor_scalar(out=rows[:, 0:16], in0=rows[:, 0:16], scalar1=-1.0, scalar2=None, op0=ALU.mult), reads=[r_gpar], writes=[r_gpar])
        cwg = lambda k, blk: gvec[:, k * 24 + blk:k * 24 + blk + 1]
        onorm = gvec[:, 120:121]
        wab_s, r_wab = self.wslots[4], self.r_wslot[4]
        wab = wab_s[:, 0:NCH * 32].rearrange("p (c n) -> p c n", c=NCH)
        kb.dma("pool", wab, d["gdn_w_in"][j, :, 4096:4128].rearrange("(c p) n -> p c n", p=128), writes=[r_wab])
        ps, rp = self.next_ps()
        for tt in range(12):
            for c in range(NCH):
                self.mm(ps[:, tt * 32:(tt + 1) * 32], self.hT[:, c, tt * 128:(tt + 1) * 128], wab[:, c, :], start=(c == 0), stop=(c == NCH - 1),
                        reads=[self.r_h[c][tt // 4], r_wab], writes=[rp], signal=(tt == 11 and c == NCH - 1))
        abt, rabt = self.next_tmp()
        ab3 = abt[:, 0:384].rearrange("p (t n) -> p t n", t=12)
        kb.op("dve", lambda: V.tensor_copy(out=abt[:, 0:384], in_=ps[:, 0:384]), reads=[rp], writes=[rabt])
        kb.op("act", lambda: A.activation(out=b_tok, in_=ab3[:, :, 16:32], func=AF.Sigmoid), reads=[rabt], writes=[r_ab])
        kb.op("dve", lambda: V.tensor_tensor(out=g_tok, in0=ab3[:, :, 0:16], in1=rows[:, 16:32].unsqueeze(1).to_broadcast([128, 12, 16]), op=ALU.add),
              reads=[rabt, r_gpar], writes=[r_ab])
        kb.op("act", lambda: A.activation(out=g_tok, in_=g_tok, func=AF.Exp), reads=[r_ab], writes=[r_ab])
        kb.op("act", lambda: A.activation(out=g_tok, in_=g_tok, func=AF.Ln, bias=1.0, scale=1.0), reads=[r_ab], writes=[r_ab])
        kb.op("dve", lambda: V.tensor_tensor(out=g_tok, in0=g_tok, in1=rows[:, 0:16].unsqueeze(1).to_broadcast([128, 12, 16]), op=ALU.mult),
              reads=[r_ab, r_gpar], writes=[r_ab])
        ps, rp = self.next_ps()
        for dr_ in range(2):
            self.mm(ps[:, dr_ * 96:(dr_ + 1) * 96].rearrange("p (t n) -> p t n", t=12), (Lf if dr_ == 0 else Lb), g_tok[:, :, dr_ * 8:(dr_ + 1) * 8], True, True,
                    reads=[r_masks, r_ab], writes=[rp], signal=(dr_ == 1))
        kb.op("dve", lambda: V.tensor_copy(out=gccol.rearrange("p a t n -> p (a t n)"), in_=ps[:, 0:192]), reads=[rp], writes=[r_ab])
        chains = [dict(tiles=list(range(8)), dir=0, seq=None), dict(tiles=list(range(7, -1, -1)), dir=1, seq=None),
                  dict(tiles=[8, 9], dir=0, seq=0), dict(tiles=[9, 8], dir=1, seq=0),
                  dict(tiles=[10, 11], dir=0, seq=1), dict(tiles=[11, 10], dir=1, seq=1)]
        evi = [0]

        def evac(out, in_, reads, writes):
            evi[0] += 1
            if evi[0] % 2 == 0:
                kb.op("act", lambda: A.copy(out=out, in_=in_), reads=reads, writes=writes)
            else:
                kb.op("dve", lambda: V.tensor_copy(out=out, in_=in_), reads=reads, writes=writes)
        wki = [0]
        segs = [(2, LLAT, 0), (1030, LCTX, LLAT), (1290, LCTX, LLAT + LCTX)]
        kb.op("dve", lambda: V.memset(pp, 0.0), writes=[r_pp])
        for h in range(NCH):
            wsl, rws = self.wslots[2 + h % 2], self.r_wslot[2 + h % 2]
            win = wsl[:, 0:4096].rearrange("p (c g n) -> p c g n", c=NCH, g=4)
            for g in range(4):
                kb.dma("pool", win[:, :, g, :], d["gdn_w_in"][j, :, g * D + h * 128:g * D + (h + 1) * 128].rearrange("(c p) n -> p c n", p=128), writes=[rws])
            self.inherit([r_cv], r_oT)
            for g in range(3):
                blk = g * 8 + h
                for b in range(NTB):
                    ps, rp = self.next_ps()
                    for c in range(NCH):
                        self.mm(ps[:], win[:, c, g, :], self.hT[:, c, b * 512:(b + 1) * 512], start=(c == 0), stop=(c == NCH - 1),
                                reads=[rws, self.r_h[c][b]], writes=[rp])
                    if b < 2:
                        evac(pp[:, 2 + b * 512:2 + (b + 1) * 512], ps[:], [rp], [r_pp])
                    else:
                        evac(pp[:, 1030:1286], ps[:, 0:256], [rp], [r_pp])
                        evac(pp[:, 1290:1546], ps[:, 256:512], [rp], [r_pp])
                for (s0, Ls, o0) in segs:
                    co = cv[:, o0:o0 + Ls]
                    kb.op("dve", lambda: V.tensor_scalar(out=co, in0=pp[:, s0 - 2:s0 - 2 + Ls], scalar1=cwg(0, blk), scalar2=None, op0=ALU.mult),
                          reads=[r_pp, r_gpar], writes=[r_cv])
                    for k in range(1, 5):
                        kb.op("dve", lambda: V.scalar_tensor_tensor(out=co, in0=pp[:, s0 - 2 + k:s0 - 2 + k + Ls], scalar=cwg(k, blk), in1=co,
                                                                    op0=ALU.mult, op1=ALU.add), reads=[r_pp, r_gpar, r_cv], writes=[r_cv])
                if g == 2:
                    kb.op("act", lambda: A.activation(out=vT, in_=cv, func=AF.Silu), reads=[r_cv], writes=[r_v])
                else:
                    dst, rdst = (qT, r_q) if g == 0 else (kT, r_k)
                    kb.op("act", lambda: A.activation(out=cv, in_=cv, func=AF.Silu), reads=[r_cv], writes=[r_cv])
                    kb.op("act", lambda: A.activation(out=sqb, in_=cv, func=AF.Square), reads=[r_cv], writes=[r_sqb])
                    for b in range(NTB):
                        ps, rp = self.next_ps()
                        self.mm(ps[:], self.onesb[:], sqb[:, b * 512:(b + 1) * 512], True, True, reads=[r_sqb, self.r_const], writes=[rp])
                        rs, rrs = self.next_rstd()
                        kb.op("dve", lambda: V.tensor_scalar(out=rs[:], in0=ps[:], scalar1=EPS, scalar2=None, op0=ALU.add), reads=[rp], writes=[rrs])
                        kb.op("act", lambda: A.activation(out=rs[:], in_=rs[:], func=AF.Sqrt), reads=[rrs], writes=[rrs])
                        kb.op("dve", lambda: V.reciprocal(out=rs[:], in_=rs[:]), reads=[rrs], writes=[rrs])
                        sc = (128.0 ** -0.5) if g == 0 else 1.0
                        kb.op("dve", lambda: V.scalar_tensor_tensor(out=dst[:, b * 512:(b + 1) * 512], in0=cv[:, b * 512:(b + 1) * 512], scalar=sc, in1=rs[:],
                                                                    op0=ALU.mult, op1=ALU.mult), reads=[r_cv, rrs], writes=[rdst])
            for b in range(NTB):
                ps, rp = self.next_ps()
                for c in range(NCH):
                    self.mm(ps[:], win[:, c, 3, :], self.hT[:, c, b * 512:(b + 1) * 512], start=(c == 0), stop=(c == NCH - 1),
                            reads=[rws, self.r_h[c][b]], writes=[rp])
                kb.op("act", lambda: A.activation(out=gateT[:, b * 512:(b + 1) * 512], in_=ps[:], func=AF.Silu), reads=[rp], writes=[r_gate])
            self.inherit(r_oT, [r_cv])
            for (src, rsrc, dst, rdst) in ((kT, r_k, ktok, r_ktok), (vT, r_v, vtok, r_vtok)):
                for g4 in range(3):
                    ps, rp = self.next_ps()
                    psb = ps[:].bitcast(BF16)
                    for i in range(4):
                        tt = g4 * 4 + i
                        kb.op("pe", lambda: P.transpose(psb[:, i * 128:(i + 1) * 128], src[:, tt * 128:(tt + 1) * 128], self.identb[:]),
                              reads=[rsrc, self.r_const], writes=[rp], signal=(i == 3))
                    evac(dst[:, g4 * 4:(g4 + 1) * 4, :], psb[:, 0:512].rearrange("p (t c) -> p t c", t=4), [rp], [rdst])
            for ci, ch in enumerate(chains):
                if ch["seq"] is None:
                    kb.dma("sp", Sst[ci], d["state_in"][j, ch["dir"], h], writes=[r_S[ci]])
                else:
                    kb.op("dve", lambda: V.memset(Sst[ci], 0.0), writes=[r_S[ci]])

            def intra(ci, step):
                ch = chains[ci]
                tt = ch["tiles"][step]
                dr_ = ch["dir"]
                sl = slots[ci][step % len(slots[ci])]
                rsl = sl["r"]
                wk = works[wki[0] % len(works)]
                wki[0] += 1
                rw = wk["r"]
                tcs = slice(tt * 128, (tt + 1) * 128)
                gcc = gccol[:, dr_, tt, h:h + 1]
                bc = b_tok[:, tt, dr_ * 8 + h:dr_ * 8 + h + 1]
                lastA, lastB = (63, 127) if dr_ == 0 else (0, 64)
                m_s = m_sl if dr_ == 0 else m_su
                m_i = m_ui if dr_ == 0 else m_li
                cols = wk["cols"]
                yield
                ps, rp = self.next_ps()
                self.mm(ps[:, 0:128], kT[:, tcs], kT[:, tcs], True, True, reads=[r_k], writes=[rp], signal=False)
                self.mm(ps[:, 128:256], kT[:, tcs], qT[:, tcs], True, True, reads=[r_k, r_q], writes=[rp])
                evac(wk["GR"], ps[:, 0:256], [rp], [rw])
                yield
                ps2, rp2 = self.next_ps()
                kb.op("pe", lambda: P.transpose(ps2[:, 0:128], gcc.to_broadcast([128, 128]), self.ident[:]), reads=[r_ab, self.r_const], writes=[rp2])
                kb.op("dve", lambda: V.tensor_scalar(out=wk["dd"], in0=ps2[:, 0:128], scalar1=gcc, scalar2=None, op0=ALU.subtract), reads=[rp2, r_ab], writes=[rw])
                kb.op("dve", lambda: V.scalar_tensor_tensor(out=wk["dd"], in0=wk["dd"], scalar=-1.0, in1=wk["dd"], op0=ALU.mult, op1=ALU.max), reads=[rw], writes=[rw])
                kb.op("act", lambda: A.activation(out=wk["Dsym"], in_=wk["dd"], func=AF.Exp, scale=-1.0), reads=[rw], writes=[rw])
                kb.op("act", lambda: A.activation(out=wk["egrow"], in_=ps2[:, 0:128], func=AF.Exp), reads=[rp2], writes=[rw])
                kb.op("dve", lambda: V.tensor_copy(out=cols[0:64, 3:4], in_=ps2[0:64, lastA:lastA + 1]), reads=[rp2], writes=[rw])
                kb.op("dve", lambda: V.tensor_copy(out=cols[64:128, 3:4], in_=ps2[64:128, lastB:lastB + 1]), reads=[rp2], writes=[rw])
                kb.op("act", lambda: A.activation(out=cols[:, 0:1], in_=gcc, func=AF.Exp), reads=[r_ab], writes=[rw])
                kb.op("dve", lambda: V.tensor_tensor(out=cols[:, 1:2], in0=cols[:, 0:1], in1=bc, op=ALU.mult), reads=[rw, r_ab], writes=[rw])
                kb.op("act", lambda: A.activation(out=cols[:, 2:3], in_=gcc, func=AF.Exp, bias=cols[:, 3:4], scale=-1.0), reads=[rw, r_ab], writes=[rw])
                kb.op("act", lambda: A.copy(out=sl["gl"][:, 0:1], in_=wk["egrow"][:, lastA:lastA + 1]), reads=[rw], writes=[rsl])
                kb.op("act", lambda: A.copy(out=sl["gl"][:, 1:2], in_=wk["egrow"][:, lastB:lastB + 1]), reads=[rw], writes=[rsl])
                mp = mpair[dr_]
                id2 = self.ident[:].unsqueeze(1).to_broadcast([128, 2, 128])
                v3 = lambda ap: ap.rearrange("p (a c) -> p a c", a=2)
                kb.op("pool", lambda: G.tensor_tensor(out=v3(wk["PX2"]), in0=wk["Dsym"].unsqueeze(1).to_broadcast([128, 2, 128]), in1=v3(mp), op=ALU.mult),
                      reads=[rw, r_masks], writes=[rw])
                yield
                kb.op("dve", lambda: V.scalar_tensor_tensor(out=wk["A0"], in0=wk["GR"][:, 0:128], scalar=bc, in1=wk["DMs"], op0=ALU.mult, op1=ALU.mult),
                      reads=[rw, r_ab], writes=[rw])
                yield
                kb.op("dve", lambda: V.tensor_tensor(out=sl["aqkT"], in0=wk["GR"][:, 128:256], in1=wk["DMi"], op=ALU.mult), reads=[rw], writes=[rsl])
                yield
                kb.op("dve", lambda: V.tensor_tensor(out=sl["qdT"], in0=qT[:, tcs], in1=wk["egrow"], op=ALU.mult), reads=[rw, r_q], writes=[rsl])
                yield
                kb.op("act", lambda: A.activation(out=wk["kbe"], in_=ktok[:, tt, :], func=AF.Copy, scale=cols[:, 1:2]), reads=[rw, r_ktok], writes=[rw])
                yield
                kb.op("act", lambda: A.activation(out=wk["vb"], in_=vtok[:, tt, :], func=AF.Copy, scale=bc), reads=[rw, r_vtok, r_ab], writes=[rw])
                yield
                kb.op("act", lambda: A.activation(out=sl["ktail"], in_=ktok[:, tt, :], func=AF.Copy, scale=cols[:, 2:3]), reads=[rw, r_ktok], writes=[rsl])
                yield
                ps3, rp3 = self.next_ps()
                kb.op("pe", lambda: P.transpose(ps3[:, 0:128], wk["A0"], self.ident[:]), reads=[rw, self.r_const], writes=[rp3])
                kb.op("act", lambda: A.copy(out=wk["AT0"], in_=ps3[:, 0:128]), reads=[rp3], writes=[rw])
                PX, PX2, PR, PA = wk["PX"], wk["PX2"], wk["PR"], wk["PA"]
                bd2 = m_bd16.unsqueeze(1).to_broadcast([128, 2, 128])
                kb.op("pool", lambda: G.tensor_tensor(out=v3(PX), in0=v3(PA), in1=bd2, op=ALU.mult), reads=[rw, r_masks], writes=[rw])
                yield
                kb.op("dve", lambda: V.tensor_tensor(out=v3(PR), in0=id2, in1=v3(PX), op=ALU.subtract), reads=[rw, self.r_const], writes=[rw])
                for it in range(3):
                    X, XT = PX[:, 0:128], PX[:, 128:256]
                    yield
                    psA, rpA = self.next_ps()
                    self.mm(psA[:, 0:128], XT, X, True, True, reads=[rw], writes=[rpA], signal=False)
                    self.mm(psA[:, 128:256], X, XT, True, True, reads=[rw], writes=[rpA])
                    kb.op("act", lambda: A.copy(out=PX2, in_=psA[:, 0:256]), reads=[rpA], writes=[rw])
                    X2, XT2 = PX2[:, 0:128], PX2[:, 128:256]
                    yield
                    psB, rpB = self.next_ps()
                    self.mm(psB[:, 0:128], XT2, PR[:, 0:128], True, True, reads=[rw], writes=[rpB], signal=False)
                    self.mm(psB[:, 128:256], X2, PR[:, 128:256], True, True, reads=[rw], writes=[rpB])
                    kb.op("dve", lambda: V.tensor_tensor(out=PR, in0=PR, in1=psB[:, 0:256], op=ALU.add), reads=[rpB, rw], writes=[rw])
                    PX, PX2 = PX2, PX
                for msk in (m_s1, m_s2):
                    ms2 = msk.unsqueeze(1).to_broadcast([128, 2, 128])
                    kb.op("pool", lambda: G.tensor_tensor(out=v3(PX), in0=v3(PA), in1=ms2, op=ALU.mult), reads=[rw, r_masks], writes=[rw])
                    X, XT = PX[:, 0:128], PX[:, 128:256]
                    yield
                    ps1, rp1 = self.next_ps()
                    self.mm(ps1[:, 0:128], XT, PR[:, 0:128], True, True, reads=[rw], writes=[rp1], signal=False)
                    self.mm(ps1[:, 128:256], X, PR[:, 128:256], True, True, reads=[rw], writes=[rp1])
                    kb.op("act", lambda: A.copy(out=PX2, in_=ps1[:, 0:256]), reads=[rp1], writes=[rw])
                    yield
                    ps2_, rp2_ = self.next_ps()
                    self.mm(ps2_[:, 0:128], PR[:, 128:256], PX2[:, 0:128], True, True, reads=[rw], writes=[rp2_], signal=False)
                    self.mm(ps2_[:, 128:256], PR[:, 0:128], PX2[:, 128:256], True, True, reads=[rw], writes=[rp2_])
                    kb.op("dve", lambda: V.tensor_tensor(out=PR, in0=PR, in1=ps2_[:, 0:256], op=ALU.subtract), reads=[rp2_, rw], writes=[rw])
                yield
                psW, rpW = self.next_ps()
                self.mm(psW[:, 0:128], wk["kbe"], wk["RT"], True, True, reads=[rw], writes=[rpW], signal=False)
                self.mm(psW[:, 128:256], wk["RT"], wk["vb"], True, True, reads=[rw], writes=[rpW])
                kb.op("act", lambda: A.copy(out=sl["WU"], in_=psW[:, 0:256]), reads=[rpW], writes=[rsl])

            def scan(ci, step):
                ch = chains[ci]
                tt = ch["tiles"][step]
                sl = slots[ci][step % len(slots[ci])]
                rsl = sl["r"]
                tcs = slice(tt * 128, (tt + 1) * 128)
                for half in ((0, 1) if ch["dir"] == 0 else (1, 0)):
                    hs = slice(half * 64, (half + 1) * 64)
                    hcs = slice(tt * 128 + half * 64, tt * 128 + (half + 1) * 64)
                    yield
                    psP, rpP = self.next_ps()
                    self.mm(psP[:, 0:128], sl["WT"], Sst[ci], True, True, reads=[rsl, r_S[ci]], writes=[rpP])
                    kb.op("dve", lambda: V.tensor_tensor(out=vnew[ci][hs, :], in0=sl["U"][hs, :], in1=psP[hs, 0:128], op=ALU.subtract),
                          reads=[rsl, rpP], writes=[r_vnew[ci]])
                    yield
                    psO, rpO = self.next_ps()
                    self.mm(psO[:, 0:64], Sst[ci], sl["qdT"][:, hs], True, False, reads=[rsl, r_S[ci]], writes=[rpO])
                    self.mm(psO[:, 0:64], vnew[ci][hs, :], sl["aqkT"][hs, hs], False, True, reads=[rsl, r_vnew[ci]], writes=[rpO])
                    psS, rpS = self.next_ps()
                    self.mm(psS[:, 0:128], sl["ktail"][hs, :], vnew[ci][hs, :], True, True, reads=[rsl, r_vnew[ci]], writes=[rpS])
                    okey = (tt, half)
                    if okey not in owritten:
                        owritten.add(okey)
                        kb.op("act", lambda: A.copy(out=oT[:, hcs], in_=psO[:, 0:64]), reads=[rpO], writes=[r_oT[tt]])
                    else:
                        kb.op("dve", lambda: V.tensor_tensor(out=oT[:, hcs], in0=oT[:, hcs], in1=psO[:, 0:64], op=ALU.add), reads=[rpO, r_oT[tt]], writes=[r_oT[tt]])
                    kb.op("dve", lambda: V.scalar_tensor_tensor(out=Sst[ci], in0=Sst[ci], scalar=sl["gl"][:, half:half + 1], in1=psS[:, 0:128], op0=ALU.mult, op1=ALU.add),
                          reads=[rpS, rsl, r_S[ci]], writes=[r_S[ci]])
                if ch["seq"] is not None and step == len(ch["tiles"]) - 1:
                    kb.dma("sp", d["ns_out"][ch["seq"], j, ch["dir"], h], Sst[ci], reads=[r_S[ci]], writes=[self.r_nsout])

            owritten = set()

            def run_interleaved(gens):
                active = list(gens)
                while active:
                    for g_ in list(active):
                        try:
                            next(g_)
                        except StopIteration:
                            active.remove(g_)
            NW = len(works)
            nI = [0] * NCHAIN
            nS = [0] * NCHAIN
            nT = [len(chains[ci]["tiles"]) for ci in range(NCHAIN)]
            while any(nS[ci] < nT[ci] for ci in range(NCHAIN)):
                gens = []
                scan_ready = [ci for ci in range(NCHAIN) if nS[ci] < nI[ci]]
                cand = [ci for ci in range(NCHAIN) if nI[ci] < nT[ci] and nI[ci] - nS[ci] < len(slots[ci])]
                cand.sort(key=lambda ci: (nI[ci] - nS[ci], ci))
                picked = cand[:NW]
                for ci in picked:
                    gens.append(intra(ci, nI[ci]))
                for ci in scan_ready:
                    gens.append(scan(ci, nS[ci]))
                run_interleaved(gens)
                for ci in picked:
                    nI[ci] += 1
                for ci in scan_ready:
                    nS[ci] += 1
            kb.op("act", lambda: A.activation(out=sqb, in_=oT, func=AF.Square), reads=r_oT, writes=[r_sqb])
            for b in range(NTB):
                ps, rp = self.next_ps()
                self.mm(ps[:], self.onesb[:], sqb[:, b * 512:(b + 1) * 512], True, True, reads=[r_sqb, self.r_const], writes=[rp])
                rs, rrs = self.next_rstd()
                kb.op("dve", lambda: V.tensor_scalar(out=rs[:], in0=ps[:], scalar1=1.0 / 128.0, scalar2=EPS, op0=ALU.mult, op1=ALU.add), reads=[rp], writes=[rrs])
                kb.op("act", lambda: A.activation(out=rs[:], in_=rs[:], func=AF.Sqrt), reads=[rrs], writes=[rrs])
                kb.op("dve", lambda: V.reciprocal(out=rs[:], in_=rs[:]), reads=[rrs], writes=[rrs])
                t0, r0 = self.next_tmp()
                kb.op("dve", lambda: V.scalar_tensor_tensor(out=t0[:], in0=oT[:, b * 512:(b + 1) * 512], scalar=onorm, in1=rs[:], op0=ALU.mult, op1=ALU.mult),
                      reads=r_oT[b * 4:(b + 1) * 4] + [rrs, r_gpar], writes=[r0])
                kb.op("dve", lambda: V.tensor_tensor(out=og[:, h, b * 512:(b + 1) * 512], in0=t0[:], in1=gateT[:, b * 512:(b + 1) * 512], op=ALU.mult),
                      reads=[r0, r_gate], writes=[r_og[h]])
        self.inherit(allx, [r_pp, r_cv, r_q, r_k, r_v, r_gate, r_sqb, r_ktok, r_vtok, r_ab, r_gpar] + r_oT + r_S + r_works + extra_slots)
        if False:
            for b in range(NTB):
                t0, r0 = self.next_tmp()
                kb.op("dve", lambda: V.tensor_copy(out=t0[:], in_=og[:, 0, b * 512:(b + 1) * 512]), reads=[r_og[0]], writes=[r0])
                kb.dma("sp", d["dbg"][5, :, b * 512:(b + 1) * 512], t0[:], reads=[r0], writes=[self.r_nsout])
        self.inherit([self.r_wslot[0]], [works[2]["r"]])
        self.inherit([self.r_wslot[1]], [works[3]["r"]])
        self.out_proj(l, d["gdn_w_out"][j], og, r_og, None, None)
        if False:
            pass
        self.inherit([self.r_big], allr2 + r_works)
        self.inherit([self.r_wslot[4]], [r_masks] + r_vnew)


    def out_proj(self, l, wdram, src, r_src, bias_fn, r_bias):
        nc, kb = self.nc, self.kb
        V, A, P = nc.vector, nc.scalar, nc.tensor
        d = self.dr
        allx = [r for rr in self.r_x for r in rr]
        kb.dma("sp", self.xT[:].rearrange("p c t -> p (c t)"), d["xsp"][:, :], writes=allx)
        wo = []
        for hlf in range(2):
            wv = self.wslots[hlf][:, 0:NCH * 512].rearrange("p (c n) -> p c n", c=NCH)
            kb.dma("pool", wv, wdram[:, hlf * 512:(hlf + 1) * 512].rearrange("(c p) n -> p c n", p=128), writes=[self.r_wslot[hlf]])
            wo.append(wv)
        for c in range(NCH):
            for b in range(NTB):
                w = 0 if b < 2 else 1
                ps, rp = self.next_ps()
                for cb in range(NCH):
                    self.mm(ps[:], wo[c // 4][:, cb, (c % 4) * 128:(c % 4 + 1) * 128], src[:, cb, b * 512:(b + 1) * 512],
                            start=(cb == 0), stop=(cb == NCH - 1), reads=[self.r_wslot[c // 4], r_src[cb]], writes=[rp])
                t0, r0 = self.next_tmp()
                if bias_fn is not None:
                    kb.op("dve", lambda: V.tensor_scalar(out=t0[:], in0=ps[:], scalar1=bias_fn(c), scalar2=self.mod(l, w, 2, c), op0=ALU.add, op1=ALU.mult),
                          reads=[rp, r_bias, self.r_mods[l]], writes=[r0])
                else:
                    kb.op("dve", lambda: V.tensor_scalar(out=t0[:], in0=ps[:], scalar1=self.mod(l, w, 2, c), scalar2=None, op0=ALU.mult),
                          reads=[rp, self.r_mods[l]], writes=[r0])
                xs = self.xT[:, c, b * 512:(b + 1) * 512]
                kb.op("dve", lambda: V.tensor_tensor(out=xs, in0=xs, in1=t0[:], op=ALU.add), reads=[r0, self.r_x[c][b]], writes=[self.r_x[c][b]])

    def inherit(self, dsts, srcs):
        evs = []
        for sres in srcs:
            if sres.w is not None:
                evs.append(sres.w)
            evs.extend(sres.rs)
        best = {}
        for ev in evs:
            k = id(ev[0])
            if k not in best or best[k][1] < ev[1]:
                best[k] = ev
        for dres in dsts:
            own = ([dres.w] if dres.w is not None else []) + list(dres.rs)
            b2 = dict(best)
            for ev in own:
                k = id(ev[0])
                if k not in b2 or b2[k][1] < ev[1]:
                    b2[k] = ev
            dres.w = None
            dres.rs = list(b2.values())

    def sin3(self, buf, rbuf, npart, ncol):
        nc, kb = self.nc, self.kb
        V, A = nc.vector, nc.scalar
        q, rq = self.next_tmp()
        bv = buf[0:npart, 0:ncol]
        qv = q[0:npart, 0:ncol]
        kb.op("act", lambda: A.activation(out=bv, in_=bv, func=AF.Sin, scale=1.0 / 3.0), reads=[rbuf], writes=[rbuf])
        kb.op("dve", lambda: V.tensor_tensor(out=qv, in0=bv, in1=bv, op=ALU.mult), reads=[rbuf], writes=[rq])
        kb.op("dve", lambda: V.tensor_scalar(out=qv, in0=qv, scalar1=-4.0, scalar2=3.0, op0=ALU.mult, op1=ALU.add), reads=[rq], writes=[rq])
        kb.op("dve", lambda: V.tensor_tensor(out=bv, in0=bv, in1=qv, op=ALU.mult), reads=[rq, rbuf], writes=[rbuf])

    def hyena_layer(self, l):
        nc, kb = self.nc, self.kb
        V, A, P, G = nc.vector, nc.scalar, nc.tensor, nc.gpsimd
        d = self.dr
        j = l // 2
        self.norm_mod(l, 2 * l, 0, 1)
        allx = [r for rr in self.r_x for r in rr]
        kb.dma("sp", d["xsp"][:, :], self.xT[:].rearrange("p c t -> p (c t)"), reads=allx)
        R1 = self.xT[:].rearrange("p c t -> p (c t)")
        PW = 1544
        pp = [R1[:, i * PW:(i + 1) * PW] for i in range(3)]
        o1 = 3 * PW
        u = [R1[:, o1 + i * T:o1 + (i + 1) * T] for i in range(3)]
        o1 += 3 * T
        zt = R1[:, o1:o1 + 768].bitcast(BF16).rearrange("p (t c) -> p t c", t=12)
        hidb = R1[:, o1 + 768:o1 + 768 + 640].bitcast(BF16)
        r_hidb = Res("hidb")
        r_pp = [Res(f"pp{i}") for i in range(3)]
        r_u = [Res(f"u{i}") for i in range(3)]
        r_zt = Res("zt")
        self.inherit(r_pp + r_u + [r_zt, r_hidb], allx)
        R2 = self.big
        zfin = R2[:, 0:NCH * T].rearrange("p (c t) -> p c t", c=NCH)
        o2 = NCH * T
        hid_lat = R2[:, o2:o2 + 2048].bitcast(F32); o2 += 2048
        hid_ctx = R2[:, o2:o2 + 512].bitcast(F32); o2 += 512
        delta_b = R2[:, o2:o2 + 2048].bitcast(F32); o2 += 2048
        fv = {}
        for nm in ("A0", "B0", "hf0", "hf1", "nhf1", "hb0", "nhb0", "hb1"):
            fv[nm] = R2[:, o2:o2 + 512].rearrange("p (q c) -> p q c", q=4); o2 += 512
        for nm in ("Ac", "Bc"):
            fv[nm] = R2[:, o2:o2 + 256].rearrange("p (q c) -> p q c", q=2); o2 += 256
        spec = R2[:, o2:o2 + 12288].bitcast(F32)
        o2 += 12288
        assert o2 <= 33792
        SP = [spec[:, i * 512:(i + 1) * 512] for i in range(10)]
        Yb = spec[:, 5120:6144].bitcast(BF16)
        Y = [Yb[:, i * 512:(i + 1) * 512] for i in range(4)]
        r_SP = [Res(f"sp{i}") for i in range(10)]
        r_Y = [Res(f"Y{i}") for i in range(4)]
        r_zfin = [Res(f"zfin{c}") for c in range(NCH)]
        r_hid = Res("hid"); r_delta = Res("delta"); r_fv = Res("fv")
        self.inherit(r_SP + r_Y + r_zfin + [r_hid, r_delta, r_fv], [self.r_big])
        if not hasattr(self, "hy_alloc"):
            self.hy_alloc = dict(
                vecT=kb.sb("s_hyvecT", [128, 120], F32), vec64=kb.sb("s_hyvec64", [64, 4], F32),
                w1=kb.sb("s_hyw1", [33, 64], F32), w2=kb.sb("s_hyw2", [64, 64], F32),
                tneg=kb.sb("s_hytneg", [128, 10], F32), mask0=kb.sb("s_hymask0", [128, 1], F32),
                w3cb=[kb.sb(f"s_hyw3cb_{i}", [64, 4, 128], F32) for i in range(2)],
                r_w3cb=[Res("w3cb0"), Res("w3cb1")], r_par=Res("hypar"))
        ha = self.hy_alloc
        vecT, vec64, w1, w2, tneg, mask0, w3cb, r_w3cb, r_par = (ha[k] for k in
            ("vecT", "vec64", "w1", "w2", "tneg", "mask0", "w3cb", "r_w3cb", "r_par"))
        kb.dma("sp", vecT[:], d["hy_vecT"][j], writes=[r_par])
        kb.dma("sp", vec64[:, 0:3], d["hy_vec64"][j], writes=[r_par])
        kb.dma("sp", w1[:], d["hy_f_w1"][j], writes=[r_par])
        kb.dma("sp", w2[:], d["hy_f_w2"][j], writes=[r_par])
        kb.dma("sp", tneg[:], d["hytneg"][:, :], writes=[r_par])
        kb.dma("sp", mask0[:], d["mask0"][:, :], writes=[r_par])
        kb.dma("sp", delta_b, d["hydelta"][0:1, :].partition_broadcast(128), writes=[r_delta])
        kb.op("dve", lambda: V.tensor_tensor(out=vec64[:, 3:4], in0=vec64[:, 0:1], in1=vec64[:, 2:3], op=ALU.mult), reads=[r_par], writes=[r_par])
        b_in = lambda blk: vecT[:, blk:blk + 1]
        cw = lambda k, blk: vecT[:, 24 + k * 24 + blk:24 + k * 24 + blk + 1]
        skp = lambda n, c: vecT[:, 96 + n * 8 + c:96 + n * 8 + c + 1]
        b_out = lambda c: vecT[:, 112 + c:113 + c]
        tabs = []
        for i in range(6):
            sl = self.wslots[i // 2]
            tv = sl[:, (i % 2) * 2048:(i % 2 + 1) * 2048].rearrange("p (q k) -> p q k", q=4)
            kb.dma("pool", tv, d["hytab"][i].rearrange("(q p) k -> p q k", p=128), writes=[self.r_wslot[i // 2]])
            tabs.append(tv)
        Cf, Sf, Cr, Sr, Ci, Si = tabs
        r_tab = [self.r_wslot[0], self.r_wslot[0], self.r_wslot[1], self.r_wslot[1], self.r_wslot[2], self.r_wslot[2]]
        rCf, rSf, rCr, rSr, rCi, rSi = r_tab
        for (zname, L, hid) in (("zemb_lat", LLAT, hid_lat), ("zemb_ctx", LCTX, hid_ctx)):
            for c0 in range(0, L, 512):
                ncol = min(512, L - c0)
                ze, rze = self.next_tmp()
                kb.dma("sp", ze[0:33, 0:ncol], d[zname][:, c0:c0 + ncol], writes=[rze])
                ps, rp = self.next_ps()
                self.mm(ps[0:64, 0:ncol], w1[:, :], ze[0:33, 0:ncol], True, True, reads=[rze, r_par], writes=[rp])
                h1, rh1 = self.next_tmp()
                kb.op("dve", lambda: V.tensor_scalar(out=h1[0:64, 0:ncol], in0=ps[0:64, 0:ncol], scalar1=vec64[:, 0:1], scalar2=vec64[:, 2:3],
                                                     op0=ALU.add, op1=ALU.mult), reads=[rp, r_par], writes=[rh1])
                self.sin3(h1, rh1, 64, ncol)
                ps2, rp2 = self.next_ps()
                self.mm(ps2[0:64, 0:ncol], w2[:, :], h1[0:64, 0:ncol], True, True, reads=[rh1, r_par], writes=[rp2])
                hv = hid[0:64, c0:c0 + ncol]
                kb.op("dve", lambda: V.tensor_scalar(out=hv, in0=ps2[0:64, 0:ncol], scalar1=vec64[:, 1:2], scalar2=vec64[:, 2:3],
                                                     op0=ALU.add, op1=ALU.mult), reads=[rp2, r_par], writes=[r_hid])
                kb.op("act", lambda: A.activation(out=hv, in_=hv, func=AF.Sin, scale=1.0 / 3.0), reads=[r_hid], writes=[r_hid])
                q, rq = self.next_tmp()
                qv = q[0:64, 0:ncol]
                kb.op("dve", lambda: V.tensor_tensor(out=qv, in0=hv, in1=hv, op=ALU.mult), reads=[r_hid], writes=[rq])
                kb.op("dve", lambda: V.tensor_scalar(out=qv, in0=qv, scalar1=-4.0, scalar2=3.0, op0=ALU.mult, op1=ALU.add), reads=[rq], writes=[rq])
                kb.op("dve", lambda: V.tensor_tensor(out=hv, in0=hv, in1=qv, op=ALU.mult), reads=[rq, r_hid], writes=[r_hid])
        kb.op("act", lambda: A.copy(out=hidb[0:64, 0:LLAT], in_=hid_lat[0:64, :]), reads=[r_hid], writes=[r_hidb])
        kb.op("act", lambda: A.copy(out=hidb[0:64, LLAT:LLAT + LCTX], in_=hid_ctx[0:64, :]), reads=[r_hid], writes=[r_hidb])
        for i in range(3):
            kb.op("dve", lambda: V.memset(pp[i], 0.0), writes=[r_pp[i]])
        segs = [(1, LLAT, 0), (1027, LCTX, LLAT), (1285, LCTX, LLAT + LCTX)]
        evi = [0]

        def evac(out, in_, reads, writes):
            evi[0] += 1
            if evi[0] % 2 == 0:
                kb.op("act", lambda: A.copy(out=out, in_=in_), reads=reads, writes=writes)
            else:
                kb.op("dve", lambda: V.tensor_copy(out=out, in_=in_), reads=reads, writes=writes)

        def spectrum(dst, rdst, terms):
            ps, rp = self.next_ps()
            tot = sum(t[4] for t in terms)
            for kb_ in range(4):
                i = 0
                for (tab, rtab, dat, rdat, nq) in terms:
                    for q in range(nq):
                        self.mm(ps[:, kb_ * 128:(kb_ + 1) * 128], tab[:, q, kb_ * 128:(kb_ + 1) * 128], dat[:, q, :],
                                start=(i == 0), stop=(i == tot - 1), reads=[rtab, rdat], writes=[rp], signal=(kb_ == 3 and i == tot - 1))
                        i += 1
            evac(dst, ps[:], [rp], [rdst])

        def cprod(dst, rdst, terms):
            acc, racc = self.next_tmp()
            n = len(terms)
            for i, (a, ra, b, rb, sg) in enumerate(terms):
                if i == 0:
                    kb.op("dve", lambda: V.tensor_tensor(out=acc[:], in0=a, in1=b, op=ALU.mult), reads=[ra, rb], writes=[racc])
                else:
                    t2, r2 = self.next_tmp()
                    kb.op("dve", lambda: V.tensor_tensor(out=t2[:], in0=a, in1=b, op=ALU.mult), reads=[ra, rb], writes=[r2])
                    o = dst if i == n - 1 else acc[:]
                    ro = rdst if i == n - 1 else racc
                    kb.op("dve", lambda: V.tensor_tensor(out=o, in0=acc[:], in1=t2[:], op=(ALU.add if sg > 0 else ALU.subtract)),
                          reads=[racc, r2], writes=[ro])

        for cb in range(NCH):
            wsl = self.wslots[3 + cb % 2]
            rws = self.r_wslot[3 + cb % 2]
            win = wsl[:, 0:NCH * 3 * 128].rearrange("p (c g n) -> p c g n", c=NCH, g=3)
            for g in range(3):
                kb.dma("pool", win[:, :, g, :], d["hy_w_in"][j, :, g * D + cb * 128:g * D + (cb + 1) * 128].rearrange("(c p) n -> p c n", p=128), writes=[rws])
            w3 = w3cb[cb % 2]
            rw3 = r_w3cb[cb % 2]
            w3 = w3[:].rearrange("r g c -> r (g c)").bitcast(BF16)[:, 0:512].rearrange("r (g c) -> r g c", g=4)
            kb.dma("pool", w3, d["hy_f_w3"][j].rearrange("r (g c) -> r g c", g=4)[:, :, cb * 128:(cb + 1) * 128], writes=[rw3])
            for g in range(3):
                blk = g * 8 + cb
                for b in range(NTB):
                    ps, rp = self.next_ps()
                    for c in range(NCH):
                        self.mm(ps[:], win[:, c, g, :], self.hT[:, c, b * 512:(b + 1) * 512], start=(c == 0), stop=(c == NCH - 1),
                                reads=[rws, self.r_h[c][b]], writes=[rp])
                    if b < 2:
                        dsts = [(pp[g][:, 1 + b * 512:1 + (b + 1) * 512], ps[:])]
                    else:
                        dsts = [(pp[g][:, 1027:1027 + 256], ps[:, 0:256]), (pp[g][:, 1285:1285 + 256], ps[:, 256:512])]
                    for (o, i_) in dsts:
                        kb.op("act", lambda: A.activation(out=o, in_=i_, func=AF.Identity, bias=b_in(blk), scale=1.0),
                              reads=[rp, r_par], writes=[r_pp[g]])
                for (s0, Ls, o0) in segs:
                    uo = u[g][:, o0:o0 + Ls]
                    kb.op("dve", lambda: V.tensor_scalar(out=uo, in0=pp[g][:, s0 - 1:s0 - 1 + Ls], scalar1=cw(0, blk), scalar2=None, op0=ALU.mult),
                          reads=[r_pp[g], r_par], writes=[r_u[g]])
                    for k in (1, 2):
                        kb.op("dve", lambda: V.scalar_tensor_tensor(out=uo, in0=pp[g][:, s0 - 1 + k:s0 - 1 + k + Ls], scalar=cw(k, blk), in1=uo,
                                                                    op0=ALU.mult, op1=ALU.add), reads=[r_pp[g], r_par, r_u[g]], writes=[r_u[g]])
            for n in range(2):
                z, rz = u[n], r_u[n]
                for g4 in range(3):
                    ps, rp = self.next_ps()
                    for i in range(4):
                        tt = g4 * 4 + i
                        kb.op("pe", lambda: P.transpose(ps[:, i * 128:(i + 1) * 128], z[:, tt * 128:(tt + 1) * 128], self.ident[:]),
                              reads=[rz, self.r_const], writes=[rp], signal=(i == 3))
                    evac(zt[:, g4 * 4:(g4 + 1) * 4, :], ps[:].rearrange("p (t c) -> p t c", t=4), [rp], [r_zt])
                for q in range(10):
                    lat = q < 8
                    hidv = hidb[0:64, q * 128:(q + 1) * 128]
                    ps, rp = self.next_ps()
                    for dr_ in range(2):
                        self.mm(ps[:, dr_ * 128:(dr_ + 1) * 128], hidv, w3[:, n * 2 + dr_, :], True, True, reads=[r_hidb, rw3], writes=[rp], signal=(dr_ == 1))
                    win_, rwin = self.next_tmp()
                    kb.op("act", lambda: A.activation(out=win_[:, 0:128], in_=delta_b[:, cb * 128:(cb + 1) * 128], func=AF.Exp, scale=tneg[:, q:q + 1]),
                          reads=[r_delta, r_par], writes=[rwin])
                    hh, rhh = self.next_tmp()
                    kb.op("dve", lambda: V.tensor_tensor(out=hh[:, 0:256].rearrange("p (a c) -> p a c", a=2), in0=ps[:, 0:256].rearrange("p (a c) -> p a c", a=2),
                                                         in1=win_[:, 0:128].unsqueeze(1).to_broadcast([128, 2, 128]), op=ALU.mult),
                          reads=[rp, rwin], writes=[rhh])
                    hf = hh[:, 0:128]
                    hb = hh[:, 128:256]
                    if q == 0 or q == 8:
                        kb.op("dve", lambda: V.tensor_scalar(out=hb, in0=hb, scalar1=mask0[:, 0:1], scalar2=None, op0=ALU.mult), reads=[rhh, r_par], writes=[rhh])
                    def wr(nm, qq, fn):
                        kb.op("dve", fn(fv[nm][:, qq, :]), reads=[rhh], writes=[r_fv])
                    if q < 4 or q >= 8:
                        nmA, nmB, qq = ("A0", "B0", q) if lat else ("Ac", "Bc", q - 8)
                        kb.op("dve", lambda: V.tensor_tensor(out=fv[nmA][:, qq, :], in0=hf, in1=hb, op=ALU.add), reads=[rhh], writes=[r_fv])
                        kb.op("dve", lambda: V.tensor_tensor(out=fv[nmB][:, qq, :], in0=hb, in1=hf, op=ALU.subtract), reads=[rhh], writes=[r_fv])
                        if lat:
                            kb.op("act", lambda: A.copy(out=fv["hf0"][:, q, :], in_=hf), reads=[rhh], writes=[r_fv])
                            kb.op("act", lambda: A.copy(out=fv["hb0"][:, q, :], in_=hb), reads=[rhh], writes=[r_fv])
                            kb.op("act", lambda: A.mul(out=fv["nhb0"][:, q, :], in_=hb, mul=-1.0), reads=[rhh], writes=[r_fv])
                    else:
                        kb.op("act", lambda: A.copy(out=fv["hf1"][:, q - 4, :], in_=hf), reads=[rhh], writes=[r_fv])
                        kb.op("act", lambda: A.copy(out=fv["hb1"][:, q - 4, :], in_=hb), reads=[rhh], writes=[r_fv])
                        kb.op("act", lambda: A.mul(out=fv["nhf1"][:, q - 4, :], in_=hf, mul=-1.0), reads=[rhh], writes=[r_fv])
                zl = [zt[:, 0:4, :], zt[:, 4:8, :]]
                spectrum(SP[0], r_SP[0], [(Cf, rCf, zl[0], r_zt, 4)])
                spectrum(SP[1], r_SP[1], [(Sf, rSf, zl[0], r_zt, 4)])
                spectrum(SP[2], r_SP[2], [(Cf, rCf, zl[1], r_zt, 4)])
                spectrum(SP[3], r_SP[3], [(Sf, rSf, zl[1], r_zt, 4)])
                spectrum(SP[4], r_SP[4], [(Cf, rCf, fv["A0"], r_fv, 4)])
                spectrum(SP[5], r_SP[5], [(Sf, rSf, fv["B0"], r_fv, 4)])
                spectrum(SP[6], r_SP[6], [(Cf, rCf, fv["hf1"], r_fv, 4), (Cr, rCr, fv["hf0"], r_fv, 4)])
                spectrum(SP[7], r_SP[7], [(Sf, rSf, fv["nhf1"], r_fv, 4), (Sr, rSr, fv["hf0"], r_fv, 4)])
                spectrum(SP[8], r_SP[8], [(Cf, rCf, fv["hb1"], r_fv, 4), (Cr, rCr, fv["hb0"], r_fv, 4)])
                spectrum(SP[9], r_SP[9], [(Sf, rSf, fv["hb1"], r_fv, 4), (Sr, rSr, fv["nhb0"], r_fv, 4)])
                S_ = lambda i: (SP[i], r_SP[i])
                def T4(a, b, sg):
                    return (SP[a], r_SP[a], SP[b], r_SP[b], sg)
                cprod(Y[0], r_Y[0], [T4(0, 4, 1), T4(1, 5, 1), T4(2, 8, 1), T4(3, 9, 1)])
                cprod(Y[1], r_Y[1], [T4(0, 5, 1), T4(1, 4, -1), T4(2, 9, 1), T4(3, 8, -1)])
                cprod(Y[2], r_Y[2], [T4(0, 6, 1), T4(1, 7, 1), T4(2, 4, 1), T4(3, 5, 1)])
                cprod(Y[3], r_Y[3], [T4(0, 7, 1), T4(1, 6, -1), T4(2, 5, 1), T4(3, 4, -1)])

                def inverse_and_gate(yre, ryre, yim, ryim, ncol, col0):
                    ps, rp = self.next_ps()
                    yre3 = yre.rearrange("p (q c) -> p q c", q=4)
                    yim3 = yim.rearrange("p (q c) -> p q c", q=4)
                    for q in range(4):
                        self.mm(ps[:, 0:ncol], yre3[:, q, :], Ci[:, q, 0:ncol], start=(q == 0), stop=False, reads=[ryre, rCi], writes=[rp])
                    for q in range(4):
                        self.mm(ps[:, 0:ncol], yim3[:, q, :], Si[:, q, 0:ncol], start=False, stop=(q == 3), reads=[ryim, rSi], writes=[rp])
                    t0, r0 = self.next_tmp()
                    zs = z[:, col0:col0 + ncol]
                    kb.op("dve", lambda: V.scalar_tensor_tensor(out=t0[:, 0:ncol], in0=zs, scalar=skp(n, cb), in1=ps[:, 0:ncol], op0=ALU.mult, op1=ALU.add),
                          reads=[rz, rp, r_par], writes=[r0])
                    un = u[n + 1][:, col0:col0 + ncol]
                    kb.op("dve", lambda: V.tensor_tensor(out=un, in0=t0[:, 0:ncol], in1=un, op=ALU.mult), reads=[r0, r_u[n + 1]], writes=[r_u[n + 1]])

                inverse_and_gate(Y[0], r_Y[0], Y[1], r_Y[1], 512, 0)
                inverse_and_gate(Y[2], r_Y[2], Y[3], r_Y[3], 512, 512)
                zc = [zt[:, 8:10, :], zt[:, 10:12, :]]
                spectrum(SP[0], r_SP[0], [(Cf, rCf, zc[0], r_zt, 2)])
                spectrum(SP[1], r_SP[1], [(Sf, rSf, zc[0], r_zt, 2)])
                spectrum(SP[2], r_SP[2], [(Cf, rCf, zc[1], r_zt, 2)])
                spectrum(SP[3], r_SP[3], [(Sf, rSf, zc[1], r_zt, 2)])
                spectrum(SP[4], r_SP[4], [(Cf, rCf, fv["Ac"], r_fv, 2)])
                spectrum(SP[5], r_SP[5], [(Sf, rSf, fv["Bc"], r_fv, 2)])
                cprod(Y[0], r_Y[0], [T4(0, 4, 1), T4(1, 5, 1)])
                cprod(Y[1], r_Y[1], [T4(0, 5, 1), T4(1, 4, -1)])
                cprod(Y[2], r_Y[2], [T4(2, 4, 1), T4(3, 5, 1)])
                cprod(Y[3], r_Y[3], [T4(2, 5, 1), T4(3, 4, -1)])
                inverse_and_gate(Y[0], r_Y[0], Y[1], r_Y[1], 256, LLAT)
                inverse_and_gate(Y[2], r_Y[2], Y[3], r_Y[3], 256, LLAT + LCTX)
            kb.op("act", lambda: A.copy(out=zfin[:, cb, :], in_=u[2]), reads=[r_u[2]], writes=[r_zfin[cb]])
        self.inherit(allx, r_pp + r_u + [r_zt, r_hidb])
        self.out_proj(l, d["hy_w_out"][j], zfin, r_zfin, b_out, r_par)
        self.inherit([self.r_big], r_SP + r_Y + r_zfin + [r_hid, r_delta, r_fv])

    def final_out(self):
        nc, kb = self.nc, self.kb
        V, A, P = nc.vector, nc.scalar, nc.tensor
        d = self.dr
        self.out_res = []
        gidx = 2 * DEPTH
        for b in range(NTB):
            ps, rp = self.next_ps()
            for c in range(NCH):
                sq, rsq = self.next_tmpb()
                kb.op("act", lambda: A.activation(out=sq[:], in_=self.xT[:, c, b * 512:(b + 1) * 512], func=AF.Square),
                      reads=[self.r_x[c][b]], writes=[rsq])
                self.mm(ps[:], self.onesb[:], sq[:], start=(c == 0), stop=(c == NCH - 1), reads=[rsq, self.r_const], writes=[rp], signal=True)
            rstd, rr = self.next_rstd()
            kb.op("dve", lambda: V.tensor_scalar(out=rstd[:], in0=ps[:], scalar1=1.0 / D, scalar2=EPS, op0=ALU.mult, op1=ALU.add),
                  reads=[rp], writes=[rr])
            kb.op("act", lambda: A.activation(out=rstd[:], in_=rstd[:], func=AF.Sqrt), reads=[rr], writes=[rr])
            kb.op("dve", lambda: V.reciprocal(out=rstd[:], in_=rstd[:]), reads=[rr], writes=[rr])
            for c in range(NCH):
                xs = self.xT[:, c, b * 512:(b + 1) * 512]
                kb.op("dve", lambda: V.scalar_tensor_tensor(out=xs, in0=xs, scalar=self.gvec[:, gidx, c:c + 1],
                                                            in1=rstd[:], op0=ALU.mult, op1=ALU.mult),
                      reads=[self.r_x[c][b], rr, self.r_gvec], writes=[self.r_x[c][b]])
        for tt in range(T // 128):
            b = tt // 4
            lat = tt < 8
            dst = d["y_lat"][tt * 128:(tt + 1) * 128, :] if lat else d["y_ctx"][(tt - 8) * 128:(tt - 7) * 128, :]
            for half in range(2):
                ps, rp = self.next_ps()
                for j in range(4):
                    c = half * 4 + j
                    kb.op("pe", lambda: P.transpose(ps[:, j * 128:(j + 1) * 128], self.xT[:, c, tt * 128:(tt + 1) * 128], self.ident[:]),
                          reads=[self.r_x[c][b], self.r_const], writes=[rp], signal=(j == 3))
                t0, r0 = self.next_tmp()
                if half == 0:
                    kb.op("act", lambda: A.copy(out=t0[:], in_=ps[:]), reads=[rp], writes=[r0])
                else:
                    kb.op("dve", lambda: V.tensor_copy(out=t0[:], in_=ps[:]), reads=[rp], writes=[r0])
                ro = Res("out")
                kb.dma("sp", dst[:, half * 512:(half + 1) * 512], t0[:], reads=[r0], writes=[ro])
                self.out_res.append(ro)


_CACHE = {}
LAST_DBG = None


def get_prog(skip=()):
    key = tuple(sorted(skip))
    if key not in _CACHE:
        _CACHE[key] = Prog(skip)
    return _CACHE[key]


def make_in_maps(inp, n=8):
    hc = host_consts()
    f = lambda a: np.ascontiguousarray(np.asarray(a, dtype=np.float32))
    shared = {
        "pos": hc["pos"], "ident": hc["ident"],
        "ada_w": f(inp["ada_w"]),
        "ffn_w_gu": f(inp["ffn_w_gu"]), "ffn_w_down": f(inp["ffn_w_down"]),
    }
    shared["gmasks"] = hc["gmasks"]
    for kk in ("gdn_w_in", "gdn_w_out"):
        shared[kk] = f(inp[kk])
    gv = []
    for j in range(2):
        gv.append(np.concatenate([f(inp["gdn_conv"][j]).reshape(5 * 24, 128).T, f(inp["gdn_onorm"][j]).reshape(128, 1)], axis=1))
    shared["gdn_vecT"] = np.ascontiguousarray(np.stack(gv, axis=0))
    shared["gdn_rows"] = np.ascontiguousarray(np.stack([f(inp["gdn_a_log"]).reshape(2, 16), f(inp["gdn_dt_bias"]).reshape(2, 16)], axis=1))
    for kk in ("hytab", "zemb_lat", "zemb_ctx", "hydelta", "hytneg", "mask0"):
        shared[kk] = hc[kk]
    for kk in ("hy_w_in", "hy_w_out", "hy_f_w1", "hy_f_w2", "hy_f_w3"):
        shared[kk] = f(inp[kk])
    shared["hy_vec64"] = np.ascontiguousarray(np.stack([f(inp["hy_f_b1"]), f(inp["hy_f_b2"]), f(inp["hy_freq"])], axis=2))
    hv = []
    for j in range(2):
        parts = [f(inp["hy_b_in"][j]).reshape(24, 128).T]
        parts.append(f(inp["hy_conv"][j]).reshape(3 * 24, 128).T)
        parts.append(f(inp["hy_skip"][j]).reshape(2 * 8, 128).T)
        parts.append(f(inp["hy_b_out"][j]).reshape(8, 128).T)
        hv.append(np.concatenate(parts, axis=1))
    shared["hy_vecT"] = np.ascontiguousarray(np.stack(hv, axis=0))
    fm = lambda v: np.ascontiguousarray(f(v).reshape(-1, 128).T)
    gl = []
    for l in range(DEPTH):
        gl.append(fm(inp["norm1_g"][l]))
        gl.append(fm(inp["norm2_g"][l]))
    gl.append(fm(inp["final_g"]))
    shared["gvecT"] = np.ascontiguousarray(np.stack(gl, axis=1))
    shared["adabT"] = np.ascontiguousarray(np.stack([fm(inp["ada_b"][l]) for l in range(DEPTH)], axis=1))
    maps = []
    xp = f(inp["x_prompt"])
    xs = f(inp["x_sample"])
    cc = f(inp["c"])
    cctx = f(inp["c_ctx"])
    for i in range(n):
        m = dict(shared)
        m["x_lat"] = np.ascontiguousarray(xs[i])
        m["x_ctx"] = np.ascontiguousarray(xp[2 * i:2 * i + 2].reshape(2 * LCTX, D))
        m["state_in"] = np.ascontiguousarray(f(inp["state_delta"])[i])
        m["cvecT"] = np.ascontiguousarray(np.stack([fm(cc[i]), fm(cctx)], axis=2))
        maps.append(m)
    return maps


def run(inp, skip=(), trace=False):
    prog = get_prog(skip)
    maps = make_in_maps(inp)
    res = run_bass_kernel_spmd(prog.nc, maps, core_ids=list(range(8)), trace=trace)
    y_lat = np.stack([r["y_lat"] for r in res.results], axis=0)
    y_ctx = np.concatenate([r["y_ctx"].reshape(2, LCTX, D) for r in res.results], axis=0)
    ns = np.concatenate([r["ns_out"] for r in res.results], axis=0)
    global LAST_DBG
    LAST_DBG = [r.get("dbg") for r in res.results]
    return (y_ctx.astype(np.float32), y_lat.astype(np.float32), ns.astype(np.float32)), res


def kernel(**inputs):
    (y_prompt, y_sample, new_state), _ = run(inputs)
    return (y_prompt, y_sample, new_state)
```

```python
import math
import os
import contextlib
import numpy as np
import concourse.bass as bass
import concourse.mybir as mybir
from concourse.bass_utils import run_bass_kernel_spmd

F32 = mybir.dt.float32
BF16 = mybir.dt.bfloat16
ALU = mybir.AluOpType
AF = mybir.ActivationFunctionType

D = 1024
NCH = 8
LLAT = 1024
LCTX = 256
T = LLAT + 2 * LCTX
NTB = 3
DFF = 2816
NFB = 22
DEPTH = 4
EPS = 1e-6
SAME_ENGINE_SYNC = True


class Res:
    __slots__ = ("w", "rs", "name", "excl")

    def __init__(self, name="", excl=False):
        self.w = None
        self.rs = []
        self.name = name
        self.excl = excl


class Eng:
    def __init__(self, name, h, sem):
        self.name = name
        self.h = h
        self.sem = sem
        self.count = 0
        self.waited = {}


class KB:
    def __init__(self, nc):
        self.nc = nc
        self.es = contextlib.ExitStack()
        self.engs = {}
        for name, h in (("pe", nc.tensor), ("act", nc.scalar), ("dve", nc.vector),
                        ("pool", nc.gpsimd), ("sp", nc.sync)):
            sem = self.es.enter_context(nc.semaphore("sem_" + name))
            self.engs[name] = Eng(name, h, sem)
        self.dsem = {}
        for q in ("sp", "pool"):
            sems = [self.es.enter_context(nc.semaphore(f"dsem_{q}{i}")) for i in range(20)]
            self.dsem[q] = {"sems": sems, "tot": [0] * len(sems), "i": 0}
        self.semkey = {}
        self.ninst = 0

    def sb(self, name, shape, dt):
        return self.es.enter_context(self.nc.sbuf_tensor(name, list(shape), dt))

    def ps(self, name, shape, dt):
        return self.es.enter_context(self.nc.psum_tensor(name, list(shape), dt))

    def _waits(self, eng, reads, writes):
        need = {}
        def add(ev):
            if ev is None:
                return
            sem, val, src = ev
            if src == eng.name:
                if eng.name == "pe" or not SAME_ENGINE_SYNC:
                    return
            k = id(sem)
            if k not in need or need[k][1] < val:
                need[k] = (sem, val, src)
        for r in reads:
            add(r.w)
        for w in writes:
            add(w.w)
            for ev in w.rs:
                add(ev)
        for k, (sem, val, src) in need.items():
            if eng.waited.get(k, 0) >= val:
                continue
            if src in self.engs:
                assert self.engs[src].count >= val, f"pending unsignaled dep {src} {val} > {self.engs[src].count}"
            eng.h.wait_ge(sem, val)
            eng.waited[k] = val
            self.ninst += 1

    def op(self, en, fn, reads=(), writes=(), signal=True):
        eng = self.engs[en]
        xr = [r for r in reads if r.excl]
        if xr:
            writes = list(writes) + [r for r in xr if r not in writes]
        self._waits(eng, reads, writes)
        ins = fn()
        self.ninst += 1
        if signal:
            eng.count += 1
            ins.then_inc(eng.sem, 1)
            ev = (eng.sem, eng.count, en)
        else:
            ev = (eng.sem, eng.count + 1, en)
        for r in reads:
            r.rs.append(ev)
        for w in writes:
            w.w = ev
            w.rs = []
        return ev

    def dma(self, q, out, in_, reads=(), writes=()):
        eng = self.engs[q]
        self._waits(eng, reads, writes)
        d = self.dsem[q]
        i = d["i"] % len(d["sems"])
        d["i"] += 1
        sem = d["sems"][i]
        k = id(sem)
        if d["tot"][i] > 0 and eng.waited.get(k, 0) < d["tot"][i]:
            eng.h.wait_ge(sem, d["tot"][i])
            eng.waited[k] = d["tot"][i]
        eng.h.dma_start(out=out, in_=in_).then_inc(sem, 16)
        self.ninst += 1
        d["tot"][i] += 16
        ev = (sem, d["tot"][i], "dma_" + q)
        for r in reads:
            r.rs.append(ev)
        for w in writes:
            w.w = ev
            w.rs = []
        return ev

    def finish(self, res_list):
        eng = self.engs["sp"]
        self._waits(eng, res_list, res_list)
        self.es.close()


def host_consts():
    c = {}
    c["ident"] = np.eye(128, dtype=np.float32)
    GRID_W = 64
    rows = LLAT // GRID_W
    r, col = np.meshgrid(np.arange(rows), np.arange(GRID_W), indexing="ij")
    quarter = D // 4
    omega = (1.0 / (10000.0 ** (np.arange(quarter, dtype=np.float32) / quarter))).astype(np.float32)

    def emb1d(p):
        a = p.reshape(-1, 1).astype(np.float32) * omega[None, :]
        return np.concatenate([np.sin(a), np.cos(a)], axis=-1)

    c["pos"] = np.concatenate([emb1d(r), emb1d(col)], axis=-1).astype(np.float32)
    B = 512
    k = np.arange(B, dtype=np.float64)
    om = 2.0 * np.pi * (k + 0.5) / (2 * B)
    jj = np.arange(B, dtype=np.float64)
    Cf = np.cos(jj[:, None] * om[None, :])
    Sf = np.sin(jj[:, None] * om[None, :])
    Cr = np.cos((B - jj)[:, None] * om[None, :]); Cr[0, :] = 0.0
    Sr = np.sin((B - jj)[:, None] * om[None, :]); Sr[0, :] = 0.0
    Ci = (2.0 / (2 * B)) * np.cos(om[:, None] * jj[None, :])
    Si = -(2.0 / (2 * B)) * np.sin(om[:, None] * jj[None, :])
    c["hytab"] = np.stack([Cf, Sf, Cr, Sr, Ci, Si], axis=0).astype(np.float32)
    def zemb(seq):
        t = np.linspace(0.0, 1.0, seq, dtype=np.float32)[:, None]
        wpos = (2.0 * math.pi / seq) * np.arange(seq, dtype=np.float32)[:, None]
        fb = np.linspace(1e-4, 15, 16, dtype=np.float32)[None, :]
        z = np.concatenate([t, np.cos(fb * wpos), -np.sin(fb * wpos)], axis=-1).astype(np.float32)
        return np.ascontiguousarray(z.T)
    c["zemb_lat"] = zemb(LLAT)
    c["zemb_ctx"] = zemb(LCTX)
    mx = math.log(1e-2) / 0.3
    mn = math.log(1e-2) / 1.5
    c["hydelta"] = np.abs(np.linspace(mn, mx, D, dtype=np.float32)).reshape(1, D).astype(np.float32)
    tn = np.zeros((128, 10), np.float32)
    for q in range(8):
        tn[:, q] = -(q * 128 + np.arange(128)) / np.float32(LLAT - 1)
    for q in range(2):
        tn[:, 8 + q] = -(q * 128 + np.arange(128)) / np.float32(LCTX - 1)
    tn = -np.stack([np.linspace(0.0, 1.0, LLAT, dtype=np.float32).reshape(8, 128).T] , 0)[0]
    tc = -np.linspace(0.0, 1.0, LCTX, dtype=np.float32).reshape(2, 128).T
    c["hytneg"] = np.ascontiguousarray(np.concatenate([tn, tc], axis=1).astype(np.float32))
    pi_, fi_ = np.meshgrid(np.arange(128), np.arange(128), indexing="ij")
    sb_ = (pi_ // 64) == (fi_ // 64)
    bd16 = (pi_ // 16) == (fi_ // 16)
    s1 = ((pi_ // 32) == (fi_ // 32)) & ~bd16
    s2 = ((pi_ // 64) == (fi_ // 64)) & ((pi_ // 32) != (fi_ // 32))
    c["gmasks"] = np.stack([(pi_ > fi_) & sb_, (fi_ >= pi_) & sb_, (pi_ < fi_) & sb_, (fi_ <= pi_) & sb_,
                            (pi_ <= fi_) & sb_, (pi_ >= fi_) & sb_, bd16, s1, s2], axis=0).astype(np.float32)
    m0 = np.ones((128, 1), np.float32); m0[0, 0] = 0.0
    c["mask0"] = m0
    return c


class Prog:
    def __init__(self, skip=()):
        self.skip = set(skip)
        nc = bass.Bass("TRN2", target_bir_lowering=False)
        self.nc = nc
        self.kb = KB(nc)
        self.build()

    def din(self, name, shape, dt=F32):
        return self.nc.dram_tensor(name, list(shape), dt, kind="ExternalInput").ap()

    def dout(self, name, shape, dt=F32):
        return self.nc.dram_tensor(name, list(shape), dt, kind="ExternalOutput").ap()

    def build(self):
        nc, kb = self.nc, self.kb
        V, A, P, G = nc.vector, nc.scalar, nc.tensor, nc.gpsimd
        x_lat = self.din("x_lat", [LLAT, D])
        x_ctx = self.din("x_ctx", [2 * LCTX, D])
        cvecT = self.din("cvecT", [128, NCH, 2])
        gvecT = self.din("gvecT", [128, 2 * DEPTH + 1, NCH])
        adabT = self.din("adabT", [128, DEPTH, 48])
        pos = self.din("pos", [LLAT, D])
        identd = self.din("ident", [128, 128])
        ada_w = self.din("ada_w", [DEPTH, D, 6 * D])
        ffn_w_gu = self.din("ffn_w_gu", [DEPTH, D, 2 * DFF])
        ffn_w_down = self.din("ffn_w_down", [DEPTH, DFF, D])
        hytab = self.din("hytab", [6, 512, 512])
        zemb_lat = self.din("zemb_lat", [33, LLAT])
        zemb_ctx = self.din("zemb_ctx", [33, LCTX])
        hydelta = self.din("hydelta", [1, D])
        hytneg = self.din("hytneg", [128, 10])
        mask0 = self.din("mask0", [128, 1])
        hy_w_in = self.din("hy_w_in", [2, D, 3 * D])
        hy_w_out = self.din("hy_w_out", [2, D, D])
        hy_f_w1 = self.din("hy_f_w1", [2, 33, 64])
        hy_f_w2 = self.din("hy_f_w2", [2, 64, 64])
        hy_f_w3 = self.din("hy_f_w3", [2, 64, 4 * D])
        hy_vec64 = self.din("hy_vec64", [2, 64, 3])
        hy_vecT = self.din("hy_vecT", [2, 128, 24 + 72 + 16 + 8])
        gmasks = self.din("gmasks", [9, 128, 128])
        gdn_w_in = self.din("gdn_w_in", [2, D, 4128])
        gdn_w_out = self.din("gdn_w_out", [2, D, D])
        gdn_vecT = self.din("gdn_vecT", [2, 128, 5 * 24 + 1])
        gdn_rows = self.din("gdn_rows", [2, 2, 16])
        state_in = self.din("state_in", [2, 2, 8, 128, 128])
        ns_out = self.dout("ns_out", [2, 2, 2, 8, 128, 128])
        import os
        self.debug = bool(os.environ.get('GDN_DEBUG'))
        dbg = self.dout("dbg", [8, 128, T]) if self.debug else None
        xsp = self.nc.dram_tensor("xsp", [128, NCH * T], F32, kind="Internal").ap()
        y_lat = self.dout("y_lat", [LLAT, D])
        y_ctx = self.dout("y_ctx", [2 * LCTX, D])
        self.dr = dict(locals())

        xT = kb.sb("xT", [128, NCH, T], F32)
        hT = kb.sb("hT", [128, NCH, T], BF16)
        self.xT, self.hT = xT, hT
        self.r_x = [[Res(f"x{c}_{b}") for b in range(NTB)] for c in range(NCH)]
        self.r_h = [[Res(f"h{c}_{b}") for b in range(NTB)] for c in range(NCH)]
        big = kb.sb("big", [128, 33792], BF16)
        self.big = big
        self.r_big = Res("big")
        ident = kb.sb("identf", [128, 128], F32)
        identb = kb.sb("identb", [128, 128], BF16)
        onesb = kb.sb("onesb", [128, 128], BF16)
        self.ident, self.identb, self.onesb = ident, identb, onesb
        r_const = Res("const")
        self.r_const = r_const
        kb.dma("sp", ident[:], identd[:, :], writes=[r_const])
        kb.op("dve", lambda: V.tensor_copy(out=identb[:], in_=ident[:]), reads=[r_const], writes=[r_const])
        kb.op("dve", lambda: V.memset(onesb[:], 1.0), writes=[r_const])
        self.NSLOT = 5
        self.wslots = [kb.sb(f"wslot{i}", [128, 4096], BF16) for i in range(self.NSLOT)]
        self.r_wslot = [Res(f"wslot{i}") for i in range(self.NSLOT)]
        self.wi = 0
        self.NPS = 7
        self.psb = [kb.ps(f"psb{i}", [128, 512], F32) for i in range(8)]
        self.r_ps = [Res(f"ps{i}", excl=True) for i in range(8)]
        self.pi = 0
        self.NTMP = 4
        self.tmp = [kb.sb(f"tmp{i}", [128, 512], F32) for i in range(self.NTMP)]
        self.r_tmp = [Res(f"tmp{i}") for i in range(self.NTMP)]
        self.ti = 0
        self.rstdb = [kb.sb(f"rstd{i}", [128, 512], F32) for i in range(2)]
        self.r_rstd = [Res(f"rstd{i}") for i in range(2)]
        self.ri = 0
        self.tmpb = [kb.sb(f"tmpb{i}", [128, 512], BF16) for i in range(self.NTMP)]
        self.r_tmpb = [Res(f"tmpb{i}") for i in range(self.NTMP)]
        self.tbi = 0
        self.mods = kb.sb("mods", [128, DEPTH, 48, 2], F32)
        self.r_mods = [Res(f"mods{l}") for l in range(DEPTH)]
        self.csil = kb.sb("csil", [128, NCH, 2], F32)
        self.csilb = kb.sb("csilb", [128, NCH, 2], BF16)
        self.r_csil = Res("csil")
        self.small = kb.sb("small", [128, 64, 2], F32)
        self.r_small = Res("small")
        self.gvec = kb.sb("gvec", [128, 2 * DEPTH + 1, NCH], F32)
        self.r_gvec = Res("gvec")
        self.adab = kb.sb("adab", [128, DEPTH, 48], F32)
        self.r_adab = Res("adab")

        self.r_nsout = Res("nsout")
        self.load_inputs()
        self.ada_pending = None
        for l in range(DEPTH):
            if l == 0:
                self.ada(l)
            else:
                self.ada_step(1000)
            if f"mix{l}" not in self.skip:
                if l % 2 == 0:
                    self.gdn_layer(l)
                else:
                    self.hyena_layer(l)
            if l + 1 < DEPTH:
                self.ada_pending = self.ada_gen(l + 1)
            if f"ffn{l}" not in self.skip:
                self.ffn_layer(l)
        self.final_out()
        kb.finish(self.out_res + [self.r_nsout])

    def next_ps(self):
        i = self.pi % self.NPS
        self.pi += 1
        return self.psb[i], self.r_ps[i]

    def next_tmp(self):
        i = self.ti % self.NTMP
        self.ti += 1
        return self.tmp[i], self.r_tmp[i]

    def next_rstd(self):
        i = self.ri % 2
        self.ri += 1
        return self.rstdb[i], self.r_rstd[i]

    def next_tmpb(self):
        i = self.tbi % self.NTMP
        self.tbi += 1
        return self.tmpb[i], self.r_tmpb[i]

    def next_wslot(self):
        i = self.wi % self.NSLOT
        self.wi += 1
        return self.wslots[i], self.r_wslot[i]

    def mm(self, out, lhsT, rhs, start, stop, reads, writes, signal=None):
        P = self.nc.tensor
        return self.kb.op("pe", lambda: P.matmul(out, lhsT, rhs, start=start, stop=stop),
                          reads=reads, writes=writes, signal=(stop if signal is None else signal))

    def load_inputs(self):
        nc, kb = self.nc, self.kb
        V, A, P = nc.vector, nc.scalar, nc.tensor
        d = self.dr
        kb.dma("sp", self.csil[:, :, :], d["cvecT"][:, :, :], writes=[self.r_csil])
        kb.op("act", lambda: A.activation(out=self.csil[:], in_=self.csil[:], func=AF.Silu),
              reads=[self.r_csil], writes=[self.r_csil])
        kb.op("dve", lambda: V.tensor_copy(out=self.csilb[:], in_=self.csil[:]), reads=[self.r_csil], writes=[self.r_csil])
        kb.dma("sp", self.gvec[:, :, :], d["gvecT"][:, :, :], writes=[self.r_gvec])
        kb.dma("sp", self.adab[:, :, :], d["adabT"][:, :, :], writes=[self.r_adab])
        for tt in range(T // 128):
            b = tt // 4
            lat = tt < 8
            src = d["x_lat"][tt * 128:(tt + 1) * 128, :] if lat else d["x_ctx"][(tt - 8) * 128:(tt - 7) * 128, :]
            for half in range(2):
                t0, r0 = self.next_tmp()
                kb.dma("sp", t0[:], src[:, half * 512:(half + 1) * 512], writes=[r0])
                if lat:
                    t1, r1 = self.next_tmp()
                    kb.dma("sp", t1[:], d["pos"][tt * 128:(tt + 1) * 128, half * 512:(half + 1) * 512], writes=[r1])
                    kb.op("dve", lambda: V.tensor_add(out=t0[:], in0=t0[:], in1=t1[:]), reads=[r0, r1], writes=[r0])
                ps, rp = self.next_ps()
                for j in range(4):
                    kb.op("pe", lambda: P.transpose(ps[:, j * 128:(j + 1) * 128], t0[:, j * 128:(j + 1) * 128], self.ident[:]),
                          reads=[r0, self.r_const], writes=[rp], signal=(j == 3))
                cs = half * 4
                kb.op("act", lambda: A.copy(out=self.xT[:, cs:cs + 4, tt * 128:(tt + 1) * 128],
                                            in_=ps[:].rearrange("p (c t) -> p c t", c=4)),
                      reads=[rp], writes=[self.r_x[c][b] for c in range(cs, cs + 4)])

    def ada_gen(self, l):
        nc, kb = self.nc, self.kb
        V, A, P = nc.vector, nc.scalar, nc.tensor
        d = self.dr
        W = 512
        ps, rp = self.psb[7], self.r_ps[7]
        for ti in range(6 * D // W):
            ws, rw = self.next_wslot()
            wf = ws[:, 0:NCH * W].rearrange("p (c n) -> p c n", c=NCH)
            kb.dma("pool", wf, d["ada_w"][l, :, ti * W:(ti + 1) * W].rearrange("(c p) n -> p c n", p=128), writes=[rw])
            for jj in range(W // 128):
                j = ti * (W // 128) + jj
                for c in range(NCH):
                    self.mm(ps[:, 2 * j:2 * j + 2], wf[:, c, jj * 128:(jj + 1) * 128], self.csilb[:, c, :],
                            start=(c == 0), stop=(c == NCH - 1), reads=[rw, self.r_csil], writes=[rp])
            yield
        kb.op("dve", lambda: V.tensor_tensor(out=self.mods[:, l, :, :], in0=ps[:, 0:96].rearrange("p (j w) -> p j w", w=2),
                                             in1=self.adab[:, l, :].unsqueeze(2).to_broadcast([128, 48, 2]), op=ALU.add),
              reads=[rp, self.r_adab], writes=[self.r_mods[l]])

    def ada(self, l):
        for _ in self.ada_gen(l):
            pass

    def ada_step(self, n):
        g = getattr(self, "ada_pending", None)
        if g is None:
            return
        for _ in range(n):
            try:
                next(g)
            except StopIteration:
                self.ada_pending = None
                return

    def mod(self, l, which, idx, c):
        return self.mods[:, l, idx * NCH + c, which:which + 1]

    def norm_mod(self, l, gidx, sh_idx, sc_idx):
        nc, kb = self.nc, self.kb
        V, A, P = nc.vector, nc.scalar, nc.tensor
        gs = self.small[:, 0:NCH, :]
        kb.op("dve", lambda: V.tensor_scalar(out=gs, in0=self.mods[:, l, sc_idx * NCH:(sc_idx + 1) * NCH, :], scalar1=1.0, scalar2=None, op0=ALU.add),
              reads=[self.r_mods[l]], writes=[self.r_small])
        kb.op("dve", lambda: V.tensor_tensor(out=gs, in0=gs, in1=self.gvec[:, gidx, :].unsqueeze(2).to_broadcast([128, NCH, 2]), op=ALU.mult),
              reads=[self.r_gvec, self.r_small], writes=[self.r_small])
        for b in range(NTB):
            w = 0 if b < 2 else 1
            ps, rp = self.next_ps()
            for c in range(NCH):
                sq, rsq = self.next_tmpb()
                kb.op("act", lambda: A.activation(out=sq[:], in_=self.xT[:, c, b * 512:(b + 1) * 512], func=AF.Square),
                      reads=[self.r_x[c][b]], writes=[rsq])
                self.mm(ps[:], self.onesb[:], sq[:], start=(c == 0), stop=(c == NCH - 1), reads=[rsq, self.r_const], writes=[rp], signal=True)
            rstd, rr = self.next_rstd()
            kb.op("dve", lambda: V.tensor_scalar(out=rstd[:], in0=ps[:], scalar1=1.0 / D, scalar2=EPS, op0=ALU.mult, op1=ALU.add),
                  reads=[rp], writes=[rr])
            kb.op("act", lambda: A.activation(out=rstd[:], in_=rstd[:], func=AF.Sqrt), reads=[rr], writes=[rr])
            kb.op("dve", lambda: V.reciprocal(out=rstd[:], in_=rstd[:]), reads=[rr], writes=[rr])
            for c in range(NCH):
                t0, r0 = self.next_tmp()
                kb.op("dve", lambda: V.scalar_tensor_tensor(out=t0[:], in0=self.xT[:, c, b * 512:(b + 1) * 512], scalar=gs[:, c, w:w + 1],
                                                            in1=rstd[:], op0=ALU.mult, op1=ALU.mult),
                      reads=[self.r_x[c][b], rr, self.r_small], writes=[r0])
                kb.op("act", lambda: A.activation(out=self.hT[:, c, b * 512:(b + 1) * 512], in_=t0[:], func=AF.Identity,
                                                  bias=self.mod(l, w, sh_idx, c), scale=1.0),
                      reads=[r0, self.r_mods[l]], writes=[self.r_h[c][b]])

    def ffn_layer(self, l):
        nc, kb = self.nc, self.kb
        V, A, P, G = nc.vector, nc.scalar, nc.tensor, nc.gpsimd
        d = self.dr
        self.norm_mod(l, 2 * l + 1, 3, 4)
        act = self.big[:, 0:NFB * T].rearrange("p (f t) -> p f t", f=NFB)
        r_act = [[Res(f"act{f}_{b}") for b in range(NTB)] for f in range(NFB)]
        for f in range(NFB):
            for b in range(NTB):
                r_act[f][b].w = self.r_big.w
                r_act[f][b].rs = list(self.r_big.rs)
        hreads = lambda b: [self.r_h[c][b] for c in range(NCH)]
        ntile = (DFF + 511) // 512
        for ti in range(ntile):
            c0 = ti * 512
            ncol = min(512, DFF - c0)
            wg, rg = self.next_wslot()
            wgv = wg[:, 0:NCH * ncol].rearrange("p (c n) -> p c n", c=NCH)
            kb.dma("pool", wgv, d["ffn_w_gu"][l, :, c0:c0 + ncol].rearrange("(c p) n -> p c n", p=128), writes=[rg])
            wu, ru = self.next_wslot()
            wuv = wu[:, 0:NCH * ncol].rearrange("p (c n) -> p c n", c=NCH)
            kb.dma("pool", wuv, d["ffn_w_gu"][l, :, DFF + c0:DFF + c0 + ncol].rearrange("(c p) n -> p c n", p=128), writes=[ru])
            self.ada_step(2)
            for jj in range(ncol // 128):
                f = ti * 4 + jj
                for b in range(NTB):
                    pg, rpg = self.next_ps()
                    for c in range(NCH):
                        self.mm(pg[:], wgv[:, c, jj * 128:(jj + 1) * 128], self.hT[:, c, b * 512:(b + 1) * 512],
                                start=(c == 0), stop=(c == NCH - 1), reads=[rg, self.r_h[c][b]], writes=[rpg])
                    pu, rpu = self.next_ps()
                    for c in range(NCH):
                        self.mm(pu[:], wuv[:, c, jj * 128:(jj + 1) * 128], self.hT[:, c, b * 512:(b + 1) * 512],
                                start=(c == 0), stop=(c == NCH - 1), reads=[ru, self.r_h[c][b]], writes=[rpu])
                    sg, rsg = self.next_tmp()
                    kb.op("act", lambda: A.activation(out=sg[:], in_=pg[:], func=AF.Silu), reads=[rpg], writes=[rsg])
                    kb.op("dve", lambda: V.tensor_tensor(out=act[:, f, b * 512:(b + 1) * 512], in0=sg[:], in1=pu[:], op=ALU.mult),
                          reads=[rsg, rpu], writes=[r_act[f][b]])
        for ti in range(D // 128):
            self.ada_step(1)
            ws, rw = self.next_wslot()
            wv = ws[:, 0:NFB * 128].rearrange("p (f n) -> p f n", f=NFB)
            kb.dma("pool", wv, d["ffn_w_down"][l, :, ti * 128:(ti + 1) * 128].rearrange("(f p) n -> p f n", p=128), writes=[rw])
            for jj in range(1):
                c = ti
                for b in range(NTB):
                    w = 0 if b < 2 else 1
                    ps, rp = self.next_ps()
                    for f in range(NFB):
                        self.mm(ps[:], wv[:, f, jj * 128:(jj + 1) * 128], act[:, f, b * 512:(b + 1) * 512],
                                start=(f == 0), stop=(f == NFB - 1), reads=[rw, r_act[f][b]], writes=[rp])
                    xs = self.xT[:, c, b * 512:(b + 1) * 512]
                    kb.op("dve", lambda: V.scalar_tensor_tensor(out=xs, in0=ps[:], scalar=self.mod(l, w, 5, c), in1=xs,
                                                                op0=ALU.mult, op1=ALU.add),
                          reads=[rp, self.r_x[c][b], self.r_mods[l]], writes=[self.r_x[c][b]])
        self.merge_res(self.r_big, [r for rr in r_act for r in rr])

    def merge_res(self, dst, srcs):
        evs = []
        for s in srcs:
            if s.w is not None:
                evs.append(s.w)
            evs.extend(s.rs)
        best = {}
        for ev in evs:
            k = id(ev[0])
            if k not in best or best[k][1] < ev[1]:
                best[k] = ev
        dst.w = None
        dst.rs = list(best.values())

    def gdn_layer(self, l):
        nc, kb = self.nc, self.kb
        V, A, P, G = nc.vector, nc.scalar, nc.tensor, nc.gpsimd
        d = self.dr
        j = l // 2
        DT = F32
        self.norm_mod(l, 2 * l, 0, 1)
        allx = [r for rr in self.r_x for r in rr]
        kb.dma("sp", d["xsp"][:, :], self.xT[:].rearrange("p c t -> p (c t)"), reads=allx)
        R1 = self.xT[:].rearrange("p c t -> p (c t)")
        pos1 = [0]

        def c1(n):
            a = R1[:, pos1[0]:pos1[0] + n]
            pos1[0] += n
            assert pos1[0] <= 12288
            return a
        R2 = self.big
        pos2 = [0]

        def c2(n):
            a = R2[:, pos2[0]:pos2[0] + n]
            pos2[0] += n
            assert pos2[0] <= 33792, pos2[0]
            return a
        PWG = T + 12
        pp = c1(PWG)
        cv = c1(T)
        oT = cv
        qT = c1(T // 2).bitcast(BF16)
        kT = c1(T // 2).bitcast(BF16)
        vT = c1(T // 2).bitcast(BF16)
        gateT = c1(T // 2).bitcast(BF16)
        sqb = pp[:, 2:2 + T // 2].bitcast(BF16)
        ktok = c1(T // 2).bitcast(BF16).rearrange("p (t c) -> p t c", t=12)
        vtok = c1(T // 2).bitcast(BF16).rearrange("p (t c) -> p t c", t=12)
        r_pp, r_cv, r_q, r_k, r_v, r_gate, r_ktok, r_vtok = (Res(n) for n in ("pp", "cv", "q", "k", "v", "gate", "ktok", "vtok"))
        r_sqb = r_pp
        r_oT = [Res(f"oT{t}") for t in range(12)]
        self.inherit([r_pp, r_cv, r_q, r_k, r_v, r_gate, r_sqb, r_ktok, r_vtok] + r_oT, allx)
        og = c2(NCH * T).rearrange("p (h t) -> p h t", h=NCH)
        r_og = [Res(f"og{h}") for h in range(NCH)]
        NCHAIN = 6
        slots = []
        extra_slots = []
        for ci in range(NCHAIN):
            ring = []
            for r_ in range(2):
                wu_ = c2(512).bitcast(F32)
                sl = dict(WU=wu_, WT=wu_[:, 0:128], U=wu_[:, 128:256], qdT=c2(256).bitcast(F32), aqkT=c2(256).bitcast(F32), ktail=c2(256).bitcast(F32),
                          gl=c2(4).bitcast(F32),
                          r=Res(f"slot{ci}_{r_}"))
                ring.append(sl)
            slots.append(ring)
        works = []
        for wi in range(2):
            wk = dict(r=Res(f"work{wi}"))
            PA, PR, PX, PX2 = c2(512).bitcast(F32), c2(512).bitcast(F32), c2(512).bitcast(F32), c2(512).bitcast(F32)
            wk["PA"], wk["PR"], wk["PX"], wk["PX2"] = PA, PR, PX, PX2
            wk["A0"], wk["AT0"] = PA[:, 0:128], PA[:, 128:256]
            wk["R"], wk["RT"] = PR[:, 0:128], PR[:, 128:256]
            wk["Dsym"], wk["egrow"] = PX[:, 0:128], PX[:, 128:256]
            wk["DMs"], wk["DMi"] = PX2[:, 0:128], PX2[:, 128:256]
            wk["kbe"], wk["vb"] = c2(256).bitcast(F32), c2(256).bitcast(F32)
            wk["dd"] = wk["Dsym"]
            wk["GR"] = c1(256)
            wk["cols"] = c2(16).bitcast(F32)
            works.append(wk)
        for wi in range(2):
            wk = dict(r=Res(f"workx{wi}"))
            base = self.wslots[wi]
            PA, PR, PX, PX2 = base[:, 0:512].bitcast(F32), base[:, 512:1024].bitcast(F32), base[:, 1024:1536].bitcast(F32), base[:, 1536:2048].bitcast(F32)
            wk["PA"], wk["PR"], wk["PX"], wk["PX2"] = PA, PR, PX, PX2
            wk["A0"], wk["AT0"] = PA[:, 0:128], PA[:, 128:256]
            wk["R"], wk["RT"] = PR[:, 0:128], PR[:, 128:256]
            wk["Dsym"], wk["egrow"] = PX[:, 0:128], PX[:, 128:256]
            wk["DMs"], wk["DMi"] = PX2[:, 0:128], PX2[:, 128:256]
            wk["kbe"], wk["vb"] = base[:, 2048:2304].bitcast(F32), base[:, 2304:2560].bitcast(F32)
            wk["dd"] = wk["Dsym"]
            wk["GR"] = base[:, 2560:3072].bitcast(F32)
            wk["cols"] = base[:, 3072:3088].bitcast(F32)
            self.inherit([wk["r"]], [self.r_wslot[wi]])
            works.append(wk)
        vnew = [self.wslots[4][:, 1792 + i * 256:1792 + (i + 1) * 256].bitcast(F32) for i in range(NCHAIN)]
        r_vnew = [Res(f"vnew{i}") for i in range(NCHAIN)]
        Sst = [c1(128) for _ in range(NCHAIN)]
        r_S = [Res(f"S{i}") for i in range(NCHAIN)]
        g_tok = c1(192).rearrange("p (t n) -> p t n", t=12)
        b_tok = c1(192).rearrange("p (t n) -> p t n", t=12)
        gccol = c1(192).rearrange("p (a t n) -> p a t n", a=2, t=12)
        rows = c1(32)
        gvec = c1(122)
        for ci in range(2):
            for r_ in range(2):
                wu_ = c1(256)
                sl = dict(WU=wu_, WT=wu_[:, 0:128], U=wu_[:, 128:256], qdT=c1(128), aqkT=c1(128), ktail=c1(128), gl=c1(2), r=Res(f"slotx{ci}_{r_}"))
                slots[ci].append(sl)
                extra_slots.append(sl["r"])
        self.inherit(extra_slots, allx)
        masks = [self.wslots[4][:, 256 + i * 256:256 + (i + 1) * 256].bitcast(F32) for i in range(6)]
        masks += [self.wslots[4][:, 3328 + i * 256:3328 + (i + 1) * 256].bitcast(F32) for i in range(3)]
        r_ab = Res("ab"); r_gpar = Res("gpar"); r_masks = Res("masks")
        allr2 = r_og + [sl["r"] for ring in slots for sl in ring]
        self.inherit(allr2, [self.r_big])
        r_works = [wk["r"] for wk in works]
        self.inherit(r_works, [self.r_big] + allx)
        self.inherit([r_ab, r_gpar] + r_S, allx)
        self.inherit([r_masks] + r_vnew, [self.r_wslot[4]])
        for ci in range(NCHAIN):
            kb.op("dve", lambda: V.memset(vnew[ci], 0.0), writes=[r_vnew[ci]])
        for i in range(9):
            kb.dma("sp", masks[i], d["gmasks"][i], writes=[r_masks])
        m_sl, m_ui, m_su, m_li, Lf, Lb, m_bd16, m_s1, m_s2 = masks
        mpair = [self.wslots[4][:, 256:768].bitcast(F32), self.wslots[4][:, 768:1280].bitcast(F32)]
        kb.dma("sp", gvec[:, 0:121], d["gdn_vecT"][j], writes=[r_gpar])
        kb.dma("sp", rows[:, 0:32], d["gdn_rows"][j:j + 1].rearrange("o a n -> o (a n)").partition_broadcast(128), writes=[r_gpar])
        kb.op("act", lambda: A.activation(out=rows[:, 0:16], in_=rows[:, 0:16], func=AF.Exp), reads=[r_gpar], writes=[r_gpar])
        kb.op("dve", lambda: V.tensor_scalar(out=rows[:, 0:16], in0=rows[:, 0:16], scalar1=-1.0, scalar2=None, op0=ALU.mult), reads=[r_gpar], writes=[r_gpar])
        cwg = lambda k, blk: gvec[:, k * 24 + blk:k * 24 + blk + 1]
        onorm = gvec[:, 120:121]
        wab_s, r_wab = self.wslots[4], self.r_wslot[4]
        wab = wab_s[:, 0:NCH * 32].rearrange("p (c n) -> p c n", c=NCH)
        kb.dma("pool", wab, d["gdn_w_in"][j, :, 4096:4128].rearrange("(c p) n -> p c n", p=128), writes=[r_wab])
        ps, rp = self.next_ps()
        for tt in range(12):
            for c in range(NCH):
                self.mm(ps[:, tt * 32:(tt + 1) * 32], self.hT[:, c, tt * 128:(tt + 1) * 128], wab[:, c, :], start=(c == 0), stop=(c == NCH - 1),
                        reads=[self.r_h[c][tt // 4], r_wab], writes=[rp], signal=(tt == 11 and c == NCH - 1))
        abt, rabt = self.next_tmp()
        ab3 = abt[:, 0:384].rearrange("p (t n) -> p t n", t=12)
        kb.op("dve", lambda: V.tensor_copy(out=abt[:, 0:384], in_=ps[:, 0:384]), reads=[rp], writes=[rabt])
        kb.op("act", lambda: A.activation(out=b_tok, in_=ab3[:, :, 16:32], func=AF.Sigmoid), reads=[rabt], writes=[r_ab])
        kb.op("dve", lambda: V.tensor_tensor(out=g_tok, in0=ab3[:, :, 0:16], in1=rows[:, 16:32].unsqueeze(1).to_broadcast([128, 12, 16]), op=ALU.add),
              reads=[rabt, r_gpar], writes=[r_ab])
        kb.op("act", lambda: A.activation(out=g_tok, in_=g_tok, func=AF.Exp), reads=[r_ab], writes=[r_ab])
        kb.op("act", lambda: A.activation(out=g_tok, in_=g_tok, func=AF.Ln, bias=1.0, scale=1.0), reads=[r_ab], writes=[r_ab])
        kb.op("dve", lambda: V.tensor_tensor(out=g_tok, in0=g_tok, in1=rows[:, 0:16].unsqueeze(1).to_broadcast([128, 12, 16]), op=ALU.mult),
              reads=[r_ab, r_gpar], writes=[r_ab])
        ps, rp = self.next_ps()
        for dr_ in range(2):
            self.mm(ps[:, dr_ * 96:(dr_ + 1) * 96].rearrange("p (t n) -> p t n", t=12), (Lf if dr_ == 0 else Lb), g_tok[:, :, dr_ * 8:(dr_ + 1) * 8], True, True,
                    reads=[r_masks, r_ab], writes=[rp], signal=(dr_ == 1))
        kb.op("dve", lambda: V.tensor_copy(out=gccol.rearrange("p a t n -> p (a t n)"), in_=ps[:, 0:192]), reads=[rp], writes=[r_ab])
        chains = [dict(tiles=list(range(8)), dir=0, seq=None), dict(tiles=list(range(7, -1, -1)), dir=1, seq=None),
                  dict(tiles=[8, 9], dir=0, seq=0), dict(tiles=[9, 8], dir=1, seq=0),
                  dict(tiles=[10, 11], dir=0, seq=1), dict(tiles=[11, 10], dir=1, seq=1)]
        evi = [0]

        def evac(out, in_, reads, writes):
            evi[0] += 1
            if evi[0] % 2 == 0:
                kb.op("act", lambda: A.copy(out=out, in_=in_), reads=reads, writes=writes)
            else:
                kb.op("dve", lambda: V.tensor_copy(out=out, in_=in_), reads=reads, writes=writes)
        wki = [0]
        segs = [(2, LLAT, 0), (1030, LCTX, LLAT), (1290, LCTX, LLAT + LCTX)]
        kb.op("dve", lambda: V.memset(pp, 0.0), writes=[r_pp])
        for h in range(NCH):
            wsl, rws = self.wslots[2 + h % 2], self.r_wslot[2 + h % 2]
            win = wsl[:, 0:4096].rearrange("p (c g n) -> p c g n", c=NCH, g=4)
            for g in range(4):
                kb.dma("pool", win[:, :, g, :], d["gdn_w_in"][j, :, g * D + h * 128:g * D + (h + 1) * 128].rearrange("(c p) n -> p c n", p=128), writes=[rws])
            self.inherit([r_cv], r_oT)
            for g in range(3):
                blk = g * 8 + h
                for b in range(NTB):
                    ps, rp = self.next_ps()
                    for c in range(NCH):
                        self.mm(ps[:], win[:, c, g, :], self.hT[:, c, b * 512:(b + 1) * 512], start=(c == 0), stop=(c == NCH - 1),
                                reads=[rws, self.r_h[c][b]], writes=[rp])
                    if b < 2:
                        evac(pp[:, 2 + b * 512:2 + (b + 1) * 512], ps[:], [rp], [r_pp])
                    else:
                        evac(pp[:, 1030:1286], ps[:, 0:256], [rp], [r_pp])
                        evac(pp[:, 1290:1546], ps[:, 256:512], [rp], [r_pp])
                for (s0, Ls, o0) in segs:
                    co = cv[:, o0:o0 + Ls]
                    kb.op("dve", lambda: V.tensor_scalar(out=co, in0=pp[:, s0 - 2:s0 - 2 + Ls], scalar1=cwg(0, blk), scalar2=None, op0=ALU.mult),
                          reads=[r_pp, r_gpar], writes=[r_cv])
                    for k in range(1, 5):
                        kb.op("dve", lambda: V.scalar_tensor_tensor(out=co, in0=pp[:, s0 - 2 + k:s0 - 2 + k + Ls], scalar=cwg(k, blk), in1=co,
                                                                    op0=ALU.mult, op1=ALU.add), reads=[r_pp, r_gpar, r_cv], writes=[r_cv])
                if g == 2:
                    kb.op("act", lambda: A.activation(out=vT, in_=cv, func=AF.Silu), reads=[r_cv], writes=[r_v])
                else:
                    dst, rdst = (qT, r_q) if g == 0 else (kT, r_k)
                    kb.op("act", lambda: A.activation(out=cv, in_=cv, func=AF.Silu), reads=[r_cv], writes=[r_cv])
                    kb.op("act", lambda: A.activation(out=sqb, in_=cv, func=AF.Square), reads=[r_cv], writes=[r_sqb])
                    for b in range(NTB):
                        ps, rp = self.next_ps()
                        self.mm(ps[:], self.onesb[:], sqb[:, b * 512:(b + 1) * 512], True, True, reads=[r_sqb, self.r_const], writes=[rp])
                        rs, rrs = self.next_rstd()
                        kb.op("dve", lambda: V.tensor_scalar(out=rs[:], in0=ps[:], scalar1=EPS, scalar2=None, op0=ALU.add), reads=[rp], writes=[rrs])
                        kb.op("act", lambda: A.activation(out=rs[:], in_=rs[:], func=AF.Sqrt), reads=[rrs], writes=[rrs])
                        kb.op("dve", lambda: V.reciprocal(out=rs[:], in_=rs[:]), reads=[rrs], writes=[rrs])
                        sc = (128.0 ** -0.5) if g == 0 else 1.0
                        kb.op("dve", lambda: V.scalar_tensor_tensor(out=dst[:, b * 512:(b + 1) * 512], in0=cv[:, b * 512:(b + 1) * 512], scalar=sc, in1=rs[:],
                                                                    op0=ALU.mult, op1=ALU.mult), reads=[r_cv, rrs], writes=[rdst])
            for b in range(NTB):
                ps, rp = self.next_ps()
                for c in range(NCH):
                    self.mm(ps[:], win[:, c, 3, :], self.hT[:, c, b * 512:(b + 1) * 512], start=(c == 0), stop=(c == NCH - 1),
                            reads=[rws, self.r_h[c][b]], writes=[rp])
                kb.op("act", lambda: A.activation(out=gateT[:, b * 512:(b + 1) * 512], in_=ps[:], func=AF.Silu), reads=[rp], writes=[r_gate])
            self.inherit(r_oT, [r_cv])
            for (src, rsrc, dst, rdst) in ((kT, r_k, ktok, r_ktok), (vT, r_v, vtok, r_vtok)):
                for g4 in range(3):
                    ps, rp = self.next_ps()
                    psb = ps[:].bitcast(BF16)
                    for i in range(4):
                        tt = g4 * 4 + i
                        kb.op("pe", lambda: P.transpose(psb[:, i * 128:(i + 1) * 128], src[:, tt * 128:(tt + 1) * 128], self.identb[:]),
                              reads=[rsrc, self.r_const], writes=[rp], signal=(i == 3))
                    evac(dst[:, g4 * 4:(g4 + 1) * 4, :], psb[:, 0:512].rearrange("p (t c) -> p t c", t=4), [rp], [rdst])
            for ci, ch in enumerate(chains):
                if ch["seq"] is None:
                    kb.dma("sp", Sst[ci], d["state_in"][j, ch["dir"], h], writes=[r_S[ci]])
                else:
                    kb.op("dve", lambda: V.memset(Sst[ci], 0.0), writes=[r_S[ci]])

            def intra(ci, step):
                ch = chains[ci]
                tt = ch["tiles"][step]
                dr_ = ch["dir"]
                sl = slots[ci][step % len(slots[ci])]
                rsl = sl["r"]
                wk = works[wki[0] % len(works)]
                wki[0] += 1
                rw = wk["r"]
                tcs = slice(tt * 128, (tt + 1) * 128)
                gcc = gccol[:, dr_, tt, h:h + 1]
                bc = b_tok[:, tt, dr_ * 8 + h:dr_ * 8 + h + 1]
                lastA, lastB = (63, 127) if dr_ == 0 else (0, 64)
                m_s = m_sl if dr_ == 0 else m_su
                m_i = m_ui if dr_ == 0 else m_li
                cols = wk["cols"]
                yield
                ps, rp = self.next_ps()
                self.mm(ps[:, 0:128], kT[:, tcs], kT[:, tcs], True, True, reads=[r_k], writes=[rp], signal=False)
                self.mm(ps[:, 128:256], kT[:, tcs], qT[:, tcs], True, True, reads=[r_k, r_q], writes=[rp])
                evac(wk["GR"], ps[:, 0:256], [rp], [rw])
                yield
                ps2, rp2 = self.next_ps()
                kb.op("pe", lambda: P.transpose(ps2[:, 0:128], gcc.to_broadcast([128, 128]), self.ident[:]), reads=[r_ab, self.r_const], writes=[rp2])
                kb.op("dve", lambda: V.tensor_scalar(out=wk["dd"], in0=ps2[:, 0:128], scalar1=gcc, scalar2=None, op0=ALU.subtract), reads=[rp2, r_ab], writes=[rw])
                kb.op("dve", lambda: V.scalar_tensor_tensor(out=wk["dd"], in0=wk["dd"], scalar=-1.0, in1=wk["dd"], op0=ALU.mult, op1=ALU.max), reads=[rw], writes=[rw])
                kb.op("act", lambda: A.activation(out=wk["Dsym"], in_=wk["dd"], func=AF.Exp, scale=-1.0), reads=[rw], writes=[rw])
                kb.op("act", lambda: A.activation(out=wk["egrow"], in_=ps2[:, 0:128], func=AF.Exp), reads=[rp2], writes=[rw])
                kb.op("dve", lambda: V.tensor_copy(out=cols[0:64, 3:4], in_=ps2[0:64, lastA:lastA + 1]), reads=[rp2], writes=[rw])
                kb.op("dve", lambda: V.tensor_copy(out=cols[64:128, 3:4], in_=ps2[64:128, lastB:lastB + 1]), reads=[rp2], writes=[rw])
                kb.op("act", lambda: A.activation(out=cols[:, 0:1], in_=gcc, func=AF.Exp), reads=[r_ab], writes=[rw])
                kb.op("dve", lambda: V.tensor_tensor(out=cols[:, 1:2], in0=cols[:, 0:1], in1=bc, op=ALU.mult), reads=[rw, r_ab], writes=[rw])
                kb.op("act", lambda: A.activation(out=cols[:, 2:3], in_=gcc, func=AF.Exp, bias=cols[:, 3:4], scale=-1.0), reads=[rw, r_ab], writes=[rw])
                kb.op("act", lambda: A.copy(out=sl["gl"][:, 0:1], in_=wk["egrow"][:, lastA:lastA + 1]), reads=[rw], writes=[rsl])
                kb.op("act", lambda: A.copy(out=sl["gl"][:, 1:2], in_=wk["egrow"][:, lastB:lastB + 1]), reads=[rw], writes=[rsl])
                mp = mpair[dr_]
                id2 = self.ident[:].unsqueeze(1).to_broadcast([128, 2, 128])
                v3 = lambda ap: ap.rearrange("p (a c) -> p a c", a=2)
                kb.op("pool", lambda: G.tensor_tensor(out=v3(wk["PX2"]), in0=wk["Dsym"].unsqueeze(1).to_broadcast([128, 2, 128]), in1=v3(mp), op=ALU.mult),
                      reads=[rw, r_masks], writes=[rw])
                yield
                kb.op("dve", lambda: V.scalar_tensor_tensor(out=wk["A0"], in0=wk["GR"][:, 0:128], scalar=bc, in1=wk["DMs"], op0=ALU.mult, op1=ALU.mult),
                      reads=[rw, r_ab], writes=[rw])
                yield
                kb.op("dve", lambda: V.tensor_tensor(out=sl["aqkT"], in0=wk["GR"][:, 128:256], in1=wk["DMi"], op=ALU.mult), reads=[rw], writes=[rsl])
                yield
                kb.op("dve", lambda: V.tensor_tensor(out=sl["qdT"], in0=qT[:, tcs], in1=wk["egrow"], op=ALU.mult), reads=[rw, r_q], writes=[rsl])
                yield
                kb.op("act", lambda: A.activation(out=wk["kbe"], in_=ktok[:, tt, :], func=AF.Copy, scale=cols[:, 1:2]), reads=[rw, r_ktok], writes=[rw])
                yield
                kb.op("act", lambda: A.activation(out=wk["vb"], in_=vtok[:, tt, :], func=AF.Copy, scale=bc), reads=[rw, r_vtok, r_ab], writes=[rw])
                yield
                kb.op("act", lambda: A.activation(out=sl["ktail"], in_=ktok[:, tt, :], func=AF.Copy, scale=cols[:, 2:3]), reads=[rw, r_ktok], writes=[rsl])
                yield
                ps3, rp3 = self.next_ps()
                kb.op("pe", lambda: P.transpose(ps3[:, 0:128], wk["A0"], self.ident[:]), reads=[rw, self.r_const], writes=[rp3])
                kb.op("act", lambda: A.copy(out=wk["AT0"], in_=ps3[:, 0:128]), reads=[rp3], writes=[rw])
                PX, PX2, PR, PA = wk["PX"], wk["PX2"], wk["PR"], wk["PA"]
                bd2 = m_bd16.unsqueeze(1).to_broadcast([128, 2, 128])
                kb.op("pool", lambda: G.tensor_tensor(out=v3(PX), in0=v3(PA), in1=bd2, op=ALU.mult), reads=[rw, r_masks], writes=[rw])
                yield
                kb.op("dve", lambda: V.tensor_tensor(out=v3(PR), in0=id2, in1=v3(PX), op=ALU.subtract), reads=[rw, self.r_const], writes=[rw])
                for it in range(3):
                    X, XT = PX[:, 0:128], PX[:, 128:256]
                    yield
                    psA, rpA = self.next_ps()
                    self.mm(psA[:, 0:128], XT, X, True, True, reads=[rw], writes=[rpA], signal=False)
                    self.mm(psA[:, 128:256], X, XT, True, True, reads=[rw], writes=[rpA])
                    kb.op("act", lambda: A.copy(out=PX2, in_=psA[:, 0:256]), reads=[rpA], writes=[rw])
                    X2, XT2 = PX2[:, 0:128], PX2[:, 128:256]
                    yield
                    psB, rpB = self.next_ps()
                    self.mm(psB[:, 0:128], XT2, PR[:, 0:128], True, True, reads=[rw], writes=[rpB], signal=False)
                    self.mm(psB[:, 128:256], X2, PR[:, 128:256], True, True, reads=[rw], writes=[rpB])
                    kb.op("dve", lambda: V.tensor_tensor(out=PR, in0=PR, in1=psB[:, 0:256], op=ALU.add), reads=[rpB, rw], writes=[rw])
                    PX, PX2 = PX2, PX
                for msk in (m_s1, m_s2):
                    ms2 = msk.unsqueeze(1).to_broadcast([128, 2, 128])
                    kb.op("pool", lambda: G.tensor_tensor(out=v3(PX), in0=v3(PA), in1=ms2, op=ALU.mult), reads=[rw, r_masks], writes=[rw])
                    X, XT = PX[:, 0:128], PX[:, 128:256]
                    yield
                    ps1, rp1 = self.next_ps()
                    self.mm(ps1[:, 0:128], XT, PR[:, 0:128], True, True, reads=[rw], writes=[rp1], signal=False)
                    self.mm(ps1[:, 128:256], X, PR[:, 128:256], True, True, reads=[rw], writes=[rp1])
                    kb.op("act", lambda: A.copy(out=PX2, in_=ps1[:, 0:256]), reads=[rp1], writes=[rw])
                    yield
                    ps2_, rp2_ = self.next_ps()
                    self.mm(ps2_[:, 0:128], PR[:, 128:256], PX2[:, 0:128], True, True, reads=[rw], writes=[rp2_], signal=False)
                    self.mm(ps2_[:, 128:256], PR[:, 0:128], PX2[:, 128:256], True, True, reads=[rw], writes=[rp2_])
                    kb.op("dve", lambda: V.tensor_tensor(out=PR, in0=PR, in1=ps2_[:, 0:256], op=ALU.subtract), reads=[rp2_, rw], writes=[rw])
                yield
                psW, rpW = self.next_ps()
                self.mm(psW[:, 0:128], wk["kbe"], wk["RT"], True, True, reads=[rw], writes=[rpW], signal=False)
                self.mm(psW[:, 128:256], wk["RT"], wk["vb"], True, True, reads=[rw], writes=[rpW])
                kb.op("act", lambda: A.copy(out=sl["WU"], in_=psW[:, 0:256]), reads=[rpW], writes=[rsl])

            def scan(ci, step):
                ch = chains[ci]
                tt = ch["tiles"][step]
                sl = slots[ci][step % len(slots[ci])]
                rsl = sl["r"]
                tcs = slice(tt * 128, (tt + 1) * 128)
                for half in ((0, 1) if ch["dir"] == 0 else (1, 0)):
                    hs = slice(half * 64, (half + 1) * 64)
                    hcs = slice(tt * 128 + half * 64, tt * 128 + (half + 1) * 64)
                    yield
                    psP, rpP = self.next_ps()
                    self.mm(psP[:, 0:128], sl["WT"], Sst[ci], True, True, reads=[rsl, r_S[ci]], writes=[rpP])
                    kb.op("dve", lambda: V.tensor_tensor(out=vnew[ci][hs, :], in0=sl["U"][hs, :], in1=psP[hs, 0:128], op=ALU.subtract),
                          reads=[rsl, rpP], writes=[r_vnew[ci]])
                    yield
                    psO, rpO = self.next_ps()
                    self.mm(psO[:, 0:64], Sst[ci], sl["qdT"][:, hs], True, False, reads=[rsl, r_S[ci]], writes=[rpO])
                    self.mm(psO[:, 0:64], vnew[ci][hs, :], sl["aqkT"][hs, hs], False, True, reads=[rsl, r_vnew[ci]], writes=[rpO])
                    psS, rpS = self.next_ps()
                    self.mm(psS[:, 0:128], sl["ktail"][hs, :], vnew[ci][hs, :], True, True, reads=[rsl, r_vnew[ci]], writes=[rpS])
                    okey = (tt, half)
                    if okey not in owritten:
                        owritten.add(okey)
                        kb.op("act", lambda: A.copy(out=oT[:, hcs], in_=psO[:, 0:64]), reads=[rpO], writes=[r_oT[tt]])
                    else:
                        kb.op("dve", lambda: V.tensor_tensor(out=oT[:, hcs], in0=oT[:, hcs], in1=psO[:, 0:64], op=ALU.add), reads=[rpO, r_oT[tt]], writes=[r_oT[tt]])
                    kb.op("dve", lambda: V.scalar_tensor_tensor(out=Sst[ci], in0=Sst[ci], scalar=sl["gl"][:, half:half + 1], in1=psS[:, 0:128], op0=ALU.mult, op1=ALU.add),
                          reads=[rpS, rsl, r_S[ci]], writes=[r_S[ci]])
                if ch["seq"] is not None and step == len(ch["tiles"]) - 1:
                    kb.dma("sp", d["ns_out"][ch["seq"], j, ch["dir"], h], Sst[ci], reads=[r_S[ci]], writes=[self.r_nsout])

            owritten = set()

            def run_interleaved(gens):
                active = list(gens)
                while active:
                    for g_ in list(active):
                        try:
                            next(g_)
                        except StopIteration:
                            active.remove(g_)
            NW = len(works)
            nI = [0] * NCHAIN
            nS = [0] * NCHAIN
            nT = [len(chains[ci]["tiles"]) for ci in range(NCHAIN)]
            while any(nS[ci] < nT[ci] for ci in range(NCHAIN)):
                gens = []
                scan_ready = [ci for ci in range(NCHAIN) if nS[ci] < nI[ci]]
                cand = [ci for ci in range(NCHAIN) if nI[ci] < nT[ci] and nI[ci] - nS[ci] < len(slots[ci])]
                cand.sort(key=lambda ci: (nI[ci] - nS[ci], ci))
                picked = cand[:NW]
                for ci in picked:
                    gens.append(intra(ci, nI[ci]))
                for ci in scan_ready:
                    gens.append(scan(ci, nS[ci]))
                run_interleaved(gens)
                for ci in picked:
                    nI[ci] += 1
                for ci in scan_ready:
                    nS[ci] += 1
            kb.op("act", lambda: A.activation(out=sqb, in_=oT, func=AF.Square), reads=r_oT, writes=[r_sqb])
            for b in range(NTB):
                ps, rp = self.next_ps()
                self.mm(ps[:], self.onesb[:], sqb[:, b * 512:(b + 1) * 512], True, True, reads=[r_sqb, self.r_const], writes=[rp])
                rs, rrs = self.next_rstd()
                kb.op("dve", lambda: V.tensor_scalar(out=rs[:], in0=ps[:], scalar1=1.0 / 128.0, scalar2=EPS, op0=ALU.mult, op1=ALU.add), reads=[rp], writes=[rrs])
                kb.op("act", lambda: A.activation(out=rs[:], in_=rs[:], func=AF.Sqrt), reads=[rrs], writes=[rrs])
                kb.op("dve", lambda: V.reciprocal(out=rs[:], in_=rs[:]), reads=[rrs], writes=[rrs])
                t0, r0 = self.next_tmp()
                kb.op("dve", lambda: V.scalar_tensor_tensor(out=t0[:], in0=oT[:, b * 512:(b + 1) * 512], scalar=onorm, in1=rs[:], op0=ALU.mult, op1=ALU.mult),
                      reads=r_oT[b * 4:(b + 1) * 4] + [rrs, r_gpar], writes=[r0])
                kb.op("dve", lambda: V.tensor_tensor(out=og[:, h, b * 512:(b + 1) * 512], in0=t0[:], in1=gateT[:, b * 512:(b + 1) * 512], op=ALU.mult),
                      reads=[r0, r_gate], writes=[r_og[h]])
        self.inherit(allx, [r_pp, r_cv, r_q, r_k, r_v, r_gate, r_sqb, r_ktok, r_vtok, r_ab, r_gpar] + r_oT + r_S + r_works + extra_slots)
        if False:
            for b in range(NTB):
                t0, r0 = self.next_tmp()
                kb.op("dve", lambda: V.tensor_copy(out=t0[:], in_=og[:, 0, b * 512:(b + 1) * 512]), reads=[r_og[0]], writes=[r0])
                kb.dma("sp", d["dbg"][5, :, b * 512:(b + 1) * 512], t0[:], reads=[r0], writes=[self.r_nsout])
        self.inherit([self.r_wslot[0]], [works[2]["r"]])
        self.inherit([self.r_wslot[1]], [works[3]["r"]])
        self.out_proj(l, d["gdn_w_out"][j], og, r_og, None, None)
        if False:
            pass
        self.inherit([self.r_big], allr2 + r_works)
        self.inherit([self.r_wslot[4]], [r_masks] + r_vnew)


    def out_proj(self, l, wdram, src, r_src, bias_fn, r_bias):
        nc, kb = self.nc, self.kb
        V, A, P = nc.vector, nc.scalar, nc.tensor
        d = self.dr
        allx = [r for rr in self.r_x for r in rr]
        for c in range(NCH):
            kb.dma("sp", self.xT[:, c, :], d["xsp"][:, c * T:(c + 1) * T], writes=self.r_x[c])
        wo = []
        for hlf in range(2):
            wv = self.wslots[hlf][:, 0:NCH * 512].rearrange("p (c n) -> p c n", c=NCH)
            kb.dma("pool", wv, wdram[:, hlf * 512:(hlf + 1) * 512].rearrange("(c p) n -> p c n", p=128), writes=[self.r_wslot[hlf]])
            wo.append(wv)
        for c in range(NCH):
            for b in range(NTB):
                w = 0 if b < 2 else 1
                ps, rp = self.next_ps()
                for cb in range(NCH):
                    self.mm(ps[:], wo[c // 4][:, cb, (c % 4) * 128:(c % 4 + 1) * 128], src[:, cb, b * 512:(b + 1) * 512],
                            start=(cb == 0), stop=(cb == NCH - 1), reads=[self.r_wslot[c // 4], r_src[cb]], writes=[rp])
                t0, r0 = self.next_tmp()
                if bias_fn is not None:
                    kb.op("dve", lambda: V.tensor_scalar(out=t0[:], in0=ps[:], scalar1=bias_fn(c), scalar2=self.mod(l, w, 2, c), op0=ALU.add, op1=ALU.mult),
                          reads=[rp, r_bias, self.r_mods[l]], writes=[r0])
                else:
                    kb.op("dve", lambda: V.tensor_scalar(out=t0[:], in0=ps[:], scalar1=self.mod(l, w, 2, c), scalar2=None, op0=ALU.mult),
                          reads=[rp, self.r_mods[l]], writes=[r0])
                xs = self.xT[:, c, b * 512:(b + 1) * 512]
                kb.op("dve", lambda: V.tensor_tensor(out=xs, in0=xs, in1=t0[:], op=ALU.add), reads=[r0, self.r_x[c][b]], writes=[self.r_x[c][b]])

    def inherit(self, dsts, srcs):
        evs = []
        for sres in srcs:
            if sres.w is not None:
                evs.append(sres.w)
            evs.extend(sres.rs)
        best = {}
        for ev in evs:
            k = id(ev[0])
            if k not in best or best[k][1] < ev[1]:
                best[k] = ev
        for dres in dsts:
            own = ([dres.w] if dres.w is not None else []) + list(dres.rs)
            b2 = dict(best)
            for ev in own:
                k = id(ev[0])
                if k not in b2 or b2[k][1] < ev[1]:
                    b2[k] = ev
            dres.w = None
            dres.rs = list(b2.values())

    def sin3(self, buf, rbuf, npart, ncol):
        nc, kb = self.nc, self.kb
        V, A = nc.vector, nc.scalar
        q, rq = self.next_tmp()
        bv = buf[0:npart, 0:ncol]
        qv = q[0:npart, 0:ncol]
        kb.op("act", lambda: A.activation(out=bv, in_=bv, func=AF.Sin, scale=1.0 / 3.0), reads=[rbuf], writes=[rbuf])
        kb.op("dve", lambda: V.tensor_tensor(out=qv, in0=bv, in1=bv, op=ALU.mult), reads=[rbuf], writes=[rq])
        kb.op("dve", lambda: V.tensor_scalar(out=qv, in0=qv, scalar1=-4.0, scalar2=3.0, op0=ALU.mult, op1=ALU.add), reads=[rq], writes=[rq])
        kb.op("dve", lambda: V.tensor_tensor(out=bv, in0=bv, in1=qv, op=ALU.mult), reads=[rq, rbuf], writes=[rbuf])

    def hyena_layer(self, l):
        nc, kb = self.nc, self.kb
        V, A, P, G = nc.vector, nc.scalar, nc.tensor, nc.gpsimd
        d = self.dr
        j = l // 2
        self.norm_mod(l, 2 * l, 0, 1)
        allx = [r for rr in self.r_x for r in rr]
        kb.dma("sp", d["xsp"][:, :], self.xT[:].rearrange("p c t -> p (c t)"), reads=allx)
        R1 = self.xT[:].rearrange("p c t -> p (c t)")
        PW = 1544
        pp = [R1[:, i * PW:(i + 1) * PW] for i in range(3)]
        o1 = 3 * PW
        u = [R1[:, o1 + i * T:o1 + (i + 1) * T] for i in range(3)]
        o1 += 3 * T
        zt = R1[:, o1:o1 + 768].bitcast(BF16).rearrange("p (t c) -> p t c", t=12)
        hidb = R1[:, o1 + 768:o1 + 768 + 640].bitcast(BF16)
        r_hidb = Res("hidb")
        r_pp = [Res(f"pp{i}") for i in range(3)]
        r_u = [Res(f"u{i}") for i in range(3)]
        r_zt = Res("zt")
        self.inherit(r_pp + r_u + [r_zt, r_hidb], allx)
        R2 = self.big
        zfin = R2[:, 0:NCH * T].rearrange("p (c t) -> p c t", c=NCH)
        o2 = NCH * T
        hid_lat = R2[:, o2:o2 + 2048].bitcast(F32); o2 += 2048
        hid_ctx = R2[:, o2:o2 + 512].bitcast(F32); o2 += 512
        delta_b = R2[:, o2:o2 + 2048].bitcast(F32); o2 += 2048
        fv = {}
        for nm in ("A0", "B0", "hf0", "hf1", "nhf1", "hb0", "nhb0", "hb1"):
            fv[nm] = R2[:, o2:o2 + 512].rearrange("p (q c) -> p q c", q=4); o2 += 512
        for nm in ("Ac", "Bc"):
            fv[nm] = R2[:, o2:o2 + 256].rearrange("p (q c) -> p q c", q=2); o2 += 256
        spec = R2[:, o2:o2 + 12288].bitcast(F32)
        o2 += 12288
        assert o2 <= 33792
        SP = [spec[:, i * 512:(i + 1) * 512] for i in range(10)]
        Yb = spec[:, 5120:6144].bitcast(BF16)
        Y = [Yb[:, i * 512:(i + 1) * 512] for i in range(4)]
        r_SP = [Res(f"sp{i}") for i in range(10)]
        r_Y = [Res(f"Y{i}") for i in range(4)]
        r_zfin = [Res(f"zfin{c}") for c in range(NCH)]
        r_hid = Res("hid"); r_delta = Res("delta"); r_fv = Res("fv")
        self.inherit(r_SP + r_Y + r_zfin + [r_hid, r_delta, r_fv], [self.r_big])
        if not hasattr(self, "hy_alloc"):
            self.hy_alloc = dict(
                vecT=kb.sb("s_hyvecT", [128, 120], F32), vec64=kb.sb("s_hyvec64", [64, 4], F32),
                w1=kb.sb("s_hyw1", [33, 64], F32), w2=kb.sb("s_hyw2", [64, 64], F32),
                tneg=kb.sb("s_hytneg", [128, 10], F32), mask0=kb.sb("s_hymask0", [128, 1], F32),
                w3cb=[kb.sb(f"s_hyw3cb_{i}", [64, 4, 128], F32) for i in range(2)],
                r_w3cb=[Res("w3cb0"), Res("w3cb1")], r_par=Res("hypar"))
        ha = self.hy_alloc
        vecT, vec64, w1, w2, tneg, mask0, w3cb, r_w3cb, r_par = (ha[k] for k in
            ("vecT", "vec64", "w1", "w2", "tneg", "mask0", "w3cb", "r_w3cb", "r_par"))
        kb.dma("sp", vecT[:], d["hy_vecT"][j], writes=[r_par])
        kb.dma("sp", vec64[:, 0:3], d["hy_vec64"][j], writes=[r_par])
        kb.dma("sp", w1[:], d["hy_f_w1"][j], writes=[r_par])
        kb.dma("sp", w2[:], d["hy_f_w2"][j], writes=[r_par])
        kb.dma("sp", tneg[:], d["hytneg"][:, :], writes=[r_par])
        kb.dma("sp", mask0[:], d["mask0"][:, :], writes=[r_par])
        kb.dma("sp", delta_b, d["hydelta"][0:1, :].partition_broadcast(128), writes=[r_delta])
        kb.op("dve", lambda: V.tensor_tensor(out=vec64[:, 3:4], in0=vec64[:, 0:1], in1=vec64[:, 2:3], op=ALU.mult), reads=[r_par], writes=[r_par])
        b_in = lambda blk: vecT[:, blk:blk + 1]
        cw = lambda k, blk: vecT[:, 24 + k * 24 + blk:24 + k * 24 + blk + 1]
        skp = lambda n, c: vecT[:, 96 + n * 8 + c:96 + n * 8 + c + 1]
        b_out = lambda c: vecT[:, 112 + c:113 + c]
        tabs = []
        for i in range(6):
            sl = self.wslots[i // 2]
            tv = sl[:, (i % 2) * 2048:(i % 2 + 1) * 2048].rearrange("p (q k) -> p q k", q=4)
            kb.dma("pool", tv, d["hytab"][i].rearrange("(q p) k -> p q k", p=128), writes=[self.r_wslot[i // 2]])
            tabs.append(tv)
        Cf, Sf, Cr, Sr, Ci, Si = tabs
        r_tab = [self.r_wslot[0], self.r_wslot[0], self.r_wslot[1], self.r_wslot[1], self.r_wslot[2], self.r_wslot[2]]
        rCf, rSf, rCr, rSr, rCi, rSi = r_tab
        for (zname, L, hid) in (("zemb_lat", LLAT, hid_lat), ("zemb_ctx", LCTX, hid_ctx)):
            for c0 in range(0, L, 512):
                ncol = min(512, L - c0)
                ze, rze = self.next_tmp()
                kb.dma("sp", ze[0:33, 0:ncol], d[zname][:, c0:c0 + ncol], writes=[rze])
                ps, rp = self.next_ps()
                self.mm(ps[0:64, 0:ncol], w1[:, :], ze[0:33, 0:ncol], True, True, reads=[rze, r_par], writes=[rp])
                h1, rh1 = self.next_tmp()
                kb.op("dve", lambda: V.tensor_scalar(out=h1[0:64, 0:ncol], in0=ps[0:64, 0:ncol], scalar1=vec64[:, 0:1], scalar2=vec64[:, 2:3],
                                                     op0=ALU.add, op1=ALU.mult), reads=[rp, r_par], writes=[rh1])
                self.sin3(h1, rh1, 64, ncol)
                ps2, rp2 = self.next_ps()
                self.mm(ps2[0:64, 0:ncol], w2[:, :], h1[0:64, 0:ncol], True, True, reads=[rh1, r_par], writes=[rp2])
                hv = hid[0:64, c0:c0 + ncol]
                kb.op("dve", lambda: V.tensor_scalar(out=hv, in0=ps2[0:64, 0:ncol], scalar1=vec64[:, 1:2], scalar2=vec64[:, 2:3],
                                                     op0=ALU.add, op1=ALU.mult), reads=[rp2, r_par], writes=[r_hid])
                kb.op("act", lambda: A.activation(out=hv, in_=hv, func=AF.Sin, scale=1.0 / 3.0), reads=[r_hid], writes=[r_hid])
                q, rq = self.next_tmp()
                qv = q[0:64, 0:ncol]
                kb.op("dve", lambda: V.tensor_tensor(out=qv, in0=hv, in1=hv, op=ALU.mult), reads=[r_hid], writes=[rq])
                kb.op("dve", lambda: V.tensor_scalar(out=qv, in0=qv, scalar1=-4.0, scalar2=3.0, op0=ALU.mult, op1=ALU.add), reads=[rq], writes=[rq])
                kb.op("dve", lambda: V.tensor_tensor(out=hv, in0=hv, in1=qv, op=ALU.mult), reads=[rq, r_hid], writes=[r_hid])
        kb.op("act", lambda: A.copy(out=hidb[0:64, 0:LLAT], in_=hid_lat[0:64, :]), reads=[r_hid], writes=[r_hidb])
        kb.op("act", lambda: A.copy(out=hidb[0:64, LLAT:LLAT + LCTX], in_=hid_ctx[0:64, :]), reads=[r_hid], writes=[r_hidb])
        for i in range(3):
            kb.op("dve", lambda: V.memset(pp[i], 0.0), writes=[r_pp[i]])
        segs = [(1, LLAT, 0), (1027, LCTX, LLAT), (1285, LCTX, LLAT + LCTX)]
        evi = [0]

        def evac(out, in_, reads, writes):
            evi[0] += 1
            if evi[0] % 2 == 0:
                kb.op("act", lambda: A.copy(out=out, in_=in_), reads=reads, writes=writes)
            else:
                kb.op("dve", lambda: V.tensor_copy(out=out, in_=in_), reads=reads, writes=writes)

        def spectrum(dst, rdst, terms):
            ps, rp = self.next_ps()
            tot = sum(t[4] for t in terms)
            for kb_ in range(4):
                i = 0
                for (tab, rtab, dat, rdat, nq) in terms:
                    for q in range(nq):
                        self.mm(ps[:, kb_ * 128:(kb_ + 1) * 128], tab[:, q, kb_ * 128:(kb_ + 1) * 128], dat[:, q, :],
                                start=(i == 0), stop=(i == tot - 1), reads=[rtab, rdat], writes=[rp], signal=(kb_ == 3 and i == tot - 1))
                        i += 1
            evac(dst, ps[:], [rp], [rdst])

        def cprod(dst, rdst, terms):
            acc, racc = self.next_tmp()
            n = len(terms)
            for i, (a, ra, b, rb, sg) in enumerate(terms):
                if i == 0:
                    kb.op("dve", lambda: V.tensor_tensor(out=acc[:], in0=a, in1=b, op=ALU.mult), reads=[ra, rb], writes=[racc])
                else:
                    t2, r2 = self.next_tmp()
                    kb.op("dve", lambda: V.tensor_tensor(out=t2[:], in0=a, in1=b, op=ALU.mult), reads=[ra, rb], writes=[r2])
                    o = dst if i == n - 1 else acc[:]
                    ro = rdst if i == n - 1 else racc
                    kb.op("dve", lambda: V.tensor_tensor(out=o, in0=acc[:], in1=t2[:], op=(ALU.add if sg > 0 else ALU.subtract)),
                          reads=[racc, r2], writes=[ro])

        for cb in range(NCH):
            wsl = self.wslots[3 + cb % 2]
            rws = self.r_wslot[3 + cb % 2]
            win = wsl[:, 0:NCH * 3 * 128].rearrange("p (c g n) -> p c g n", c=NCH, g=3)
            for g in range(3):
                kb.dma("pool", win[:, :, g, :], d["hy_w_in"][j, :, g * D + cb * 128:g * D + (cb + 1) * 128].rearrange("(c p) n -> p c n", p=128), writes=[rws])
            w3 = w3cb[cb % 2]
            rw3 = r_w3cb[cb % 2]
            w3 = w3[:].rearrange("r g c -> r (g c)").bitcast(BF16)[:, 0:512].rearrange("r (g c) -> r g c", g=4)
            kb.dma("pool", w3, d["hy_f_w3"][j].rearrange("r (g c) -> r g c", g=4)[:, :, cb * 128:(cb + 1) * 128], writes=[rw3])
            for g in range(3):
                blk = g * 8 + cb
                for b in range(NTB):
                    ps, rp = self.next_ps()
                    for c in range(NCH):
                        self.mm(ps[:], win[:, c, g, :], self.hT[:, c, b * 512:(b + 1) * 512], start=(c == 0), stop=(c == NCH - 1),
                                reads=[rws, self.r_h[c][b]], writes=[rp])
                    if b < 2:
                        dsts = [(pp[g][:, 1 + b * 512:1 + (b + 1) * 512], ps[:])]
                    else:
                        dsts = [(pp[g][:, 1027:1027 + 256], ps[:, 0:256]), (pp[g][:, 1285:1285 + 256], ps[:, 256:512])]
                    for (o, i_) in dsts:
                        kb.op("act", lambda: A.activation(out=o, in_=i_, func=AF.Identity, bias=b_in(blk), scale=1.0),
                              reads=[rp, r_par], writes=[r_pp[g]])
                for (s0, Ls, o0) in segs:
                    uo = u[g][:, o0:o0 + Ls]
                    kb.op("dve", lambda: V.tensor_scalar(out=uo, in0=pp[g][:, s0 - 1:s0 - 1 + Ls], scalar1=cw(0, blk), scalar2=None, op0=ALU.mult),
                          reads=[r_pp[g], r_par], writes=[r_u[g]])
                    for k in (1, 2):
                        kb.op("dve", lambda: V.scalar_tensor_tensor(out=uo, in0=pp[g][:, s0 - 1 + k:s0 - 1 + k + Ls], scalar=cw(k, blk), in1=uo,
                                                                    op0=ALU.mult, op1=ALU.add), reads=[r_pp[g], r_par, r_u[g]], writes=[r_u[g]])
            for n in range(2):
                z, rz = u[n], r_u[n]
                for g4 in range(3):
                    ps, rp = self.next_ps()
                    for i in range(4):
                        tt = g4 * 4 + i
                        kb.op("pe", lambda: P.transpose(ps[:, i * 128:(i + 1) * 128], z[:, tt * 128:(tt + 1) * 128], self.ident[:]),
                              reads=[rz, self.r_const], writes=[rp], signal=(i == 3))
                    evac(zt[:, g4 * 4:(g4 + 1) * 4, :], ps[:].rearrange("p (t c) -> p t c", t=4), [rp], [r_zt])
                for q in range(10):
                    lat = q < 8
                    hidv = hidb[0:64, q * 128:(q + 1) * 128]
                    ps, rp = self.next_ps()
                    for dr_ in range(2):
                        self.mm(ps[:, dr_ * 128:(dr_ + 1) * 128], hidv, w3[:, n * 2 + dr_, :], True, True, reads=[r_hidb, rw3], writes=[rp], signal=(dr_ == 1))
                    win_, rwin = self.next_tmp()
                    kb.op("act", lambda: A.activation(out=win_[:, 0:128], in_=delta_b[:, cb * 128:(cb + 1) * 128], func=AF.Exp, scale=tneg[:, q:q + 1]),
                          reads=[r_delta, r_par], writes=[rwin])
                    hh, rhh = self.next_tmp()
                    kb.op("dve", lambda: V.tensor_tensor(out=hh[:, 0:256].rearrange("p (a c) -> p a c", a=2), in0=ps[:, 0:256].rearrange("p (a c) -> p a c", a=2),
                                                         in1=win_[:, 0:128].unsqueeze(1).to_broadcast([128, 2, 128]), op=ALU.mult),
                          reads=[rp, rwin], writes=[rhh])
                    hf = hh[:, 0:128]
                    hb = hh[:, 128:256]
                    if q == 0 or q == 8:
                        kb.op("dve", lambda: V.tensor_scalar(out=hb, in0=hb, scalar1=mask0[:, 0:1], scalar2=None, op0=ALU.mult), reads=[rhh, r_par], writes=[rhh])
                    def wr(nm, qq, fn):
                        kb.op("dve", fn(fv[nm][:, qq, :]), reads=[rhh], writes=[r_fv])
                    if q < 4 or q >= 8:
                        nmA, nmB, qq = ("A0", "B0", q) if lat else ("Ac", "Bc", q - 8)
                        kb.op("dve", lambda: V.tensor_tensor(out=fv[nmA][:, qq, :], in0=hf, in1=hb, op=ALU.add), reads=[rhh], writes=[r_fv])
                        kb.op("dve", lambda: V.tensor_tensor(out=fv[nmB][:, qq, :], in0=hb, in1=hf, op=ALU.subtract), reads=[rhh], writes=[r_fv])
                        if lat:
                            kb.op("act", lambda: A.copy(out=fv["hf0"][:, q, :], in_=hf), reads=[rhh], writes=[r_fv])
                            kb.op("act", lambda: A.copy(out=fv["hb0"][:, q, :], in_=hb), reads=[rhh], writes=[r_fv])
                            kb.op("act", lambda: A.mul(out=fv["nhb0"][:, q, :], in_=hb, mul=-1.0), reads=[rhh], writes=[r_fv])
                    else:
                        kb.op("act", lambda: A.copy(out=fv["hf1"][:, q - 4, :], in_=hf), reads=[rhh], writes=[r_fv])
                        kb.op("act", lambda: A.copy(out=fv["hb1"][:, q - 4, :], in_=hb), reads=[rhh], writes=[r_fv])
                        kb.op("act", lambda: A.mul(out=fv["nhf1"][:, q - 4, :], in_=hf, mul=-1.0), reads=[rhh], writes=[r_fv])
                zl = [zt[:, 0:4, :], zt[:, 4:8, :]]
                spectrum(SP[0], r_SP[0], [(Cf, rCf, zl[0], r_zt, 4)])
                spectrum(SP[1], r_SP[1], [(Sf, rSf, zl[0], r_zt, 4)])
                spectrum(SP[2], r_SP[2], [(Cf, rCf, zl[1], r_zt, 4)])
                spectrum(SP[3], r_SP[3], [(Sf, rSf, zl[1], r_zt, 4)])
                spectrum(SP[4], r_SP[4], [(Cf, rCf, fv["A0"], r_fv, 4)])
                spectrum(SP[5], r_SP[5], [(Sf, rSf, fv["B0"], r_fv, 4)])
                spectrum(SP[6], r_SP[6], [(Cf, rCf, fv["hf1"], r_fv, 4), (Cr, rCr, fv["hf0"], r_fv, 4)])
                spectrum(SP[7], r_SP[7], [(Sf, rSf, fv["nhf1"], r_fv, 4), (Sr, rSr, fv["hf0"], r_fv, 4)])
                spectrum(SP[8], r_SP[8], [(Cf, rCf, fv["hb1"], r_fv, 4), (Cr, rCr, fv["hb0"], r_fv, 4)])
                spectrum(SP[9], r_SP[9], [(Sf, rSf, fv["hb1"], r_fv, 4), (Sr, rSr, fv["nhb0"], r_fv, 4)])
                S_ = lambda i: (SP[i], r_SP[i])
                def T4(a, b, sg):
                    return (SP[a], r_SP[a], SP[b], r_SP[b], sg)
                cprod(Y[0], r_Y[0], [T4(0, 4, 1), T4(1, 5, 1), T4(2, 8, 1), T4(3, 9, 1)])
                cprod(Y[1], r_Y[1], [T4(0, 5, 1), T4(1, 4, -1), T4(2, 9, 1), T4(3, 8, -1)])
                cprod(Y[2], r_Y[2], [T4(0, 6, 1), T4(1, 7, 1), T4(2, 4, 1), T4(3, 5, 1)])
                cprod(Y[3], r_Y[3], [T4(0, 7, 1), T4(1, 6, -1), T4(2, 5, 1), T4(3, 4, -1)])

                def inverse_and_gate(yre, ryre, yim, ryim, ncol, col0):
                    ps, rp = self.next_ps()
                    yre3 = yre.rearrange("p (q c) -> p q c", q=4)
                    yim3 = yim.rearrange("p (q c) -> p q c", q=4)
                    for q in range(4):
                        self.mm(ps[:, 0:ncol], yre3[:, q, :], Ci[:, q, 0:ncol], start=(q == 0), stop=False, reads=[ryre, rCi], writes=[rp])
                    for q in range(4):
                        self.mm(ps[:, 0:ncol], yim3[:, q, :], Si[:, q, 0:ncol], start=False, stop=(q == 3), reads=[ryim, rSi], writes=[rp])
                    t0, r0 = self.next_tmp()
                    zs = z[:, col0:col0 + ncol]
                    kb.op("dve", lambda: V.scalar_tensor_tensor(out=t0[:, 0:ncol], in0=zs, scalar=skp(n, cb), in1=ps[:, 0:ncol], op0=ALU.mult, op1=ALU.add),
                          reads=[rz, rp, r_par], writes=[r0])
                    un = u[n + 1][:, col0:col0 + ncol]
                    kb.op("dve", lambda: V.tensor_tensor(out=un, in0=t0[:, 0:ncol], in1=un, op=ALU.mult), reads=[r0, r_u[n + 1]], writes=[r_u[n + 1]])

                inverse_and_gate(Y[0], r_Y[0], Y[1], r_Y[1], 512, 0)
                inverse_and_gate(Y[2], r_Y[2], Y[3], r_Y[3], 512, 512)
                zc = [zt[:, 8:10, :], zt[:, 10:12, :]]
                spectrum(SP[0], r_SP[0], [(Cf, rCf, zc[0], r_zt, 2)])
                spectrum(SP[1], r_SP[1], [(Sf, rSf, zc[0], r_zt, 2)])
                spectrum(SP[2], r_SP[2], [(Cf, rCf, zc[1], r_zt, 2)])
                spectrum(SP[3], r_SP[3], [(Sf, rSf, zc[1], r_zt, 2)])
                spectrum(SP[4], r_SP[4], [(Cf, rCf, fv["Ac"], r_fv, 2)])
                spectrum(SP[5], r_SP[5], [(Sf, rSf, fv["Bc"], r_fv, 2)])
                cprod(Y[0], r_Y[0], [T4(0, 4, 1), T4(1, 5, 1)])
                cprod(Y[1], r_Y[1], [T4(0, 5, 1), T4(1, 4, -1)])
                cprod(Y[2], r_Y[2], [T4(2, 4, 1), T4(3, 5, 1)])
                cprod(Y[3], r_Y[3], [T4(2, 5, 1), T4(3, 4, -1)])
                inverse_and_gate(Y[0], r_Y[0], Y[1], r_Y[1], 256, LLAT)
                inverse_and_gate(Y[2], r_Y[2], Y[3], r_Y[3], 256, LLAT + LCTX)
            kb.op("act", lambda: A.copy(out=zfin[:, cb, :], in_=u[2]), reads=[r_u[2]], writes=[r_zfin[cb]])
        self.inherit(allx, r_pp + r_u + [r_zt, r_hidb])
        self.out_proj(l, d["hy_w_out"][j], zfin, r_zfin, b_out, r_par)
        self.inherit([self.r_big], r_SP + r_Y + r_zfin + [r_hid, r_delta, r_fv])

    def final_out(self):
        nc, kb = self.nc, self.kb
        V, A, P = nc.vector, nc.scalar, nc.tensor
        d = self.dr
        self.out_res = []
        gidx = 2 * DEPTH
        for b in range(NTB):
            ps, rp = self.next_ps()
            for c in range(NCH):
                sq, rsq = self.next_tmpb()
                kb.op("act", lambda: A.activation(out=sq[:], in_=self.xT[:, c, b * 512:(b + 1) * 512], func=AF.Square),
                      reads=[self.r_x[c][b]], writes=[rsq])
                self.mm(ps[:], self.onesb[:], sq[:], start=(c == 0), stop=(c == NCH - 1), reads=[rsq, self.r_const], writes=[rp], signal=True)
            rstd, rr = self.next_rstd()
            kb.op("dve", lambda: V.tensor_scalar(out=rstd[:], in0=ps[:], scalar1=1.0 / D, scalar2=EPS, op0=ALU.mult, op1=ALU.add),
                  reads=[rp], writes=[rr])
            kb.op("act", lambda: A.activation(out=rstd[:], in_=rstd[:], func=AF.Sqrt), reads=[rr], writes=[rr])
            kb.op("dve", lambda: V.reciprocal(out=rstd[:], in_=rstd[:]), reads=[rr], writes=[rr])
            for c in range(NCH):
                xs = self.xT[:, c, b * 512:(b + 1) * 512]
                kb.op("dve", lambda: V.scalar_tensor_tensor(out=xs, in0=xs, scalar=self.gvec[:, gidx, c:c + 1],
                                                            in1=rstd[:], op0=ALU.mult, op1=ALU.mult),
                      reads=[self.r_x[c][b], rr, self.r_gvec], writes=[self.r_x[c][b]])
        for tt in range(T // 128):
            b = tt // 4
            lat = tt < 8
            dst = d["y_lat"][tt * 128:(tt + 1) * 128, :] if lat else d["y_ctx"][(tt - 8) * 128:(tt - 7) * 128, :]
            for half in range(2):
                ps, rp = self.next_ps()
                for j in range(4):
                    c = half * 4 + j
                    kb.op("pe", lambda: P.transpose(ps[:, j * 128:(j + 1) * 128], self.xT[:, c, tt * 128:(tt + 1) * 128], self.ident[:]),
                          reads=[self.r_x[c][b], self.r_const], writes=[rp], signal=(j == 3))
                t0, r0 = self.next_tmp()
                if half == 0:
                    kb.op("act", lambda: A.copy(out=t0[:], in_=ps[:]), reads=[rp], writes=[r0])
                else:
                    kb.op("dve", lambda: V.tensor_copy(out=t0[:], in_=ps[:]), reads=[rp], writes=[r0])
                ro = Res("out")
                kb.dma("sp", dst[:, half * 512:(half + 1) * 512], t0[:], reads=[r0], writes=[ro])
                self.out_res.append(ro)


_CACHE = {}
LAST_DBG = None


def get_prog(skip=()):
    key = tuple(sorted(skip))
    if key not in _CACHE:
        _CACHE[key] = Prog(skip)
    return _CACHE[key]


def make_in_maps(inp, n=8):
    hc = host_consts()
    f = lambda a: np.ascontiguousarray(np.asarray(a, dtype=np.float32))
    shared = {
        "pos": hc["pos"], "ident": hc["ident"],
        "ada_w": f(inp["ada_w"]),
        "ffn_w_gu": f(inp["ffn_w_gu"]), "ffn_w_down": f(inp["ffn_w_down"]),
    }
    shared["gmasks"] = hc["gmasks"]
    for kk in ("gdn_w_in", "gdn_w_out"):
        shared[kk] = f(inp[kk])
    gv = []
    for j in range(2):
        gv.append(np.concatenate([f(inp["gdn_conv"][j]).reshape(5 * 24, 128).T, f(inp["gdn_onorm"][j]).reshape(128, 1)], axis=1))
    shared["gdn_vecT"] = np.ascontiguousarray(np.stack(gv, axis=0))
    shared["gdn_rows"] = np.ascontiguousarray(np.stack([f(inp["gdn_a_log"]).reshape(2, 16), f(inp["gdn_dt_bias"]).reshape(2, 16)], axis=1))
    for kk in ("hytab", "zemb_lat", "zemb_ctx", "hydelta", "hytneg", "mask0"):
        shared[kk] = hc[kk]
    for kk in ("hy_w_in", "hy_w_out", "hy_f_w1", "hy_f_w2", "hy_f_w3"):
        shared[kk] = f(inp[kk])
    shared["hy_vec64"] = np.ascontiguousarray(np.stack([f(inp["hy_f_b1"]), f(inp["hy_f_b2"]), f(inp["hy_freq"])], axis=2))
    hv = []
    for j in range(2):
        parts = [f(inp["hy_b_in"][j]).reshape(24, 128).T]
        parts.append(f(inp["hy_conv"][j]).reshape(3 * 24, 128).T)
        parts.append(f(inp["hy_skip"][j]).reshape(2 * 8, 128).T)
        parts.append(f(inp["hy_b_out"][j]).reshape(8, 128).T)
        hv.append(np.concatenate(parts, axis=1))
    shared["hy_vecT"] = np.ascontiguousarray(np.stack(hv, axis=0))
    fm = lambda v: np.ascontiguousarray(f(v).reshape(-1, 128).T)
    gl = []
    for l in range(DEPTH):
        gl.append(fm(inp["norm1_g"][l]))
        gl.append(fm(inp["norm2_g"][l]))
    gl.append(fm(inp["final_g"]))
    shared["gvecT"] = np.ascontiguousarray(np.stack(gl, axis=1))
    shared["adabT"] = np.ascontiguousarray(np.stack([fm(inp["ada_b"][l]) for l in range(DEPTH)], axis=1))
    maps = []
    xp = f(inp["x_prompt"])
    xs = f(inp["x_sample"])
    cc = f(inp["c"])
    cctx = f(inp["c_ctx"])
    for i in range(n):
        m = dict(shared)
        m["x_lat"] = np.ascontiguousarray(xs[i])
        m["x_ctx"] = np.ascontiguousarray(xp[2 * i:2 * i + 2].reshape(2 * LCTX, D))
        m["state_in"] = np.ascontiguousarray(f(inp["state_delta"])[i])
        m["cvecT"] = np.ascontiguousarray(np.stack([fm(cc[i]), fm(cctx)], axis=2))
        maps.append(m)
    return maps


def run(inp, skip=(), trace=False):
    prog = get_prog(skip)
    maps = make_in_maps(inp)
    res = run_bass_kernel_spmd(prog.nc, maps, core_ids=list(range(8)), trace=trace)
    y_lat = np.stack([r["y_lat"] for r in res.results], axis=0)
    y_ctx = np.concatenate([r["y_ctx"].reshape(2, LCTX, D) for r in res.results], axis=0)
    ns = np.concatenate([r["ns_out"] for r in res.results], axis=0)
    global LAST_DBG
    LAST_DBG = [r.get("dbg") for r in res.results]
    return (y_ctx.astype(np.float32), y_lat.astype(np.float32), ns.astype(np.float32)), res


def kernel(**inputs):
    (y_prompt, y_sample, new_state), _ = run(inputs)
    return (y_prompt, y_sample, new_state)
```

```python
import math
import os
import contextlib
import numpy as np
import concourse.bass as bass
import concourse.mybir as mybir
from concourse.bass_utils import run_bass_kernel_spmd

F32 = mybir.dt.float32
BF16 = mybir.dt.bfloat16
ALU = mybir.AluOpType
AF = mybir.ActivationFunctionType

D = 1024
NCH = 8
LLAT = 1024
LCTX = 256
T = LLAT + 2 * LCTX
NTB = 3
DFF = 2816
NFB = 22
DEPTH = 4
EPS = 1e-6
SAME_ENGINE_SYNC = True


class Res:
    __slots__ = ("w", "rs", "name", "excl")

    def __init__(self, name="", excl=False):
        self.w = None
        self.rs = []
        self.name = name
        self.excl = excl


class Eng:
    def __init__(self, name, h, sem):
        self.name = name
        self.h = h
        self.sem = sem
        self.count = 0
        self.waited = {}


class KB:
    def __init__(self, nc):
        self.nc = nc
        self.es = contextlib.ExitStack()
        self.engs = {}
        for name, h in (("pe", nc.tensor), ("act", nc.scalar), ("dve", nc.vector),
                        ("pool", nc.gpsimd), ("sp", nc.sync)):
            sem = self.es.enter_context(nc.semaphore("sem_" + name))
            self.engs[name] = Eng(name, h, sem)
        self.dsem = {}
        for q in ("sp", "pool"):
            sems = [self.es.enter_context(nc.semaphore(f"dsem_{q}{i}")) for i in range(20)]
            self.dsem[q] = {"sems": sems, "tot": [0] * len(sems), "i": 0}
        self.semkey = {}
        self.ninst = 0

    def sb(self, name, shape, dt):
        return self.es.enter_context(self.nc.sbuf_tensor(name, list(shape), dt))

    def ps(self, name, shape, dt):
        return self.es.enter_context(self.nc.psum_tensor(name, list(shape), dt))

    def _waits(self, eng, reads, writes):
        need = {}
        def add(ev):
            if ev is None:
                return
            sem, val, src = ev
            if src == eng.name:
                if eng.name == "pe" or not SAME_ENGINE_SYNC:
                    return
            k = id(sem)
            if k not in need or need[k][1] < val:
                need[k] = (sem, val, src)
        for r in reads:
            add(r.w)
        for w in writes:
            add(w.w)
            for ev in w.rs:
                add(ev)
        for k, (sem, val, src) in need.items():
            if eng.waited.get(k, 0) >= val:
                continue
            if src in self.engs:
                assert self.engs[src].count >= val, f"pending unsignaled dep {src} {val} > {self.engs[src].count}"
            eng.h.wait_ge(sem, val)
            eng.waited[k] = val
            self.ninst += 1

    def op(self, en, fn, reads=(), writes=(), signal=True):
        eng = self.engs[en]
        xr = [r for r in reads if r.excl]
        if xr:
            writes = list(writes) + [r for r in xr if r not in writes]
        self._waits(eng, reads, writes)
        ins = fn()
        self.ninst += 1
        if signal:
            eng.count += 1
            ins.then_inc(eng.sem, 1)
            ev = (eng.sem, eng.count, en)
        else:
            ev = (eng.sem, eng.count + 1, en)
        for r in reads:
            r.rs.append(ev)
        for w in writes:
            w.w = ev
            w.rs = []
        return ev

    def dma(self, q, out, in_, reads=(), writes=()):
        eng = self.engs[q]
        self._waits(eng, reads, writes)
        d = self.dsem[q]
        i = d["i"] % len(d["sems"])
        d["i"] += 1
        sem = d["sems"][i]
        k = id(sem)
        if d["tot"][i] > 0 and eng.waited.get(k, 0) < d["tot"][i]:
            eng.h.wait_ge(sem, d["tot"][i])
            eng.waited[k] = d["tot"][i]
        eng.h.dma_start(out=out, in_=in_).then_inc(sem, 16)
        self.ninst += 1
        d["tot"][i] += 16
        ev = (sem, d["tot"][i], "dma_" + q)
        for r in reads:
            r.rs.append(ev)
        for w in writes:
            w.w = ev
            w.rs = []
        return ev

    def finish(self, res_list):
        eng = self.engs["sp"]
        self._waits(eng, res_list, res_list)
        self.es.close()


def host_consts():
    c = {}
    c["ident"] = np.eye(128, dtype=np.float32)
    GRID_W = 64
    rows = LLAT // GRID_W
    r, col = np.meshgrid(np.arange(rows), np.arange(GRID_W), indexing="ij")
    quarter = D // 4
    omega = (1.0 / (10000.0 ** (np.arange(quarter, dtype=np.float32) / quarter))).astype(np.float32)

    def emb1d(p):
        a = p.reshape(-1, 1).astype(np.float32) * omega[None, :]
        return np.concatenate([np.sin(a), np.cos(a)], axis=-1)

    c["pos"] = np.concatenate([emb1d(r), emb1d(col)], axis=-1).astype(np.float32)
    B = 512
    k = np.arange(B, dtype=np.float64)
    om = 2.0 * np.pi * (k + 0.5) / (2 * B)
    jj = np.arange(B, dtype=np.float64)
    Cf = np.cos(jj[:, None] * om[None, :])
    Sf = np.sin(jj[:, None] * om[None, :])
    Cr = np.cos((B - jj)[:, None] * om[None, :]); Cr[0, :] = 0.0
    Sr = np.sin((B - jj)[:, None] * om[None, :]); Sr[0, :] = 0.0
    Ci = (2.0 / (2 * B)) * np.cos(om[:, None] * jj[None, :])
    Si = -(2.0 / (2 * B)) * np.sin(om[:, None] * jj[None, :])
    c["hytab"] = np.stack([Cf, Sf, Cr, Sr, Ci, Si], axis=0).astype(np.float32)
    def zemb(seq):
        t = np.linspace(0.0, 1.0, seq, dtype=np.float32)[:, None]
        wpos = (2.0 * math.pi / seq) * np.arange(seq, dtype=np.float32)[:, None]
        fb = np.linspace(1e-4, 15, 16, dtype=np.float32)[None, :]
        z = np.concatenate([t, np.cos(fb * wpos), -np.sin(fb * wpos)], axis=-1).astype(np.float32)
        return np.ascontiguousarray(z.T)
    c["zemb_lat"] = zemb(LLAT)
    c["zemb_ctx"] = zemb(LCTX)
    mx = math.log(1e-2) / 0.3
    mn = math.log(1e-2) / 1.5
    c["hydelta"] = np.abs(np.linspace(mn, mx, D, dtype=np.float32)).reshape(1, D).astype(np.float32)
    tn = np.zeros((128, 10), np.float32)
    for q in range(8):
        tn[:, q] = -(q * 128 + np.arange(128)) / np.float32(LLAT - 1)
    for q in range(2):
        tn[:, 8 + q] = -(q * 128 + np.arange(128)) / np.float32(LCTX - 1)
    tn = -np.stack([np.linspace(0.0, 1.0, LLAT, dtype=np.float32).reshape(8, 128).T] , 0)[0]
    tc = -np.linspace(0.0, 1.0, LCTX, dtype=np.float32).reshape(2, 128).T
    c["hytneg"] = np.ascontiguousarray(np.concatenate([tn, tc], axis=1).astype(np.float32))
    pi_, fi_ = np.meshgrid(np.arange(128), np.arange(128), indexing="ij")
    sb_ = (pi_ // 64) == (fi_ // 64)
    bd16 = (pi_ // 16) == (fi_ // 16)
    s1 = ((pi_ // 32) == (fi_ // 32)) & ~bd16
    s2 = ((pi_ // 64) == (fi_ // 64)) & ((pi_ // 32) != (fi_ // 32))
    c["gmasks"] = np.stack([(pi_ > fi_) & sb_, (fi_ >= pi_) & sb_, (pi_ < fi_) & sb_, (fi_ <= pi_) & sb_,
                            (pi_ <= fi_) & sb_, (pi_ >= fi_) & sb_, bd16, s1, s2], axis=0).astype(np.float32)
    m0 = np.ones((128, 1), np.float32); m0[0, 0] = 0.0
    c["mask0"] = m0
    return c


class Prog:
    def __init__(self, skip=()):
        self.skip = set(skip)
        nc = bass.Bass("TRN2", target_bir_lowering=False)
        self.nc = nc
        self.kb = KB(nc)
        self.build()

    def din(self, name, shape, dt=F32):
        return self.nc.dram_tensor(name, list(shape), dt, kind="ExternalInput").ap()

    def dout(self, name, shape, dt=F32):
        return self.nc.dram_tensor(name, list(shape), dt, kind="ExternalOutput").ap()

    def build(self):
        nc, kb = self.nc, self.kb
        V, A, P, G = nc.vector, nc.scalar, nc.tensor, nc.gpsimd
        x_lat = self.din("x_lat", [LLAT, D])
        x_ctx = self.din("x_ctx", [2 * LCTX, D])
        cvecT = self.din("cvecT", [128, NCH, 2])
        gvecT = self.din("gvecT", [128, 2 * DEPTH + 1, NCH])
        adabT = self.din("adabT", [128, DEPTH, 48])
        pos = self.din("pos", [LLAT, D])
        identd = self.din("ident", [128, 128])
        ada_w = self.din("ada_w", [DEPTH, D, 6 * D])
        ffn_w_gu = self.din("ffn_w_gu", [DEPTH, D, 2 * DFF])
        ffn_w_down = self.din("ffn_w_down", [DEPTH, DFF, D])
        hytab = self.din("hytab", [6, 512, 512])
        zemb_lat = self.din("zemb_lat", [33, LLAT])
        zemb_ctx = self.din("zemb_ctx", [33, LCTX])
        hydelta = self.din("hydelta", [1, D])
        hytneg = self.din("hytneg", [128, 10])
        mask0 = self.din("mask0", [128, 1])
        hy_w_in = self.din("hy_w_in", [2, D, 3 * D])
        hy_w_out = self.din("hy_w_out", [2, D, D])
        hy_f_w1 = self.din("hy_f_w1", [2, 33, 64])
        hy_f_w2 = self.din("hy_f_w2", [2, 64, 64])
        hy_f_w3 = self.din("hy_f_w3", [2, 64, 4 * D])
        hy_vec64 = self.din("hy_vec64", [2, 64, 3])
        hy_vecT = self.din("hy_vecT", [2, 128, 24 + 72 + 16 + 8])
        gmasks = self.din("gmasks", [9, 128, 128])
        gdn_w_in = self.din("gdn_w_in", [2, D, 4128])
        gdn_w_out = self.din("gdn_w_out", [2, D, D])
        gdn_vecT = self.din("gdn_vecT", [2, 128, 5 * 24 + 1])
        gdn_rows = self.din("gdn_rows", [2, 2, 16])
        state_in = self.din("state_in", [2, 2, 8, 128, 128])
        ns_out = self.dout("ns_out", [2, 2, 2, 8, 128, 128])
        import os
        self.debug = bool(os.environ.get('GDN_DEBUG'))
        dbg = self.dout("dbg", [8, 128, T]) if self.debug else None
        xsp = self.nc.dram_tensor("xsp", [128, NCH * T], F32, kind="Internal").ap()
        y_lat = self.dout("y_lat", [LLAT, D])
        y_ctx = self.dout("y_ctx", [2 * LCTX, D])
        self.dr = dict(locals())

        xT = kb.sb("xT", [128, NCH, T], F32)
        hT = kb.sb("hT", [128, NCH, T], BF16)
        self.xT, self.hT = xT, hT
        self.r_x = [[Res(f"x{c}_{b}") for b in range(NTB)] for c in range(NCH)]
        self.r_h = [[Res(f"h{c}_{b}") for b in range(NTB)] for c in range(NCH)]
        big = kb.sb("big", [128, 33792], BF16)
        self.big = big
        self.r_big = Res("big")
        ident = kb.sb("identf", [128, 128], F32)
        identb = kb.sb("identb", [128, 128], BF16)
        onesb = kb.sb("onesb", [128, 128], BF16)
        self.ident, self.identb, self.onesb = ident, identb, onesb
        r_const = Res("const")
        self.r_const = r_const
        kb.dma("sp", ident[:], identd[:, :], writes=[r_const])
        kb.op("dve", lambda: V.tensor_copy(out=identb[:], in_=ident[:]), reads=[r_const], writes=[r_const])
        kb.op("dve", lambda: V.memset(onesb[:], 1.0), writes=[r_const])
        self.NSLOT = 5
        self.wslots = [kb.sb(f"wslot{i}", [128, 4096], BF16) for i in range(self.NSLOT)]
        self.r_wslot = [Res(f"wslot{i}") for i in range(self.NSLOT)]
        self.wi = 0
        self.NPS = 7
        self.psb = [kb.ps(f"psb{i}", [128, 512], F32) for i in range(8)]
        self.r_ps = [Res(f"ps{i}", excl=True) for i in range(8)]
        self.pi = 0
        self.NTMP = 4
        self.tmp = [kb.sb(f"tmp{i}", [128, 512], F32) for i in range(self.NTMP)]
        self.r_tmp = [Res(f"tmp{i}") for i in range(self.NTMP)]
        self.ti = 0
        self.rstdb = [kb.sb(f"rstd{i}", [128, 512], F32) for i in range(2)]
        self.r_rstd = [Res(f"rstd{i}") for i in range(2)]
        self.ri = 0
        self.tmpb = [kb.sb(f"tmpb{i}", [128, 512], BF16) for i in range(self.NTMP)]
        self.r_tmpb = [Res(f"tmpb{i}") for i in range(self.NTMP)]
        self.tbi = 0
        self.mods = kb.sb("mods", [128, DEPTH, 48, 2], F32)
        self.r_mods = [Res(f"mods{l}") for l in range(DEPTH)]
        self.csil = kb.sb("csil", [128, NCH, 2], F32)
        self.csilb = kb.sb("csilb", [128, NCH, 2], BF16)
        self.r_csil = Res("csil")
        self.small = kb.sb("small", [128, 64, 2], F32)
        self.r_small = Res("small")
        self.gvec = kb.sb("gvec", [128, 2 * DEPTH + 1, NCH], F32)
        self.r_gvec = Res("gvec")
        self.adab = kb.sb("adab", [128, DEPTH, 48], F32)
        self.r_adab = Res("adab")

        self.r_nsout = Res("nsout")
        self.load_inputs()
        self.ada_pending = None
        for l in range(DEPTH):
            if l == 0:
                self.ada(l)
            else:
                self.ada_step(1000)
            if f"mix{l}" not in self.skip:
                if l % 2 == 0:
                    self.gdn_layer(l)
                else:
                    self.hyena_layer(l)
            if l + 1 < DEPTH:
                self.ada_pending = self.ada_gen(l + 1)
            if f"ffn{l}" not in self.skip:
                self.ffn_layer(l)
        self.final_out()
        kb.finish(self.out_res + [self.r_nsout])

    def next_ps(self):
        i = self.pi % self.NPS
        self.pi += 1
        return self.psb[i], self.r_ps[i]

    def next_tmp(self):
        i = self.ti % self.NTMP
        self.ti += 1
        return self.tmp[i], self.r_tmp[i]

    def next_rstd(self):
        i = self.ri % 2
        self.ri += 1
        return self.rstdb[i], self.r_rstd[i]

    def next_tmpb(self):
        i = self.tbi % self.NTMP
        self.tbi += 1
        return self.tmpb[i], self.r_tmpb[i]

    def next_wslot(self):
        i = self.wi % self.NSLOT
        self.wi += 1
        return self.wslots[i], self.r_wslot[i]

    def mm(self, out, lhsT, rhs, start, stop, reads, writes, signal=None):
        P = self.nc.tensor
        return self.kb.op("pe", lambda: P.matmul(out, lhsT, rhs, start=start, stop=stop),
                          reads=reads, writes=writes, signal=(stop if signal is None else signal))

    def load_inputs(self):
        nc, kb = self.nc, self.kb
        V, A, P = nc.vector, nc.scalar, nc.tensor
        d = self.dr
        kb.dma("sp", self.csil[:, :, :], d["cvecT"][:, :, :], writes=[self.r_csil])
        kb.op("act", lambda: A.activation(out=self.csil[:], in_=self.csil[:], func=AF.Silu),
              reads=[self.r_csil], writes=[self.r_csil])
        kb.op("dve", lambda: V.tensor_copy(out=self.csilb[:], in_=self.csil[:]), reads=[self.r_csil], writes=[self.r_csil])
        kb.dma("sp", self.gvec[:, :, :], d["gvecT"][:, :, :], writes=[self.r_gvec])
        kb.dma("sp", self.adab[:, :, :], d["adabT"][:, :, :], writes=[self.r_adab])
        for tt in range(T // 128):
            b = tt // 4
            lat = tt < 8
            src = d["x_lat"][tt * 128:(tt + 1) * 128, :] if lat else d["x_ctx"][(tt - 8) * 128:(tt - 7) * 128, :]
            for half in range(2):
                t0, r0 = self.next_tmp()
                kb.dma("sp", t0[:], src[:, half * 512:(half + 1) * 512], writes=[r0])
                if lat:
                    t1, r1 = self.next_tmp()
                    kb.dma("sp", t1[:], d["pos"][tt * 128:(tt + 1) * 128, half * 512:(half + 1) * 512], writes=[r1])
                    kb.op("dve", lambda: V.tensor_add(out=t0[:], in0=t0[:], in1=t1[:]), reads=[r0, r1], writes=[r0])
                ps, rp = self.next_ps()
                for j in range(4):
                    kb.op("pe", lambda: P.transpose(ps[:, j * 128:(j + 1) * 128], t0[:, j * 128:(j + 1) * 128], self.ident[:]),
                          reads=[r0, self.r_const], writes=[rp], signal=(j == 3))
                cs = half * 4
                kb.op("act", lambda: A.copy(out=self.xT[:, cs:cs + 4, tt * 128:(tt + 1) * 128],
                                            in_=ps[:].rearrange("p (c t) -> p c t", c=4)),
                      reads=[rp], writes=[self.r_x[c][b] for c in range(cs, cs + 4)])

    def ada_gen(self, l):
        nc, kb = self.nc, self.kb
        V, A, P = nc.vector, nc.scalar, nc.tensor
        d = self.dr
        W = 512
        ps, rp = self.psb[7], self.r_ps[7]
        for ti in range(6 * D // W):
            ws, rw = self.next_wslot()
            wf = ws[:, 0:NCH * W].rearrange("p (c n) -> p c n", c=NCH)
            kb.dma("pool", wf, d["ada_w"][l, :, ti * W:(ti + 1) * W].rearrange("(c p) n -> p c n", p=128), writes=[rw])
            for jj in range(W // 128):
                j = ti * (W // 128) + jj
                for c in range(NCH):
                    self.mm(ps[:, 2 * j:2 * j + 2], wf[:, c, jj * 128:(jj + 1) * 128], self.csilb[:, c, :],
                            start=(c == 0), stop=(c == NCH - 1), reads=[rw, self.r_csil], writes=[rp])
            yield
        kb.op("dve", lambda: V.tensor_tensor(out=self.mods[:, l, :, :], in0=ps[:, 0:96].rearrange("p (j w) -> p j w", w=2),
                                             in1=self.adab[:, l, :].unsqueeze(2).to_broadcast([128, 48, 2]), op=ALU.add),
              reads=[rp, self.r_adab], writes=[self.r_mods[l]])

    def ada(self, l):
        for _ in self.ada_gen(l):
            pass

    def ada_step(self, n):
        g = getattr(self, "ada_pending", None)
        if g is None:
            return
        for _ in range(n):
            try:
                next(g)
            except StopIteration:
                self.ada_pending = None
                return

    def mod(self, l, which, idx, c):
        return self.mods[:, l, idx * NCH + c, which:which + 1]

    def norm_mod(self, l, gidx, sh_idx, sc_idx):
        nc, kb = self.nc, self.kb
        V, A, P = nc.vector, nc.scalar, nc.tensor
        gs = self.small[:, 0:NCH, :]
        kb.op("dve", lambda: V.tensor_scalar(out=gs, in0=self.mods[:, l, sc_idx * NCH:(sc_idx + 1) * NCH, :], scalar1=1.0, scalar2=None, op0=ALU.add),
              reads=[self.r_mods[l]], writes=[self.r_small])
        kb.op("dve", lambda: V.tensor_tensor(out=gs, in0=gs, in1=self.gvec[:, gidx, :].unsqueeze(2).to_broadcast([128, NCH, 2]), op=ALU.mult),
              reads=[self.r_gvec, self.r_small], writes=[self.r_small])
        for b in range(NTB):
            w = 0 if b < 2 else 1
            ps, rp = self.next_ps()
            for c in range(NCH):
                sq, rsq = self.next_tmpb()
                kb.op("act", lambda: A.activation(out=sq[:], in_=self.xT[:, c, b * 512:(b + 1) * 512], func=AF.Square),
                      reads=[self.r_x[c][b]], writes=[rsq])
                self.mm(ps[:], self.onesb[:], sq[:], start=(c == 0), stop=(c == NCH - 1), reads=[rsq, self.r_const], writes=[rp], signal=True)
            rstd, rr = self.next_rstd()
            kb.op("dve", lambda: V.tensor_scalar(out=rstd[:], in0=ps[:], scalar1=1.0 / D, scalar2=EPS, op0=ALU.mult, op1=ALU.add),
                  reads=[rp], writes=[rr])
            kb.op("act", lambda: A.activation(out=rstd[:], in_=rstd[:], func=AF.Sqrt), reads=[rr], writes=[rr])
            kb.op("dve", lambda: V.reciprocal(out=rstd[:], in_=rstd[:]), reads=[rr], writes=[rr])
            for c in range(NCH):
                t0, r0 = self.next_tmp()
                kb.op("dve", lambda: V.scalar_tensor_tensor(out=t0[:], in0=self.xT[:, c, b * 512:(b + 1) * 512], scalar=gs[:, c, w:w + 1],
                                                            in1=rstd[:], op0=ALU.mult, op1=ALU.mult),
                      reads=[self.r_x[c][b], rr, self.r_small], writes=[r0])
                kb.op("act", lambda: A.activation(out=self.hT[:, c, b * 512:(b + 1) * 512], in_=t0[:], func=AF.Identity,
                                                  bias=self.mod(l, w, sh_idx, c), scale=1.0),
                      reads=[r0, self.r_mods[l]], writes=[self.r_h[c][b]])

    def ffn_layer(self, l):
        nc, kb = self.nc, self.kb
        V, A, P, G = nc.vector, nc.scalar, nc.tensor, nc.gpsimd
        d = self.dr
        self.norm_mod(l, 2 * l + 1, 3, 4)
        act = self.big[:, 0:NFB * T].rearrange("p (f t) -> p f t", f=NFB)
        r_act = [[Res(f"act{f}_{b}") for b in range(NTB)] for f in range(NFB)]
        for f in range(NFB):
            for b in range(NTB):
                r_act[f][b].w = self.r_big.w
                r_act[f][b].rs = list(self.r_big.rs)
        hreads = lambda b: [self.r_h[c][b] for c in range(NCH)]
        ntile = (DFF + 511) // 512
        for ti in range(ntile):
            c0 = ti * 512
            ncol = min(512, DFF - c0)
            wg, rg = self.next_wslot()
            wgv = wg[:, 0:NCH * ncol].rearrange("p (c n) -> p c n", c=NCH)
            kb.dma("pool", wgv, d["ffn_w_gu"][l, :, c0:c0 + ncol].rearrange("(c p) n -> p c n", p=128), writes=[rg])
            wu, ru = self.next_wslot()
            wuv = wu[:, 0:NCH * ncol].rearrange("p (c n) -> p c n", c=NCH)
            kb.dma("pool", wuv, d["ffn_w_gu"][l, :, DFF + c0:DFF + c0 + ncol].rearrange("(c p) n -> p c n", p=128), writes=[ru])
            self.ada_step(2)
            for jj in range(ncol // 128):
                f = ti * 4 + jj
                for b in range(NTB):
                    pg, rpg = self.next_ps()
                    for c in range(NCH):
                        self.mm(pg[:], wgv[:, c, jj * 128:(jj + 1) * 128], self.hT[:, c, b * 512:(b + 1) * 512],
                                start=(c == 0), stop=(c == NCH - 1), reads=[rg, self.r_h[c][b]], writes=[rpg])
                    pu, rpu = self.next_ps()
                    for c in range(NCH):
                        self.mm(pu[:], wuv[:, c, jj * 128:(jj + 1) * 128], self.hT[:, c, b * 512:(b + 1) * 512],
                                start=(c == 0), stop=(c == NCH - 1), reads=[ru, self.r_h[c][b]], writes=[rpu])
                    sg, rsg = self.next_tmp()
                    kb.op("act", lambda: A.activation(out=sg[:], in_=pg[:], func=AF.Silu), reads=[rpg], writes=[rsg])
                    kb.op("dve", lambda: V.tensor_tensor(out=act[:, f, b * 512:(b + 1) * 512], in0=sg[:], in1=pu[:], op=ALU.mult),
                          reads=[rsg, rpu], writes=[r_act[f][b]])
        for ti in range(D // 128):
            self.ada_step(1)
            ws, rw = self.next_wslot()
            wv = ws[:, 0:NFB * 128].rearrange("p (f n) -> p f n", f=NFB)
            kb.dma("pool", wv, d["ffn_w_down"][l, :, ti * 128:(ti + 1) * 128].rearrange("(f p) n -> p f n", p=128), writes=[rw])
            for jj in range(1):
                c = ti
                for b in range(NTB):
                    w = 0 if b < 2 else 1
                    ps, rp = self.next_ps()
                    for f in range(NFB):
                        self.mm(ps[:], wv[:, f, jj * 128:(jj + 1) * 128], act[:, f, b * 512:(b + 1) * 512],
                                start=(f == 0), stop=(f == NFB - 1), reads=[rw, r_act[f][b]], writes=[rp])
                    xs = self.xT[:, c, b * 512:(b + 1) * 512]
                    kb.op("dve", lambda: V.scalar_tensor_tensor(out=xs, in0=ps[:], scalar=self.mod(l, w, 5, c), in1=xs,
                                                                op0=ALU.mult, op1=ALU.add),
                          reads=[rp, self.r_x[c][b], self.r_mods[l]], writes=[self.r_x[c][b]])
        self.merge_res(self.r_big, [r for rr in r_act for r in rr])

    def merge_res(self, dst, srcs):
        evs = []
        for s in srcs:
            if s.w is not None:
                evs.append(s.w)
            evs.extend(s.rs)
        best = {}
        for ev in evs:
            k = id(ev[0])
            if k not in best or best[k][1] < ev[1]:
                best[k] = ev
        dst.w = None
        dst.rs = list(best.values())

    def gdn_layer(self, l):
        nc, kb = self.nc, self.kb
        V, A, P, G = nc.vector, nc.scalar, nc.tensor, nc.gpsimd
        d = self.dr
        j = l // 2
        DT = F32
        self.norm_mod(l, 2 * l, 0, 1)
        allx = [r for rr in self.r_x for r in rr]
        kb.dma("sp", d["xsp"][:, :], self.xT[:].rearrange("p c t -> p (c t)"), reads=allx)
        R1 = self.xT[:].rearrange("p c t -> p (c t)")
        pos1 = [0]

        def c1(n):
            a = R1[:, pos1[0]:pos1[0] + n]
            pos1[0] += n
            assert pos1[0] <= 12288
            return a
        R2 = self.big
        pos2 = [0]

        def c2(n):
            a = R2[:, pos2[0]:pos2[0] + n]
            pos2[0] += n
            assert pos2[0] <= 33792, pos2[0]
            return a
        PWG = T + 12
        pp = c1(PWG)
        cv = c1(T)
        oT = cv
        qT = c1(T // 2).bitcast(BF16)
        kT = c1(T // 2).bitcast(BF16)
        vT = c1(T // 2).bitcast(BF16)
        gateT = c1(T // 2).bitcast(BF16)
        sqb = pp[:, 2:2 + T // 2].bitcast(BF16)
        ktok = c1(T // 2).bitcast(BF16).rearrange("p (t c) -> p t c", t=12)
        vtok = c1(T // 2).bitcast(BF16).rearrange("p (t c) -> p t c", t=12)
        r_pp, r_cv, r_q, r_k, r_v, r_gate, r_ktok, r_vtok = (Res(n) for n in ("pp", "cv", "q", "k", "v", "gate", "ktok", "vtok"))
        r_sqb = r_pp
        r_oT = [Res(f"oT{t}") for t in range(12)]
        self.inherit([r_pp, r_cv, r_q, r_k, r_v, r_gate, r_sqb, r_ktok, r_vtok] + r_oT, allx)
        og = c2(NCH * T).rearrange("p (h t) -> p h t", h=NCH)
        r_og = [Res(f"og{h}") for h in range(NCH)]
        NCHAIN = 6
        slots = []
        extra_slots = []
        for ci in range(NCHAIN):
            ring = []
            for r_ in range(2):
                wu_ = c2(512).bitcast(F32)
                sl = dict(WU=wu_, WT=wu_[:, 0:128], U=wu_[:, 128:256], qdT=c2(256).bitcast(F32), aqkT=c2(256).bitcast(F32), ktail=c2(256).bitcast(F32),
                          gl=c2(4).bitcast(F32),
                          r=Res(f"slot{ci}_{r_}"))
                ring.append(sl)
            slots.append(ring)
        works = []
        for wi in range(2):
            wk = dict(r=Res(f"work{wi}"))
            PA, PR, PX, PX2 = c2(512).bitcast(F32), c2(512).bitcast(F32), c2(512).bitcast(F32), c2(512).bitcast(F32)
            wk["PA"], wk["PR"], wk["PX"], wk["PX2"] = PA, PR, PX, PX2
            wk["A0"], wk["AT0"] = PA[:, 0:128], PA[:, 128:256]
            wk["R"], wk["RT"] = PR[:, 0:128], PR[:, 128:256]
            wk["Dsym"], wk["egrow"] = PX[:, 0:128], PX[:, 128:256]
            wk["DMs"], wk["DMi"] = PX2[:, 0:128], PX2[:, 128:256]
            wk["kbe"], wk["vb"] = c2(256).bitcast(F32), c2(256).bitcast(F32)
            wk["dd"] = wk["Dsym"]
            wk["GR"] = c1(256)
            wk["cols"] = c2(16).bitcast(F32)
            works.append(wk)
        for wi in range(2):
            wk = dict(r=Res(f"workx{wi}"))
            base = self.wslots[wi]
            PA, PR, PX, PX2 = base[:, 0:512].bitcast(F32), base[:, 512:1024].bitcast(F32), base[:, 1024:1536].bitcast(F32), base[:, 1536:2048].bitcast(F32)
            wk["PA"], wk["PR"], wk["PX"], wk["PX2"] = PA, PR, PX, PX2
            wk["A0"], wk["AT0"] = PA[:, 0:128], PA[:, 128:256]
            wk["R"], wk["RT"] = PR[:, 0:128], PR[:, 128:256]
            wk["Dsym"], wk["egrow"] = PX[:, 0:128], PX[:, 128:256]
            wk["DMs"], wk["DMi"] = PX2[:, 0:128], PX2[:, 128:256]
            wk["kbe"], wk["vb"] = base[:, 2048:2304].bitcast(F32), base[:, 2304:2560].bitcast(F32)
            wk["dd"] = wk["Dsym"]
            wk["GR"] = base[:, 2560:3072].bitcast(F32)
            wk["cols"] = base[:, 3072:3088].bitcast(F32)
            self.inherit([wk["r"]], [self.r_wslot[wi]])
            works.append(wk)
        vnew = [self.wslots[4][:, 1792 + i * 256:1792 + (i + 1) * 256].bitcast(F32) for i in range(NCHAIN)]
        r_vnew = [Res(f"vnew{i}") for i in range(NCHAIN)]
        Sst = [c1(128) for _ in range(NCHAIN)]
        r_S = [Res(f"S{i}") for i in range(NCHAIN)]
        g_tok = c1(192).rearrange("p (t n) -> p t n", t=12)
        b_tok = c1(192).rearrange("p (t n) -> p t n", t=12)
        gccol = c1(192).rearrange("p (a t n) -> p a t n", a=2, t=12)
        rows = c1(32)
        gvec = c1(122)
        for ci in range(2):
            for r_ in range(2):
                wu_ = c1(256)
                sl = dict(WU=wu_, WT=wu_[:, 0:128], U=wu_[:, 128:256], qdT=c1(128), aqkT=c1(128), ktail=c1(128), gl=c1(2), r=Res(f"slotx{ci}_{r_}"))
                slots[ci].append(sl)
                extra_slots.append(sl["r"])
        self.inherit(extra_slots, allx)
        masks = [self.wslots[4][:, 256 + i * 256:256 + (i + 1) * 256].bitcast(F32) for i in range(6)]
        masks += [self.wslots[4][:, 3328 + i * 256:3328 + (i + 1) * 256].bitcast(F32) for i in range(3)]
        r_ab = Res("ab"); r_gpar = Res("gpar"); r_masks = Res("masks")
        allr2 = r_og + [sl["r"] for ring in slots for sl in ring]
        self.inherit(allr2, [self.r_big])
        r_works = [wk["r"] for wk in works]
        self.inherit(r_works, [self.r_big] + allx)
        self.inherit([r_ab, r_gpar] + r_S, allx)
        self.inherit([r_masks] + r_vnew, [self.r_wslot[4]])
        for ci in range(NCHAIN):
            kb.op("dve", lambda: V.memset(vnew[ci], 0.0), writes=[r_vnew[ci]])
        for i in range(9):
            kb.dma("sp", masks[i], d["gmasks"][i], writes=[r_masks])
        m_sl, m_ui, m_su, m_li, Lf, Lb, m_bd16, m_s1, m_s2 = masks
        mpair = [self.wslots[4][:, 256:768].bitcast(F32), self.wslots[4][:, 768:1280].bitcast(F32)]
        kb.dma("sp", gvec[:, 0:121], d["gdn_vecT"][j], writes=[r_gpar])
        kb.dma("sp", rows[:, 0:32], d["gdn_rows"][j:j + 1].rearrange("o a n -> o (a n)").partition_broadcast(128), writes=[r_gpar])
        kb.op("act", lambda: A.activation(out=rows[:, 0:16], in_=rows[:, 0:16], func=AF.Exp), reads=[r_gpar], writes=[r_gpar])
        kb.op("dve", lambda: V.tensor_scalar(out=rows[:, 0:16], in0=rows[:, 0:16], scalar1=-1.0, scalar2=None, op0=ALU.mult), reads=[r_gpar], writes=[r_gpar])
        cwg = lambda k, blk: gvec[:, k * 24 + blk:k * 24 + blk + 1]
        onorm = gvec[:, 120:121]
        wab_s, r_wab = self.wslots[4], self.r_wslot[4]
        wab = wab_s[:, 0:NCH * 32].rearrange("p (c n) -> p c n", c=NCH)
        kb.dma("pool", wab, d["gdn_w_in"][j, :, 4096:4128].rearrange("(c p) n -> p c n", p=128), writes=[r_wab])
        ps, rp = self.next_ps()
        for tt in range(12):
            for c in range(NCH):
                self.mm(ps[:, tt * 32:(tt + 1) * 32], self.hT[:, c, tt * 128:(tt + 1) * 128], wab[:, c, :], start=(c == 0), stop=(c == NCH - 1),
                        reads=[self.r_h[c][tt // 4], r_wab], writes=[rp], signal=(tt == 11 and c == NCH - 1))
        abt, rabt = self.next_tmp()
        ab3 = abt[:, 0:384].rearrange("p (t n) -> p t n", t=12)
        kb.op("dve", lambda: V.tensor_copy(out=abt[:, 0:384], in_=ps[:, 0:384]), reads=[rp], writes=[rabt])
        kb.op("act", lambda: A.activation(out=b_tok, in_=ab3[:, :, 16:32], func=AF.Sigmoid), reads=[rabt], writes=[r_ab])
        kb.op("dve", lambda: V.tensor_tensor(out=g_tok, in0=ab3[:, :, 0:16], in1=rows[:, 16:32].unsqueeze(1).to_broadcast([128, 12, 16]), op=ALU.add),
              reads=[rabt, r_gpar], writes=[r_ab])
        kb.op("act", lambda: A.activation(out=g_tok, in_=g_tok, func=AF.Exp), reads=[r_ab], writes=[r_ab])
        kb.op("act", lambda: A.activation(out=g_tok, in_=g_tok, func=AF.Ln, bias=1.0, scale=1.0), reads=[r_ab], writes=[r_ab])
        kb.op("dve", lambda: V.tensor_tensor(out=g_tok, in0=g_tok, in1=rows[:, 0:16].unsqueeze(1).to_broadcast([128, 12, 16]), op=ALU.mult),
              reads=[r_ab, r_gpar], writes=[r_ab])
        ps, rp = self.next_ps()
        for dr_ in range(2):
            self.mm(ps[:, dr_ * 96:(dr_ + 1) * 96].rearrange("p (t n) -> p t n", t=12), (Lf if dr_ == 0 else Lb), g_tok[:, :, dr_ * 8:(dr_ + 1) * 8], True, True,
                    reads=[r_masks, r_ab], writes=[rp], signal=(dr_ == 1))
        kb.op("dve", lambda: V.tensor_copy(out=gccol.rearrange("p a t n -> p (a t n)"), in_=ps[:, 0:192]), reads=[rp], writes=[r_ab])
        chains = [dict(tiles=list(range(8)), dir=0, seq=None), dict(tiles=list(range(7, -1, -1)), dir=1, seq=None),
                  dict(tiles=[8, 9], dir=0, seq=0), dict(tiles=[9, 8], dir=1, seq=0),
                  dict(tiles=[10, 11], dir=0, seq=1), dict(tiles=[11, 10], dir=1, seq=1)]
        evi = [0]

        def evac(out, in_, reads, writes):
            evi[0] += 1
            if evi[0] % 2 == 0:
                kb.op("act", lambda: A.copy(out=out, in_=in_), reads=reads, writes=writes)
            else:
                kb.op("dve", lambda: V.tensor_copy(out=out, in_=in_), reads=reads, writes=writes)
        wki = [0]
        segs = [(2, LLAT, 0), (1030, LCTX, LLAT), (1290, LCTX, LLAT + LCTX)]
        kb.op("dve", lambda: V.memset(pp, 0.0), writes=[r_pp])
        for h in range(NCH):
            wsl, rws = self.wslots[2 + h % 2], self.r_wslot[2 + h % 2]
            win = wsl[:, 0:4096].rearrange("p (c g n) -> p c g n", c=NCH, g=4)
            for g in range(4):
                kb.dma("pool", win[:, :, g, :], d["gdn_w_in"][j, :, g * D + h * 128:g * D + (h + 1) * 128].rearrange("(c p) n -> p c n", p=128), writes=[rws])
            self.inherit([r_cv], r_oT)
            for g in range(3):
                blk = g * 8 + h
                for b in range(NTB):
                    ps, rp = self.next_ps()
                    for c in range(NCH):
                        self.mm(ps[:], win[:, c, g, :], self.hT[:, c, b * 512:(b + 1) * 512], start=(c == 0), stop=(c == NCH - 1),
                                reads=[rws, self.r_h[c][b]], writes=[rp])
                    if b < 2:
                        evac(pp[:, 2 + b * 512:2 + (b + 1) * 512], ps[:], [rp], [r_pp])
                    else:
                        evac(pp[:, 1030:1286], ps[:, 0:256], [rp], [r_pp])
                        evac(pp[:, 1290:1546], ps[:, 256:512], [rp], [r_pp])
                for (s0, Ls, o0) in segs:
                    co = cv[:, o0:o0 + Ls]
                    kb.op("dve", lambda: V.tensor_scalar(out=co, in0=pp[:, s0 - 2:s0 - 2 + Ls], scalar1=cwg(0, blk), scalar2=None, op0=ALU.mult),
                          reads=[r_pp, r_gpar], writes=[r_cv])
                    for k in range(1, 5):
                        kb.op("dve", lambda: V.scalar_tensor_tensor(out=co, in0=pp[:, s0 - 2 + k:s0 - 2 + k + Ls], scalar=cwg(k, blk), in1=co,
                                                                    op0=ALU.mult, op1=ALU.add), reads=[r_pp, r_gpar, r_cv], writes=[r_cv])
                if g == 2:
                    kb.op("act", lambda: A.activation(out=vT, in_=cv, func=AF.Silu), reads=[r_cv], writes=[r_v])
                else:
                    dst, rdst = (qT, r_q) if g == 0 else (kT, r_k)
                    kb.op("act", lambda: A.activation(out=cv, in_=cv, func=AF.Silu), reads=[r_cv], writes=[r_cv])
                    kb.op("act", lambda: A.activation(out=sqb, in_=cv, func=AF.Square), reads=[r_cv], writes=[r_sqb])
                    for b in range(NTB):
                        ps, rp = self.next_ps()
                        self.mm(ps[:], self.onesb[:], sqb[:, b * 512:(b + 1) * 512], True, True, reads=[r_sqb, self.r_const], writes=[rp])
                        rs, rrs = self.next_rstd()
                        kb.op("dve", lambda: V.tensor_scalar(out=rs[:], in0=ps[:], scalar1=EPS, scalar2=None, op0=ALU.add), reads=[rp], writes=[rrs])
                        kb.op("act", lambda: A.activation(out=rs[:], in_=rs[:], func=AF.Sqrt), reads=[rrs], writes=[rrs])
                        kb.op("dve", lambda: V.reciprocal(out=rs[:], in_=rs[:]), reads=[rrs], writes=[rrs])
                        sc = (128.0 ** -0.5) if g == 0 else 1.0
                        kb.op("dve", lambda: V.scalar_tensor_tensor(out=dst[:, b * 512:(b + 1) * 512], in0=cv[:, b * 512:(b + 1) * 512], scalar=sc, in1=rs[:],
                                                                    op0=ALU.mult, op1=ALU.mult), reads=[r_cv, rrs], writes=[rdst])
            for b in range(NTB):
                ps, rp = self.next_ps()
                for c in range(NCH):
                    self.mm(ps[:], win[:, c, 3, :], self.hT[:, c, b * 512:(b + 1) * 512], start=(c == 0), stop=(c == NCH - 1),
                            reads=[rws, self.r_h[c][b]], writes=[rp])
                kb.op("act", lambda: A.activation(out=gateT[:, b * 512:(b + 1) * 512], in_=ps[:], func=AF.Silu), reads=[rp], writes=[r_gate])
            self.inherit(r_oT, [r_cv])
            for (src, rsrc, dst, rdst) in ((kT, r_k, ktok, r_ktok), (vT, r_v, vtok, r_vtok)):
                for g4 in range(3):
                    ps, rp = self.next_ps()
                    psb = ps[:].bitcast(BF16)
                    for i in range(4):
                        tt = g4 * 4 + i
                        kb.op("pe", lambda: P.transpose(psb[:, i * 128:(i + 1) * 128], src[:, tt * 128:(tt + 1) * 128], self.identb[:]),
                              reads=[rsrc, self.r_const], writes=[rp], signal=(i == 3))
                    evac(dst[:, g4 * 4:(g4 + 1) * 4, :], psb[:, 0:512].rearrange("p (t c) -> p t c", t=4), [rp], [rdst])
            for ci, ch in enumerate(chains):
                if ch["seq"] is None:
                    kb.dma("sp", Sst[ci], d["state_in"][j, ch["dir"], h], writes=[r_S[ci]])
                else:
                    kb.op("dve", lambda: V.memset(Sst[ci], 0.0), writes=[r_S[ci]])

            def intra(ci, step):
                ch = chains[ci]
                tt = ch["tiles"][step]
                dr_ = ch["dir"]
                sl = slots[ci][step % len(slots[ci])]
                rsl = sl["r"]
                wk = works[wki[0] % len(works)]
                wki[0] += 1
                rw = wk["r"]
                tcs = slice(tt * 128, (tt + 1) * 128)
                gcc = gccol[:, dr_, tt, h:h + 1]
                bc = b_tok[:, tt, dr_ * 8 + h:dr_ * 8 + h + 1]
                lastA, lastB = (63, 127) if dr_ == 0 else (0, 64)
                m_s = m_sl if dr_ == 0 else m_su
                m_i = m_ui if dr_ == 0 else m_li
                cols = wk["cols"]
                yield
                ps, rp = self.next_ps()
                self.mm(ps[:, 0:128], kT[:, tcs], kT[:, tcs], True, True, reads=[r_k], writes=[rp], signal=False)
                self.mm(ps[:, 128:256], kT[:, tcs], qT[:, tcs], True, True, reads=[r_k, r_q], writes=[rp])
                evac(wk["GR"], ps[:, 0:256], [rp], [rw])
                yield
                ps2, rp2 = self.next_ps()
                kb.op("pe", lambda: P.transpose(ps2[:, 0:128], gcc.to_broadcast([128, 128]), self.ident[:]), reads=[r_ab, self.r_const], writes=[rp2])
                kb.op("dve", lambda: V.tensor_scalar(out=wk["dd"], in0=ps2[:, 0:128], scalar1=gcc, scalar2=None, op0=ALU.subtract), reads=[rp2, r_ab], writes=[rw])
                kb.op("dve", lambda: V.scalar_tensor_tensor(out=wk["dd"], in0=wk["dd"], scalar=-1.0, in1=wk["dd"], op0=ALU.mult, op1=ALU.max), reads=[rw], writes=[rw])
                kb.op("act", lambda: A.activation(out=wk["Dsym"], in_=wk["dd"], func=AF.Exp, scale=-1.0), reads=[rw], writes=[rw])
                kb.op("act", lambda: A.activation(out=wk["egrow"], in_=ps2[:, 0:128], func=AF.Exp), reads=[rp2], writes=[rw])
                kb.op("dve", lambda: V.tensor_copy(out=cols[0:64, 3:4], in_=ps2[0:64, lastA:lastA + 1]), reads=[rp2], writes=[rw])
                kb.op("dve", lambda: V.tensor_copy(out=cols[64:128, 3:4], in_=ps2[64:128, lastB:lastB + 1]), reads=[rp2], writes=[rw])
                kb.op("act", lambda: A.activation(out=cols[:, 0:1], in_=gcc, func=AF.Exp), reads=[r_ab], writes=[rw])
                kb.op("dve", lambda: V.tensor_tensor(out=cols[:, 1:2], in0=cols[:, 0:1], in1=bc, op=ALU.mult), reads=[rw, r_ab], writes=[rw])
                kb.op("act", lambda: A.activation(out=cols[:, 2:3], in_=gcc, func=AF.Exp, bias=cols[:, 3:4], scale=-1.0), reads=[rw, r_ab], writes=[rw])
                kb.op("act", lambda: A.copy(out=sl["gl"][:, 0:1], in_=wk["egrow"][:, lastA:lastA + 1]), reads=[rw], writes=[rsl])
                kb.op("act", lambda: A.copy(out=sl["gl"][:, 1:2], in_=wk["egrow"][:, lastB:lastB + 1]), reads=[rw], writes=[rsl])
                mp = mpair[dr_]
                id2 = self.ident[:].unsqueeze(1).to_broadcast([128, 2, 128])
                v3 = lambda ap: ap.rearrange("p (a c) -> p a c", a=2)
                kb.op("pool", lambda: G.tensor_tensor(out=v3(wk["PX2"]), in0=wk["Dsym"].unsqueeze(1).to_broadcast([128, 2, 128]), in1=v3(mp), op=ALU.mult),
                      reads=[rw, r_masks], writes=[rw])
                yield
                kb.op("dve", lambda: V.scalar_tensor_tensor(out=wk["A0"], in0=wk["GR"][:, 0:128], scalar=bc, in1=wk["DMs"], op0=ALU.mult, op1=ALU.mult),
                      reads=[rw, r_ab], writes=[rw])
                yield
                kb.op("dve", lambda: V.tensor_tensor(out=sl["aqkT"], in0=wk["GR"][:, 128:256], in1=wk["DMi"], op=ALU.mult), reads=[rw], writes=[rsl])
                yield
                kb.op("dve", lambda: V.tensor_tensor(out=sl["qdT"], in0=qT[:, tcs], in1=wk["egrow"], op=ALU.mult), reads=[rw, r_q], writes=[rsl])
                yield
                kb.op("act", lambda: A.activation(out=wk["kbe"], in_=ktok[:, tt, :], func=AF.Copy, scale=cols[:, 1:2]), reads=[rw, r_ktok], writes=[rw])
                yield
                kb.op("act", lambda: A.activation(out=wk["vb"], in_=vtok[:, tt, :], func=AF.Copy, scale=bc), reads=[rw, r_vtok, r_ab], writes=[rw])
                yield
                kb.op("act", lambda: A.activation(out=sl["ktail"], in_=ktok[:, tt, :], func=AF.Copy, scale=cols[:, 2:3]), reads=[rw, r_ktok], writes=[rsl])
                yield
                ps3, rp3 = self.next_ps()
                kb.op("pe", lambda: P.transpose(ps3[:, 0:128], wk["A0"], self.ident[:]), reads=[rw, self.r_const], writes=[rp3])
                kb.op("act", lambda: A.copy(out=wk["AT0"], in_=ps3[:, 0:128]), reads=[rp3], writes=[rw])
                PX, PX2, PR, PA = wk["PX"], wk["PX2"], wk["PR"], wk["PA"]
                bd2 = m_bd16.unsqueeze(1).to_broadcast([128, 2, 128])
                kb.op("pool", lambda: G.tensor_tensor(out=v3(PX), in0=v3(PA), in1=bd2, op=ALU.mult), reads=[rw, r_masks], writes=[rw])
                yield
                kb.op("dve", lambda: V.tensor_tensor(out=v3(PR), in0=id2, in1=v3(PX), op=ALU.subtract), reads=[rw, self.r_const], writes=[rw])
                for it in range(3):
                    X, XT = PX[:, 0:128], PX[:, 128:256]
                    yield
                    psA, rpA = self.next_ps()
                    self.mm(psA[:, 0:128], XT, X, True, True, reads=[rw], writes=[rpA], signal=False)
                    self.mm(psA[:, 128:256], X, XT, True, True, reads=[rw], writes=[rpA])
                    kb.op("act", lambda: A.copy(out=PX2, in_=psA[:, 0:256]), reads=[rpA], writes=[rw])
                    X2, XT2 = PX2[:, 0:128], PX2[:, 128:256]
                    yield
                    psB, rpB = self.next_ps()
                    self.mm(psB[:, 0:128], XT2, PR[:, 0:128], True, True, reads=[rw], writes=[rpB], signal=False)
                    self.mm(psB[:, 128:256], X2, PR[:, 128:256], True, True, reads=[rw], writes=[rpB])
                    kb.op("dve", lambda: V.tensor_tensor(out=PR, in0=PR, in1=psB[:, 0:256], op=ALU.add), reads=[rpB, rw], writes=[rw])
                    PX, PX2 = PX2, PX
                for msk in (m_s1, m_s2):
                    ms2 = msk.unsqueeze(1).to_broadcast([128, 2, 128])
                    kb.op("pool", lambda: G.tensor_tensor(out=v3(PX), in0=v3(PA), in1=ms2, op=ALU.mult), reads=[rw, r_masks], writes=[rw])
                    X, XT = PX[:, 0:128], PX[:, 128:256]
                    yield
                    ps1, rp1 = self.next_ps()
                    self.mm(ps1[:, 0:128], XT, PR[:, 0:128], True, True, reads=[rw], writes=[rp1], signal=False)
                    self.mm(ps1[:, 128:256], X, PR[:, 128:256], True, True, reads=[rw], writes=[rp1])
                    kb.op("act", lambda: A.copy(out=PX2, in_=ps1[:, 0:256]), reads=[rp1], writes=[rw])
                    yield
                    ps2_, rp2_ = self.next_ps()
                    self.mm(ps2_[:, 0:128], PR[:, 128:256], PX2[:, 0:128], True, True, reads=[rw], writes=[rp2_], signal=False)
                    self.mm(ps2_[:, 128:256], PR[:, 0:128], PX2[:, 128:256], True, True, reads=[rw], writes=[rp2_])
                    kb.op("dve", lambda: V.tensor_tensor(out=PR, in0=PR, in1=ps2_[:, 0:256], op=ALU.subtract), reads=[rp2_, rw], writes=[rw])
                yield
                psW, rpW = self.next_ps()
                self.mm(psW[:, 0:128], wk["kbe"], wk["RT"], True, True, reads=[rw], writes=[rpW], signal=False)
                self.mm(psW[:, 128:256], wk["RT"], wk["vb"], True, True, reads=[rw], writes=[rpW])
                kb.op("act", lambda: A.copy(out=sl["WU"], in_=psW[:, 0:256]), reads=[rpW], writes=[rsl])

            def scan(ci, step):
                ch = chains[ci]
                tt = ch["tiles"][step]
                sl = slots[ci][step % len(slots[ci])]
                rsl = sl["r"]
                tcs = slice(tt * 128, (tt + 1) * 128)
                for half in ((0, 1) if ch["dir"] == 0 else (1, 0)):
                    hs = slice(half * 64, (half + 1) * 64)
                    hcs = slice(tt * 128 + half * 64, tt * 128 + (half + 1) * 64)
                    yield
                    psP, rpP = self.next_ps()
                    self.mm(psP[:, 0:128], sl["WT"], Sst[ci], True, True, reads=[rsl, r_S[ci]], writes=[rpP])
                    kb.op("dve", lambda: V.tensor_tensor(out=vnew[ci][hs, :], in0=sl["U"][hs, :], in1=psP[hs, 0:128], op=ALU.subtract),
                          reads=[rsl, rpP], writes=[r_vnew[ci]])
                    yield
                    psO, rpO = self.next_ps()
                    self.mm(psO[:, 0:64], Sst[ci], sl["qdT"][:, hs], True, False, reads=[rsl, r_S[ci]], writes=[rpO])
                    self.mm(psO[:, 0:64], vnew[ci][hs, :], sl["aqkT"][hs, hs], False, True, reads=[rsl, r_vnew[ci]], writes=[rpO])
                    psS, rpS = self.next_ps()
                    self.mm(psS[:, 0:128], sl["ktail"][hs, :], vnew[ci][hs, :], True, True, reads=[rsl, r_vnew[ci]], writes=[rpS])
                    okey = (tt, half)
                    if okey not in owritten:
                        owritten.add(okey)
                        kb.op("act", lambda: A.copy(out=oT[:, hcs], in_=psO[:, 0:64]), reads=[rpO], writes=[r_oT[tt]])
                    else:
                        kb.op("dve", lambda: V.tensor_tensor(out=oT[:, hcs], in0=oT[:, hcs], in1=psO[:, 0:64], op=ALU.add), reads=[rpO, r_oT[tt]], writes=[r_oT[tt]])
                    kb.op("dve", lambda: V.scalar_tensor_tensor(out=Sst[ci], in0=Sst[ci], scalar=sl["gl"][:, half:half + 1], in1=psS[:, 0:128], op0=ALU.mult, op1=ALU.add),
                          reads=[rpS, rsl, r_S[ci]], writes=[r_S[ci]])
                if ch["seq"] is not None and step == len(ch["tiles"]) - 1:
                    kb.dma("sp", d["ns_out"][ch["seq"], j, ch["dir"], h], Sst[ci], reads=[r_S[ci]], writes=[self.r_nsout])

            owritten = set()

            def run_interleaved(gens):
                active = list(gens)
                while active:
                    for g_ in list(active):
                        try:
                            next(g_)
                        except StopIteration:
                            active.remove(g_)
            NW = len(works)
            nI = [0] * NCHAIN
            nS = [0] * NCHAIN
            nT = [len(chains[ci]["tiles"]) for ci in range(NCHAIN)]
            while any(nS[ci] < nT[ci] for ci in range(NCHAIN)):
                gens = []
                scan_ready = [ci for ci in range(NCHAIN) if nS[ci] < nI[ci]]
                cand = []
                for ci in range(NCHAIN):
                    st = nI[ci]
                    while st < nT[ci] and st - nS[ci] < len(slots[ci]):
                        cand.append((st - nS[ci], ci, st))
                        st += 1
                cand.sort()
                picked = cand[:NW]
                for (_, ci, st) in picked:
                    gens.append(intra(ci, st))
                for ci in scan_ready:
                    gens.append(scan(ci, nS[ci]))
                run_interleaved(gens)
                for (_, ci, st) in picked:
                    nI[ci] = max(nI[ci], st + 1)
                for ci in scan_ready:
                    nS[ci] += 1
            kb.op("act", lambda: A.activation(out=sqb, in_=oT, func=AF.Square), reads=r_oT, writes=[r_sqb])
            for b in range(NTB):
                ps, rp = self.next_ps()
                self.mm(ps[:], self.onesb[:], sqb[:, b * 512:(b + 1) * 512], True, True, reads=[r_sqb, self.r_const], writes=[rp])
                rs, rrs = self.next_rstd()
                kb.op("dve", lambda: V.tensor_scalar(out=rs[:], in0=ps[:], scalar1=1.0 / 128.0, scalar2=EPS, op0=ALU.mult, op1=ALU.add), reads=[rp], writes=[rrs])
                kb.op("act", lambda: A.activation(out=rs[:], in_=rs[:], func=AF.Sqrt), reads=[rrs], writes=[rrs])
                kb.op("dve", lambda: V.reciprocal(out=rs[:], in_=rs[:]), reads=[rrs], writes=[rrs])
                t0, r0 = self.next_tmp()
                kb.op("dve", lambda: V.scalar_tensor_tensor(out=t0[:], in0=oT[:, b * 512:(b + 1) * 512], scalar=onorm, in1=rs[:], op0=ALU.mult, op1=ALU.mult),
                      reads=r_oT[b * 4:(b + 1) * 4] + [rrs, r_gpar], writes=[r0])
                kb.op("dve", lambda: V.tensor_tensor(out=og[:, h, b * 512:(b + 1) * 512], in0=t0[:], in1=gateT[:, b * 512:(b + 1) * 512], op=ALU.mult),
                      reads=[r0, r_gate], writes=[r_og[h]])
        self.inherit(allx, [r_pp, r_cv, r_q, r_k, r_v, r_gate, r_sqb, r_ktok, r_vtok, r_ab, r_gpar] + r_oT + r_S + r_works + extra_slots)
        if False:
            for b in range(NTB):
                t0, r0 = self.next_tmp()
                kb.op("dve", lambda: V.tensor_copy(out=t0[:], in_=og[:, 0, b * 512:(b + 1) * 512]), reads=[r_og[0]], writes=[r0])
                kb.dma("sp", d["dbg"][5, :, b * 512:(b + 1) * 512], t0[:], reads=[r0], writes=[self.r_nsout])
        self.inherit([self.r_wslot[0]], [works[2]["r"]])
        self.inherit([self.r_wslot[1]], [works[3]["r"]])
        self.out_proj(l, d["gdn_w_out"][j], og, r_og, None, None)
        if False:
            pass
        self.inherit([self.r_big], allr2 + r_works)
        self.inherit([self.r_wslot[4]], [r_masks] + r_vnew)


    def out_proj(self, l, wdram, src, r_src, bias_fn, r_bias):
        nc, kb = self.nc, self.kb
        V, A, P = nc.vector, nc.scalar, nc.tensor
        d = self.dr
        allx = [r for rr in self.r_x for r in rr]
        kb.dma("sp", self.xT[:].rearrange("p c t -> p (c t)"), d["xsp"][:, :], writes=allx)
        wo = []
        for hlf in range(2):
            wv = self.wslots[hlf][:, 0:NCH * 512].rearrange("p (c n) -> p c n", c=NCH)
            kb.dma("pool", wv, wdram[:, hlf * 512:(hlf + 1) * 512].rearrange("(c p) n -> p c n", p=128), writes=[self.r_wslot[hlf]])
            wo.append(wv)
        for c in range(NCH):
            for b in range(NTB):
                w = 0 if b < 2 else 1
                ps, rp = self.next_ps()
                for cb in range(NCH):
                    self.mm(ps[:], wo[c // 4][:, cb, (c % 4) * 128:(c % 4 + 1) * 128], src[:, cb, b * 512:(b + 1) * 512],
                            start=(cb == 0), stop=(cb == NCH - 1), reads=[self.r_wslot[c // 4], r_src[cb]], writes=[rp])
                t0, r0 = self.next_tmp()
                if bias_fn is not None:
                    kb.op("dve", lambda: V.tensor_scalar(out=t0[:], in0=ps[:], scalar1=bias_fn(c), scalar2=self.mod(l, w, 2, c), op0=ALU.add, op1=ALU.mult),
                          reads=[rp, r_bias, self.r_mods[l]], writes=[r0])
                else:
                    kb.op("dve", lambda: V.tensor_scalar(out=t0[:], in0=ps[:], scalar1=self.mod(l, w, 2, c), scalar2=None, op0=ALU.mult),
                          reads=[rp, self.r_mods[l]], writes=[r0])
                xs = self.xT[:, c, b * 512:(b + 1) * 512]
                kb.op("dve", lambda: V.tensor_tensor(out=xs, in0=xs, in1=t0[:], op=ALU.add), reads=[r0, self.r_x[c][b]], writes=[self.r_x[c][b]])

    def inherit(self, dsts, srcs):
        evs = []
        for sres in srcs:
            if sres.w is not None:
                evs.append(sres.w)
            evs.extend(sres.rs)
        best = {}
        for ev in evs:
            k = id(ev[0])
            if k not in best or best[k][1] < ev[1]:
                best[k] = ev
        for dres in dsts:
            own = ([dres.w] if dres.w is not None else []) + list(dres.rs)
            b2 = dict(best)
            for ev in own:
                k = id(ev[0])
                if k not in b2 or b2[k][1] < ev[1]:
                    b2[k] = ev
            dres.w = None
            dres.rs = list(b2.values())

    def sin3(self, buf, rbuf, npart, ncol):
        nc, kb = self.nc, self.kb
        V, A = nc.vector, nc.scalar
        q, rq = self.next_tmp()
        bv = buf[0:npart, 0:ncol]
        qv = q[0:npart, 0:ncol]
        kb.op("act", lambda: A.activation(out=bv, in_=bv, func=AF.Sin, scale=1.0 / 3.0), reads=[rbuf], writes=[rbuf])
        kb.op("dve", lambda: V.tensor_tensor(out=qv, in0=bv, in1=bv, op=ALU.mult), reads=[rbuf], writes=[rq])
        kb.op("dve", lambda: V.tensor_scalar(out=qv, in0=qv, scalar1=-4.0, scalar2=3.0, op0=ALU.mult, op1=ALU.add), reads=[rq], writes=[rq])
        kb.op("dve", lambda: V.tensor_tensor(out=bv, in0=bv, in1=qv, op=ALU.mult), reads=[rq, rbuf], writes=[rbuf])

    def hyena_layer(self, l):
        nc, kb = self.nc, self.kb
        V, A, P, G = nc.vector, nc.scalar, nc.tensor, nc.gpsimd
        d = self.dr
        j = l // 2
        self.norm_mod(l, 2 * l, 0, 1)
        allx = [r for rr in self.r_x for r in rr]
        kb.dma("sp", d["xsp"][:, :], self.xT[:].rearrange("p c t -> p (c t)"), reads=allx)
        R1 = self.xT[:].rearrange("p c t -> p (c t)")
        PW = 1544
        pp = [R1[:, i * PW:(i + 1) * PW] for i in range(3)]
        o1 = 3 * PW
        u = [R1[:, o1 + i * T:o1 + (i + 1) * T] for i in range(3)]
        o1 += 3 * T
        zt = R1[:, o1:o1 + 768].bitcast(BF16).rearrange("p (t c) -> p t c", t=12)
        hidb = R1[:, o1 + 768:o1 + 768 + 640].bitcast(BF16)
        r_hidb = Res("hidb")
        r_pp = [Res(f"pp{i}") for i in range(3)]
        r_u = [Res(f"u{i}") for i in range(3)]
        r_zt = Res("zt")
        self.inherit(r_pp + r_u + [r_zt, r_hidb], allx)
        R2 = self.big
        zfin = R2[:, 0:NCH * T].rearrange("p (c t) -> p c t", c=NCH)
        o2 = NCH * T
        hid_lat = R2[:, o2:o2 + 2048].bitcast(F32); o2 += 2048
        hid_ctx = R2[:, o2:o2 + 512].bitcast(F32); o2 += 512
        delta_b = R2[:, o2:o2 + 2048].bitcast(F32); o2 += 2048
        fv = {}
        for nm in ("A0", "B0", "hf0", "hf1", "nhf1", "hb0", "nhb0", "hb1"):
            fv[nm] = R2[:, o2:o2 + 512].rearrange("p (q c) -> p q c", q=4); o2 += 512
        for nm in ("Ac", "Bc"):
            fv[nm] = R2[:, o2:o2 + 256].rearrange("p (q c) -> p q c", q=2); o2 += 256
        spec = R2[:, o2:o2 + 12288].bitcast(F32)
        o2 += 12288
        assert o2 <= 33792
        SP = [spec[:, i * 512:(i + 1) * 512] for i in range(10)]
        Yb = spec[:, 5120:6144].bitcast(BF16)
        Y = [Yb[:, i * 512:(i + 1) * 512] for i in range(4)]
        r_SP = [Res(f"sp{i}") for i in range(10)]
        r_Y = [Res(f"Y{i}") for i in range(4)]
        r_zfin = [Res(f"zfin{c}") for c in range(NCH)]
        r_hid = Res("hid"); r_delta = Res("delta"); r_fv = Res("fv")
        self.inherit(r_SP + r_Y + r_zfin + [r_hid, r_delta, r_fv], [self.r_big])
        if not hasattr(self, "hy_alloc"):
            self.hy_alloc = dict(
                vecT=kb.sb("s_hyvecT", [128, 120], F32), vec64=kb.sb("s_hyvec64", [64, 4], F32),
                w1=kb.sb("s_hyw1", [33, 64], F32), w2=kb.sb("s_hyw2", [64, 64], F32),
                tneg=kb.sb("s_hytneg", [128, 10], F32), mask0=kb.sb("s_hymask0", [128, 1], F32),
                w3cb=[kb.sb(f"s_hyw3cb_{i}", [64, 4, 128], F32) for i in range(2)],
                r_w3cb=[Res("w3cb0"), Res("w3cb1")], r_par=Res("hypar"))
        ha = self.hy_alloc
        vecT, vec64, w1, w2, tneg, mask0, w3cb, r_w3cb, r_par = (ha[k] for k in
            ("vecT", "vec64", "w1", "w2", "tneg", "mask0", "w3cb", "r_w3cb", "r_par"))
        kb.dma("sp", vecT[:], d["hy_vecT"][j], writes=[r_par])
        kb.dma("sp", vec64[:, 0:3], d["hy_vec64"][j], writes=[r_par])
        kb.dma("sp", w1[:], d["hy_f_w1"][j], writes=[r_par])
        kb.dma("sp", w2[:], d["hy_f_w2"][j], writes=[r_par])
        kb.dma("sp", tneg[:], d["hytneg"][:, :], writes=[r_par])
        kb.dma("sp", mask0[:], d["mask0"][:, :], writes=[r_par])
        kb.dma("sp", delta_b, d["hydelta"][0:1, :].partition_broadcast(128), writes=[r_delta])
        kb.op("dve", lambda: V.tensor_tensor(out=vec64[:, 3:4], in0=vec64[:, 0:1], in1=vec64[:, 2:3], op=ALU.mult), reads=[r_par], writes=[r_par])
        b_in = lambda blk: vecT[:, blk:blk + 1]
        cw = lambda k, blk: vecT[:, 24 + k * 24 + blk:24 + k * 24 + blk + 1]
        skp = lambda n, c: vecT[:, 96 + n * 8 + c:96 + n * 8 + c + 1]
        b_out = lambda c: vecT[:, 112 + c:113 + c]
        tabs = []
        for i in range(6):
            sl = self.wslots[i // 2]
            tv = sl[:, (i % 2) * 2048:(i % 2 + 1) * 2048].rearrange("p (q k) -> p q k", q=4)
            kb.dma("pool", tv, d["hytab"][i].rearrange("(q p) k -> p q k", p=128), writes=[self.r_wslot[i // 2]])
            tabs.append(tv)
        Cf, Sf, Cr, Sr, Ci, Si = tabs
        r_tab = [self.r_wslot[0], self.r_wslot[0], self.r_wslot[1], self.r_wslot[1], self.r_wslot[2], self.r_wslot[2]]
        rCf, rSf, rCr, rSr, rCi, rSi = r_tab
        for (zname, L, hid) in (("zemb_lat", LLAT, hid_lat), ("zemb_ctx", LCTX, hid_ctx)):
            for c0 in range(0, L, 512):
                ncol = min(512, L - c0)
                ze, rze = self.next_tmp()
                kb.dma("sp", ze[0:33, 0:ncol], d[zname][:, c0:c0 + ncol], writes=[rze])
                ps, rp = self.next_ps()
                self.mm(ps[0:64, 0:ncol], w1[:, :], ze[0:33, 0:ncol], True, True, reads=[rze, r_par], writes=[rp])
                h1, rh1 = self.next_tmp()
                kb.op("dve", lambda: V.tensor_scalar(out=h1[0:64, 0:ncol], in0=ps[0:64, 0:ncol], scalar1=vec64[:, 0:1], scalar2=vec64[:, 2:3],
                                                     op0=ALU.add, op1=ALU.mult), reads=[rp, r_par], writes=[rh1])
                self.sin3(h1, rh1, 64, ncol)
                ps2, rp2 = self.next_ps()
                self.mm(ps2[0:64, 0:ncol], w2[:, :], h1[0:64, 0:ncol], True, True, reads=[rh1, r_par], writes=[rp2])
                hv = hid[0:64, c0:c0 + ncol]
                kb.op("dve", lambda: V.tensor_scalar(out=hv, in0=ps2[0:64, 0:ncol], scalar1=vec64[:, 1:2], scalar2=vec64[:, 2:3],
                                                     op0=ALU.add, op1=ALU.mult), reads=[rp2, r_par], writes=[r_hid])
                kb.op("act", lambda: A.activation(out=hv, in_=hv, func=AF.Sin, scale=1.0 / 3.0), reads=[r_hid], writes=[r_hid])
                q, rq = self.next_tmp()
                qv = q[0:64, 0:ncol]
                kb.op("dve", lambda: V.tensor_tensor(out=qv, in0=hv, in1=hv, op=ALU.mult), reads=[r_hid], writes=[rq])
                kb.op("dve", lambda: V.tensor_scalar(out=qv, in0=qv, scalar1=-4.0, scalar2=3.0, op0=ALU.mult, op1=ALU.add), reads=[rq], writes=[rq])
                kb.op("dve", lambda: V.tensor_tensor(out=hv, in0=hv, in1=qv, op=ALU.mult), reads=[rq, r_hid], writes=[r_hid])
        kb.op("act", lambda: A.copy(out=hidb[0:64, 0:LLAT], in_=hid_lat[0:64, :]), reads=[r_hid], writes=[r_hidb])
        kb.op("act", lambda: A.copy(out=hidb[0:64, LLAT:LLAT + LCTX], in_=hid_ctx[0:64, :]), reads=[r_hid], writes=[r_hidb])
        for i in range(3):
            kb.op("dve", lambda: V.memset(pp[i], 0.0), writes=[r_pp[i]])
        segs = [(1, LLAT, 0), (1027, LCTX, LLAT), (1285, LCTX, LLAT + LCTX)]
        evi = [0]

        def evac(out, in_, reads, writes):
            evi[0] += 1
            if evi[0] % 2 == 0:
                kb.op("act", lambda: A.copy(out=out, in_=in_), reads=reads, writes=writes)
            else:
                kb.op("dve", lambda: V.tensor_copy(out=out, in_=in_), reads=reads, writes=writes)

        def spectrum(dst, rdst, terms):
            ps, rp = self.next_ps()
            tot = sum(t[4] for t in terms)
            for kb_ in range(4):
                i = 0
                for (tab, rtab, dat, rdat, nq) in terms:
                    for q in range(nq):
                        self.mm(ps[:, kb_ * 128:(kb_ + 1) * 128], tab[:, q, kb_ * 128:(kb_ + 1) * 128], dat[:, q, :],
                                start=(i == 0), stop=(i == tot - 1), reads=[rtab, rdat], writes=[rp], signal=(kb_ == 3 and i == tot - 1))
                        i += 1
            evac(dst, ps[:], [rp], [rdst])

        def cprod(dst, rdst, terms):
            acc, racc = self.next_tmp()
            n = len(terms)
            for i, (a, ra, b, rb, sg) in enumerate(terms):
                if i == 0:
                    kb.op("dve", lambda: V.tensor_tensor(out=acc[:], in0=a, in1=b, op=ALU.mult), reads=[ra, rb], writes=[racc])
                else:
                    t2, r2 = self.next_tmp()
                    kb.op("dve", lambda: V.tensor_tensor(out=t2[:], in0=a, in1=b, op=ALU.mult), reads=[ra, rb], writes=[r2])
                    o = dst if i == n - 1 else acc[:]
                    ro = rdst if i == n - 1 else racc
                    kb.op("dve", lambda: V.tensor_tensor(out=o, in0=acc[:], in1=t2[:], op=(ALU.add if sg > 0 else ALU.subtract)),
                          reads=[racc, r2], writes=[ro])

        for cb in range(NCH):
            wsl = self.wslots[3 + cb % 2]
            rws = self.r_wslot[3 + cb % 2]
            win = wsl[:, 0:NCH * 3 * 128].rearrange("p (c g n) -> p c g n", c=NCH, g=3)
            for g in range(3):
                kb.dma("pool", win[:, :, g, :], d["hy_w_in"][j, :, g * D + cb * 128:g * D + (cb + 1) * 128].rearrange("(c p) n -> p c n", p=128), writes=[rws])
            w3 = w3cb[cb % 2]
            rw3 = r_w3cb[cb % 2]
            w3 = w3[:].rearrange("r g c -> r (g c)").bitcast(BF16)[:, 0:512].rearrange("r (g c) -> r g c", g=4)
            kb.dma("pool", w3, d["hy_f_w3"][j].rearrange("r (g c) -> r g c", g=4)[:, :, cb * 128:(cb + 1) * 128], writes=[rw3])
            for g in range(3):
                blk = g * 8 + cb
                for b in range(NTB):
                    ps, rp = self.next_ps()
                    for c in range(NCH):
                        self.mm(ps[:], win[:, c, g, :], self.hT[:, c, b * 512:(b + 1) * 512], start=(c == 0), stop=(c == NCH - 1),
                                reads=[rws, self.r_h[c][b]], writes=[rp])
                    if b < 2:
                        dsts = [(pp[g][:, 1 + b * 512:1 + (b + 1) * 512], ps[:])]
                    else:
                        dsts = [(pp[g][:, 1027:1027 + 256], ps[:, 0:256]), (pp[g][:, 1285:1285 + 256], ps[:, 256:512])]
                    for (o, i_) in dsts:
                        kb.op("act", lambda: A.activation(out=o, in_=i_, func=AF.Identity, bias=b_in(blk), scale=1.0),
                              reads=[rp, r_par], writes=[r_pp[g]])
                for (s0, Ls, o0) in segs:
                    uo = u[g][:, o0:o0 + Ls]
                    kb.op("dve", lambda: V.tensor_scalar(out=uo, in0=pp[g][:, s0 - 1:s0 - 1 + Ls], scalar1=cw(0, blk), scalar2=None, op0=ALU.mult),
                          reads=[r_pp[g], r_par], writes=[r_u[g]])
                    for k in (1, 2):
                        kb.op("dve", lambda: V.scalar_tensor_tensor(out=uo, in0=pp[g][:, s0 - 1 + k:s0 - 1 + k + Ls], scalar=cw(k, blk), in1=uo,
                                                                    op0=ALU.mult, op1=ALU.add), reads=[r_pp[g], r_par, r_u[g]], writes=[r_u[g]])
            for n in range(2):
                z, rz = u[n], r_u[n]
                for g4 in range(3):
                    ps, rp = self.next_ps()
                    for i in range(4):
                        tt = g4 * 4 + i
                        kb.op("pe", lambda: P.transpose(ps[:, i * 128:(i + 1) * 128], z[:, tt * 128:(tt + 1) * 128], self.ident[:]),
                              reads=[rz, self.r_const], writes=[rp], signal=(i == 3))
                    evac(zt[:, g4 * 4:(g4 + 1) * 4, :], ps[:].rearrange("p (t c) -> p t c", t=4), [rp], [r_zt])
                for q in range(10):
                    lat = q < 8
                    hidv = hidb[0:64, q * 128:(q + 1) * 128]
                    ps, rp = self.next_ps()
                    for dr_ in range(2):
                        self.mm(ps[:, dr_ * 128:(dr_ + 1) * 128], hidv, w3[:, n * 2 + dr_, :], True, True, reads=[r_hidb, rw3], writes=[rp], signal=(dr_ == 1))
                    win_, rwin = self.next_tmp()
                    kb.op("act", lambda: A.activation(out=win_[:, 0:128], in_=delta_b[:, cb * 128:(cb + 1) * 128], func=AF.Exp, scale=tneg[:, q:q + 1]),
                          reads=[r_delta, r_par], writes=[rwin])
                    hh, rhh = self.next_tmp()
                    kb.op("dve", lambda: V.tensor_tensor(out=hh[:, 0:256].rearrange("p (a c) -> p a c", a=2), in0=ps[:, 0:256].rearrange("p (a c) -> p a c", a=2),
                                                         in1=win_[:, 0:128].unsqueeze(1).to_broadcast([128, 2, 128]), op=ALU.mult),
                          reads=[rp, rwin], writes=[rhh])
                    hf = hh[:, 0:128]
                    hb = hh[:, 128:256]
                    if q == 0 or q == 8:
                        kb.op("dve", lambda: V.tensor_scalar(out=hb, in0=hb, scalar1=mask0[:, 0:1], scalar2=None, op0=ALU.mult), reads=[rhh, r_par], writes=[rhh])
                    def wr(nm, qq, fn):
                        kb.op("dve", fn(fv[nm][:, qq, :]), reads=[rhh], writes=[r_fv])
                    if q < 4 or q >= 8:
                        nmA, nmB, qq = ("A0", "B0", q) if lat else ("Ac", "Bc", q - 8)
                        kb.op("dve", lambda: V.tensor_tensor(out=fv[nmA][:, qq, :], in0=hf, in1=hb, op=ALU.add), reads=[rhh], writes=[r_fv])
                        kb.op("dve", lambda: V.tensor_tensor(out=fv[nmB][:, qq, :], in0=hb, in1=hf, op=ALU.subtract), reads=[rhh], writes=[r_fv])
                        if lat:
                            kb.op("act", lambda: A.copy(out=fv["hf0"][:, q, :], in_=hf), reads=[rhh], writes=[r_fv])
                            kb.op("act", lambda: A.copy(out=fv["hb0"][:, q, :], in_=hb), reads=[rhh], writes=[r_fv])
                            kb.op("act", lambda: A.mul(out=fv["nhb0"][:, q, :], in_=hb, mul=-1.0), reads=[rhh], writes=[r_fv])
                    else:
                        kb.op("act", lambda: A.copy(out=fv["hf1"][:, q - 4, :], in_=hf), reads=[rhh], writes=[r_fv])
                        kb.op("act", lambda: A.copy(out=fv["hb1"][:, q - 4, :], in_=hb), reads=[rhh], writes=[r_fv])
                        kb.op("act", lambda: A.mul(out=fv["nhf1"][:, q - 4, :], in_=hf, mul=-1.0), reads=[rhh], writes=[r_fv])
                zl = [zt[:, 0:4, :], zt[:, 4:8, :]]
                spectrum(SP[0], r_SP[0], [(Cf, rCf, zl[0], r_zt, 4)])
                spectrum(SP[1], r_SP[1], [(Sf, rSf, zl[0], r_zt, 4)])
                spectrum(SP[2], r_SP[2], [(Cf, rCf, zl[1], r_zt, 4)])
                spectrum(SP[3], r_SP[3], [(Sf, rSf, zl[1], r_zt, 4)])
                spectrum(SP[4], r_SP[4], [(Cf, rCf, fv["A0"], r_fv, 4)])
                spectrum(SP[5], r_SP[5], [(Sf, rSf, fv["B0"], r_fv, 4)])
                spectrum(SP[6], r_SP[6], [(Cf, rCf, fv["hf1"], r_fv, 4), (Cr, rCr, fv["hf0"], r_fv, 4)])
                spectrum(SP[7], r_SP[7], [(Sf, rSf, fv["nhf1"], r_fv, 4), (Sr, rSr, fv["hf0"], r_fv, 4)])
                spectrum(SP[8], r_SP[8], [(Cf, rCf, fv["hb1"], r_fv, 4), (Cr, rCr, fv["hb0"], r_fv, 4)])
                spectrum(SP[9], r_SP[9], [(Sf, rSf, fv["hb1"], r_fv, 4), (Sr, rSr, fv["nhb0"], r_fv, 4)])
                S_ = lambda i: (SP[i], r_SP[i])
                def T4(a, b, sg):
                    return (SP[a], r_SP[a], SP[b], r_SP[b], sg)
                cprod(Y[0], r_Y[0], [T4(0, 4, 1), T4(1, 5, 1), T4(2, 8, 1), T4(3, 9, 1)])
                cprod(Y[1], r_Y[1], [T4(0, 5, 1), T4(1, 4, -1), T4(2, 9, 1), T4(3, 8, -1)])
                cprod(Y[2], r_Y[2], [T4(0, 6, 1), T4(1, 7, 1), T4(2, 4, 1), T4(3, 5, 1)])
                cprod(Y[3], r_Y[3], [T4(0, 7, 1), T4(1, 6, -1), T4(2, 5, 1), T4(3, 4, -1)])

                def inverse_and_gate(yre, ryre, yim, ryim, ncol, col0):
                    ps, rp = self.next_ps()
                    yre3 = yre.rearrange("p (q c) -> p q c", q=4)
                    yim3 = yim.rearrange("p (q c) -> p q c", q=4)
                    for q in range(4):
                        self.mm(ps[:, 0:ncol], yre3[:, q, :], Ci[:, q, 0:ncol], start=(q == 0), stop=False, reads=[ryre, rCi], writes=[rp])
                    for q in range(4):
                        self.mm(ps[:, 0:ncol], yim3[:, q, :], Si[:, q, 0:ncol], start=False, stop=(q == 3), reads=[ryim, rSi], writes=[rp])
                    t0, r0 = self.next_tmp()
                    zs = z[:, col0:col0 + ncol]
                    kb.op("dve", lambda: V.scalar_tensor_tensor(out=t0[:, 0:ncol], in0=zs, scalar=skp(n, cb), in1=ps[:, 0:ncol], op0=ALU.mult, op1=ALU.add),
                          reads=[rz, rp, r_par], writes=[r0])
                    un = u[n + 1][:, col0:col0 + ncol]
                    kb.op("dve", lambda: V.tensor_tensor(out=un, in0=t0[:, 0:ncol], in1=un, op=ALU.mult), reads=[r0, r_u[n + 1]], writes=[r_u[n + 1]])

                inverse_and_gate(Y[0], r_Y[0], Y[1], r_Y[1], 512, 0)
                inverse_and_gate(Y[2], r_Y[2], Y[3], r_Y[3], 512, 512)
                zc = [zt[:, 8:10, :], zt[:, 10:12, :]]
                spectrum(SP[0], r_SP[0], [(Cf, rCf, zc[0], r_zt, 2)])
                spectrum(SP[1], r_SP[1], [(Sf, rSf, zc[0], r_zt, 2)])
                spectrum(SP[2], r_SP[2], [(Cf, rCf, zc[1], r_zt, 2)])
                spectrum(SP[3], r_SP[3], [(Sf, rSf, zc[1], r_zt, 2)])
                spectrum(SP[4], r_SP[4], [(Cf, rCf, fv["Ac"], r_fv, 2)])
                spectrum(SP[5], r_SP[5], [(Sf, rSf, fv["Bc"], r_fv, 2)])
                cprod(Y[0], r_Y[0], [T4(0, 4, 1), T4(1, 5, 1)])
                cprod(Y[1], r_Y[1], [T4(0, 5, 1), T4(1, 4, -1)])
                cprod(Y[2], r_Y[2], [T4(2, 4, 1), T4(3, 5, 1)])
                cprod(Y[3], r_Y[3], [T4(2, 5, 1), T4(3, 4, -1)])
                inverse_and_gate(Y[0], r_Y[0], Y[1], r_Y[1], 256, LLAT)
                inverse_and_gate(Y[2], r_Y[2], Y[3], r_Y[3], 256, LLAT + LCTX)
            kb.op("act", lambda: A.copy(out=zfin[:, cb, :], in_=u[2]), reads=[r_u[2]], writes=[r_zfin[cb]])
        self.inherit(allx, r_pp + r_u + [r_zt, r_hidb])
        self.out_proj(l, d["hy_w_out"][j], zfin, r_zfin, b_out, r_par)
        self.inherit([self.r_big], r_SP + r_Y + r_zfin + [r_hid, r_delta, r_fv])

    def final_out(self):
        nc, kb = self.nc, self.kb
        V, A, P = nc.vector, nc.scalar, nc.tensor
        d = self.dr
        self.out_res = []
        gidx = 2 * DEPTH
        for b in range(NTB):
            ps, rp = self.next_ps()
            for c in range(NCH):
                sq, rsq = self.next_tmpb()
                kb.op("act", lambda: A.activation(out=sq[:], in_=self.xT[:, c, b * 512:(b + 1) * 512], func=AF.Square),
                      reads=[self.r_x[c][b]], writes=[rsq])
                self.mm(ps[:], self.onesb[:], sq[:], start=(c == 0), stop=(c == NCH - 1), reads=[rsq, self.r_const], writes=[rp], signal=True)
            rstd, rr = self.next_rstd()
            kb.op("dve", lambda: V.tensor_scalar(out=rstd[:], in0=ps[:], scalar1=1.0 / D, scalar2=EPS, op0=ALU.mult, op1=ALU.add),
                  reads=[rp], writes=[rr])
            kb.op("act", lambda: A.activation(out=rstd[:], in_=rstd[:], func=AF.Sqrt), reads=[rr], writes=[rr])
            kb.op("dve", lambda: V.reciprocal(out=rstd[:], in_=rstd[:]), reads=[rr], writes=[rr])
            for c in range(NCH):
                xs = self.xT[:, c, b * 512:(b + 1) * 512]
                kb.op("dve", lambda: V.scalar_tensor_tensor(out=xs, in0=xs, scalar=self.gvec[:, gidx, c:c + 1],
                                                            in1=rstd[:], op0=ALU.mult, op1=ALU.mult),
                      reads=[self.r_x[c][b], rr, self.r_gvec], writes=[self.r_x[c][b]])
        for tt in range(T // 128):
            b = tt // 4
            lat = tt < 8
            dst = d["y_lat"][tt * 128:(tt + 1) * 128, :] if lat else d["y_ctx"][(tt - 8) * 128:(tt - 7) * 128, :]
            for half in range(2):
                ps, rp = self.next_ps()
                for j in range(4):
                    c = half * 4 + j
                    kb.op("pe", lambda: P.transpose(ps[:, j * 128:(j + 1) * 128], self.xT[:, c, tt * 128:(tt + 1) * 128], self.ident[:]),
                          reads=[self.r_x[c][b], self.r_const], writes=[rp], signal=(j == 3))
                t0, r0 = self.next_tmp()
                if half == 0:
                    kb.op("act", lambda: A.copy(out=t0[:], in_=ps[:]), reads=[rp], writes=[r0])
                else:
                    kb.op("dve", lambda: V.tensor_copy(out=t0[:], in_=ps[:]), reads=[rp], writes=[r0])
                ro = Res("out")
                kb.dma("sp", dst[:, half * 512:(half + 1) * 512], t0[:], reads=[r0], writes=[ro])
                self.out_res.append(ro)


_CACHE = {}
LAST_DBG = None


def get_prog(skip=()):
    key = tuple(sorted(skip))
    if key not in _CACHE:
        _CACHE[key] = Prog(skip)
    return _CACHE[key]


def make_in_maps(inp, n=8):
    hc = host_consts()
    f = lambda a: np.ascontiguousarray(np.asarray(a, dtype=np.float32))
    shared = {
        "pos": hc["pos"], "ident": hc["ident"],
        "ada_w": f(inp["ada_w"]),
        "ffn_w_gu": f(inp["ffn_w_gu"]), "ffn_w_down": f(inp["ffn_w_down"]),
    }
    shared["gmasks"] = hc["gmasks"]
    for kk in ("gdn_w_in", "gdn_w_out"):
        shared[kk] = f(inp[kk])
    gv = []
    for j in range(2):
        gv.append(np.concatenate([f(inp["gdn_conv"][j]).reshape(5 * 24, 128).T, f(inp["gdn_onorm"][j]).reshape(128, 1)], axis=1))
    shared["gdn_vecT"] = np.ascontiguousarray(np.stack(gv, axis=0))
    shared["gdn_rows"] = np.ascontiguousarray(np.stack([f(inp["gdn_a_log"]).reshape(2, 16), f(inp["gdn_dt_bias"]).reshape(2, 16)], axis=1))
    for kk in ("hytab", "zemb_lat", "zemb_ctx", "hydelta", "hytneg", "mask0"):
        shared[kk] = hc[kk]
    for kk in ("hy_w_in", "hy_w_out", "hy_f_w1", "hy_f_w2", "hy_f_w3"):
        shared[kk] = f(inp[kk])
    shared["hy_vec64"] = np.ascontiguousarray(np.stack([f(inp["hy_f_b1"]), f(inp["hy_f_b2"]), f(inp["hy_freq"])], axis=2))
    hv = []
    for j in range(2):
        parts = [f(inp["hy_b_in"][j]).reshape(24, 128).T]
        parts.append(f(inp["hy_conv"][j]).reshape(3 * 24, 128).T)
        parts.append(f(inp["hy_skip"][j]).reshape(2 * 8, 128).T)
        parts.append(f(inp["hy_b_out"][j]).reshape(8, 128).T)
        hv.append(np.concatenate(parts, axis=1))
    shared["hy_vecT"] = np.ascontiguousarray(np.stack(hv, axis=0))
    fm = lambda v: np.ascontiguousarray(f(v).reshape(-1, 128).T)
    gl = []
    for l in range(DEPTH):
        gl.append(fm(inp["norm1_g"][l]))
        gl.append(fm(inp["norm2_g"][l]))
    gl.append(fm(inp["final_g"]))
    shared["gvecT"] = np.ascontiguousarray(np.stack(gl, axis=1))
    shared["adabT"] = np.ascontiguousarray(np.stack([fm(inp["ada_b"][l]) for l in range(DEPTH)], axis=1))
    maps = []
    xp = f(inp["x_prompt"])
    xs = f(inp["x_sample"])
    cc = f(inp["c"])
    cctx = f(inp["c_ctx"])
    for i in range(n):
        m = dict(shared)
        m["x_lat"] = np.ascontiguousarray(xs[i])
        m["x_ctx"] = np.ascontiguousarray(xp[2 * i:2 * i + 2].reshape(2 * LCTX, D))
        m["state_in"] = np.ascontiguousarray(f(inp["state_delta"])[i])
        m["cvecT"] = np.ascontiguousarray(np.stack([fm(cc[i]), fm(cctx)], axis=2))
        maps.append(m)
    return maps


def run(inp, skip=(), trace=False):
    prog = get_prog(skip)
    maps = make_in_maps(inp)
    res = run_bass_kernel_spmd(prog.nc, maps, core_ids=list(range(8)), trace=trace)
    y_lat = np.stack([r["y_lat"] for r in res.results], axis=0)
    y_ctx = np.concatenate([r["y_ctx"].reshape(2, LCTX, D) for r in res.results], axis=0)
    ns = np.concatenate([r["ns_out"] for r in res.results], axis=0)
    global LAST_DBG
    LAST_DBG = [r.get("dbg") for r in res.results]
    return (y_ctx.astype(np.float32), y_lat.astype(np.float32), ns.astype(np.float32)), res


def kernel(**inputs):
    (y_prompt, y_sample, new_state), _ = run(inputs)
    return (y_prompt, y_sample, new_state)
```

```python
import math
import os
import contextlib
import numpy as np
import concourse.bass as bass
import concourse.mybir as mybir
from concourse.bass_utils import run_bass_kernel_spmd

F32 = mybir.dt.float32
BF16 = mybir.dt.bfloat16
ALU = mybir.AluOpType
AF = mybir.ActivationFunctionType

D = 1024
NCH = 8
LLAT = 1024
LCTX = 256
T = LLAT + 2 * LCTX
NTB = 3
DFF = 2816
NFB = 22
DEPTH = 4
EPS = 1e-6
SAME_ENGINE_SYNC = True


class Res:
    __slots__ = ("w", "rs", "name", "excl")

    def __init__(self, name="", excl=False):
        self.w = None
        self.rs = []
        self.name = name
        self.excl = excl


class Eng:
    def __init__(self, name, h, sem):
        self.name = name
        self.h = h
        self.sem = sem
        self.count = 0
        self.waited = {}


class KB:
    def __init__(self, nc):
        self.nc = nc
        self.es = contextlib.ExitStack()
        self.engs = {}
        for name, h in (("pe", nc.tensor), ("act", nc.scalar), ("dve", nc.vector),
                        ("pool", nc.gpsimd), ("sp", nc.sync)):
            sem = self.es.enter_context(nc.semaphore("sem_" + name))
            self.engs[name] = Eng(name, h, sem)
        self.dsem = {}
        for q in ("sp", "pool"):
            sems = [self.es.enter_context(nc.semaphore(f"dsem_{q}{i}")) for i in range(20)]
            self.dsem[q] = {"sems": sems, "tot": [0] * len(sems), "i": 0}
        self.semkey = {}
        self.ninst = 0

    def sb(self, name, shape, dt):
        return self.es.enter_context(self.nc.sbuf_tensor(name, list(shape), dt))

    def ps(self, name, shape, dt):
        return self.es.enter_context(self.nc.psum_tensor(name, list(shape), dt))

    def _waits(self, eng, reads, writes):
        need = {}
        def add(ev):
            if ev is None:
                return
            sem, val, src = ev
            if src == eng.name:
                if eng.name == "pe" or not SAME_ENGINE_SYNC:
                    return
            k = id(sem)
            if k not in need or need[k][1] < val:
                need[k] = (sem, val, src)
        for r in reads:
            add(r.w)
        for w in writes:
            add(w.w)
            for ev in w.rs:
                add(ev)
        for k, (sem, val, src) in need.items():
            if eng.waited.get(k, 0) >= val:
                continue
            if src in self.engs:
                assert self.engs[src].count >= val, f"pending unsignaled dep {src} {val} > {self.engs[src].count}"
            eng.h.wait_ge(sem, val)
            eng.waited[k] = val
            self.ninst += 1

    def op(self, en, fn, reads=(), writes=(), signal=True):
        eng = self.engs[en]
        xr = [r for r in reads if r.excl]
        if xr:
            writes = list(writes) + [r for r in xr if r not in writes]
        self._waits(eng, reads, writes)
        ins = fn()
        self.ninst += 1
        if signal:
            eng.count += 1
            ins.then_inc(eng.sem, 1)
            ev = (eng.sem, eng.count, en)
        else:
            ev = (eng.sem, eng.count + 1, en)
        for r in reads:
            r.rs.append(ev)
        for w in writes:
            w.w = ev
            w.rs = []
        return ev

    def dma(self, q, out, in_, reads=(), writes=()):
        eng = self.engs[q]
        self._waits(eng, reads, writes)
        d = self.dsem[q]
        i = d["i"] % len(d["sems"])
        d["i"] += 1
        sem = d["sems"][i]
        k = id(sem)
        if d["tot"][i] > 0 and eng.waited.get(k, 0) < d["tot"][i]:
            eng.h.wait_ge(sem, d["tot"][i])
            eng.waited[k] = d["tot"][i]
        eng.h.dma_start(out=out, in_=in_).then_inc(sem, 16)
        self.ninst += 1
        d["tot"][i] += 16
        ev = (sem, d["tot"][i], "dma_" + q)
        for r in reads:
            r.rs.append(ev)
        for w in writes:
            w.w = ev
            w.rs = []
        return ev

    def finish(self, res_list):
        eng = self.engs["sp"]
        self._waits(eng, res_list, res_list)
        self.es.close()


def host_consts():
    c = {}
    c["ident"] = np.eye(128, dtype=np.float32)
    GRID_W = 64
    rows = LLAT // GRID_W
    r, col = np.meshgrid(np.arange(rows), np.arange(GRID_W), indexing="ij")
    quarter = D // 4
    omega = (1.0 / (10000.0 ** (np.arange(quarter, dtype=np.float32) / quarter))).astype(np.float32)

    def emb1d(p):
        a = p.reshape(-1, 1).astype(np.float32) * omega[None, :]
        return np.concatenate([np.sin(a), np.cos(a)], axis=-1)

    c["pos"] = np.concatenate([emb1d(r), emb1d(col)], axis=-1).astype(np.float32)
    B = 512
    k = np.arange(B, dtype=np.float64)
    om = 2.0 * np.pi * (k + 0.5) / (2 * B)
    jj = np.arange(B, dtype=np.float64)
    Cf = np.cos(jj[:, None] * om[None, :])
    Sf = np.sin(jj[:, None] * om[None, :])
    Cr = np.cos((B - jj)[:, None] * om[None, :]); Cr[0, :] = 0.0
    Sr = np.sin((B - jj)[:, None] * om[None, :]); Sr[0, :] = 0.0
    Ci = (2.0 / (2 * B)) * np.cos(om[:, None] * jj[None, :])
    Si = -(2.0 / (2 * B)) * np.sin(om[:, None] * jj[None, :])
    c["hytab"] = np.stack([Cf, Sf, Cr, Sr, Ci, Si], axis=0).astype(np.float32)
    def zemb(seq):
        t = np.linspace(0.0, 1.0, seq, dtype=np.float32)[:, None]
        wpos = (2.0 * math.pi / seq) * np.arange(seq, dtype=np.float32)[:, None]
        fb = np.linspace(1e-4, 15, 16, dtype=np.float32)[None, :]
        z = np.concatenate([t, np.cos(fb * wpos), -np.sin(fb * wpos)], axis=-1).astype(np.float32)
        return np.ascontiguousarray(z.T)
    c["zemb_lat"] = zemb(LLAT)
    c["zemb_ctx"] = zemb(LCTX)
    mx = math.log(1e-2) / 0.3
    mn = math.log(1e-2) / 1.5
    c["hydelta"] = np.abs(np.linspace(mn, mx, D, dtype=np.float32)).reshape(1, D).astype(np.float32)
    tn = np.zeros((128, 10), np.float32)
    for q in range(8):
        tn[:, q] = -(q * 128 + np.arange(128)) / np.float32(LLAT - 1)
    for q in range(2):
        tn[:, 8 + q] = -(q * 128 + np.arange(128)) / np.float32(LCTX - 1)
    tn = -np.stack([np.linspace(0.0, 1.0, LLAT, dtype=np.float32).reshape(8, 128).T] , 0)[0]
    tc = -np.linspace(0.0, 1.0, LCTX, dtype=np.float32).reshape(2, 128).T
    c["hytneg"] = np.ascontiguousarray(np.concatenate([tn, tc], axis=1).astype(np.float32))
    pi_, fi_ = np.meshgrid(np.arange(128), np.arange(128), indexing="ij")
    sb_ = (pi_ // 64) == (fi_ // 64)
    bd16 = (pi_ // 16) == (fi_ // 16)
    s1 = ((pi_ // 32) == (fi_ // 32)) & ~bd16
    s2 = ((pi_ // 64) == (fi_ // 64)) & ((pi_ // 32) != (fi_ // 32))
    c["gmasks"] = np.stack([(pi_ > fi_) & sb_, (fi_ >= pi_) & sb_, (pi_ < fi_) & sb_, (fi_ <= pi_) & sb_,
                            (pi_ <= fi_) & sb_, (pi_ >= fi_) & sb_, bd16, s1, s2], axis=0).astype(np.float32)
    m0 = np.ones((128, 1), np.float32); m0[0, 0] = 0.0
    c["mask0"] = m0
    return c


class Prog:
    def __init__(self, skip=()):
        self.skip = set(skip)
        nc = bass.Bass("TRN2", target_bir_lowering=False)
        self.nc = nc
        self.kb = KB(nc)
        self.build()

    def din(self, name, shape, dt=F32):
        return self.nc.dram_tensor(name, list(shape), dt, kind="ExternalInput").ap()

    def dout(self, name, shape, dt=F32):
        return self.nc.dram_tensor(name, list(shape), dt, kind="ExternalOutput").ap()

    def build(self):
        nc, kb = self.nc, self.kb
        V, A, P, G = nc.vector, nc.scalar, nc.tensor, nc.gpsimd
        x_lat = self.din("x_lat", [LLAT, D])
        x_ctx = self.din("x_ctx", [2 * LCTX, D])
        cvecT = self.din("cvecT", [128, NCH, 2])
        gvecT = self.din("gvecT", [128, 2 * DEPTH + 1, NCH])
        adabT = self.din("adabT", [128, DEPTH, 48])
        pos = self.din("pos", [LLAT, D])
        identd = self.din("ident", [128, 128])
        ada_w = self.din("ada_w", [DEPTH, D, 6 * D])
        ffn_w_gu = self.din("ffn_w_gu", [DEPTH, D, 2 * DFF])
        ffn_w_down = self.din("ffn_w_down", [DEPTH, DFF, D])
        hytab = self.din("hytab", [6, 512, 512])
        zemb_lat = self.din("zemb_lat", [33, LLAT])
        zemb_ctx = self.din("zemb_ctx", [33, LCTX])
        hydelta = self.din("hydelta", [1, D])
        hytneg = self.din("hytneg", [128, 10])
        mask0 = self.din("mask0", [128, 1])
        hy_w_in = self.din("hy_w_in", [2, D, 3 * D])
        hy_w_out = self.din("hy_w_out", [2, D, D])
        hy_f_w1 = self.din("hy_f_w1", [2, 33, 64])
        hy_f_w2 = self.din("hy_f_w2", [2, 64, 64])
        hy_f_w3 = self.din("hy_f_w3", [2, 64, 4 * D])
        hy_vec64 = self.din("hy_vec64", [2, 64, 3])
        hy_vecT = self.din("hy_vecT", [2, 128, 24 + 72 + 16 + 8])
        gmasks = self.din("gmasks", [9, 128, 128])
        gdn_w_in = self.din("gdn_w_in", [2, D, 4128])
        gdn_w_out = self.din("gdn_w_out", [2, D, D])
        gdn_vecT = self.din("gdn_vecT", [2, 128, 5 * 24 + 1])
        gdn_rows = self.din("gdn_rows", [2, 2, 16])
        state_in = self.din("state_in", [2, 2, 8, 128, 128])
        ns_out = self.dout("ns_out", [2, 2, 2, 8, 128, 128])
        import os
        self.debug = bool(os.environ.get('GDN_DEBUG'))
        dbg = self.dout("dbg", [8, 128, T]) if self.debug else None
        xsp = self.nc.dram_tensor("xsp", [128, NCH * T], F32, kind="Internal").ap()
        y_lat = self.dout("y_lat", [LLAT, D])
        y_ctx = self.dout("y_ctx", [2 * LCTX, D])
        self.dr = dict(locals())

        xT = kb.sb("xT", [128, NCH, T], F32)
        hT = kb.sb("hT", [128, NCH, T], BF16)
        self.xT, self.hT = xT, hT
        self.r_x = [[Res(f"x{c}_{b}") for b in range(NTB)] for c in range(NCH)]
        self.r_h = [[Res(f"h{c}_{b}") for b in range(NTB)] for c in range(NCH)]
        big = kb.sb("big", [128, 33792], BF16)
        self.big = big
        self.r_big = Res("big")
        ident = kb.sb("identf", [128, 128], F32)
        identb = kb.sb("identb", [128, 128], BF16)
        onesb = kb.sb("onesb", [128, 128], BF16)
        self.ident, self.identb, self.onesb = ident, identb, onesb
        r_const = Res("const")
        self.r_const = r_const
        kb.dma("sp", ident[:], identd[:, :], writes=[r_const])
        kb.op("dve", lambda: V.tensor_copy(out=identb[:], in_=ident[:]), reads=[r_const], writes=[r_const])
        kb.op("dve", lambda: V.memset(onesb[:], 1.0), writes=[r_const])
        self.NSLOT = 5
        self.wslots = [kb.sb(f"wslot{i}", [128, 4096], BF16) for i in range(self.NSLOT)]
        self.r_wslot = [Res(f"wslot{i}") for i in range(self.NSLOT)]
        self.wi = 0
        self.NPS = 7
        self.psb = [kb.ps(f"psb{i}", [128, 512], F32) for i in range(8)]
        self.r_ps = [Res(f"ps{i}", excl=True) for i in range(8)]
        self.pi = 0
        self.NTMP = 4
        self.tmp = [kb.sb(f"tmp{i}", [128, 512], F32) for i in range(self.NTMP)]
        self.r_tmp = [Res(f"tmp{i}") for i in range(self.NTMP)]
        self.ti = 0
        self.rstdb = [kb.sb(f"rstd{i}", [128, 512], F32) for i in range(2)]
        self.r_rstd = [Res(f"rstd{i}") for i in range(2)]
        self.ri = 0
        self.tmpb = [kb.sb(f"tmpb{i}", [128, 512], BF16) for i in range(self.NTMP)]
        self.r_tmpb = [Res(f"tmpb{i}") for i in range(self.NTMP)]
        self.tbi = 0
        self.mods = kb.sb("mods", [128, DEPTH, 48, 2], F32)
        self.r_mods = [Res(f"mods{l}") for l in range(DEPTH)]
        self.csil = kb.sb("csil", [128, NCH, 2], F32)
        self.csilb = kb.sb("csilb", [128, NCH, 2], BF16)
        self.r_csil = Res("csil")
        self.small = kb.sb("small", [128, 64, 2], F32)
        self.r_small = Res("small")
        self.gvec = kb.sb("gvec", [128, 2 * DEPTH + 1, NCH], F32)
        self.r_gvec = Res("gvec")
        self.adab = kb.sb("adab", [128, DEPTH, 48], F32)
        self.r_adab = Res("adab")

        self.r_nsout = Res("nsout")
        self.load_inputs()
        self.ada_pending = None
        for l in range(DEPTH):
            if l == 0:
                self.ada(l)
            else:
                self.ada_step(1000)
            if f"mix{l}" not in self.skip:
                if l % 2 == 0:
                    self.gdn_layer(l)
                else:
                    self.hyena_layer(l)
            if l + 1 < DEPTH:
                self.ada_pending = self.ada_gen(l + 1)
            if f"ffn{l}" not in self.skip:
                self.ffn_layer(l)
        self.final_out()
        kb.finish(self.out_res + [self.r_nsout])

    def next_ps(self):
        i = self.pi % self.NPS
        self.pi += 1
        return self.psb[i], self.r_ps[i]

    def next_tmp(self):
        i = self.ti % self.NTMP
        self.ti += 1
        return self.tmp[i], self.r_tmp[i]

    def next_rstd(self):
        i = self.ri % 2
        self.ri += 1
        return self.rstdb[i], self.r_rstd[i]

    def next_tmpb(self):
        i = self.tbi % self.NTMP
        self.tbi += 1
        return self.tmpb[i], self.r_tmpb[i]

    def next_wslot(self):
        i = self.wi % self.NSLOT
        self.wi += 1
        return self.wslots[i], self.r_wslot[i]

    def mm(self, out, lhsT, rhs, start, stop, reads, writes, signal=None):
        P = self.nc.tensor
        return self.kb.op("pe", lambda: P.matmul(out, lhsT, rhs, start=start, stop=stop),
                          reads=reads, writes=writes, signal=(stop if signal is None else signal))

    def load_inputs(self):
        nc, kb = self.nc, self.kb
        V, A, P = nc.vector, nc.scalar, nc.tensor
        d = self.dr
        kb.dma("sp", self.csil[:, :, :], d["cvecT"][:, :, :], writes=[self.r_csil])
        kb.op("act", lambda: A.activation(out=self.csil[:], in_=self.csil[:], func=AF.Silu),
              reads=[self.r_csil], writes=[self.r_csil])
        kb.op("dve", lambda: V.tensor_copy(out=self.csilb[:], in_=self.csil[:]), reads=[self.r_csil], writes=[self.r_csil])
        kb.dma("sp", self.gvec[:, :, :], d["gvecT"][:, :, :], writes=[self.r_gvec])
        kb.dma("sp", self.adab[:, :, :], d["adabT"][:, :, :], writes=[self.r_adab])
        for tt in range(T // 128):
            b = tt // 4
            lat = tt < 8
            src = d["x_lat"][tt * 128:(tt + 1) * 128, :] if lat else d["x_ctx"][(tt - 8) * 128:(tt - 7) * 128, :]
            for half in range(2):
                t0, r0 = self.next_tmp()
                kb.dma("sp", t0[:], src[:, half * 512:(half + 1) * 512], writes=[r0])
                if lat:
                    t1, r1 = self.next_tmp()
                    kb.dma("sp", t1[:], d["pos"][tt * 128:(tt + 1) * 128, half * 512:(half + 1) * 512], writes=[r1])
                    kb.op("dve", lambda: V.tensor_add(out=t0[:], in0=t0[:], in1=t1[:]), reads=[r0, r1], writes=[r0])
                ps, rp = self.next_ps()
                for j in range(4):
                    kb.op("pe", lambda: P.transpose(ps[:, j * 128:(j + 1) * 128], t0[:, j * 128:(j + 1) * 128], self.ident[:]),
                          reads=[r0, self.r_const], writes=[rp], signal=(j == 3))
                cs = half * 4
                kb.op("act", lambda: A.copy(out=self.xT[:, cs:cs + 4, tt * 128:(tt + 1) * 128],
                                            in_=ps[:].rearrange("p (c t) -> p c t", c=4)),
                      reads=[rp], writes=[self.r_x[c][b] for c in range(cs, cs + 4)])

    def ada_gen(self, l):
        nc, kb = self.nc, self.kb
        V, A, P = nc.vector, nc.scalar, nc.tensor
        d = self.dr
        W = 512
        ps, rp = self.psb[7], self.r_ps[7]
        for ti in range(6 * D // W):
            ws, rw = self.next_wslot()
            wf = ws[:, 0:NCH * W].rearrange("p (c n) -> p c n", c=NCH)
            kb.dma("pool", wf, d["ada_w"][l, :, ti * W:(ti + 1) * W].rearrange("(c p) n -> p c n", p=128), writes=[rw])
            for jj in range(W // 128):
                j = ti * (W // 128) + jj
                for c in range(NCH):
                    self.mm(ps[:, 2 * j:2 * j + 2], wf[:, c, jj * 128:(jj + 1) * 128], self.csilb[:, c, :],
                            start=(c == 0), stop=(c == NCH - 1), reads=[rw, self.r_csil], writes=[rp])
            yield
        kb.op("dve", lambda: V.tensor_tensor(out=self.mods[:, l, :, :], in0=ps[:, 0:96].rearrange("p (j w) -> p j w", w=2),
                                             in1=self.adab[:, l, :].unsqueeze(2).to_broadcast([128, 48, 2]), op=ALU.add),
              reads=[rp, self.r_adab], writes=[self.r_mods[l]])

    def ada(self, l):
        for _ in self.ada_gen(l):
            pass

    def ada_step(self, n):
        g = getattr(self, "ada_pending", None)
        if g is None:
            return
        for _ in range(n):
            try:
                next(g)
            except StopIteration:
                self.ada_pending = None
                return

    def mod(self, l, which, idx, c):
        return self.mods[:, l, idx * NCH + c, which:which + 1]

    def norm_mod(self, l, gidx, sh_idx, sc_idx):
        nc, kb = self.nc, self.kb
        V, A, P = nc.vector, nc.scalar, nc.tensor
        gs = self.small[:, 0:NCH, :]
        kb.op("dve", lambda: V.tensor_scalar(out=gs, in0=self.mods[:, l, sc_idx * NCH:(sc_idx + 1) * NCH, :], scalar1=1.0, scalar2=None, op0=ALU.add),
              reads=[self.r_mods[l]], writes=[self.r_small])
        kb.op("dve", lambda: V.tensor_tensor(out=gs, in0=gs, in1=self.gvec[:, gidx, :].unsqueeze(2).to_broadcast([128, NCH, 2]), op=ALU.mult),
              reads=[self.r_gvec, self.r_small], writes=[self.r_small])
        for b in range(NTB):
            w = 0 if b < 2 else 1
            ps, rp = self.next_ps()
            for c in range(NCH):
                sq, rsq = self.next_tmpb()
                kb.op("act", lambda: A.activation(out=sq[:], in_=self.xT[:, c, b * 512:(b + 1) * 512], func=AF.Square),
                      reads=[self.r_x[c][b]], writes=[rsq])
                self.mm(ps[:], self.onesb[:], sq[:], start=(c == 0), stop=(c == NCH - 1), reads=[rsq, self.r_const], writes=[rp], signal=True)
            rstd, rr = self.next_rstd()
            kb.op("dve", lambda: V.tensor_scalar(out=rstd[:], in0=ps[:], scalar1=1.0 / D, scalar2=EPS, op0=ALU.mult, op1=ALU.add),
                  reads=[rp], writes=[rr])
            kb.op("act", lambda: A.activation(out=rstd[:], in_=rstd[:], func=AF.Sqrt), reads=[rr], writes=[rr])
            kb.op("dve", lambda: V.reciprocal(out=rstd[:], in_=rstd[:]), reads=[rr], writes=[rr])
            for c in range(NCH):
                t0, r0 = self.next_tmp()
                kb.op("dve", lambda: V.scalar_tensor_tensor(out=t0[:], in0=self.xT[:, c, b * 512:(b + 1) * 512], scalar=gs[:, c, w:w + 1],
                                                            in1=rstd[:], op0=ALU.mult, op1=ALU.mult),
                      reads=[self.r_x[c][b], rr, self.r_small], writes=[r0])
                kb.op("act", lambda: A.activation(out=self.hT[:, c, b * 512:(b + 1) * 512], in_=t0[:], func=AF.Identity,
                                                  bias=self.mod(l, w, sh_idx, c), scale=1.0),
                      reads=[r0, self.r_mods[l]], writes=[self.r_h[c][b]])

    def ffn_layer(self, l):
        nc, kb = self.nc, self.kb
        V, A, P, G = nc.vector, nc.scalar, nc.tensor, nc.gpsimd
        d = self.dr
        self.norm_mod(l, 2 * l + 1, 3, 4)
        act = self.big[:, 0:NFB * T].rearrange("p (f t) -> p f t", f=NFB)
        r_act = [[Res(f"act{f}_{b}") for b in range(NTB)] for f in range(NFB)]
        for f in range(NFB):
            for b in range(NTB):
                r_act[f][b].w = self.r_big.w
                r_act[f][b].rs = list(self.r_big.rs)
        hreads = lambda b: [self.r_h[c][b] for c in range(NCH)]
        ntile = (DFF + 511) // 512
        for ti in range(ntile):
            c0 = ti * 512
            ncol = min(512, DFF - c0)
            wg, rg = self.next_wslot()
            wgv = wg[:, 0:NCH * ncol].rearrange("p (c n) -> p c n", c=NCH)
            kb.dma("pool", wgv, d["ffn_w_gu"][l, :, c0:c0 + ncol].rearrange("(c p) n -> p c n", p=128), writes=[rg])
            wu, ru = self.next_wslot()
            wuv = wu[:, 0:NCH * ncol].rearrange("p (c n) -> p c n", c=NCH)
            kb.dma("pool", wuv, d["ffn_w_gu"][l, :, DFF + c0:DFF + c0 + ncol].rearrange("(c p) n -> p c n", p=128), writes=[ru])
            self.ada_step(2)
            for jj in range(ncol // 128):
                f = ti * 4 + jj
                for b in range(NTB):
                    pg, rpg = self.next_ps()
                    for c in range(NCH):
                        self.mm(pg[:], wgv[:, c, jj * 128:(jj + 1) * 128], self.hT[:, c, b * 512:(b + 1) * 512],
                                start=(c == 0), stop=(c == NCH - 1), reads=[rg, self.r_h[c][b]], writes=[rpg])
                    pu, rpu = self.next_ps()
                    for c in range(NCH):
                        self.mm(pu[:], wuv[:, c, jj * 128:(jj + 1) * 128], self.hT[:, c, b * 512:(b + 1) * 512],
                                start=(c == 0), stop=(c == NCH - 1), reads=[ru, self.r_h[c][b]], writes=[rpu])
                    sg, rsg = self.next_tmp()
                    kb.op("act", lambda: A.activation(out=sg[:], in_=pg[:], func=AF.Silu), reads=[rpg], writes=[rsg])
                    kb.op("dve", lambda: V.tensor_tensor(out=act[:, f, b * 512:(b + 1) * 512], in0=sg[:], in1=pu[:], op=ALU.mult),
                          reads=[rsg, rpu], writes=[r_act[f][b]])
        for ti in range(D // 128):
            self.ada_step(1)
            ws, rw = self.next_wslot()
            wv = ws[:, 0:NFB * 128].rearrange("p (f n) -> p f n", f=NFB)
            kb.dma("pool", wv, d["ffn_w_down"][l, :, ti * 128:(ti + 1) * 128].rearrange("(f p) n -> p f n", p=128), writes=[rw])
            for jj in range(1):
                c = ti
                for b in range(NTB):
                    w = 0 if b < 2 else 1
                    ps, rp = self.next_ps()
                    for f in range(NFB):
                        self.mm(ps[:], wv[:, f, jj * 128:(jj + 1) * 128], act[:, f, b * 512:(b + 1) * 512],
                                start=(f == 0), stop=(f == NFB - 1), reads=[rw, r_act[f][b]], writes=[rp])
                    xs = self.xT[:, c, b * 512:(b + 1) * 512]
                    kb.op("dve", lambda: V.scalar_tensor_tensor(out=xs, in0=ps[:], scalar=self.mod(l, w, 5, c), in1=xs,
                                                                op0=ALU.mult, op1=ALU.add),
                          reads=[rp, self.r_x[c][b], self.r_mods[l]], writes=[self.r_x[c][b]])
        self.merge_res(self.r_big, [r for rr in r_act for r in rr])

    def merge_res(self, dst, srcs):
        evs = []
        for s in srcs:
            if s.w is not None:
                evs.append(s.w)
            evs.extend(s.rs)
        best = {}
        for ev in evs:
            k = id(ev[0])
            if k not in best or best[k][1] < ev[1]:
                best[k] = ev
        dst.w = None
        dst.rs = list(best.values())

    def gdn_layer(self, l):
        nc, kb = self.nc, self.kb
        V, A, P, G = nc.vector, nc.scalar, nc.tensor, nc.gpsimd
        d = self.dr
        j = l // 2
        DT = F32
        self.norm_mod(l, 2 * l, 0, 1)
        allx = [r for rr in self.r_x for r in rr]
        kb.dma("sp", d["xsp"][:, :], self.xT[:].rearrange("p c t -> p (c t)"), reads=allx)
        R1 = self.xT[:].rearrange("p c t -> p (c t)")
        pos1 = [0]

        def c1(n):
            a = R1[:, pos1[0]:pos1[0] + n]
            pos1[0] += n
            assert pos1[0] <= 12288
            return a
        R2 = self.big
        pos2 = [0]

        def c2(n):
            a = R2[:, pos2[0]:pos2[0] + n]
            pos2[0] += n
            assert pos2[0] <= 33792, pos2[0]
            return a
        PWG = T + 12
        pp = c1(PWG)
        cv = c1(T)
        oT = cv
        qT = c1(T // 2).bitcast(BF16)
        kT = c1(T // 2).bitcast(BF16)
        vT = c1(T // 2).bitcast(BF16)
        gateT = c1(T // 2).bitcast(BF16)
        sqb = pp[:, 2:2 + T // 2].bitcast(BF16)
        ktok = c1(T // 2).bitcast(BF16).rearrange("p (t c) -> p t c", t=12)
        vtok = c1(T // 2).bitcast(BF16).rearrange("p (t c) -> p t c", t=12)
        r_pp, r_cv, r_q, r_k, r_v, r_gate, r_ktok, r_vtok = (Res(n) for n in ("pp", "cv", "q", "k", "v", "gate", "ktok", "vtok"))
        r_sqb = r_pp
        r_oT = [Res(f"oT{t}") for t in range(12)]
        self.inherit([r_pp, r_cv, r_q, r_k, r_v, r_gate, r_sqb, r_ktok, r_vtok] + r_oT, allx)
        og = c2(NCH * T).rearrange("p (h t) -> p h t", h=NCH)
        r_og = [Res(f"og{h}") for h in range(NCH)]
        NCHAIN = 6
        slots = []
        extra_slots = []
        for ci in range(NCHAIN):
            ring = []
            for r_ in range(2):
                wu_ = c2(512).bitcast(F32)
                sl = dict(WU=wu_, WT=wu_[:, 0:128], U=wu_[:, 128:256], qdT=c2(256).bitcast(F32), aqkT=c2(256).bitcast(F32), ktail=c2(256).bitcast(F32),
                          gl=c2(4).bitcast(F32),
                          r=Res(f"slot{ci}_{r_}"))
                ring.append(sl)
            slots.append(ring)
        works = []
        for wi in range(2):
            wk = dict(r=Res(f"work{wi}"))
            PA, PR, PX, PX2 = c2(512).bitcast(F32), c2(512).bitcast(F32), c2(512).bitcast(F32), c2(512).bitcast(F32)
            wk["PA"], wk["PR"], wk["PX"], wk["PX2"] = PA, PR, PX, PX2
            wk["A0"], wk["AT0"] = PA[:, 0:128], PA[:, 128:256]
            wk["R"], wk["RT"] = PR[:, 0:128], PR[:, 128:256]
            wk["Dsym"], wk["egrow"] = PX[:, 0:128], PX[:, 128:256]
            wk["DMs"], wk["DMi"] = PX2[:, 0:128], PX2[:, 128:256]
            wk["kbe"], wk["vb"] = c2(256).bitcast(F32), c2(256).bitcast(F32)
            wk["dd"] = wk["Dsym"]
            wk["GR"] = c1(256)
            wk["cols"] = c2(16).bitcast(F32)
            works.append(wk)
        for wi in range(2):
            wk = dict(r=Res(f"workx{wi}"))
            base = self.wslots[wi]
            PA, PR, PX, PX2 = base[:, 0:512].bitcast(F32), base[:, 512:1024].bitcast(F32), base[:, 1024:1536].bitcast(F32), base[:, 1536:2048].bitcast(F32)
            wk["PA"], wk["PR"], wk["PX"], wk["PX2"] = PA, PR, PX, PX2
            wk["A0"], wk["AT0"] = PA[:, 0:128], PA[:, 128:256]
            wk["R"], wk["RT"] = PR[:, 0:128], PR[:, 128:256]
            wk["Dsym"], wk["egrow"] = PX[:, 0:128], PX[:, 128:256]
            wk["DMs"], wk["DMi"] = PX2[:, 0:128], PX2[:, 128:256]
            wk["kbe"], wk["vb"] = base[:, 2048:2304].bitcast(F32), base[:, 2304:2560].bitcast(F32)
            wk["dd"] = wk["Dsym"]
            wk["GR"] = base[:, 2560:3072].bitcast(F32)
            wk["cols"] = base[:, 3072:3088].bitcast(F32)
            self.inherit([wk["r"]], [self.r_wslot[wi]])
            works.append(wk)
        vnew = [self.wslots[4][:, 1792 + i * 256:1792 + (i + 1) * 256].bitcast(F32) for i in range(NCHAIN)]
        r_vnew = [Res(f"vnew{i}") for i in range(NCHAIN)]
        Sst = [c1(128) for _ in range(NCHAIN)]
        r_S = [Res(f"S{i}") for i in range(NCHAIN)]
        g_tok = c1(192).rearrange("p (t n) -> p t n", t=12)
        b_tok = c1(192).rearrange("p (t n) -> p t n", t=12)
        gccol = c1(192).rearrange("p (a t n) -> p a t n", a=2, t=12)
        rows = c1(32)
        gvec = c1(122)
        for ci in range(2):
            for r_ in range(2):
                wu_ = c1(256)
                sl = dict(WU=wu_, WT=wu_[:, 0:128], U=wu_[:, 128:256], qdT=c1(128), aqkT=c1(128), ktail=c1(128), gl=c1(2), r=Res(f"slotx{ci}_{r_}"))
                slots[ci].append(sl)
                extra_slots.append(sl["r"])
        self.inherit(extra_slots, allx)
        masks = [self.wslots[4][:, 256 + i * 256:256 + (i + 1) * 256].bitcast(F32) for i in range(6)]
        masks += [self.wslots[4][:, 3328 + i * 256:3328 + (i + 1) * 256].bitcast(F32) for i in range(3)]
        r_ab = Res("ab"); r_gpar = Res("gpar"); r_masks = Res("masks")
        allr2 = r_og + [sl["r"] for ring in slots for sl in ring]
        self.inherit(allr2, [self.r_big])
        r_works = [wk["r"] for wk in works]
        self.inherit(r_works, [self.r_big] + allx)
        self.inherit([r_ab, r_gpar] + r_S, allx)
        self.inherit([r_masks] + r_vnew, [self.r_wslot[4]])
        for ci in range(NCHAIN):
            kb.op("dve", lambda: V.memset(vnew[ci], 0.0), writes=[r_vnew[ci]])
        for i in range(9):
            kb.dma("sp", masks[i], d["gmasks"][i], writes=[r_masks])
        m_sl, m_ui, m_su, m_li, Lf, Lb, m_bd16, m_s1, m_s2 = masks
        mpair = [self.wslots[4][:, 256:768].bitcast(F32), self.wslots[4][:, 768:1280].bitcast(F32)]
        kb.dma("sp", gvec[:, 0:121], d["gdn_vecT"][j], writes=[r_gpar])
        kb.dma("sp", rows[:, 0:32], d["gdn_rows"][j:j + 1].rearrange("o a n -> o (a n)").partition_broadcast(128), writes=[r_gpar])
        kb.op("act", lambda: A.activation(out=rows[:, 0:16], in_=rows[:, 0:16], func=AF.Exp), reads=[r_gpar], writes=[r_gpar])
        kb.op("dve", lambda: V.tensor_scalar(out=rows[:, 0:16], in0=rows[:, 0:16], scalar1=-1.0, scalar2=None, op0=ALU.mult), reads=[r_gpar], writes=[r_gpar])
        cwg = lambda k, blk: gvec[:, k * 24 + blk:k * 24 + blk + 1]
        onorm = gvec[:, 120:121]
        wab_s, r_wab = self.wslots[4], self.r_wslot[4]
        wab = wab_s[:, 0:NCH * 32].rearrange("p (c n) -> p c n", c=NCH)
        kb.dma("pool", wab, d["gdn_w_in"][j, :, 4096:4128].rearrange("(c p) n -> p c n", p=128), writes=[r_wab])
        ps, rp = self.next_ps()
        for tt in range(12):
            for c in range(NCH):
                self.mm(ps[:, tt * 32:(tt + 1) * 32], self.hT[:, c, tt * 128:(tt + 1) * 128], wab[:, c, :], start=(c == 0), stop=(c == NCH - 1),
                        reads=[self.r_h[c][tt // 4], r_wab], writes=[rp], signal=(tt == 11 and c == NCH - 1))
        abt, rabt = self.next_tmp()
        ab3 = abt[:, 0:384].rearrange("p (t n) -> p t n", t=12)
        kb.op("dve", lambda: V.tensor_copy(out=abt[:, 0:384], in_=ps[:, 0:384]), reads=[rp], writes=[rabt])
        kb.op("act", lambda: A.activation(out=b_tok, in_=ab3[:, :, 16:32], func=AF.Sigmoid), reads=[rabt], writes=[r_ab])
        kb.op("dve", lambda: V.tensor_tensor(out=g_tok, in0=ab3[:, :, 0:16], in1=rows[:, 16:32].unsqueeze(1).to_broadcast([128, 12, 16]), op=ALU.add),
              reads=[rabt, r_gpar], writes=[r_ab])
        kb.op("act", lambda: A.activation(out=g_tok, in_=g_tok, func=AF.Exp), reads=[r_ab], writes=[r_ab])
        kb.op("act", lambda: A.activation(out=g_tok, in_=g_tok, func=AF.Ln, bias=1.0, scale=1.0), reads=[r_ab], writes=[r_ab])
        kb.op("dve", lambda: V.tensor_tensor(out=g_tok, in0=g_tok, in1=rows[:, 0:16].unsqueeze(1).to_broadcast([128, 12, 16]), op=ALU.mult),
              reads=[r_ab, r_gpar], writes=[r_ab])
        ps, rp = self.next_ps()
        for dr_ in range(2):
            self.mm(ps[:, dr_ * 96:(dr_ + 1) * 96].rearrange("p (t n) -> p t n", t=12), (Lf if dr_ == 0 else Lb), g_tok[:, :, dr_ * 8:(dr_ + 1) * 8], True, True,
                    reads=[r_masks, r_ab], writes=[rp], signal=(dr_ == 1))
        kb.op("dve", lambda: V.tensor_copy(out=gccol.rearrange("p a t n -> p (a t n)"), in_=ps[:, 0:192]), reads=[rp], writes=[r_ab])
        chains = [dict(tiles=list(range(8)), dir=0, seq=None), dict(tiles=list(range(7, -1, -1)), dir=1, seq=None),
                  dict(tiles=[8, 9], dir=0, seq=0), dict(tiles=[9, 8], dir=1, seq=0),
                  dict(tiles=[10, 11], dir=0, seq=1), dict(tiles=[11, 10], dir=1, seq=1)]
        evi = [0]

        def evac(out, in_, reads, writes):
            evi[0] += 1
            if evi[0] % 2 == 0:
                kb.op("act", lambda: A.copy(out=out, in_=in_), reads=reads, writes=writes)
            else:
                kb.op("dve", lambda: V.tensor_copy(out=out, in_=in_), reads=reads, writes=writes)
        wki = [0]
        segs = [(2, LLAT, 0), (1030, LCTX, LLAT), (1290, LCTX, LLAT + LCTX)]
        kb.op("dve", lambda: V.memset(pp, 0.0), writes=[r_pp])
        for h in range(NCH):
            wsl, rws = self.wslots[2 + h % 2], self.r_wslot[2 + h % 2]
            win = wsl[:, 0:4096].rearrange("p (c g n) -> p c g n", c=NCH, g=4)
            for g in range(4):
                kb.dma("pool", win[:, :, g, :], d["gdn_w_in"][j, :, g * D + h * 128:g * D + (h + 1) * 128].rearrange("(c p) n -> p c n", p=128), writes=[rws])
            self.inherit([r_cv], r_oT)
            for g in range(3):
                blk = g * 8 + h
                for b in range(NTB):
                    ps, rp = self.next_ps()
                    for c in range(NCH):
                        self.mm(ps[:], win[:, c, g, :], self.hT[:, c, b * 512:(b + 1) * 512], start=(c == 0), stop=(c == NCH - 1),
                                reads=[rws, self.r_h[c][b]], writes=[rp])
                    if b < 2:
                        evac(pp[:, 2 + b * 512:2 + (b + 1) * 512], ps[:], [rp], [r_pp])
                    else:
                        evac(pp[:, 1030:1286], ps[:, 0:256], [rp], [r_pp])
                        evac(pp[:, 1290:1546], ps[:, 256:512], [rp], [r_pp])
                for (s0, Ls, o0) in segs:
                    co = cv[:, o0:o0 + Ls]
                    kb.op("dve", lambda: V.tensor_scalar(out=co, in0=pp[:, s0 - 2:s0 - 2 + Ls], scalar1=cwg(0, blk), scalar2=None, op0=ALU.mult),
                          reads=[r_pp, r_gpar], writes=[r_cv])
                    for k in range(1, 5):
                        kb.op("dve", lambda: V.scalar_tensor_tensor(out=co, in0=pp[:, s0 - 2 + k:s0 - 2 + k + Ls], scalar=cwg(k, blk), in1=co,
                                                                    op0=ALU.mult, op1=ALU.add), reads=[r_pp, r_gpar, r_cv], writes=[r_cv])
                if g == 2:
                    kb.op("act", lambda: A.activation(out=vT, in_=cv, func=AF.Silu), reads=[r_cv], writes=[r_v])
                else:
                    dst, rdst = (qT, r_q) if g == 0 else (kT, r_k)
                    kb.op("act", lambda: A.activation(out=cv, in_=cv, func=AF.Silu), reads=[r_cv], writes=[r_cv])
                    kb.op("act", lambda: A.activation(out=sqb, in_=cv, func=AF.Square), reads=[r_cv], writes=[r_sqb])
                    for b in range(NTB):
                        ps, rp = self.next_ps()
                        self.mm(ps[:], self.onesb[:], sqb[:, b * 512:(b + 1) * 512], True, True, reads=[r_sqb, self.r_const], writes=[rp])
                        rs, rrs = self.next_rstd()
                        kb.op("dve", lambda: V.tensor_scalar(out=rs[:], in0=ps[:], scalar1=EPS, scalar2=None, op0=ALU.add), reads=[rp], writes=[rrs])
                        kb.op("act", lambda: A.activation(out=rs[:], in_=rs[:], func=AF.Sqrt), reads=[rrs], writes=[rrs])
                        kb.op("dve", lambda: V.reciprocal(out=rs[:], in_=rs[:]), reads=[rrs], writes=[rrs])
                        sc = (128.0 ** -0.5) if g == 0 else 1.0
                        kb.op("dve", lambda: V.scalar_tensor_tensor(out=dst[:, b * 512:(b + 1) * 512], in0=cv[:, b * 512:(b + 1) * 512], scalar=sc, in1=rs[:],
                                                                    op0=ALU.mult, op1=ALU.mult), reads=[r_cv, rrs], writes=[rdst])
            for b in range(NTB):
                ps, rp = self.next_ps()
                for c in range(NCH):
                    self.mm(ps[:], win[:, c, 3, :], self.hT[:, c, b * 512:(b + 1) * 512], start=(c == 0), stop=(c == NCH - 1),
                            reads=[rws, self.r_h[c][b]], writes=[rp])
                kb.op("act", lambda: A.activation(out=gateT[:, b * 512:(b + 1) * 512], in_=ps[:], func=AF.Silu), reads=[rp], writes=[r_gate])
            self.inherit(r_oT, [r_cv])
            for (src, rsrc, dst, rdst) in ((kT, r_k, ktok, r_ktok), (vT, r_v, vtok, r_vtok)):
                for g4 in range(3):
                    ps, rp = self.next_ps()
                    psb = ps[:].bitcast(BF16)
                    for i in range(4):
                        tt = g4 * 4 + i
                        kb.op("pe", lambda: P.transpose(psb[:, i * 128:(i + 1) * 128], src[:, tt * 128:(tt + 1) * 128], self.identb[:]),
                              reads=[rsrc, self.r_const], writes=[rp], signal=(i == 3))
                    evac(dst[:, g4 * 4:(g4 + 1) * 4, :], psb[:, 0:512].rearrange("p (t c) -> p t c", t=4), [rp], [rdst])
            for ci, ch in enumerate(chains):
                if ch["seq"] is None:
                    kb.dma("sp", Sst[ci], d["state_in"][j, ch["dir"], h], writes=[r_S[ci]])
                else:
                    kb.op("dve", lambda: V.memset(Sst[ci], 0.0), writes=[r_S[ci]])

            def intra(ci, step):
                ch = chains[ci]
                tt = ch["tiles"][step]
                dr_ = ch["dir"]
                sl = slots[ci][step % len(slots[ci])]
                rsl = sl["r"]
                wk = works[wki[0] % len(works)]
                wki[0] += 1
                rw = wk["r"]
                tcs = slice(tt * 128, (tt + 1) * 128)
                gcc = gccol[:, dr_, tt, h:h + 1]
                bc = b_tok[:, tt, dr_ * 8 + h:dr_ * 8 + h + 1]
                lastA, lastB = (63, 127) if dr_ == 0 else (0, 64)
                m_s = m_sl if dr_ == 0 else m_su
                m_i = m_ui if dr_ == 0 else m_li
                cols = wk["cols"]
                yield
                ps, rp = self.next_ps()
                self.mm(ps[:, 0:128], kT[:, tcs], kT[:, tcs], True, True, reads=[r_k], writes=[rp], signal=False)
                self.mm(ps[:, 128:256], kT[:, tcs], qT[:, tcs], True, True, reads=[r_k, r_q], writes=[rp])
                evac(wk["GR"], ps[:, 0:256], [rp], [rw])
                yield
                ps2, rp2 = self.next_ps()
                kb.op("pe", lambda: P.transpose(ps2[:, 0:128], gcc.to_broadcast([128, 128]), self.ident[:]), reads=[r_ab, self.r_const], writes=[rp2])
                kb.op("dve", lambda: V.tensor_scalar(out=wk["dd"], in0=ps2[:, 0:128], scalar1=gcc, scalar2=None, op0=ALU.subtract), reads=[rp2, r_ab], writes=[rw])
                kb.op("dve", lambda: V.scalar_tensor_tensor(out=wk["dd"], in0=wk["dd"], scalar=-1.0, in1=wk["dd"], op0=ALU.mult, op1=ALU.max), reads=[rw], writes=[rw])
                kb.op("act", lambda: A.activation(out=wk["Dsym"], in_=wk["dd"], func=AF.Exp, scale=-1.0), reads=[rw], writes=[rw])
                kb.op("act", lambda: A.activation(out=wk["egrow"], in_=ps2[:, 0:128], func=AF.Exp), reads=[rp2], writes=[rw])
                kb.op("dve", lambda: V.tensor_copy(out=cols[0:64, 3:4], in_=ps2[0:64, lastA:lastA + 1]), reads=[rp2], writes=[rw])
                kb.op("dve", lambda: V.tensor_copy(out=cols[64:128, 3:4], in_=ps2[64:128, lastB:lastB + 1]), reads=[rp2], writes=[rw])
                kb.op("act", lambda: A.activation(out=cols[:, 0:1], in_=gcc, func=AF.Exp), reads=[r_ab], writes=[rw])
                kb.op("dve", lambda: V.tensor_tensor(out=cols[:, 1:2], in0=cols[:, 0:1], in1=bc, op=ALU.mult), reads=[rw, r_ab], writes=[rw])
                kb.op("act", lambda: A.activation(out=cols[:, 2:3], in_=gcc, func=AF.Exp, bias=cols[:, 3:4], scale=-1.0), reads=[rw, r_ab], writes=[rw])
                kb.op("act", lambda: A.copy(out=sl["gl"][:, 0:1], in_=wk["egrow"][:, lastA:lastA + 1]), reads=[rw], writes=[rsl])
                kb.op("act", lambda: A.copy(out=sl["gl"][:, 1:2], in_=wk["egrow"][:, lastB:lastB + 1]), reads=[rw], writes=[rsl])
                mp = mpair[dr_]
                id2 = self.ident[:].unsqueeze(1).to_broadcast([128, 2, 128])
                v3 = lambda ap: ap.rearrange("p (a c) -> p a c", a=2)
                kb.op("pool", lambda: G.tensor_tensor(out=v3(wk["PX2"]), in0=wk["Dsym"].unsqueeze(1).to_broadcast([128, 2, 128]), in1=v3(mp), op=ALU.mult),
                      reads=[rw, r_masks], writes=[rw])
                yield
                kb.op("dve", lambda: V.scalar_tensor_tensor(out=wk["A0"], in0=wk["GR"][:, 0:128], scalar=bc, in1=wk["DMs"], op0=ALU.mult, op1=ALU.mult),
                      reads=[rw, r_ab], writes=[rw])
                yield
                kb.op("dve", lambda: V.tensor_tensor(out=sl["aqkT"], in0=wk["GR"][:, 128:256], in1=wk["DMi"], op=ALU.mult), reads=[rw], writes=[rsl])
                yield
                kb.op("dve", lambda: V.tensor_tensor(out=sl["qdT"], in0=qT[:, tcs], in1=wk["egrow"], op=ALU.mult), reads=[rw, r_q], writes=[rsl])
                yield
                kb.op("act", lambda: A.activation(out=wk["kbe"], in_=ktok[:, tt, :], func=AF.Copy, scale=cols[:, 1:2]), reads=[rw, r_ktok], writes=[rw])
                yield
                kb.op("act", lambda: A.activation(out=wk["vb"], in_=vtok[:, tt, :], func=AF.Copy, scale=bc), reads=[rw, r_vtok, r_ab], writes=[rw])
                yield
                kb.op("act", lambda: A.activation(out=sl["ktail"], in_=ktok[:, tt, :], func=AF.Copy, scale=cols[:, 2:3]), reads=[rw, r_ktok], writes=[rsl])
                yield
                ps3, rp3 = self.next_ps()
                kb.op("pe", lambda: P.transpose(ps3[:, 0:128], wk["A0"], self.ident[:]), reads=[rw, self.r_const], writes=[rp3])
                kb.op("act", lambda: A.copy(out=wk["AT0"], in_=ps3[:, 0:128]), reads=[rp3], writes=[rw])
                PX, PX2, PR, PA = wk["PX"], wk["PX2"], wk["PR"], wk["PA"]
                bd2 = m_bd16.unsqueeze(1).to_broadcast([128, 2, 128])
                kb.op("pool", lambda: G.tensor_tensor(out=v3(PX), in0=v3(PA), in1=bd2, op=ALU.mult), reads=[rw, r_masks], writes=[rw])
                yield
                kb.op("dve", lambda: V.tensor_tensor(out=v3(PR), in0=id2, in1=v3(PX), op=ALU.subtract), reads=[rw, self.r_const], writes=[rw])
                for it in range(3):
                    X, XT = PX[:, 0:128], PX[:, 128:256]
                    yield
                    psA, rpA = self.next_ps()
                    self.mm(psA[:, 0:128], XT, X, True, True, reads=[rw], writes=[rpA], signal=False)
                    self.mm(psA[:, 128:256], X, XT, True, True, reads=[rw], writes=[rpA])
                    kb.op("act", lambda: A.copy(out=PX2, in_=psA[:, 0:256]), reads=[rpA], writes=[rw])
                    X2, XT2 = PX2[:, 0:128], PX2[:, 128:256]
                    yield
                    psB, rpB = self.next_ps()
                    self.mm(psB[:, 0:128], XT2, PR[:, 0:128], True, True, reads=[rw], writes=[rpB], signal=False)
                    self.mm(psB[:, 128:256], X2, PR[:, 128:256], True, True, reads=[rw], writes=[rpB])
                    kb.op("dve", lambda: V.tensor_tensor(out=PR, in0=PR, in1=psB[:, 0:256], op=ALU.add), reads=[rpB, rw], writes=[rw])
                    PX, PX2 = PX2, PX
                for msk in (m_s1, m_s2):
                    ms2 = msk.unsqueeze(1).to_broadcast([128, 2, 128])
                    kb.op("pool", lambda: G.tensor_tensor(out=v3(PX), in0=v3(PA), in1=ms2, op=ALU.mult), reads=[rw, r_masks], writes=[rw])
                    X, XT = PX[:, 0:128], PX[:, 128:256]
                    yield
                    ps1, rp1 = self.next_ps()
                    self.mm(ps1[:, 0:128], XT, PR[:, 0:128], True, True, reads=[rw], writes=[rp1], signal=False)
                    self.mm(ps1[:, 128:256], X, PR[:, 128:256], True, True, reads=[rw], writes=[rp1])
                    kb.op("act", lambda: A.copy(out=PX2, in_=ps1[:, 0:256]), reads=[rp1], writes=[rw])
                    yield
                    ps2_, rp2_ = self.next_ps()
                    self.mm(ps2_[:, 0:128], PR[:, 128:256], PX2[:, 0:128], True, True, reads=[rw], writes=[rp2_], signal=False)
                    self.mm(ps2_[:, 128:256], PR[:, 0:128], PX2[:, 128:256], True, True, reads=[rw], writes=[rp2_])
                    kb.op("dve", lambda: V.tensor_tensor(out=PR, in0=PR, in1=ps2_[:, 0:256], op=ALU.subtract), reads=[rp2_, rw], writes=[rw])
                yield
                psW, rpW = self.next_ps()
                self.mm(psW[:, 0:128], wk["kbe"], wk["RT"], True, True, reads=[rw], writes=[rpW], signal=False)
                self.mm(psW[:, 128:256], wk["RT"], wk["vb"], True, True, reads=[rw], writes=[rpW])
                kb.op("act", lambda: A.copy(out=sl["WU"], in_=psW[:, 0:256]), reads=[rpW], writes=[rsl])

            def scan(ci, step):
                ch = chains[ci]
                tt = ch["tiles"][step]
                sl = slots[ci][step % len(slots[ci])]
                rsl = sl["r"]
                tcs = slice(tt * 128, (tt + 1) * 128)
                for half in ((0, 1) if ch["dir"] == 0 else (1, 0)):
                    hs = slice(half * 64, (half + 1) * 64)
                    hcs = slice(tt * 128 + half * 64, tt * 128 + (half + 1) * 64)
                    yield
                    psP, rpP = self.next_ps()
                    self.mm(psP[:, 0:128], sl["WT"], Sst[ci], True, True, reads=[rsl, r_S[ci]], writes=[rpP])
                    kb.op("dve", lambda: V.tensor_tensor(out=vnew[ci][hs, :], in0=sl["U"][hs, :], in1=psP[hs, 0:128], op=ALU.subtract),
                          reads=[rsl, rpP], writes=[r_vnew[ci]])
                    yield
                    psO, rpO = self.next_ps()
                    self.mm(psO[:, 0:64], Sst[ci], sl["qdT"][:, hs], True, False, reads=[rsl, r_S[ci]], writes=[rpO])
                    self.mm(psO[:, 0:64], vnew[ci][hs, :], sl["aqkT"][hs, hs], False, True, reads=[rsl, r_vnew[ci]], writes=[rpO])
                    psS, rpS = self.next_ps()
                    self.mm(psS[:, 0:128], sl["ktail"][hs, :], vnew[ci][hs, :], True, True, reads=[rsl, r_vnew[ci]], writes=[rpS])
                    okey = (tt, half)
                    if okey not in owritten:
                        owritten.add(okey)
                        kb.op("act", lambda: A.copy(out=oT[:, hcs], in_=psO[:, 0:64]), reads=[rpO], writes=[r_oT[tt]])
                    else:
                        kb.op("dve", lambda: V.tensor_tensor(out=oT[:, hcs], in0=oT[:, hcs], in1=psO[:, 0:64], op=ALU.add), reads=[rpO, r_oT[tt]], writes=[r_oT[tt]])
                    kb.op("dve", lambda: V.scalar_tensor_tensor(out=Sst[ci], in0=Sst[ci], scalar=sl["gl"][:, half:half + 1], in1=psS[:, 0:128], op0=ALU.mult, op1=ALU.add),
                          reads=[rpS, rsl, r_S[ci]], writes=[r_S[ci]])
                if ch["seq"] is not None and step == len(ch["tiles"]) - 1:
                    kb.dma("sp", d["ns_out"][ch["seq"], j, ch["dir"], h], Sst[ci], reads=[r_S[ci]], writes=[self.r_nsout])

            owritten = set()

            def run_interleaved(gens):
                active = list(gens)
                while active:
                    for g_ in list(active):
                        try:
                            next(g_)
                        except StopIteration:
                            active.remove(g_)
            NW = len(works)
            nI = [0] * NCHAIN
            nS = [0] * NCHAIN
            nT = [len(chains[ci]["tiles"]) for ci in range(NCHAIN)]
            while any(nS[ci] < nT[ci] for ci in range(NCHAIN)):
                gens = []
                scan_ready = [ci for ci in range(NCHAIN) if nS[ci] < nI[ci]]
                cand = []
                for ci in range(NCHAIN):
                    st = nI[ci]
                    while st < nT[ci] and st - nS[ci] < len(slots[ci]):
                        cand.append((st - nS[ci], ci, st))
                        st += 1
                cand.sort()
                picked = cand[:NW]
                for ci in scan_ready:
                    gens.append(scan(ci, nS[ci]))
                for (_, ci, st) in picked:
                    gens.append(intra(ci, st))
                run_interleaved(gens)
                for (_, ci, st) in picked:
                    nI[ci] = max(nI[ci], st + 1)
                for ci in scan_ready:
                    nS[ci] += 1
            kb.op("act", lambda: A.activation(out=sqb, in_=oT, func=AF.Square), reads=r_oT, writes=[r_sqb])
            for b in range(NTB):
                ps, rp = self.next_ps()
                self.mm(ps[:], self.onesb[:], sqb[:, b * 512:(b + 1) * 512], True, True, reads=[r_sqb, self.r_const], writes=[rp])
                rs, rrs = self.next_rstd()
                kb.op("dve", lambda: V.tensor_scalar(out=rs[:], in0=ps[:], scalar1=1.0 / 128.0, scalar2=EPS, op0=ALU.mult, op1=ALU.add), reads=[rp], writes=[rrs])
                kb.op("act", lambda: A.activation(out=rs[:], in_=rs[:], func=AF.Sqrt), reads=[rrs], writes=[rrs])
                kb.op("dve", lambda: V.reciprocal(out=rs[:], in_=rs[:]), reads=[rrs], writes=[rrs])
                t0, r0 = self.next_tmp()
                kb.op("dve", lambda: V.scalar_tensor_tensor(out=t0[:], in0=oT[:, b * 512:(b + 1) * 512], scalar=onorm, in1=rs[:], op0=ALU.mult, op1=ALU.mult),
                      reads=r_oT[b * 4:(b + 1) * 4] + [rrs, r_gpar], writes=[r0])
                kb.op("dve", lambda: V.tensor_tensor(out=og[:, h, b * 512:(b + 1) * 512], in0=t0[:], in1=gateT[:, b * 512:(b + 1) * 512], op=ALU.mult),
                      reads=[r0, r_gate], writes=[r_og[h]])
        self.inherit(allx, [r_pp, r_cv, r_q, r_k, r_v, r_gate, r_sqb, r_ktok, r_vtok, r_ab, r_gpar] + r_oT + r_S + r_works + extra_slots)
        if False:
            for b in range(NTB):
                t0, r0 = self.next_tmp()
                kb.op("dve", lambda: V.tensor_copy(out=t0[:], in_=og[:, 0, b * 512:(b + 1) * 512]), reads=[r_og[0]], writes=[r0])
                kb.dma("sp", d["dbg"][5, :, b * 512:(b + 1) * 512], t0[:], reads=[r0], writes=[self.r_nsout])
        self.inherit([self.r_wslot[0]], [works[2]["r"]])
        self.inherit([self.r_wslot[1]], [works[3]["r"]])
        self.out_proj(l, d["gdn_w_out"][j], og, r_og, None, None)
        if False:
            pass
        self.inherit([self.r_big], allr2 + r_works)
        self.inherit([self.r_wslot[4]], [r_masks] + r_vnew)


    def out_proj(self, l, wdram, src, r_src, bias_fn, r_bias):
        nc, kb = self.nc, self.kb
        V, A, P = nc.vector, nc.scalar, nc.tensor
        d = self.dr
        allx = [r for rr in self.r_x for r in rr]
        kb.dma("sp", self.xT[:].rearrange("p c t -> p (c t)"), d["xsp"][:, :], writes=allx)
        wo = []
        for hlf in range(2):
            wv = self.wslots[hlf][:, 0:NCH * 512].rearrange("p (c n) -> p c n", c=NCH)
            kb.dma("pool", wv, wdram[:, hlf * 512:(hlf + 1) * 512].rearrange("(c p) n -> p c n", p=128), writes=[self.r_wslot[hlf]])
            wo.append(wv)
        for c in range(NCH):
            for b in range(NTB):
                w = 0 if b < 2 else 1
                ps, rp = self.next_ps()
                for cb in range(NCH):
                    self.mm(ps[:], wo[c // 4][:, cb, (c % 4) * 128:(c % 4 + 1) * 128], src[:, cb, b * 512:(b + 1) * 512],
                            start=(cb == 0), stop=(cb == NCH - 1), reads=[self.r_wslot[c // 4], r_src[cb]], writes=[rp])
                t0, r0 = self.next_tmp()
                if bias_fn is not None:
                    kb.op("dve", lambda: V.tensor_scalar(out=t0[:], in0=ps[:], scalar1=bias_fn(c), scalar2=self.mod(l, w, 2, c), op0=ALU.add, op1=ALU.mult),
                          reads=[rp, r_bias, self.r_mods[l]], writes=[r0])
                else:
                    kb.op("dve", lambda: V.tensor_scalar(out=t0[:], in0=ps[:], scalar1=self.mod(l, w, 2, c), scalar2=None, op0=ALU.mult),
                          reads=[rp, self.r_mods[l]], writes=[r0])
                xs = self.xT[:, c, b * 512:(b + 1) * 512]
                kb.op("dve", lambda: V.tensor_tensor(out=xs, in0=xs, in1=t0[:], op=ALU.add), reads=[r0, self.r_x[c][b]], writes=[self.r_x[c][b]])

    def inherit(self, dsts, srcs):
        evs = []
        for sres in srcs:
            if sres.w is not None:
                evs.append(sres.w)
            evs.extend(sres.rs)
        best = {}
        for ev in evs:
            k = id(ev[0])
            if k not in best or best[k][1] < ev[1]:
                best[k] = ev
        for dres in dsts:
            own = ([dres.w] if dres.w is not None else []) + list(dres.rs)
            b2 = dict(best)
            for ev in own:
                k = id(ev[0])
                if k not in b2 or b2[k][1] < ev[1]:
                    b2[k] = ev
            dres.w = None
            dres.rs = list(b2.values())

    def sin3(self, buf, rbuf, npart, ncol):
        nc, kb = self.nc, self.kb
        V, A = nc.vector, nc.scalar
        q, rq = self.next_tmp()
        bv = buf[0:npart, 0:ncol]
        qv = q[0:npart, 0:ncol]
        kb.op("act", lambda: A.activation(out=bv, in_=bv, func=AF.Sin, scale=1.0 / 3.0), reads=[rbuf], writes=[rbuf])
        kb.op("dve", lambda: V.tensor_tensor(out=qv, in0=bv, in1=bv, op=ALU.mult), reads=[rbuf], writes=[rq])
        kb.op("dve", lambda: V.tensor_scalar(out=qv, in0=qv, scalar1=-4.0, scalar2=3.0, op0=ALU.mult, op1=ALU.add), reads=[rq], writes=[rq])
        kb.op("dve", lambda: V.tensor_tensor(out=bv, in0=bv, in1=qv, op=ALU.mult), reads=[rq, rbuf], writes=[rbuf])

    def hyena_layer(self, l):
        nc, kb = self.nc, self.kb
        V, A, P, G = nc.vector, nc.scalar, nc.tensor, nc.gpsimd
        d = self.dr
        j = l // 2
        self.norm_mod(l, 2 * l, 0, 1)
        allx = [r for rr in self.r_x for r in rr]
        kb.dma("sp", d["xsp"][:, :], self.xT[:].rearrange("p c t -> p (c t)"), reads=allx)
        R1 = self.xT[:].rearrange("p c t -> p (c t)")
        PW = 1544
        pp = [R1[:, i * PW:(i + 1) * PW] for i in range(3)]
        o1 = 3 * PW
        u = [R1[:, o1 + i * T:o1 + (i + 1) * T] for i in range(3)]
        o1 += 3 * T
        zt = R1[:, o1:o1 + 768].bitcast(BF16).rearrange("p (t c) -> p t c", t=12)
        hidb = R1[:, o1 + 768:o1 + 768 + 640].bitcast(BF16)
        r_hidb = Res("hidb")
        r_pp = [Res(f"pp{i}") for i in range(3)]
        r_u = [Res(f"u{i}") for i in range(3)]
        r_zt = Res("zt")
        self.inherit(r_pp + r_u + [r_zt, r_hidb], allx)
        R2 = self.big
        zfin = R2[:, 0:NCH * T].rearrange("p (c t) -> p c t", c=NCH)
        o2 = NCH * T
        hid_lat = R2[:, o2:o2 + 2048].bitcast(F32); o2 += 2048
        hid_ctx = R2[:, o2:o2 + 512].bitcast(F32); o2 += 512
        delta_b = R2[:, o2:o2 + 2048].bitcast(F32); o2 += 2048
        fv = {}
        for nm in ("A0", "B0", "hf0", "hf1", "nhf1", "hb0", "nhb0", "hb1"):
            fv[nm] = R2[:, o2:o2 + 512].rearrange("p (q c) -> p q c", q=4); o2 += 512
        for nm in ("Ac", "Bc"):
            fv[nm] = R2[:, o2:o2 + 256].rearrange("p (q c) -> p q c", q=2); o2 += 256
        spec = R2[:, o2:o2 + 12288].bitcast(F32)
        o2 += 12288
        assert o2 <= 33792
        SP = [spec[:, i * 512:(i + 1) * 512] for i in range(10)]
        Yb = spec[:, 5120:6144].bitcast(BF16)
        Y = [Yb[:, i * 512:(i + 1) * 512] for i in range(4)]
        r_SP = [Res(f"sp{i}") for i in range(10)]
        r_Y = [Res(f"Y{i}") for i in range(4)]
        r_zfin = [Res(f"zfin{c}") for c in range(NCH)]
        r_hid = Res("hid"); r_delta = Res("delta"); r_fv = Res("fv")
        self.inherit(r_SP + r_Y + r_zfin + [r_hid, r_delta, r_fv], [self.r_big])
        if not hasattr(self, "hy_alloc"):
            self.hy_alloc = dict(
                vecT=kb.sb("s_hyvecT", [128, 120], F32), vec64=kb.sb("s_hyvec64", [64, 4], F32),
                w1=kb.sb("s_hyw1", [33, 64], F32), w2=kb.sb("s_hyw2", [64, 64], F32),
                tneg=kb.sb("s_hytneg", [128, 10], F32), mask0=kb.sb("s_hymask0", [128, 1], F32),
                w3cb=[kb.sb(f"s_hyw3cb_{i}", [64, 4, 128], F32) for i in range(2)],
                r_w3cb=[Res("w3cb0"), Res("w3cb1")], r_par=Res("hypar"))
        ha = self.hy_alloc
        vecT, vec64, w1, w2, tneg, mask0, w3cb, r_w3cb, r_par = (ha[k] for k in
            ("vecT", "vec64", "w1", "w2", "tneg", "mask0", "w3cb", "r_w3cb", "r_par"))
        kb.dma("sp", vecT[:], d["hy_vecT"][j], writes=[r_par])
        kb.dma("sp", vec64[:, 0:3], d["hy_vec64"][j], writes=[r_par])
        kb.dma("sp", w1[:], d["hy_f_w1"][j], writes=[r_par])
        kb.dma("sp", w2[:], d["hy_f_w2"][j], writes=[r_par])
        kb.dma("sp", tneg[:], d["hytneg"][:, :], writes=[r_par])
        kb.dma("sp", mask0[:], d["mask0"][:, :], writes=[r_par])
        kb.dma("sp", delta_b, d["hydelta"][0:1, :].partition_broadcast(128), writes=[r_delta])
        kb.op("dve", lambda: V.tensor_tensor(out=vec64[:, 3:4], in0=vec64[:, 0:1], in1=vec64[:, 2:3], op=ALU.mult), reads=[r_par], writes=[r_par])
        b_in = lambda blk: vecT[:, blk:blk + 1]
        cw = lambda k, blk: vecT[:, 24 + k * 24 + blk:24 + k * 24 + blk + 1]
        skp = lambda n, c: vecT[:, 96 + n * 8 + c:96 + n * 8 + c + 1]
        b_out = lambda c: vecT[:, 112 + c:113 + c]
        tabs = []
        for i in range(6):
            sl = self.wslots[i // 2]
            tv = sl[:, (i % 2) * 2048:(i % 2 + 1) * 2048].rearrange("p (q k) -> p q k", q=4)
            kb.dma("pool", tv, d["hytab"][i].rearrange("(q p) k -> p q k", p=128), writes=[self.r_wslot[i // 2]])
            tabs.append(tv)
        Cf, Sf, Cr, Sr, Ci, Si = tabs
        r_tab = [self.r_wslot[0], self.r_wslot[0], self.r_wslot[1], self.r_wslot[1], self.r_wslot[2], self.r_wslot[2]]
        rCf, rSf, rCr, rSr, rCi, rSi = r_tab
        for (zname, L, hid) in (("zemb_lat", LLAT, hid_lat), ("zemb_ctx", LCTX, hid_ctx)):
            for c0 in range(0, L, 512):
                ncol = min(512, L - c0)
                ze, rze = self.next_tmp()
                kb.dma("sp", ze[0:33, 0:ncol], d[zname][:, c0:c0 + ncol], writes=[rze])
                ps, rp = self.next_ps()
                self.mm(ps[0:64, 0:ncol], w1[:, :], ze[0:33, 0:ncol], True, True, reads=[rze, r_par], writes=[rp])
                h1, rh1 = self.next_tmp()
                kb.op("dve", lambda: V.tensor_scalar(out=h1[0:64, 0:ncol], in0=ps[0:64, 0:ncol], scalar1=vec64[:, 0:1], scalar2=vec64[:, 2:3],
                                                     op0=ALU.add, op1=ALU.mult), reads=[rp, r_par], writes=[rh1])
                self.sin3(h1, rh1, 64, ncol)
                ps2, rp2 = self.next_ps()
                self.mm(ps2[0:64, 0:ncol], w2[:, :], h1[0:64, 0:ncol], True, True, reads=[rh1, r_par], writes=[rp2])
                hv = hid[0:64, c0:c0 + ncol]
                kb.op("dve", lambda: V.tensor_scalar(out=hv, in0=ps2[0:64, 0:ncol], scalar1=vec64[:, 1:2], scalar2=vec64[:, 2:3],
                                                     op0=ALU.add, op1=ALU.mult), reads=[rp2, r_par], writes=[r_hid])
                kb.op("act", lambda: A.activation(out=hv, in_=hv, func=AF.Sin, scale=1.0 / 3.0), reads=[r_hid], writes=[r_hid])
                q, rq = self.next_tmp()
                qv = q[0:64, 0:ncol]
                kb.op("dve", lambda: V.tensor_tensor(out=qv, in0=hv, in1=hv, op=ALU.mult), reads=[r_hid], writes=[rq])
                kb.op("dve", lambda: V.tensor_scalar(out=qv, in0=qv, scalar1=-4.0, scalar2=3.0, op0=ALU.mult, op1=ALU.add), reads=[rq], writes=[rq])
                kb.op("dve", lambda: V.tensor_tensor(out=hv, in0=hv, in1=qv, op=ALU.mult), reads=[rq, r_hid], writes=[r_hid])
        kb.op("act", lambda: A.copy(out=hidb[0:64, 0:LLAT], in_=hid_lat[0:64, :]), reads=[r_hid], writes=[r_hidb])
        kb.op("act", lambda: A.copy(out=hidb[0:64, LLAT:LLAT + LCTX], in_=hid_ctx[0:64, :]), reads=[r_hid], writes=[r_hidb])
        for i in range(3):
            kb.op("dve", lambda: V.memset(pp[i], 0.0), writes=[r_pp[i]])
        segs = [(1, LLAT, 0), (1027, LCTX, LLAT), (1285, LCTX, LLAT + LCTX)]
        evi = [0]

        def evac(out, in_, reads, writes):
            evi[0] += 1
            if evi[0] % 2 == 0:
                kb.op("act", lambda: A.copy(out=out, in_=in_), reads=reads, writes=writes)
            else:
                kb.op("dve", lambda: V.tensor_copy(out=out, in_=in_), reads=reads, writes=writes)

        def spectrum(dst, rdst, terms):
            ps, rp = self.next_ps()
            tot = sum(t[4] for t in terms)
            for kb_ in range(4):
                i = 0
                for (tab, rtab, dat, rdat, nq) in terms:
                    for q in range(nq):
                        self.mm(ps[:, kb_ * 128:(kb_ + 1) * 128], tab[:, q, kb_ * 128:(kb_ + 1) * 128], dat[:, q, :],
                                start=(i == 0), stop=(i == tot - 1), reads=[rtab, rdat], writes=[rp], signal=(kb_ == 3 and i == tot - 1))
                        i += 1
            evac(dst, ps[:], [rp], [rdst])

        def cprod(dst, rdst, terms):
            acc, racc = self.next_tmp()
            n = len(terms)
            for i, (a, ra, b, rb, sg) in enumerate(terms):
                if i == 0:
                    kb.op("dve", lambda: V.tensor_tensor(out=acc[:], in0=a, in1=b, op=ALU.mult), reads=[ra, rb], writes=[racc])
                else:
                    t2, r2 = self.next_tmp()
                    kb.op("dve", lambda: V.tensor_tensor(out=t2[:], in0=a, in1=b, op=ALU.mult), reads=[ra, rb], writes=[r2])
                    o = dst if i == n - 1 else acc[:]
                    ro = rdst if i == n - 1 else racc
                    kb.op("dve", lambda: V.tensor_tensor(out=o, in0=acc[:], in1=t2[:], op=(ALU.add if sg > 0 else ALU.subtract)),
                          reads=[racc, r2], writes=[ro])

        for cb in range(NCH):
            wsl = self.wslots[3 + cb % 2]
            rws = self.r_wslot[3 + cb % 2]
            win = wsl[:, 0:NCH * 3 * 128].rearrange("p (c g n) -> p c g n", c=NCH, g=3)
            for g in range(3):
                kb.dma("pool", win[:, :, g, :], d["hy_w_in"][j, :, g * D + cb * 128:g * D + (cb + 1) * 128].rearrange("(c p) n -> p c n", p=128), writes=[rws])
            w3 = w3cb[cb % 2]
            rw3 = r_w3cb[cb % 2]
            w3 = w3[:].rearrange("r g c -> r (g c)").bitcast(BF16)[:, 0:512].rearrange("r (g c) -> r g c", g=4)
            kb.dma("pool", w3, d["hy_f_w3"][j].rearrange("r (g c) -> r g c", g=4)[:, :, cb * 128:(cb + 1) * 128], writes=[rw3])
            for g in range(3):
                blk = g * 8 + cb
                for b in range(NTB):
                    ps, rp = self.next_ps()
                    for c in range(NCH):
                        self.mm(ps[:], win[:, c, g, :], self.hT[:, c, b * 512:(b + 1) * 512], start=(c == 0), stop=(c == NCH - 1),
                                reads=[rws, self.r_h[c][b]], writes=[rp])
                    if b < 2:
                        dsts = [(pp[g][:, 1 + b * 512:1 + (b + 1) * 512], ps[:])]
                    else:
                        dsts = [(pp[g][:, 1027:1027 + 256], ps[:, 0:256]), (pp[g][:, 1285:1285 + 256], ps[:, 256:512])]
                    for (o, i_) in dsts:
                        kb.op("act", lambda: A.activation(out=o, in_=i_, func=AF.Identity, bias=b_in(blk), scale=1.0),
                              reads=[rp, r_par], writes=[r_pp[g]])
                for (s0, Ls, o0) in segs:
                    uo = u[g][:, o0:o0 + Ls]
                    kb.op("dve", lambda: V.tensor_scalar(out=uo, in0=pp[g][:, s0 - 1:s0 - 1 + Ls], scalar1=cw(0, blk), scalar2=None, op0=ALU.mult),
                          reads=[r_pp[g], r_par], writes=[r_u[g]])
                    for k in (1, 2):
                        kb.op("dve", lambda: V.scalar_tensor_tensor(out=uo, in0=pp[g][:, s0 - 1 + k:s0 - 1 + k + Ls], scalar=cw(k, blk), in1=uo,
                                                                    op0=ALU.mult, op1=ALU.add), reads=[r_pp[g], r_par, r_u[g]], writes=[r_u[g]])
            for n in range(2):
                z, rz = u[n], r_u[n]
                for g4 in range(3):
                    ps, rp = self.next_ps()
                    for i in range(4):
                        tt = g4 * 4 + i
                        kb.op("pe", lambda: P.transpose(ps[:, i * 128:(i + 1) * 128], z[:, tt * 128:(tt + 1) * 128], self.ident[:]),
                              reads=[rz, self.r_const], writes=[rp], signal=(i == 3))
                    evac(zt[:, g4 * 4:(g4 + 1) * 4, :], ps[:].rearrange("p (t c) -> p t c", t=4), [rp], [r_zt])
                for q in range(10):
                    lat = q < 8
                    hidv = hidb[0:64, q * 128:(q + 1) * 128]
                    ps, rp = self.next_ps()
                    for dr_ in range(2):
                        self.mm(ps[:, dr_ * 128:(dr_ + 1) * 128], hidv, w3[:, n * 2 + dr_, :], True, True, reads=[r_hidb, rw3], writes=[rp], signal=(dr_ == 1))
                    win_, rwin = self.next_tmp()
                    kb.op("act", lambda: A.activation(out=win_[:, 0:128], in_=delta_b[:, cb * 128:(cb + 1) * 128], func=AF.Exp, scale=tneg[:, q:q + 1]),
                          reads=[r_delta, r_par], writes=[rwin])
                    hh, rhh = self.next_tmp()
                    kb.op("dve", lambda: V.tensor_tensor(out=hh[:, 0:256].rearrange("p (a c) -> p a c", a=2), in0=ps[:, 0:256].rearrange("p (a c) -> p a c", a=2),
                                                         in1=win_[:, 0:128].unsqueeze(1).to_broadcast([128, 2, 128]), op=ALU.mult),
                          reads=[rp, rwin], writes=[rhh])
                    hf = hh[:, 0:128]
                    hb = hh[:, 128:256]
                    if q == 0 or q == 8:
                        kb.op("dve", lambda: V.tensor_scalar(out=hb, in0=hb, scalar1=mask0[:, 0:1], scalar2=None, op0=ALU.mult), reads=[rhh, r_par], writes=[rhh])
                    def wr(nm, qq, fn):
                        kb.op("dve", fn(fv[nm][:, qq, :]), reads=[rhh], writes=[r_fv])
                    if q < 4 or q >= 8:
                        nmA, nmB, qq = ("A0", "B0", q) if lat else ("Ac", "Bc", q - 8)
                        kb.op("dve", lambda: V.tensor_tensor(out=fv[nmA][:, qq, :], in0=hf, in1=hb, op=ALU.add), reads=[rhh], writes=[r_fv])
                        kb.op("dve", lambda: V.tensor_tensor(out=fv[nmB][:, qq, :], in0=hb, in1=hf, op=ALU.subtract), reads=[rhh], writes=[r_fv])
                        if lat:
                            kb.op("act", lambda: A.copy(out=fv["hf0"][:, q, :], in_=hf), reads=[rhh], writes=[r_fv])
                            kb.op("act", lambda: A.copy(out=fv["hb0"][:, q, :], in_=hb), reads=[rhh], writes=[r_fv])
                            kb.op("act", lambda: A.mul(out=fv["nhb0"][:, q, :], in_=hb, mul=-1.0), reads=[rhh], writes=[r_fv])
                    else:
                        kb.op("act", lambda: A.copy(out=fv["hf1"][:, q - 4, :], in_=hf), reads=[rhh], writes=[r_fv])
                        kb.op("act", lambda: A.copy(out=fv["hb1"][:, q - 4, :], in_=hb), reads=[rhh], writes=[r_fv])
                        kb.op("act", lambda: A.mul(out=fv["nhf1"][:, q - 4, :], in_=hf, mul=-1.0), reads=[rhh], writes=[r_fv])
                zl = [zt[:, 0:4, :], zt[:, 4:8, :]]
                spectrum(SP[0], r_SP[0], [(Cf, rCf, zl[0], r_zt, 4)])
                spectrum(SP[1], r_SP[1], [(Sf, rSf, zl[0], r_zt, 4)])
                spectrum(SP[2], r_SP[2], [(Cf, rCf, zl[1], r_zt, 4)])
                spectrum(SP[3], r_SP[3], [(Sf, rSf, zl[1], r_zt, 4)])
                spectrum(SP[4], r_SP[4], [(Cf, rCf, fv["A0"], r_fv, 4)])
                spectrum(SP[5], r_SP[5], [(Sf, rSf, fv["B0"], r_fv, 4)])
                spectrum(SP[6], r_SP[6], [(Cf, rCf, fv["hf1"], r_fv, 4), (Cr, rCr, fv["hf0"], r_fv, 4)])
                spectrum(SP[7], r_SP[7], [(Sf, rSf, fv["nhf1"], r_fv, 4), (Sr, rSr, fv["hf0"], r_fv, 4)])
                spectrum(SP[8], r_SP[8], [(Cf, rCf, fv["hb1"], r_fv, 4), (Cr, rCr, fv["hb0"], r_fv, 4)])
                spectrum(SP[9], r_SP[9], [(Sf, rSf, fv["hb1"], r_fv, 4), (Sr, rSr, fv["nhb0"], r_fv, 4)])
                S_ = lambda i: (SP[i], r_SP[i])
                def T4(a, b, sg):
                    return (SP[a], r_SP[a], SP[b], r_SP[b], sg)
                cprod(Y[0], r_Y[0], [T4(0, 4, 1), T4(1, 5, 1), T4(2, 8, 1), T4(3, 9, 1)])
                cprod(Y[1], r_Y[1], [T4(0, 5, 1), T4(1, 4, -1), T4(2, 9, 1), T4(3, 8, -1)])
                cprod(Y[2], r_Y[2], [T4(0, 6, 1), T4(1, 7, 1), T4(2, 4, 1), T4(3, 5, 1)])
                cprod(Y[3], r_Y[3], [T4(0, 7, 1), T4(1, 6, -1), T4(2, 5, 1), T4(3, 4, -1)])

                def inverse_and_gate(yre, ryre, yim, ryim, ncol, col0):
                    ps, rp = self.next_ps()
                    yre3 = yre.rearrange("p (q c) -> p q c", q=4)
                    yim3 = yim.rearrange("p (q c) -> p q c", q=4)
                    for q in range(4):
                        self.mm(ps[:, 0:ncol], yre3[:, q, :], Ci[:, q, 0:ncol], start=(q == 0), stop=False, reads=[ryre, rCi], writes=[rp])
                    for q in range(4):
                        self.mm(ps[:, 0:ncol], yim3[:, q, :], Si[:, q, 0:ncol], start=False, stop=(q == 3), reads=[ryim, rSi], writes=[rp])
                    t0, r0 = self.next_tmp()
                    zs = z[:, col0:col0 + ncol]
                    kb.op("dve", lambda: V.scalar_tensor_tensor(out=t0[:, 0:ncol], in0=zs, scalar=skp(n, cb), in1=ps[:, 0:ncol], op0=ALU.mult, op1=ALU.add),
                          reads=[rz, rp, r_par], writes=[r0])
                    un = u[n + 1][:, col0:col0 + ncol]
                    kb.op("dve", lambda: V.tensor_tensor(out=un, in0=t0[:, 0:ncol], in1=un, op=ALU.mult), reads=[r0, r_u[n + 1]], writes=[r_u[n + 1]])

                inverse_and_gate(Y[0], r_Y[0], Y[1], r_Y[1], 512, 0)
                inverse_and_gate(Y[2], r_Y[2], Y[3], r_Y[3], 512, 512)
                zc = [zt[:, 8:10, :], zt[:, 10:12, :]]
                spectrum(SP[0], r_SP[0], [(Cf, rCf, zc[0], r_zt, 2)])
                spectrum(SP[1], r_SP[1], [(Sf, rSf, zc[0], r_zt, 2)])
                spectrum(SP[2], r_SP[2], [(Cf, rCf, zc[1], r_zt, 2)])
                spectrum(SP[3], r_SP[3], [(Sf, rSf, zc[1], r_zt, 2)])
                spectrum(SP[4], r_SP[4], [(Cf, rCf, fv["Ac"], r_fv, 2)])
                spectrum(SP[5], r_SP[5], [(Sf, rSf, fv["Bc"], r_fv, 2)])
                cprod(Y[0], r_Y[0], [T4(0, 4, 1), T4(1, 5, 1)])
                cprod(Y[1], r_Y[1], [T4(0, 5, 1), T4(1, 4, -1)])
                cprod(Y[2], r_Y[2], [T4(2, 4, 1), T4(3, 5, 1)])
                cprod(Y[3], r_Y[3], [T4(2, 5, 1), T4(3, 4, -1)])
                inverse_and_gate(Y[0], r_Y[0], Y[1], r_Y[1], 256, LLAT)
                inverse_and_gate(Y[2], r_Y[2], Y[3], r_Y[3], 256, LLAT + LCTX)
            kb.op("act", lambda: A.copy(out=zfin[:, cb, :], in_=u[2]), reads=[r_u[2]], writes=[r_zfin[cb]])
        self.inherit(allx, r_pp + r_u + [r_zt, r_hidb])
        self.out_proj(l, d["hy_w_out"][j], zfin, r_zfin, b_out, r_par)
        self.inherit([self.r_big], r_SP + r_Y + r_zfin + [r_hid, r_delta, r_fv])

    def final_out(self):
        nc, kb = self.nc, self.kb
        V, A, P = nc.vector, nc.scalar, nc.tensor
        d = self.dr
        self.out_res = []
        gidx = 2 * DEPTH
        for b in range(NTB):
            ps, rp = self.next_ps()
            for c in range(NCH):
                sq, rsq = self.next_tmpb()
                kb.op("act", lambda: A.activation(out=sq[:], in_=self.xT[:, c, b * 512:(b + 1) * 512], func=AF.Square),
                      reads=[self.r_x[c][b]], writes=[rsq])
                self.mm(ps[:], self.onesb[:], sq[:], start=(c == 0), stop=(c == NCH - 1), reads=[rsq, self.r_const], writes=[rp], signal=True)
            rstd, rr = self.next_rstd()
            kb.op("dve", lambda: V.tensor_scalar(out=rstd[:], in0=ps[:], scalar1=1.0 / D, scalar2=EPS, op0=ALU.mult, op1=ALU.add),
                  reads=[rp], writes=[rr])
            kb.op("act", lambda: A.activation(out=rstd[:], in_=rstd[:], func=AF.Sqrt), reads=[rr], writes=[rr])
            kb.op("dve", lambda: V.reciprocal(out=rstd[:], in_=rstd[:]), reads=[rr], writes=[rr])
            for c in range(NCH):
                xs = self.xT[:, c, b * 512:(b + 1) * 512]
                kb.op("dve", lambda: V.scalar_tensor_tensor(out=xs, in0=xs, scalar=self.gvec[:, gidx, c:c + 1],
                                                            in1=rstd[:], op0=ALU.mult, op1=ALU.mult),
                      reads=[self.r_x[c][b], rr, self.r_gvec], writes=[self.r_x[c][b]])
        for tt in range(T // 128):
            b = tt // 4
            lat = tt < 8
            dst = d["y_lat"][tt * 128:(tt + 1) * 128, :] if lat else d["y_ctx"][(tt - 8) * 128:(tt - 7) * 128, :]
            for half in range(2):
                ps, rp = self.next_ps()
                for j in range(4):
                    c = half * 4 + j
                    kb.op("pe", lambda: P.transpose(ps[:, j * 128:(j + 1) * 128], self.xT[:, c, tt * 128:(tt + 1) * 128], self.ident[:]),
                          reads=[self.r_x[c][b], self.r_const], writes=[rp], signal=(j == 3))
                t0, r0 = self.next_tmp()
                if half == 0:
                    kb.op("act", lambda: A.copy(out=t0[:], in_=ps[:]), reads=[rp], writes=[r0])
                else:
                    kb.op("dve", lambda: V.tensor_copy(out=t0[:], in_=ps[:]), reads=[rp], writes=[r0])
                ro = Res("out")
                kb.dma("sp", dst[:, half * 512:(half + 1) * 512], t0[:], reads=[r0], writes=[ro])
                self.out_res.append(ro)


_CACHE = {}
LAST_DBG = None


def get_prog(skip=()):
    key = tuple(sorted(skip))
    if key not in _CACHE:
        _CACHE[key] = Prog(skip)
    return _CACHE[key]


def make_in_maps(inp, n=8):
    hc = host_consts()
    f = lambda a: np.ascontiguousarray(np.asarray(a, dtype=np.float32))
    shared = {
        "pos": hc["pos"], "ident": hc["ident"],
        "ada_w": f(inp["ada_w"]),
        "ffn_w_gu": f(inp["ffn_w_gu"]), "ffn_w_down": f(inp["ffn_w_down"]),
    }
    shared["gmasks"] = hc["gmasks"]
    for kk in ("gdn_w_in", "gdn_w_out"):
        shared[kk] = f(inp[kk])
    gv = []
    for j in range(2):
        gv.append(np.concatenate([f(inp["gdn_conv"][j]).reshape(5 * 24, 128).T, f(inp["gdn_onorm"][j]).reshape(128, 1)], axis=1))
    shared["gdn_vecT"] = np.ascontiguousarray(np.stack(gv, axis=0))
    shared["gdn_rows"] = np.ascontiguousarray(np.stack([f(inp["gdn_a_log"]).reshape(2, 16), f(inp["gdn_dt_bias"]).reshape(2, 16)], axis=1))
    for kk in ("hytab", "zemb_lat", "zemb_ctx", "hydelta", "hytneg", "mask0"):
        shared[kk] = hc[kk]
    for kk in ("hy_w_in", "hy_w_out", "hy_f_w1", "hy_f_w2", "hy_f_w3"):
        shared[kk] = f(inp[kk])
    shared["hy_vec64"] = np.ascontiguousarray(np.stack([f(inp["hy_f_b1"]), f(inp["hy_f_b2"]), f(inp["hy_freq"])], axis=2))
    hv = []
    for j in range(2):
        parts = [f(inp["hy_b_in"][j]).reshape(24, 128).T]
        parts.append(f(inp["hy_conv"][j]).reshape(3 * 24, 128).T)
        parts.append(f(inp["hy_skip"][j]).reshape(2 * 8, 128).T)
        parts.append(f(inp["hy_b_out"][j]).reshape(8, 128).T)
        hv.append(np.concatenate(parts, axis=1))
    shared["hy_vecT"] = np.ascontiguousarray(np.stack(hv, axis=0))
    fm = lambda v: np.ascontiguousarray(f(v).reshape(-1, 128).T)
    gl = []
    for l in range(DEPTH):
        gl.append(fm(inp["norm1_g"][l]))
        gl.append(fm(inp["norm2_g"][l]))
    gl.append(fm(inp["final_g"]))
    shared["gvecT"] = np.ascontiguousarray(np.stack(gl, axis=1))
    shared["adabT"] = np.ascontiguousarray(np.stack([fm(inp["ada_b"][l]) for l in range(DEPTH)], axis=1))
    maps = []
    xp = f(inp["x_prompt"])
    xs = f(inp["x_sample"])
    cc = f(inp["c"])
    cctx = f(inp["c_ctx"])
    for i in range(n):
        m = dict(shared)
        m["x_lat"] = np.ascontiguousarray(xs[i])
        m["x_ctx"] = np.ascontiguousarray(xp[2 * i:2 * i + 2].reshape(2 * LCTX, D))
        m["state_in"] = np.ascontiguousarray(f(inp["state_delta"])[i])
        m["cvecT"] = np.ascontiguousarray(np.stack([fm(cc[i]), fm(cctx)], axis=2))
        maps.append(m)
    return maps


def run(inp, skip=(), trace=False):
    prog = get_prog(skip)
    maps = make_in_maps(inp)
    res = run_bass_kernel_spmd(prog.nc, maps, core_ids=list(range(8)), trace=trace)
    y_lat = np.stack([r["y_lat"] for r in res.results], axis=0)
    y_ctx = np.concatenate([r["y_ctx"].reshape(2, LCTX, D) for r in res.results], axis=0)
    ns = np.concatenate([r["ns_out"] for r in res.results], axis=0)
    global LAST_DBG
    LAST_DBG = [r.get("dbg") for r in res.results]
    return (y_ctx.astype(np.float32), y_lat.astype(np.float32), ns.astype(np.float32)), res


def kernel(**inputs):
    (y_prompt, y_sample, new_state), _ = run(inputs)
    return (y_prompt, y_sample, new_state)
```
